# Optimizing a Trainium2 kernel written in Bass

```python
import math
import jax, jax.numpy as jnp
from jax import lax
import numpy as np

D_MODEL = 1024
BATCH = 8
SEQ = 4096
DEPTH = 4

PLE_DIM = 256
D_FF = 2816
N_MIXERS = 2
CONV_KERNEL = 31
HEAD_DIM = 64
N_HEADS = D_MODEL // HEAD_DIM
N_KV_GROUPS = 4
HEADS_PER_GROUP = N_HEADS // N_KV_GROUPS
Q_DIM = N_HEADS * HEAD_DIM
KV_DIM = N_KV_GROUPS * HEAD_DIM
NSA_IN = Q_DIM + 6 * KV_DIM + 3 * N_HEADS
CMP_LEN = 32
CMP_STRIDE = 16
CMP_HIDDEN = 256
SEL_BLOCK = 64
N_SELECT = 16
WINDOW = 512
N_BUCKETS = 32
MAX_EXACT = 16
MAX_DISTANCE = 2048
Q_BLOCK = 128
SEL_Q_CHUNK = 32
RMS_EPS = 1e-6
LN_EPS = 1e-5
FORCE_SCORE = 1e9

kernel_name = "hybrid_conformer_conv_nsa_macaron_trunk"


def rms_norm(x, g):
    xf = x.astype(jnp.float32)
    y = xf * lax.rsqrt(jnp.mean(xf * xf, axis=-1, keepdims=True) + RMS_EPS)
    return (y * g.astype(jnp.float32)).astype(x.dtype)


def swiglu(x, wg, wu, wd):
    return (jax.nn.silu(x @ wg) * (x @ wu)) @ wd


def t5_bucket(dist):
    n = jnp.maximum(dist, 0)
    nf = jnp.maximum(n, 1).astype(jnp.float32)
    large = MAX_EXACT + (jnp.log(nf / MAX_EXACT) / math.log(MAX_DISTANCE / MAX_EXACT)
                         * (N_BUCKETS - MAX_EXACT)).astype(jnp.int32)
    large = jnp.minimum(large, N_BUCKETS - 1)
    return jnp.where(n < MAX_EXACT, n, large)


def masked_softmax(s, mask):
    s = jnp.where(mask, s.astype(jnp.float32), -1e30)
    m = jnp.max(s, axis=-1, keepdims=True)
    e = jnp.exp(s - m) * mask
    return e / jnp.maximum(jnp.sum(e, axis=-1, keepdims=True), 1e-30)


def conformer_conv(h, w_pw1, b_pw1, w_dw, b_dw, ln_g, ln_b, w_pw2, b_pw2):
    u = h @ w_pw1 + b_pw1
    a, g = jnp.split(u, 2, axis=-1)
    u = a * jax.nn.sigmoid(g)
    u = lax.conv_general_dilated(
        u, w_dw[:, None, :].astype(u.dtype), window_strides=(1,),
        padding=[(CONV_KERNEL - 1, 0)], dimension_numbers=('NWC', 'WIO', 'NWC'),
        feature_group_count=D_MODEL) + b_dw
    uf = u.astype(jnp.float32)
    mu = jnp.mean(uf, axis=-1, keepdims=True)
    var = jnp.mean(jnp.square(uf - mu), axis=-1, keepdims=True)
    uf = (uf - mu) * lax.rsqrt(var + LN_EPS) * ln_g.astype(jnp.float32) + ln_b.astype(jnp.float32)
    u = jax.nn.silu(uf).astype(h.dtype)
    return u @ w_pw2 + b_pw2


def compress_blocks(k, pos, w1, w2):
    B, S = k.shape[0], k.shape[1]
    n_cmp = (S - CMP_LEN) // CMP_STRIDE + 1
    idx = jnp.arange(n_cmp)[:, None] * CMP_STRIDE + jnp.arange(CMP_LEN)[None, :]
    blk = k[:, idx] + pos[None, None, :, None, :]
    blk = blk.transpose(0, 1, 3, 2, 4).reshape(B, n_cmp, N_KV_GROUPS, CMP_LEN * HEAD_DIM)
    return jax.nn.gelu(blk @ w1) @ w2


def nsa_mixer(h, w_in, pos_k, pos_v, wk1, wk2, wv1, wv2, w_out, rel_bias):
    B, S, _ = h.shape
    G, R = N_KV_GROUPS, HEADS_PER_GROUP
    scale = HEAD_DIM ** -0.5
    proj = h @ w_in
    q = proj[..., :Q_DIM].reshape(B, S, G, R, HEAD_DIM)
    kvs = proj[..., Q_DIM:Q_DIM + 6 * KV_DIM].reshape(B, S, 6, G, HEAD_DIM)
    k_c, v_c, k_s, v_s, k_w, v_w = (kvs[:, :, j] for j in range(6))
    gates = jax.nn.sigmoid(proj[..., Q_DIM + 6 * KV_DIM:].astype(jnp.float32))
    gates = gates.reshape(B, S, G, R, 3).astype(h.dtype)

    n_qb = S // Q_BLOCK
    n_cmp = (S - CMP_LEN) // CMP_STRIDE + 1
    n_sel = S // SEL_BLOCK
    k_sel = min(N_SELECT, n_sel)
    q_blocks = q.reshape(B, n_qb, Q_BLOCK, G, R, HEAD_DIM).transpose(1, 0, 2, 3, 4, 5)
    qb_ids = jnp.arange(n_qb)

    kc = compress_blocks(k_c, pos_k, wk1, wk2)
    vc = compress_blocks(v_c, pos_v, wv1, wv2)
    cmp_start = jnp.arange(n_cmp) * CMP_STRIDE
    cmp_end = cmp_start + CMP_LEN - 1
    sel_start = jnp.arange(n_sel) * SEL_BLOCK
    overlap = ((cmp_start[:, None] < sel_start[None, :] + SEL_BLOCK)
               & (cmp_start[:, None] + CMP_LEN > sel_start[None, :])).astype(jnp.float32)

    def cmp_step(args):
        qb, bid = args
        t = bid * Q_BLOCK + jnp.arange(Q_BLOCK)
        dist = t[:, None] - cmp_end[None, :]
        bias = rel_bias[t5_bucket(dist)].reshape(Q_BLOCK, n_cmp, G, R).transpose(2, 3, 0, 1)
        s = jnp.einsum('bqgrd,bcgd->bgrqc', qb, kc).astype(jnp.float32) * scale + bias.astype(jnp.float32)
        pr = masked_softmax(s, dist >= 0)
        o = jnp.einsum('bgrqc,bcgd->bqgrd', pr.astype(vc.dtype), vc)
        imp = jnp.einsum('bgrqc,cn->bgqn', pr, overlap)
        j = jnp.arange(n_sel)[None, :]
        cur = (t // SEL_BLOCK)[:, None]
        valid = j * SEL_BLOCK <= t[:, None]
        forced = (j == 0) | (j == cur) | (j == cur - 1)
        score = jnp.where(forced, FORCE_SCORE, jnp.where(valid, imp, -1.0))
        _, sel = lax.top_k(score, k_sel)
        return o, sel.astype(jnp.int32)

    o_c, sel_idx = lax.map(cmp_step, (q_blocks, qb_ids))
    o_c = o_c.transpose(1, 0, 2, 3, 4, 5).reshape(B, S, G, R, HEAD_DIM)
    sel_idx = sel_idx.transpose(1, 2, 0, 3, 4).reshape(B, G, S, k_sel)

    kb = k_s.reshape(B, n_sel, SEL_BLOCK, G, HEAD_DIM).transpose(0, 3, 1, 2, 4)
    vb = v_s.reshape(B, n_sel, SEL_BLOCK, G, HEAD_DIM).transpose(0, 3, 1, 2, 4)
    n_qs = S // SEL_Q_CHUNK
    q_chunks = q.reshape(B, n_qs, SEL_Q_CHUNK, G, R, HEAD_DIM).transpose(1, 0, 2, 3, 4, 5)
    idx_chunks = sel_idx.reshape(B, G, n_qs, SEL_Q_CHUNK, k_sel).transpose(2, 0, 1, 3, 4)
    table_g = rel_bias.reshape(N_BUCKETS, G, R).transpose(1, 0, 2)
    gather = jax.vmap(jax.vmap(lambda blocks, ids: blocks[ids]))
    n_keys = k_sel * SEL_BLOCK

    def sel_step(args):
        qc, ic, cid = args
        ks = gather(kb, ic).reshape(B, G, SEL_Q_CHUNK, n_keys, HEAD_DIM)
        vs = gather(vb, ic).reshape(B, G, SEL_Q_CHUNK, n_keys, HEAD_DIM)
        kpos = (ic[..., None] * SEL_BLOCK + jnp.arange(SEL_BLOCK)).reshape(B, G, SEL_Q_CHUNK, n_keys)
        t = cid * SEL_Q_CHUNK + jnp.arange(SEL_Q_CHUNK)
        dist = t[None, None, :, None] - kpos
        bias = jax.vmap(lambda tb, bk: tb[bk], in_axes=(0, 1), out_axes=1)(table_g, t5_bucket(dist))
        bias = bias.transpose(0, 1, 4, 2, 3)
        s = jnp.einsum('bqgrd,bgqnd->bgrqn', qc, ks).astype(jnp.float32) * scale + bias.astype(jnp.float32)
        pr = masked_softmax(s, (dist >= 0)[:, :, None])
        return jnp.einsum('bgrqn,bgqnd->bqgrd', pr.astype(vs.dtype), vs)

    o_s = lax.map(sel_step, (q_chunks, idx_chunks, jnp.arange(n_qs)))
    o_s = o_s.transpose(1, 0, 2, 3, 4, 5).reshape(B, S, G, R, HEAD_DIM)

    kw_pad = jnp.pad(k_w, ((0, 0), (WINDOW, 0), (0, 0), (0, 0)))
    vw_pad = jnp.pad(v_w, ((0, 0), (WINDOW, 0), (0, 0), (0, 0)))
    slab = Q_BLOCK + WINDOW

    def win_step(args):
        qb, bid = args
        s0 = bid * Q_BLOCK
        ks = lax.dynamic_slice_in_dim(kw_pad, s0, slab, axis=1)
        vs = lax.dynamic_slice_in_dim(vw_pad, s0, slab, axis=1)
        t = s0 + jnp.arange(Q_BLOCK)
        kpos = s0 - WINDOW + jnp.arange(slab)
        dist = t[:, None] - kpos[None, :]
        mask = (dist >= 0) & (dist < WINDOW) & (kpos[None, :] >= 0)
        bias = rel_bias[t5_bucket(dist)].reshape(Q_BLOCK, slab, G, R).transpose(2, 3, 0, 1)
        s = jnp.einsum('bqgrd,bkgd->bgrqk', qb, ks).astype(jnp.float32) * scale + bias.astype(jnp.float32)
        pr = masked_softmax(s, mask)
        return jnp.einsum('bgrqk,bkgd->bqgrd', pr.astype(vs.dtype), vs)

    o_w = lax.map(win_step, (q_blocks, qb_ids))
    o_w = o_w.transpose(1, 0, 2, 3, 4, 5).reshape(B, S, G, R, HEAD_DIM)

    o = gates[..., 0, None] * o_c + gates[..., 1, None] * o_s + gates[..., 2, None] * o_w
    return o.reshape(B, S, Q_DIM) @ w_out


def setup_inputs(seed: int = 0) -> dict:
    key = jax.random.key(seed)
    ks = iter(jax.random.split(key, 48))
    n_conv = (DEPTH + 1) // 2
    n_nsa = DEPTH // 2
    f32 = jnp.float32

    def w(shape, fan_in):
        return jax.random.normal(next(ks), shape, f32) * (fan_in ** -0.5)

    def gain(shape):
        return 1.0 + 0.02 * jax.random.normal(next(ks), shape, f32)

    def small(shape):
        return 0.02 * jax.random.normal(next(ks), shape, f32)

    return {
        "x": jax.random.normal(next(ks), (BATCH, SEQ, D_MODEL), f32),
        "p": jax.random.normal(next(ks), (DEPTH, BATCH, SEQ, PLE_DIM), f32),
        "rel_bias": 0.5 * jax.random.normal(next(ks), (N_BUCKETS, N_HEADS), f32),
        "ffn1_norm": gain((DEPTH, D_MODEL)),
        "ffn1_w_gate": w((DEPTH, D_MODEL, D_FF), D_MODEL),
        "ffn1_w_up": w((DEPTH, D_MODEL, D_FF), D_MODEL),
        "ffn1_w_down": w((DEPTH, D_FF, D_MODEL), D_FF),
        "mix_norm": gain((DEPTH, D_MODEL)),
        "ffn2_norm": gain((DEPTH, D_MODEL)),
        "ffn2_w_gate": w((DEPTH, D_MODEL, D_FF), D_MODEL),
        "ffn2_w_up": w((DEPTH, D_MODEL, D_FF), D_MODEL),
        "ffn2_w_down": w((DEPTH, D_FF, D_MODEL), D_FF),
        "ple_norm": gain((DEPTH, D_MODEL)),
        "ple_w_gate": w((DEPTH, D_MODEL, D_MODEL), D_MODEL),
        "ple_w_in": w((DEPTH, PLE_DIM, D_MODEL), PLE_DIM),
        "conv_w_pw1": w((n_conv, D_MODEL, 2 * D_MODEL), D_MODEL),
        "conv_b_pw1": small((n_conv, 2 * D_MODEL)),
        "conv_w_dw": w((n_conv, CONV_KERNEL, D_MODEL), CONV_KERNEL),
        "conv_b_dw": small((n_conv, D_MODEL)),
        "conv_ln_g": gain((n_conv, D_MODEL)),
        "conv_ln_b": small((n_conv, D_MODEL)),
        "conv_w_pw2": w((n_conv, D_MODEL, D_MODEL), D_MODEL),
        "conv_b_pw2": small((n_conv, D_MODEL)),
        "nsa_w_in": w((n_nsa, D_MODEL, NSA_IN), D_MODEL),
        "nsa_cmp_pos_k": small((n_nsa, CMP_LEN, HEAD_DIM)),
        "nsa_cmp_pos_v": small((n_nsa, CMP_LEN, HEAD_DIM)),
        "nsa_cmp_wk1": w((n_nsa, CMP_LEN * HEAD_DIM, CMP_HIDDEN), CMP_LEN * HEAD_DIM),
        "nsa_cmp_wk2": w((n_nsa, CMP_HIDDEN, HEAD_DIM), CMP_HIDDEN),
        "nsa_cmp_wv1": w((n_nsa, CMP_LEN * HEAD_DIM, CMP_HIDDEN), CMP_LEN * HEAD_DIM),
        "nsa_cmp_wv2": w((n_nsa, CMP_HIDDEN, HEAD_DIM), CMP_HIDDEN),
        "nsa_w_out": w((n_nsa, Q_DIM, D_MODEL), Q_DIM),
        "final_norm": gain((D_MODEL,)),
    }


def reference(x, p, rel_bias, ffn1_norm, ffn1_w_gate, ffn1_w_up, ffn1_w_down, mix_norm,
              ffn2_norm, ffn2_w_gate, ffn2_w_up, ffn2_w_down, ple_norm, ple_w_gate, ple_w_in,
              conv_w_pw1, conv_b_pw1, conv_w_dw, conv_b_dw, conv_ln_g, conv_ln_b, conv_w_pw2, conv_b_pw2,
              nsa_w_in, nsa_cmp_pos_k, nsa_cmp_pos_v, nsa_cmp_wk1, nsa_cmp_wk2, nsa_cmp_wv1, nsa_cmp_wv2,
              nsa_w_out, final_norm):
    for i in range(DEPTH):
        x = x + 0.5 * swiglu(rms_norm(x, ffn1_norm[i]), ffn1_w_gate[i], ffn1_w_up[i], ffn1_w_down[i])
        h = rms_norm(x, mix_norm[i])
        j = i // N_MIXERS
        if i % N_MIXERS == 0:
            x = x + conformer_conv(h, conv_w_pw1[j], conv_b_pw1[j], conv_w_dw[j], conv_b_dw[j],
                                   conv_ln_g[j], conv_ln_b[j], conv_w_pw2[j], conv_b_pw2[j])
        else:
            x = x + nsa_mixer(h, nsa_w_in[j], nsa_cmp_pos_k[j], nsa_cmp_pos_v[j], nsa_cmp_wk1[j],
                              nsa_cmp_wk2[j], nsa_cmp_wv1[j], nsa_cmp_wv2[j], nsa_w_out[j], rel_bias)
        x = x + 0.5 * swiglu(rms_norm(x, ffn2_norm[i]), ffn2_w_gate[i], ffn2_w_up[i], ffn2_w_down[i])
        gate = jax.nn.sigmoid(rms_norm(x, ple_norm[i]) @ ple_w_gate[i])
        x = x + gate * (p[i] @ ple_w_in[i])
    return rms_norm(x, final_norm)
```

```python
import contextlib
import os
import math
import numpy as np
import concourse.bass as bass
import concourse.mybir as mybir
from concourse.bass_utils import run_bass_kernel_spmd

F32 = mybir.dt.float32
BF16 = mybir.dt.bfloat16
AF = mybir.ActivationFunctionType
ALU = mybir.AluOpType

S = 4096
D = 1024
DFF = 2816
NT = 8
TW_ = 512
ENGS = ["pe", "act", "dve", "pool", "sp"]
DMA_RING = {"sp": 8, "act": 2, "pool": 6, "pe": 2, "dve": 2}


class Tk:
    __slots__ = ("w", "r")

    def __init__(self):
        self.w = None
        self.r = []


def tks(n):
    return [Tk() for _ in range(n)]


class Op:
    __slots__ = ("eng", "fn", "deps", "is_dma", "need_sig", "sigval", "ev")

    def __init__(self, eng, fn, is_dma):
        self.eng = eng
        self.fn = fn
        self.deps = []
        self.is_dma = is_dma
        self.need_sig = False
        self.sigval = None
        self.ev = None


class Prog:
    def __init__(self, nc):
        self.nc = nc
        self.ops = {e: [] for e in ENGS}
        self.last = {e: None for e in ENGS}
        self.dmas_since_barrier = []
        self.pending = {e: [] for e in ENGS}

    def _rec(self, eng, fn, reads, writes, is_dma):
        op = Op(eng, fn, is_dma)
        deps = []
        for t in reads:
            if t.w is not None:
                deps.append((t.w, 0))
        for t in writes:
            if t.w is not None:
                deps.append((t.w, 1))
            for r in t.r:
                deps.append((r, 1))
        for d in self.pending[eng]:
            deps.append((d, 0))
        self.pending[eng] = []
        op.deps = deps
        for t in writes:
            t.w = op
            t.r = []
        for t in reads:
            if t.w is not op:
                t.r.append(op)
        self.ops[eng].append(op)
        self.last[eng] = op
        if is_dma:
            self.dmas_since_barrier.append(op)
        return op

    def op(self, eng, fn, reads=(), writes=()):
        return self._rec(eng, fn, list(reads), list(writes), False)

    def dma(self, eng, out, in_, reads=(), writes=()):
        def fn(e):
            return e.dma_start(out=out, in_=in_)
        return self._rec(eng, fn, list(reads), list(writes), True)

    def barrier(self):
        deps = [o for o in self.last.values() if o is not None] + self.dmas_since_barrier
        self.dmas_since_barrier = []
        for e in ENGS:
            self.pending[e] = list(deps)

    @staticmethod
    def _skip(d, ename, kind):
        if d.eng == ename:
            if ename in ("pe", "sp"):
                return True
            if kind == 1:
                return True
        return False

    def emit(self):
        nc = self.nc
        with contextlib.ExitStack() as st:
            esem = {e: st.enter_context(nc.semaphore("s_" + e)) for e in ENGS}
            rings = {e: [st.enter_context(nc.semaphore("d_%s%d" % (e, i))) for i in range(DMA_RING[e])]
                     for e in ENGS}
            for e in ENGS:
                for op in self.ops[e]:
                    for d, kind in op.deps:
                        if d.is_dma or self._skip(d, e, kind):
                            continue
                        d.need_sig = True
            for e in ENGS:
                c = 0
                ring_cnt = [0] * DMA_RING[e]
                nd = 0
                for op in self.ops[e]:
                    if op.is_dma:
                        slot = nd % DMA_RING[e]
                        prev = ring_cnt[slot] * 16
                        ring_cnt[slot] += 1
                        op.ev = (rings[e][slot], ring_cnt[slot] * 16, prev)
                        nd += 1
                    elif op.need_sig:
                        c += 1
                        op.sigval = c
            if os.environ.get("NSA_VERBOSE"):
                print("ops per engine", {e: len(self.ops[e]) for e in ENGS},
                      "sig counts", {e: max([o.sigval or 0 for o in self.ops[e]] + [0]) for e in ENGS},
                      "ring max", {e: max([o.ev[1] for o in self.ops[e] if o.is_dma] + [0]) for e in ENGS}, flush=True)
            blk = st.enter_context(nc.Block())

            def run(ename, eng):
                known = {}

                def wait(sem, val):
                    k = id(sem)
                    if known.get(k, 0) >= val:
                        return
                    known[k] = val
                    eng.wait_ge(sem, val)
                for op in self.ops[ename]:
                    for d, kind in op.deps:
                        if d.is_dma:
                            wait(d.ev[0], d.ev[1])
                        elif not self._skip(d, ename, kind):
                            wait(esem[d.eng], d.sigval)
                    if op.is_dma:
                        sem, tgt, prev = op.ev
                        if prev > 0:
                            wait(sem, prev)
                        op.fn(eng).then_inc(sem, 16)
                    else:
                        ins = op.fn(eng)
                        if op.need_sig:
                            ins.then_inc(esem[ename], 1)

            @blk.sync
            def _(sync):
                run("sp", sync)

            @blk.tensor
            def _(tensor):
                run("pe", tensor)

            @blk.scalar
            def _(scalar):
                run("act", scalar)

            @blk.vector
            def _(vector):
                run("dve", vector)

            @blk.gpsimd
            def _(gpsimd):
                run("pool", gpsimd)


def MM(out, lhsT, rhs, start, stop):
    return lambda e: e.matmul(out, lhsT=lhsT, rhs=rhs, start=start, stop=stop)


def TRN(out, in_, ident):
    return lambda e: e.transpose(out=out, in_=in_, identity=ident)


def ACTF(out, in_, func, bias=None, scale=None):
    kw = {}
    if bias is not None:
        kw["bias"] = bias
    if scale is not None:
        kw["scale"] = scale
    return lambda e: e.activation(out=out, in_=in_, func=func, **kw)


def TT(out, in0, in1, op):
    return lambda e: e.tensor_tensor(out=out, in0=in0, in1=in1, op=op)


def STT(out, in0, scalar, in1, op0, op1):
    return lambda e: e.scalar_tensor_tensor(out=out, in0=in0, scalar=scalar, in1=in1, op0=op0, op1=op1)


def TS(out, in0, s1, s2, op0, op1=None):
    if op1 is None:
        return lambda e: e.tensor_scalar(out=out, in0=in0, scalar1=s1, scalar2=None, op0=op0)
    return lambda e: e.tensor_scalar(out=out, in0=in0, scalar1=s1, scalar2=s2, op0=op0, op1=op1)


def CP(out, in_):
    return lambda e: e.tensor_copy(out=out, in_=in_)


def MSET(out, v):
    return lambda e: e.memset(out, v)


def RCP(out, in_):
    return lambda e: e.reciprocal(out=out, in_=in_)


NSA_OFF = 4080
NSA_M = 8192
NSAW_OFF = 512
NSAW_M = 2048


def _t5_bucket_np(n):
    n = np.maximum(n, 0)
    nf = np.maximum(n, 1).astype(np.float32)
    large = 16 + (np.log(nf / np.float32(16)) / np.float32(math.log(128.0)) * np.float32(16)).astype(np.int32)
    large = np.minimum(large, 31)
    return np.where(n < 16, n, large)


def lin_layout(w):
    K, M = w.shape
    kc, mc = K // 128, M // 128
    return np.ascontiguousarray(w.reshape(kc, 128, mc, 128).transpose(2, 1, 0, 3).reshape(mc, 128, kc * 128))


def fm(v):
    return np.ascontiguousarray(v.reshape(-1, 128).T)


class VecPack:
    def __init__(self):
        self.cols = []
        self.idx = {}
        self.n = 0

    def add(self, name, arr):
        arr = np.asarray(arr, np.float32)
        assert arr.shape[0] == 128
        self.idx[name] = (self.n, arr.shape[1])
        self.cols.append(arr)
        self.n += arr.shape[1]

    def build(self):
        return np.ascontiguousarray(np.concatenate(self.cols, axis=1))


def host_prepare(inp):
    f = lambda a: np.asarray(a, np.float32)
    shared = {}
    vp = VecPack()
    for l in range(4):
        for nm in ("ffn1_norm", "mix_norm", "ffn2_norm", "ple_norm"):
            vp.add("%s%d" % (nm, l), fm(f(inp[nm])[l]))
        for fi, pre in enumerate(("ffn1", "ffn2")):
            wg = f(inp[pre + "_w_gate"])[l]
            wu = f(inp[pre + "_w_up"])[l]
            wd = f(inp[pre + "_w_down"])[l]
            for hf in range(2):
                cs = slice(hf * 1408, (hf + 1) * 1408)
                shared["wg_%d_%d_%d" % (l, fi, hf)] = lin_layout(wg[:, cs])
                shared["wu_%d_%d_%d" % (l, fi, hf)] = lin_layout(wu[:, cs])
                shared["wd_%d_%d_%d" % (l, fi, hf)] = lin_layout(wd[cs, :])
        shared["pleg_%d" % l] = lin_layout(f(inp["ple_w_gate"])[l])
        shared["plei_%d" % l] = lin_layout(f(inp["ple_w_in"])[l])
    vp.add("final_norm", fm(f(inp["final_norm"])))
    for j in range(2):
        shared["pw1_%d" % j] = lin_layout(f(inp["conv_w_pw1"])[j])
        shared["pw2_%d" % j] = lin_layout(f(inp["conv_w_pw2"])[j])
        vp.add("b_pw1_%d" % j, fm(f(inp["conv_b_pw1"])[j]))
        vp.add("b_dw_%d" % j, fm(f(inp["conv_b_dw"])[j]))
        vp.add("ln_g_%d" % j, fm(f(inp["conv_ln_g"])[j]))
        vp.add("ln_b_%d" % j, fm(f(inp["conv_ln_b"])[j]))
        vp.add("b_pw2_%d" % j, fm(f(inp["conv_b_pw2"])[j]))
        wdw = f(inp["conv_w_dw"])[j]
        vp.add("w_dw_%d" % j, np.ascontiguousarray(wdw.reshape(31, 8, 128).transpose(2, 1, 0).reshape(128, 248)))
        w_in = f(inp["nsa_w_in"])[j]
        cols = [np.arange(1024)]
        for g in range(4):
            for kind in (0, 1, 2, 4):
                c = 1024 + kind * 256 + g * 64 + np.arange(64)
                cols.append(np.concatenate([c, c]))
        cols = np.concatenate(cols)
        shared["nsa_wf_%d" % j] = lin_layout(w_in[:, cols])
        tc = np.concatenate([1024 + 3 * 256 + np.arange(256), 1024 + 5 * 256 + np.arange(256), 2560 + np.arange(48)])
        shared["nsa_wt_%d" % j] = np.ascontiguousarray(w_in[:, tc].reshape(8, 128, 560).transpose(1, 0, 2))
        shared["nsa_wo_%d" % j] = lin_layout(f(inp["nsa_w_out"])[j])
        for nm, src in (("wk1", "nsa_cmp_wk1"), ("wv1", "nsa_cmp_wv1")):
            w1 = f(inp[src])[j]
            shared["%s_%d" % (nm, j)] = np.ascontiguousarray(w1.reshape(32, 64, 256).transpose(1, 0, 2))
        w2k = f(inp["nsa_cmp_wk2"])[j]
        w2kd = np.concatenate([w2k, w2k], axis=1)
        shared["w2k_%d" % j] = np.ascontiguousarray(w2kd.reshape(2, 128, 128).transpose(1, 0, 2))
        w2v = f(inp["nsa_cmp_wv2"])[j]
        shared["w2v_%d" % j] = np.ascontiguousarray(w2v.reshape(2, 128, 64).transpose(1, 0, 2))
        shared["posk_%d" % j] = np.ascontiguousarray(f(inp["nsa_cmp_pos_k"])[j].T)
        shared["posv_%d" % j] = np.ascontiguousarray(f(inp["nsa_cmp_pos_v"])[j].T)
    rb = f(inp["rel_bias"])
    ext = np.concatenate([rb, np.full((1, 16), -30000.0, np.float32)], axis=0)
    dist = np.arange(NSA_M) - NSA_OFF
    idx = np.where(dist >= 0, _t5_bucket_np(dist), 32)
    shared["gvec"] = np.ascontiguousarray(ext[idx].T)
    distw = np.arange(NSAW_M) - NSAW_OFF
    idxw = np.where((distw >= 0) & (distw < 512), _t5_bucket_np(distw), 32)
    shared["gwvec"] = np.ascontiguousarray(ext[idxw].T)
    vp.add("eps_rms", np.full((128, 1), 1e-6, np.float32))
    vp.add("eps_ln", np.full((128, 1), 1e-5, np.float32))
    shared["vecs"] = vp.build()
    shared["ident"] = np.eye(128, dtype=np.float32)
    t = np.arange(S)
    j = np.arange(64)[None, :]
    cur = (t // 64)[:, None]
    valid = (j * 64 <= t[:, None])
    forced = (j == 0) | (j == cur) | (j == cur - 1)
    vm = (valid & ~forced).astype(np.float32)
    am = np.where(forced, 1e9, np.where(valid, 0.0, -1.0)).astype(np.float32)
    shared["vmask"] = np.ascontiguousarray(vm.reshape(32, 128, 64).transpose(1, 0, 2))
    shared["amask"] = np.ascontiguousarray(am.reshape(32, 128, 64).transpose(1, 0, 2))
    c = np.arange(256)[:, None]
    ov = ((c * 16 < j * 64 + 64) & (c * 16 + 32 > j * 64) & (c < 255)).astype(np.float32)
    shared["overlap"] = np.ascontiguousarray(ov.reshape(2, 128, 64).transpose(1, 0, 2))
    ex = np.zeros((64, 32, 128), np.float32)
    for kt in range(32):
        ex[2 * kt, kt, 0:64] = 1.0
        ex[2 * kt + 1, kt, 64:128] = 1.0
    shared["expand"] = ex
    return shared, vp.idx


ARENA_F32 = 47600


class Ctx:
    pass


def build(nc, shapes, vidx, layers=(0, 1, 2, 3), stages=("ffn1", "mix", "ffn2", "ple"), do_final=True):
    P = Prog(nc)
    C = Ctx()
    din = {}
    for name, shp in shapes.items():
        din[name] = nc.dram_tensor(name, list(shp), F32, kind="ExternalInput").ap()
    out_d = nc.dram_tensor("out", [S, D], F32, kind="ExternalOutput").ap()

    def scratch(name, shape, dt):
        return nc.dram_tensor(name, list(shape), dt, kind="Internal").ap()
    X = [scratch("xs%d" % i, [8, 128, S], F32) for i in range(3)]
    qkT_d = scratch("qkT", [24, 128, S], BF16)
    vtok_d = scratch("vtok", [8, 128, 32, 65], BF16)
    gates_d = scratch("gates", [128, 32, 48], F32)
    oT_d = scratch("oT", [8, 128, S], BF16)
    grep_d = scratch("grep", [16, 128 * NSA_M], F32)
    gwrep_d = scratch("gwrep", [16, 128 * NSAW_M], F32)

    with contextlib.ExitStack() as st:
        arena = st.enter_context(nc.sbuf_tensor("arena", [128, ARENA_F32], F32))
        psb = [st.enter_context(nc.psum_tensor("psb%d" % i, [128, 512], F32)) for i in range(8)]
        pst = tks(8)
        top = [0]

        def alloc(nfree, dt=F32):
            n32 = nfree if dt == F32 else (nfree + 1) // 2
            assert top[0] + n32 <= ARENA_F32, ("arena overflow", top[0], n32)
            a = arena[:, top[0]:top[0] + n32]
            top[0] += n32
            if dt != F32:
                a = a.bitcast(dt)
                a = a[:, 0:nfree]
            return a

        def a3(nfree, dt, **kw):
            pat = kw.pop("pat")
            return alloc(nfree, dt).rearrange(pat, **kw)

        nv = shapes["vecs"][1]
        vecs = alloc(nv)
        vecs_tk = Tk()
        P.dma("sp", vecs, din["vecs"], writes=[vecs_tk])
        ident = alloc(128)
        ident_tk = Tk()
        P.dma("sp", ident, din["ident"], writes=[ident_tk])
        ones_bf = alloc(128, BF16)
        ones_tk = Tk()
        P.op("dve", MSET(ones_bf, 1.0), writes=[ones_tk])
        ident_bf = alloc(128, BF16)
        identbf_tk = Tk()
        P.op("dve", CP(ident_bf, ident), reads=[ident_tk], writes=[identbf_tk])
        base_top = top[0]

        def V(name, c0=0, n=None):
            o, w = vidx[name]
            if n is None:
                n = w - c0
            return vecs[:, o + c0:o + c0 + n]

        def xtile_ap(Xd, i):
            return Xd[:, :, i * TW_:(i + 1) * TW_].rearrange("c p t -> p c t")

        def load_w(dst, src, mc, wt):
            for m in range(mc):
                P.dma("pool", dst[:, m, :], src[m], writes=[wt[m]])

        def rmsnorm(xt, xtk, gain, hT, htk, sq, sqtk, rstd, rstd_tk, out_f32=None):
            for c in range(8):
                sl = c % 2
                P.op("act", ACTF(sq[:, sl, :], xt[:, c, :], AF.Square), reads=[xtk[c]], writes=[sqtk[sl]])
                P.op("pe", MM(psb[6][:, :], ones_bf, sq[:, sl, :], c == 0, c == 7),
                     reads=[sqtk[sl], ones_tk], writes=[pst[6]])
            P.op("act", ACTF(rstd, psb[6][:, :], AF.Sqrt, bias=V("eps_rms"), scale=1.0 / D),
                 reads=[pst[6], vecs_tk], writes=[rstd_tk])
            P.op("dve", RCP(rstd, rstd), reads=[rstd_tk], writes=[rstd_tk])
            for c in range(8):
                P.op("dve", STT(hT[:, c, :], xt[:, c, :], gain[:, c:c + 1], rstd, ALU.mult, ALU.mult),
                     reads=[xtk[c], rstd_tk, vecs_tk], writes=[htk[c]])

        def pass_input():
            xin = [alloc(D) for _ in range(2)]
            xin_tk = tks(2)
            xo = [a3(8 * 128, F32, pat="p (c t) -> p c t", c=8) for _ in range(2)]
            xo_tk = tks(2)
            for tt in range(32):
                b = tt % 2
                P.dma("sp", xin[b], din["x"][tt * 128:(tt + 1) * 128, :], writes=[xin_tk[b]])
                for c in range(8):
                    bank = c // 4
                    P.op("pe", TRN(psb[bank][:, (c % 4) * 128:(c % 4 + 1) * 128], xin[b][:, c * 128:(c + 1) * 128], ident),
                         reads=[xin_tk[b], ident_tk], writes=[pst[bank]])
                P.op("act", ACTF(xo[b][:, 0:4, :], psb[0][:, :].rearrange("p (c t) -> p c t", c=4), AF.Copy),
                     reads=[pst[0]], writes=[xo_tk[b]])
                P.op("dve", CP(xo[b][:, 4:8, :], psb[1][:, :].rearrange("p (c t) -> p c t", c=4)),
                     reads=[pst[1], xo_tk[b]], writes=[xo_tk[b]])
                P.dma("sp", X[0][:, :, tt * 128:(tt + 1) * 128].rearrange("c p t -> p c t"), xo[b], reads=[xo_tk[b]])

        def pass_ffn_half(l, fi, hf, Xn, Xr, Xo, norm_name):
            wg = a3(11 * 1024, BF16, pat="p (m f) -> p m f", m=11)
            wu = a3(11 * 1024, BF16, pat="p (m f) -> p m f", m=11)
            wd = a3(8 * 1408, BF16, pat="p (m f) -> p m f", m=8)
            wg_tk, wu_tk, wd_tk = tks(11), tks(11), tks(8)
            key = "%d_%d_%d" % (l, fi, hf)
            for m in range(11):
                P.dma("pool", wg[:, m, :], din["wg_" + key][m], writes=[wg_tk[m]])
                P.dma("pool", wu[:, m, :], din["wu_" + key][m], writes=[wu_tk[m]])
            load_w(wd, din["wd_" + key], 8, wd_tk)
            same = Xn is Xr
            xn = [a3(8 * 512, F32, pat="p (c t) -> p c t", c=8) for _ in range(2)]
            xn_tk = [tks(8) for _ in range(2)]
            if same:
                xr, xr_tk = xn, xn_tk
            else:
                xr = [a3(8 * 512, F32, pat="p (c t) -> p c t", c=8) for _ in range(2)]
                xr_tk = [tks(8) for _ in range(2)]
            hT = a3(8 * 512, BF16, pat="p (c t) -> p c t", c=8)
            htk = tks(8)
            sq = a3(2 * 512, BF16, pat="p (c t) -> p c t", c=2)
            sqtk = tks(2)
            rstd = alloc(512)
            rstd_tk = Tk()
            act_ = a3(11 * 512, BF16, pat="p (c t) -> p c t", c=11)
            atk = tks(11)
            sg = [alloc(512) for _ in range(2)]
            sgtk = tks(2)
            gain = V(norm_name)

            def load(i):
                b = i % 2
                P.dma("sp", xn[b], xtile_ap(Xn, i), writes=xn_tk[b])
                if not same:
                    P.dma("sp", xr[b], xtile_ap(Xr, i), writes=xr_tk[b])
            load(0)
            for i in range(NT):
                b = i % 2
                if i + 1 < NT:
                    load(i + 1)
                rmsnorm(xn[b], xn_tk[b], gain, hT, htk, sq, sqtk, rstd, rstd_tk)
                for j in range(11):
                    pg, pu = j % 2, 2 + j % 2
                    for c in range(8):
                        P.op("pe", MM(psb[pg][:, :], wg[:, j, c * 128:(c + 1) * 128], hT[:, c, :], c == 0, c == 7),
                             reads=[wg_tk[j], htk[c]], writes=[pst[pg]])
                    for c in range(8):
                        P.op("pe", MM(psb[pu][:, :], wu[:, j, c * 128:(c + 1) * 128], hT[:, c, :], c == 0, c == 7),
                             reads=[wu_tk[j], htk[c]], writes=[pst[pu]])
                    P.op("act", ACTF(sg[j % 2], psb[pg][:, :], AF.Silu), reads=[pst[pg]], writes=[sgtk[j % 2]])
                    P.op("dve", TT(act_[:, j, :], sg[j % 2], psb[pu][:, :], ALU.mult),
                         reads=[sgtk[j % 2], pst[pu]], writes=[atk[j]])
                for m in range(8):
                    py = 4 + m % 2
                    for j in range(11):
                        P.op("pe", MM(psb[py][:, :], wd[:, m, j * 128:(j + 1) * 128], act_[:, j, :], j == 0, j == 10),
                             reads=[wd_tk[m], atk[j]], writes=[pst[py]])
                    P.op("dve", STT(xr[b][:, m, :], psb[py][:, :], 0.5, xr[b][:, m, :], ALU.mult, ALU.add),
                         reads=[pst[py], xr_tk[b][m]], writes=[xr_tk[b][m]])
                P.dma("sp", xtile_ap(Xo, i), xr[b], reads=xr_tk[b])

        def pass_ple(l, Xc):
            wgp = a3(8 * 1024, BF16, pat="p (m f) -> p m f", m=8)
            wip = a3(8 * 256, BF16, pat="p (m f) -> p m f", m=8)
            wgp_tk, wip_tk = tks(8), tks(8)
            load_w(wgp, din["pleg_%d" % l], 8, wgp_tk)
            load_w(wip, din["plei_%d" % l], 8, wip_tk)
            xn = [a3(8 * 512, F32, pat="p (c t) -> p c t", c=8) for _ in range(2)]
            xn_tk = [tks(8) for _ in range(2)]
            pin = [a3(4 * 256, F32, pat="p (s f) -> p s f", s=4) for _ in range(2)]
            pin_tk = tks(2)
            pT = a3(2 * 512, BF16, pat="p (c t) -> p c t", c=2)
            pT_tk = tks(2)
            hT = a3(8 * 512, BF16, pat="p (c t) -> p c t", c=8)
            htk = tks(8)
            sq = a3(2 * 512, BF16, pat="p (c t) -> p c t", c=2)
            sqtk = tks(2)
            rstd = alloc(512)
            rstd_tk = Tk()
            sg = [alloc(512) for _ in range(2)]
            sgtk = tks(2)
            gain = V("ple_norm%d" % l)
            pl = din["p"][l]

            def load(i):
                b = i % 2
                P.dma("sp", xn[b], xtile_ap(Xc, i), writes=xn_tk[b])
                P.dma("sp", pin[b], pl[i * 512:(i + 1) * 512, :].rearrange("(s p) f -> p s f", p=128), writes=[pin_tk[b]])
            load(0)
            for i in range(NT):
                b = i % 2
                if i + 1 < NT:
                    load(i + 1)
                rmsnorm(xn[b], xn_tk[b], gain, hT, htk, sq, sqtk, rstd, rstd_tk)
                for kc in range(2):
                    for s in range(4):
                        P.op("pe", TRN(psb[7][:, s * 128:(s + 1) * 128], pin[b][:, s, kc * 128:(kc + 1) * 128], ident),
                             reads=[pin_tk[b], ident_tk], writes=[pst[7]])
                    P.op("act", ACTF(pT[:, kc, :], psb[7][:, :], AF.Copy), reads=[pst[7]], writes=[pT_tk[kc]])
                for m in range(8):
                    pg, pi = m % 2, 2 + m % 2
                    for c in range(8):
                        P.op("pe", MM(psb[pg][:, :], wgp[:, m, c * 128:(c + 1) * 128], hT[:, c, :], c == 0, c == 7),
                             reads=[wgp_tk[m], htk[c]], writes=[pst[pg]])
                    for c in range(2):
                        P.op("pe", MM(psb[pi][:, :], wip[:, m, c * 128:(c + 1) * 128], pT[:, c, :], c == 0, c == 1),
                             reads=[wip_tk[m], pT_tk[c]], writes=[pst[pi]])
                    P.op("act", ACTF(sg[m % 2], psb[pg][:, :], AF.Sigmoid), reads=[pst[pg]], writes=[sgtk[m % 2]])
                    P.op("dve", TT(sg[m % 2], sg[m % 2], psb[pi][:, :], ALU.mult),
                         reads=[sgtk[m % 2], pst[pi]], writes=[sgtk[m % 2]])
                    P.op("dve", TT(xn[b][:, m, :], xn[b][:, m, :], sg[m % 2], ALU.add),
                         reads=[sgtk[m % 2], xn_tk[b][m]], writes=[xn_tk[b][m]])
                P.dma("sp", xtile_ap(Xc, i), xn[b], reads=xn_tk[b])

        def pass_conv(l, jx, Xc):
            w1 = a3(16 * 1024, BF16, pat="p (m f) -> p m f", m=16)
            w2 = a3(8 * 1024, BF16, pat="p (m f) -> p m f", m=8)
            w1_tk, w2_tk = tks(16), tks(8)
            load_w(w1, din["pw1_%d" % jx], 16, w1_tk)
            load_w(w2, din["pw2_%d" % jx], 8, w2_tk)
            diag = a3(31 * 8 * 128, BF16, pat="p (j c m) -> p j c m", j=31, c=8)
            diag_tk = tks(8)
            wdw = V("w_dw_%d" % jx)
            for c in range(8):
                for j in range(31):
                    P.op("dve", TS(diag[:, j, c, :], ident_bf, wdw[:, c * 31 + j:c * 31 + j + 1], None, ALU.mult),
                         reads=[identbf_tk, vecs_tk], writes=[diag_tk[c]])
            xn = [a3(8 * 512, F32, pat="p (c t) -> p c t", c=8)] * 2
            xn_tk = [tks(8)] * 2
            hT = a3(8 * 512, BF16, pat="p (c t) -> p c t", c=8)
            htk = tks(8)
            sq = a3(2 * 512, BF16, pat="p (c t) -> p c t", c=2)
            sqtk = tks(2)
            rstd = alloc(512)
            rstd_tk = Tk()
            ub = a3(8 * 542, BF16, pat="p (c t) -> p c t", c=8)
            ub_tk = tks(8)
            yb = a3(8 * 512, F32, pat="p (c t) -> p c t", c=8)
            yb_tk = tks(8)
            ybf = a3(2 * 512, BF16, pat="p (c t) -> p c t", c=2)
            ybf_tk = tks(2)
            ysq = a3(2 * 512, BF16, pat="p (c t) -> p c t", c=2)
            ysq_tk = tks(2)
            sg = [alloc(512) for _ in range(2)]
            sgtk = tks(2)
            mu = alloc(512)
            mu_tk = Tk()
            rs = alloc(512)
            rs_tk = Tk()
            tmp = alloc(512)
            tmp_tk = Tk()
            gain = V("mix_norm%d" % l)
            b1 = V("b_pw1_%d" % jx)
            bdw = V("b_dw_%d" % jx)
            lng = V("ln_g_%d" % jx)
            lnb = V("ln_b_%d" % jx)
            b2 = V("b_pw2_%d" % jx)
            for c in range(8):
                P.op("dve", MSET(ub[:, c, 0:30], 0.0), writes=[ub_tk[c]])

            def load(i):
                b = i % 2
                P.dma("sp", xn[b], xtile_ap(Xc, i), writes=xn_tk[b])
            for i in range(NT):
                b = i % 2
                load(i)
                rmsnorm(xn[b], xn_tk[b], gain, hT, htk, sq, sqtk, rstd, rstd_tk)
                for m in range(8):
                    pa, pg = m % 2, 2 + m % 2
                    for c in range(8):
                        P.op("pe", MM(psb[pa][:, :], w1[:, m, c * 128:(c + 1) * 128], hT[:, c, :], c == 0, c == 7),
                             reads=[w1_tk[m], htk[c]], writes=[pst[pa]])
                    for c in range(8):
                        P.op("pe", MM(psb[pg][:, :], w1[:, 8 + m, c * 128:(c + 1) * 128], hT[:, c, :], c == 0, c == 7),
                             reads=[w1_tk[8 + m], htk[c]], writes=[pst[pg]])
                    P.op("act", ACTF(sg[m % 2], psb[pg][:, :], AF.Sigmoid, bias=b1[:, 8 + m:9 + m]),
                         reads=[pst[pg], vecs_tk], writes=[sgtk[m % 2]])
                    P.op("dve", STT(ub[:, m, 30:542], psb[pa][:, :], b1[:, m:m + 1], sg[m % 2], ALU.add, ALU.mult),
                         reads=[pst[pa], sgtk[m % 2], vecs_tk], writes=[ub_tk[m]])
                for m in range(8):
                    py = 4 + m % 2
                    for j in range(31):
                        P.op("pe", MM(psb[py][:, :], diag[:, j, m, :], ub[:, m, j:j + 512], j == 0, j == 30),
                             reads=[diag_tk[m], ub_tk[m]], writes=[pst[py]])
                    P.op("act", ACTF(yb[:, m, :], psb[py][:, :], AF.Identity, bias=bdw[:, m:m + 1]),
                         reads=[pst[py], vecs_tk], writes=[yb_tk[m]])
                    P.op("dve", CP(ub[:, m, 0:30], ub[:, m, 512:542]), reads=[ub_tk[m]], writes=[ub_tk[m]])
                    sl = m % 2
                    P.op("dve", CP(ybf[:, sl, :], yb[:, m, :]), reads=[yb_tk[m]], writes=[ybf_tk[sl]])
                    P.op("act", ACTF(ysq[:, sl, :], yb[:, m, :], AF.Square), reads=[yb_tk[m]], writes=[ysq_tk[sl]])
                    P.op("pe", MM(psb[6][:, :], ones_bf, ybf[:, sl, :], m == 0, m == 7),
                         reads=[ybf_tk[sl], ones_tk], writes=[pst[6]])
                    P.op("pe", MM(psb[7][:, :], ones_bf, ysq[:, sl, :], m == 0, m == 7),
                         reads=[ysq_tk[sl], ones_tk], writes=[pst[7]])
                P.op("dve", TS(mu, psb[6][:, :], 1.0 / D, None, ALU.mult), reads=[pst[6]], writes=[mu_tk])
                P.op("dve", TT(tmp, mu, mu, ALU.mult), reads=[mu_tk], writes=[tmp_tk])
                P.op("dve", STT(tmp, psb[7][:, :], 1.0 / D, tmp, ALU.mult, ALU.subtract),
                     reads=[pst[7], tmp_tk], writes=[tmp_tk])
                P.op("dve", TS(tmp, tmp, 0.0, None, ALU.max), reads=[tmp_tk], writes=[tmp_tk])
                P.op("act", ACTF(rs, tmp, AF.Sqrt, bias=V("eps_ln"), scale=1.0), reads=[tmp_tk, vecs_tk], writes=[rs_tk])
                P.op("dve", RCP(rs, rs), reads=[rs_tk], writes=[rs_tk])
                P.op("dve", STT(mu, mu, -1.0, rs, ALU.mult, ALU.mult), reads=[mu_tk, rs_tk], writes=[mu_tk])
                for m in range(8):
                    P.op("dve", TT(yb[:, m, :], yb[:, m, :], rs, ALU.mult), reads=[yb_tk[m], rs_tk], writes=[yb_tk[m]])
                    P.op("dve", TT(yb[:, m, :], yb[:, m, :], mu, ALU.add), reads=[yb_tk[m], mu_tk], writes=[yb_tk[m]])
                    P.op("act", ACTF(hT[:, m, :], yb[:, m, :], AF.Silu, bias=lnb[:, m:m + 1], scale=lng[:, m:m + 1]),
                         reads=[yb_tk[m], vecs_tk], writes=[htk[m]])
                for m in range(8):
                    po = m % 2
                    for c in range(8):
                        P.op("pe", MM(psb[po][:, :], w2[:, m, c * 128:(c + 1) * 128], hT[:, c, :], c == 0, c == 7),
                             reads=[w2_tk[m], htk[c]], writes=[pst[po]])
                    P.op("dve", STT(xn[b][:, m, :], psb[po][:, :], b2[:, m:m + 1], xn[b][:, m, :], ALU.add, ALU.add),
                         reads=[pst[po], xn_tk[b][m], vecs_tk], writes=[xn_tk[b][m]])
                P.dma("sp", xtile_ap(Xc, i), xn[b], reads=xn_tk[b])

        def pass_nsa_tables():
            for h in range(16):
                src = bass.AP(tensor=din["gvec"].tensor, offset=h * NSA_M, ap=[[0, 128], [1, NSA_M]])
                dst = bass.AP(tensor=grep_d.tensor, offset=h * 128 * NSA_M, ap=[[NSA_M, 128], [1, NSA_M]])
                P.dma("sp", dst, src)
                src = bass.AP(tensor=din["gwvec"].tensor, offset=h * NSAW_M, ap=[[0, 128], [1, NSAW_M]])
                dst = bass.AP(tensor=gwrep_d.tensor, offset=h * 128 * NSAW_M, ap=[[NSAW_M, 128], [1, NSAW_M]])
                P.dma("sp", dst, src)

        def pass_nsa_proj(l, jx, Xc):
            wf = a3(24 * 1024, BF16, pat="p (m f) -> p m f", m=24)
            wf_tk = tks(24)
            load_w(wf, din["nsa_wf_%d" % jx], 24, wf_tk)
            wt = a3(8 * 560, BF16, pat="p (c f) -> p c f", c=8)
            wt_tk = Tk()
            P.dma("pool", wt, din["nsa_wt_%d" % jx], writes=[wt_tk])
            xn = [a3(8 * 512, F32, pat="p (c t) -> p c t", c=8) for _ in range(2)]
            xn_tk = [tks(8) for _ in range(2)]
            hT = a3(8 * 512, BF16, pat="p (c t) -> p c t", c=8)
            htk = tks(8)
            sq = a3(2 * 512, BF16, pat="p (c t) -> p c t", c=2)
            sqtk = tks(2)
            rstd = alloc(512)
            rstd_tk = Tk()
            stg = [a3(24 * 512, BF16, pat="p (m t) -> p m t", m=24) for _ in range(2)]
            stg_tk = tks(2)
            vst = [a3(4 * 8 * 65, BF16, pat="p (s k e) -> p s k e", s=4, k=8) for _ in range(2)]
            vst_tk = tks(2)
            gst = [a3(4 * 48, F32, pat="p (s f) -> p s f", s=4) for _ in range(2)]
            gst_tk = tks(2)
            gain = V("mix_norm%d" % l)
            for b in range(2):
                P.op("dve", MSET(vst[b], 1.0), writes=[vst_tk[b]])

            def load(i):
                b = i % 2
                P.dma("sp", xn[b], xtile_ap(Xc, i), writes=xn_tk[b])
            load(0)
            for i in range(NT):
                b = i % 2
                if i + 1 < NT:
                    load(i + 1)
                rmsnorm(xn[b], xn_tk[b], gain, hT, htk, sq, sqtk, rstd, rstd_tk)
                for m in range(24):
                    pb = m % 4
                    for c in range(8):
                        P.op("pe", MM(psb[pb][:, :], wf[:, m, c * 128:(c + 1) * 128], hT[:, c, :], c == 0, c == 7),
                             reads=[wf_tk[m], htk[c]], writes=[pst[pb]])
                    if m % 2 == 0:
                        P.op("act", ACTF(stg[b][:, m, :], psb[pb][:, :], AF.Copy), reads=[pst[pb]], writes=[stg_tk[b]])
                    else:
                        P.op("dve", CP(stg[b][:, m, :], psb[pb][:, :]), reads=[pst[pb]], writes=[stg_tk[b]])
                P.dma("sp", qkT_d[:, :, i * 512:(i + 1) * 512].rearrange("m p t -> p m t"), stg[b], reads=[stg_tk[b]])
                for s in range(4):
                    pv, pg = 4 + s % 2, 6 + s % 2
                    for c in range(8):
                        P.op("pe", MM(psb[pv][:, :], hT[:, c, s * 128:(s + 1) * 128], wt[:, c, 0:512], c == 0, c == 7),
                             reads=[wt_tk, htk[c]], writes=[pst[pv]])
                    for c in range(8):
                        P.op("pe", MM(psb[pg][:, 0:48], hT[:, c, s * 128:(s + 1) * 128], wt[:, c, 512:560], c == 0, c == 7),
                             reads=[wt_tk, htk[c]], writes=[pst[pg]])
                    P.op("dve", CP(vst[b][:, s, :, 0:64], psb[pv][:, :].rearrange("p (k e) -> p k e", k=8)),
                         reads=[pst[pv]], writes=[vst_tk[b]])
                    P.op("act", ACTF(gst[b][:, s, :], psb[pg][:, 0:48], AF.Sigmoid), reads=[pst[pg]], writes=[gst_tk[b]])
                for s in range(4):
                    P.dma("sp", vtok_d[:, :, i * 4 + s, :].rearrange("k p e -> p k e"), vst[b][:, s, :, :], reads=[vst_tk[b]])
                P.dma("sp", gates_d[:, i * 4:(i + 1) * 4, :], gst[b], reads=[gst_tk[b]])

        def pass_nsa_group(jx, g):
            q = a3(2 * S, BF16, pat="p (c t) -> p c t", c=2)
            q_tk = Tk()
            P.dma("sp", q, qkT_d[2 * g:2 * g + 2].rearrange("m p t -> p m t"), writes=[q_tk])
            ks = alloc(S, BF16)
            kw = alloc(S, BF16)
            ks_tk, kw_tk = Tk(), Tk()
            P.dma("sp", ks, qkT_d[8 + 4 * g + 2], writes=[ks_tk])
            P.dma("sp", kw, qkT_d[8 + 4 * g + 3], writes=[kw_tk])
            vs = a3(32 * 65, BF16, pat="p (s e) -> p s e", s=32)
            vw = a3(32 * 65, BF16, pat="p (s e) -> p s e", s=32)
            vs_tk, vw_tk = Tk(), Tk()
            P.dma("sp", vs, vtok_d[g], writes=[vs_tk])
            P.dma("sp", vw, vtok_d[4 + g], writes=[vw_tk])
            gt = a3(32 * 12, F32, pat="p (s f) -> p s f", s=32)
            gt_tk = Tk()
            P.dma("sp", gt, gates_d[:, :, 12 * g:12 * g + 12], writes=[gt_tk])
            kcT = alloc(256, BF16)
            kcT_tk = Tk()
            vca = a3(2 * 130, BF16, pat="p (c e) -> p c e", c=2)
            vca_tk = Tk()
            ovl = a3(2 * 64, F32, pat="p (c e) -> p c e", c=2)
            ovl_tk = Tk()
            P.dma("sp", ovl, din["overlap"], writes=[ovl_tk])
            oacc = a3(32 * 256, F32, pat="p (s f) -> p s f", s=32)
            oacc_tk = tks(32)
            imp = a3(32 * 64, F32, pat="p (s f) -> p s f", s=32)
            imp_tk = tks(32)
            nmT = alloc(S, BF16)
            nmT_tk = tks(8)
            exp_ = a3(32 * 128, BF16, pat="p (k m) -> p k m", k=32)
            exp_tk = Tk()
            P.dma("pool", exp_[0:64], din["expand"], writes=[exp_tk])
            exp2_tk = Tk()
            P.dma("pool", exp_[64:128], din["expand"], writes=[exp2_tk])
            Ef = [alloc(512) for _ in range(2)]
            Ef_tk = tks(2)
            Eb = [alloc(512, BF16) for _ in range(2)]
            Eb_tk = tks(2)
            sm = alloc(64)
            sm_tk = tks(8)
            grp_top = top[0]

            kc = alloc(S, BF16)
            vc = alloc(S, BF16)
            kc_tk, vc_tk = Tk(), Tk()
            P.dma("sp", kc, qkT_d[8 + 4 * g + 0], writes=[kc_tk])
            P.dma("sp", vc, qkT_d[8 + 4 * g + 1], writes=[vc_tk])
            wk1 = a3(32 * 256, BF16, pat="p (l j) -> p l j", l=32)
            wv1 = a3(32 * 256, BF16, pat="p (l j) -> p l j", l=32)
            wk1_tk, wv1_tk = Tk(), Tk()
            P.dma("pool", wk1[0:64], din["wk1_%d" % jx], writes=[wk1_tk])
            P.dma("pool", wv1[0:64], din["wv1_%d" % jx], writes=[wv1_tk])
            w2k = a3(2 * 128, BF16, pat="p (c m) -> p c m", c=2)
            w2v = a3(2 * 64, BF16, pat="p (c m) -> p c m", c=2)
            w2k_tk, w2v_tk = Tk(), Tk()
            P.dma("pool", w2k, din["w2k_%d" % jx], writes=[w2k_tk])
            P.dma("pool", w2v, din["w2v_%d" % jx], writes=[w2v_tk])
            posk = alloc(32, BF16)
            posv = alloc(32, BF16)
            posk_tk, posv_tk = Tk(), Tk()
            P.dma("pool", posk[0:64], din["posk_%d" % jx], writes=[posk_tk])
            P.dma("pool", posv[0:64], din["posv_%d" % jx], writes=[posv_tk])
            cb = alloc(4)
            cb_tk = Tk()
            xg = a3(2 * 256, F32, pat="p (c t) -> p c t", c=2)
            xg_tk = tks(2)
            t1 = a3(2 * 256, F32, pat="p (c t) -> p c t", c=2)
            t1_tk = tks(2)
            gl = a3(2 * 256, BF16, pat="p (c t) -> p c t", c=2)
            gl_tk = tks(2)
            P.op("dve", MSET(kcT, 0.0), writes=[kcT_tk])
            P.op("dve", MSET(vca, 0.0), writes=[vca_tk])
            P.op("dve", MSET(vca[:, :, 64:65], 1.0), writes=[vca_tk])
            P.op("dve", CP(vca[:, :, 65:129], ovl), reads=[ovl_tk], writes=[vca_tk])
            for which, (src, src_tk, w1s, w1_tk, pos, pos_tk) in enumerate(
                    ((kc, kc_tk, wk1, wk1_tk, posk, posk_tk), (vc, vc_tk, wv1, wv1_tk, posv, posv_tk))):
                for jc in range(2):
                    for l_ in range(32):
                        P.op("pe", MM(psb[4][:, jc:jc + 1], w1s[0:64, l_, jc * 128:(jc + 1) * 128], pos[0:64, l_:l_ + 1],
                                      l_ == 0, l_ == 31), reads=[w1_tk, pos_tk], writes=[pst[4]])
                P.op("dve", CP(cb[:, 0:2], psb[4][:, 0:2]), reads=[pst[4]], writes=[cb_tk])
                for jc in range(2):
                    pb = 5 + jc
                    for l_ in range(32):
                        P.op("pe", MM(psb[pb][:, 0:255], w1s[0:64, l_, jc * 128:(jc + 1) * 128],
                                      src[0:64, l_:l_ + 16 * 254 + 1:16], l_ == 0, l_ == 31),
                             reads=[w1_tk, src_tk], writes=[pst[pb]])
                    P.op("act", ACTF(xg[:, jc, 0:255], psb[pb][:, 0:255], AF.Identity, bias=cb[:, jc:jc + 1]),
                         reads=[pst[pb], cb_tk], writes=[xg_tk[jc]])
                    P.op("dve", TT(t1[:, jc, 0:255], xg[:, jc, 0:255], xg[:, jc, 0:255], ALU.mult),
                         reads=[xg_tk[jc]], writes=[t1_tk[jc]])
                    P.op("dve", TS(t1[:, jc, 0:255], t1[:, jc, 0:255], 0.044715, 1.0, ALU.mult, ALU.add),
                         reads=[t1_tk[jc]], writes=[t1_tk[jc]])
                    P.op("dve", TT(t1[:, jc, 0:255], t1[:, jc, 0:255], xg[:, jc, 0:255], ALU.mult),
                         reads=[t1_tk[jc], xg_tk[jc]], writes=[t1_tk[jc]])
                    P.op("act", ACTF(t1[:, jc, 0:255], t1[:, jc, 0:255], AF.Sigmoid, scale=1.5957691),
                         reads=[t1_tk[jc]], writes=[t1_tk[jc]])
                    P.op("dve", TT(gl[:, jc, 0:255], t1[:, jc, 0:255], xg[:, jc, 0:255], ALU.mult),
                         reads=[t1_tk[jc], xg_tk[jc]], writes=[gl_tk[jc]])
                if which == 0:
                    for jc in range(2):
                        P.op("pe", MM(psb[7][:, 0:255], w2k[:, jc, :], gl[:, jc, 0:255], jc == 0, jc == 1),
                             reads=[w2k_tk, gl_tk[jc]], writes=[pst[7]])
                    P.op("act", ACTF(kcT[:, 0:255], psb[7][:, 0:255], AF.Copy), reads=[pst[7]], writes=[kcT_tk])
                else:
                    for ct in range(2):
                        ncr = 128 if ct == 0 else 127
                        for jc in range(2):
                            P.op("pe", MM(psb[7][0:ncr, 0:64], gl[:, jc, ct * 128:ct * 128 + ncr], w2v[:, jc, :], jc == 0, jc == 1),
                                 reads=[w2v_tk, gl_tk[jc]], writes=[pst[7]])
                        P.op("act", ACTF(vca[0:ncr, ct, 0:64], psb[7][0:ncr, 0:64], AF.Copy), reads=[pst[7]], writes=[vca_tk])
            P.barrier()
            top[0] = grp_top
            if DBG < 3:
                return

            tab = alloc(6144)
            tab_tk = tks(3)
            tabw = alloc(1408)
            tabw_tk = Tk()
            vmk = a3(32 * 64, F32, pat="p (s f) -> p s f", s=32)
            amk = a3(32 * 64, F32, pat="p (s f) -> p s f", s=32)
            vmk_tk, amk_tk = Tk(), Tk()
            P.dma("sp", vmk, din["vmask"], writes=[vmk_tk])
            P.dma("sp", amk, din["amask"], writes=[amk_tk])
            sc = alloc(64)
            sc2 = alloc(64)
            m8a = alloc(8)
            m8b = alloc(8)
            nm = alloc(128)
            sc_tk, sc2_tk, m8a_tk, m8b_tk, nm_tk = Tk(), Tk(), Tk(), Tk(), Tk()
            eslot = [0]

            def finalize(ps_ap, ps_tk, zcol, qt, r, gi, first):
                k = (qt + gi) % 8
                z = sm[:, k * 4:k * 4 + 1]
                gz = sm[:, k * 4 + 1:k * 4 + 2]
                P.op("dve", TS(z, zcol, 1e-30, None, ALU.max), reads=[ps_tk], writes=[sm_tk[k]])
                P.op("dve", RCP(z, z), reads=[sm_tk[k]], writes=[sm_tk[k]])
                P.op("dve", TT(gz, z, gt[:, qt, r * 3 + gi:r * 3 + gi + 1], ALU.mult), reads=[sm_tk[k], gt_tk], writes=[sm_tk[k]])
                o = oacc[:, qt, r * 64:(r + 1) * 64]
                if first:
                    P.op("dve", TS(o, ps_ap, gz, None, ALU.mult), reads=[ps_tk, sm_tk[k]], writes=[oacc_tk[qt]])
                else:
                    P.op("dve", STT(o, ps_ap, gz, o, ALU.mult, ALU.add), reads=[ps_tk, sm_tk[k], oacc_tk[qt]], writes=[oacc_tk[qt]])
                return z, sm_tk[k]

            for r in range(4):
                h = 4 * g + r
                pr, hf = r // 2, r % 2
                rows = slice(hf * 64, hf * 64 + 64)
                src = bass.AP(tensor=grep_d.tensor, offset=h * 128 * NSA_M + (NSA_OFF - 2048),
                              ap=[[NSA_M - 16, 128], [1, 6144]])
                P.dma("sp", tab, src, writes=tab_tk)
                for pz in range(3):
                    P.op("act", ACTF(tab[:, pz * 2048:(pz + 1) * 2048], tab[:, pz * 2048:(pz + 1) * 2048], AF.Exp),
                         reads=[tab_tk[pz]], writes=[tab_tk[pz]])
                for QB in range(8):
                    cts = [0] if QB < 4 else [0, 1]
                    for ct in cts:
                        e_ = eslot[0] % 2
                        eslot[0] += 1
                        P.op("pe", MM(psb[e_][:, :], kcT[rows, ct * 128:(ct + 1) * 128], q[rows, pr, QB * 512:(QB + 1) * 512], True, True),
                             reads=[kcT_tk, q_tk], writes=[pst[e_]])
                        P.op("act", ACTF(Ef[e_], psb[e_][:, :], AF.Exp, scale=0.125), reads=[pst[e_]], writes=[Ef_tk[e_]])
                        sj = QB * 512 + 2017 - 2048 * ct
                        P.op("dve", TT(Eb[e_], Ef[e_], tab[:, sj:sj + 512], ALU.mult),
                             reads=[Ef_tk[e_]] + tab_tk, writes=[Eb_tk[e_]])
                        for s in range(4):
                            bank = 2 + s
                            P.op("pe", MM(psb[bank][:, 0:129], Eb[e_][:, s * 128:(s + 1) * 128],
                                          vca[:, ct, 0:129], ct == cts[0], ct == cts[-1]),
                                 reads=[Eb_tk[e_], vca_tk], writes=[pst[bank]])
                    for s in range(4):
                        bank = 2 + s
                        qt = QB * 4 + s
                        c0 = 0
                        z, ztk = finalize(psb[bank][:, c0:c0 + 64], pst[bank], psb[bank][:, c0 + 64:c0 + 65], qt, r, 0, True)
                        if r == 0:
                            P.op("dve", TS(imp[:, qt, :], psb[bank][:, c0 + 65:c0 + 129], z, None, ALU.mult),
                                 reads=[pst[bank], ztk], writes=[imp_tk[qt]])
                        else:
                            P.op("dve", STT(imp[:, qt, :], psb[bank][:, c0 + 65:c0 + 129], z, imp[:, qt, :], ALU.mult, ALU.add),
                                 reads=[pst[bank], ztk, imp_tk[qt]], writes=[imp_tk[qt]])
            if DBG < 4:
                return
            for qt in range(32):
                P.op("dve", TT(sc, imp[:, qt, :], vmk[:, qt, :], ALU.mult), reads=[imp_tk[qt], vmk_tk], writes=[sc_tk])
                P.op("dve", TT(sc, sc, amk[:, qt, :], ALU.add), reads=[sc_tk, amk_tk], writes=[sc_tk])
                P.op("dve", lambda e: e.max(out=m8a, in_=sc), reads=[sc_tk], writes=[m8a_tk])
                P.op("dve", lambda e: e.match_replace(out=sc2, in_to_replace=m8a, in_values=sc, imm_value=-1e30),
                     reads=[sc_tk, m8a_tk], writes=[sc2_tk])
                P.op("dve", lambda e: e.max(out=m8b, in_=sc2), reads=[sc2_tk], writes=[m8b_tk])
                P.op("dve", TS(nm[:, 0:64], sc, m8b[:, 7:8], -30000.0, ALU.is_lt, ALU.mult), reads=[sc_tk, m8b_tk], writes=[nm_tk])
                P.op("dve", TS(nm[:, 64:128], sc, m8b[:, 7:8], -30000.0, ALU.is_lt, ALU.mult), reads=[sc_tk, m8b_tk, nm_tk], writes=[nm_tk])
                P.op("pe", TRN(psb[7][:, 0:128], nm, ident), reads=[nm_tk, ident_tk], writes=[pst[7]])
                P.op("act", ACTF(nmT[:, qt * 128:(qt + 1) * 128], psb[7][:, 0:128], AF.Copy),
                     reads=[pst[7]], writes=[nmT_tk[qt // 4]])
            if DBG < 5:
                return
            SUB = int(os.environ.get("NSA_SUB", "15"))
            for r in (range(4) if SUB & 4 else range(1)):
                h = 4 * g + r
                pr, hf = r // 2, r % 2
                rows = slice(hf * 64, hf * 64 + 64)
                src = bass.AP(tensor=grep_d.tensor, offset=h * 128 * NSA_M + (NSA_OFF - 384),
                              ap=[[NSA_M - 1, 128], [1, 2560]])
                P.dma("sp", tab[:, 0:2560], src, writes=tab_tk)
                P.op("act", ACTF(tab[:, 0:2560], tab[:, 0:2560], AF.Exp), reads=tab_tk, writes=tab_tk)
                src = bass.AP(tensor=gwrep_d.tensor, offset=h * 128 * NSAW_M + (NSAW_OFF - 384),
                              ap=[[NSAW_M - 1, 128], [1, 1408]])
                P.dma("sp", tabw, src, writes=[tabw_tk])
                P.op("act", ACTF(tabw, tabw, AF.Exp), reads=[tabw_tk], writes=[tabw_tk])
                for QB in (range(8) if SUB & 8 else range(2)):
                    qs = slice(QB * 512, (QB + 1) * 512)
                    for kt in range(4 * QB + 4):
                        e_ = eslot[0] % 2
                        eslot[0] += 1
                        P.op("pe", MM(psb[e_][:, :], ks[rows, kt * 128:(kt + 1) * 128], q[rows, pr, qs], True, not (SUB & 1)),
                             reads=[ks_tk, q_tk], writes=[pst[e_]])
                        if SUB & 1:
                            P.op("pe", MM(psb[e_][:, :], exp_[rows, kt, :], nmT[rows, qs], False, True),
                                 reads=[exp_tk, exp2_tk, nmT_tk[QB]], writes=[pst[e_]])
                        P.op("act", ACTF(Ef[e_], psb[e_][:, :], AF.Exp, scale=0.125), reads=[pst[e_]], writes=[Ef_tk[e_]])
                        off = min(QB * 512 - kt * 128, 1664) + 384
                        P.op("dve", TT(Eb[e_], Ef[e_], tab[:, off:off + 512], ALU.mult),
                             reads=[Ef_tk[e_]] + tab_tk, writes=[Eb_tk[e_]])
                        for s in range(4):
                            if kt <= 4 * QB + s:
                                P.op("pe", MM(psb[2 + s][:, 0:65], Eb[e_][:, s * 128:(s + 1) * 128], vs[:, kt, :],
                                              kt == 0, kt == 4 * QB + s), reads=[Eb_tk[e_], vs_tk], writes=[pst[2 + s]])
                    for s in range(4):
                        finalize(psb[2 + s][:, 0:64], pst[2 + s], psb[2 + s][:, 64:65], QB * 4 + s, r, 1, False)
                    for kt in (range(max(0, 4 * QB - 4), 4 * QB + 4) if SUB & 2 else ()):
                        e_ = eslot[0] % 2
                        eslot[0] += 1
                        P.op("pe", MM(psb[e_][:, :], kw[rows, kt * 128:(kt + 1) * 128], q[rows, pr, qs], True, True),
                             reads=[kw_tk, q_tk], writes=[pst[e_]])
                        P.op("act", ACTF(Ef[e_], psb[e_][:, :], AF.Exp, scale=0.125), reads=[pst[e_]], writes=[Ef_tk[e_]])
                        off = QB * 512 - kt * 128 + 384
                        P.op("dve", TT(Eb[e_], Ef[e_], tabw[:, off:off + 512], ALU.mult),
                             reads=[Ef_tk[e_], tabw_tk], writes=[Eb_tk[e_]])
                        for s in range(4):
                            lo = max(0, 4 * QB + s - 4)
                            if lo <= kt <= 4 * QB + s:
                                P.op("pe", MM(psb[2 + s][:, 0:65], Eb[e_][:, s * 128:(s + 1) * 128], vw[:, kt, :],
                                              kt == lo, kt == 4 * QB + s), reads=[Eb_tk[e_], vw_tk], writes=[pst[2 + s]])
                    for s in (range(4) if SUB & 2 else ()):
                        finalize(psb[2 + s][:, 0:64], pst[2 + s], psb[2 + s][:, 64:65], QB * 4 + s, r, 2, False)
            if DBG < 6:
                return
            ost = [a3(2 * 512, BF16, pat="p (c t) -> p c t", c=2) for _ in range(2)]
            ost_tk = tks(2)
            for QB in range(8):
                b = QB % 2
                for cc in range(2):
                    bank = 4 + cc
                    for s in range(4):
                        qt = QB * 4 + s
                        P.op("pe", TRN(psb[bank][:, s * 128:(s + 1) * 128], oacc[:, qt, cc * 128:(cc + 1) * 128], ident),
                             reads=[oacc_tk[qt], ident_tk], writes=[pst[bank]])
                    P.op("act", ACTF(ost[b][:, cc, :], psb[bank][:, :], AF.Copy), reads=[pst[bank]], writes=[ost_tk[b]])
                P.dma("sp", oT_d[2 * g:2 * g + 2, :, QB * 512:(QB + 1) * 512].rearrange("m p t -> p m t"), ost[b], reads=[ost_tk[b]])

        def pass_nsa_out(jx, Xc):
            wo = a3(8 * 1024, BF16, pat="p (m f) -> p m f", m=8)
            wo_tk = tks(8)
            load_w(wo, din["nsa_wo_%d" % jx], 8, wo_tk)
            xn = [a3(8 * 512, F32, pat="p (c t) -> p c t", c=8) for _ in range(2)]
            xn_tk = [tks(8) for _ in range(2)]
            ot = [a3(8 * 512, BF16, pat="p (c t) -> p c t", c=8) for _ in range(2)]
            ot_tk = tks(2)

            def load(i):
                b = i % 2
                P.dma("sp", xn[b], xtile_ap(Xc, i), writes=xn_tk[b])
                P.dma("sp", ot[b], oT_d[:, :, i * 512:(i + 1) * 512].rearrange("m p t -> p m t"), writes=[ot_tk[b]])
            load(0)
            for i in range(NT):
                b = i % 2
                if i + 1 < NT:
                    load(i + 1)
                for m in range(8):
                    po = m % 2
                    for c in range(8):
                        P.op("pe", MM(psb[po][:, :], wo[:, m, c * 128:(c + 1) * 128], ot[b][:, c, :], c == 0, c == 7),
                             reads=[wo_tk[m], ot_tk[b]], writes=[pst[po]])
                    P.op("dve", TT(xn[b][:, m, :], xn[b][:, m, :], psb[po][:, :], ALU.add),
                         reads=[pst[po], xn_tk[b][m]], writes=[xn_tk[b][m]])
                P.dma("sp", xtile_ap(Xc, i), xn[b], reads=xn_tk[b])

        def pass_final(Xc, do_norm):
            xn = [a3(8 * 512, F32, pat="p (c t) -> p c t", c=8) for _ in range(2)]
            xn_tk = [tks(8) for _ in range(2)]
            sq = a3(2 * 512, BF16, pat="p (c t) -> p c t", c=2)
            sqtk = tks(2)
            rstd = alloc(512)
            rstd_tk = Tk()
            yo = [alloc(D) for _ in range(2)]
            yo_tk = tks(2)
            gain = V("final_norm")

            def load(i):
                b = i % 2
                P.dma("sp", xn[b], xtile_ap(Xc, i), writes=xn_tk[b])
            load(0)
            for i in range(NT):
                b = i % 2
                if i + 1 < NT:
                    load(i + 1)
                xt, xtk = xn[b], xn_tk[b]
                if do_norm:
                    for c in range(8):
                        sl = c % 2
                        P.op("act", ACTF(sq[:, sl, :], xt[:, c, :], AF.Square), reads=[xtk[c]], writes=[sqtk[sl]])
                        P.op("pe", MM(psb[6][:, :], ones_bf, sq[:, sl, :], c == 0, c == 7),
                             reads=[sqtk[sl], ones_tk], writes=[pst[6]])
                    P.op("act", ACTF(rstd, psb[6][:, :], AF.Sqrt, bias=V("eps_rms"), scale=1.0 / D),
                         reads=[pst[6], vecs_tk], writes=[rstd_tk])
                    P.op("dve", RCP(rstd, rstd), reads=[rstd_tk], writes=[rstd_tk])
                    for c in range(8):
                        P.op("dve", STT(xt[:, c, :], xt[:, c, :], gain[:, c:c + 1], rstd, ALU.mult, ALU.mult),
                             reads=[xtk[c], rstd_tk, vecs_tk], writes=[xtk[c]])
                for s in range(4):
                    yb_ = (i * 4 + s) % 2
                    for c in range(8):
                        bank = c // 4
                        P.op("pe", TRN(psb[bank][:, (c % 4) * 128:(c % 4 + 1) * 128], xt[:, c, s * 128:(s + 1) * 128], ident),
                             reads=[xtk[c], ident_tk], writes=[pst[bank]])
                    P.op("act", ACTF(yo[yb_][:, 0:512], psb[0][:, :], AF.Copy), reads=[pst[0]], writes=[yo_tk[yb_]])
                    P.op("dve", CP(yo[yb_][:, 512:1024], psb[1][:, :]), reads=[pst[1], yo_tk[yb_]], writes=[yo_tk[yb_]])
                    t0 = i * 512 + s * 128
                    P.dma("sp", out_d[t0:t0 + 128, :], yo[yb_], reads=[yo_tk[yb_]])

        def phase(fn, *a):
            top[0] = base_top
            fn(*a)
            P.barrier()

        phase(pass_input)
        has_nsa = any((l % 2 == 1) for l in layers) and "mix" in stages
        if has_nsa:
            phase(pass_nsa_tables)
        cur = 0
        for l in layers:
            jx = l // 2
            if "ffn1" in stages:
                a_, b_, c_ = cur, (cur + 1) % 3, (cur + 2) % 3
                phase(pass_ffn_half, l, 0, 0, X[a_], X[a_], X[b_], "ffn1_norm%d" % l)
                phase(pass_ffn_half, l, 0, 1, X[a_], X[b_], X[c_], "ffn1_norm%d" % l)
                cur = c_
            if "mix" in stages:
                if l % 2 == 0:
                    phase(pass_conv, l, jx, X[cur])
                else:
                    phase(pass_nsa_proj, l, jx, X[cur])
                    if DBG >= 2:
                        for g in range(4 if DBG >= 9 else 1):
                            phase(pass_nsa_group, jx, g)
                    if DBG >= 9:
                        phase(pass_nsa_out, jx, X[cur])
            if "ffn2" in stages:
                a_, b_, c_ = cur, (cur + 1) % 3, (cur + 2) % 3
                phase(pass_ffn_half, l, 1, 0, X[a_], X[a_], X[b_], "ffn2_norm%d" % l)
                phase(pass_ffn_half, l, 1, 1, X[a_], X[b_], X[c_], "ffn2_norm%d" % l)
                cur = c_
            if "ple" in stages:
                phase(pass_ple, l, X[cur])
        phase(pass_final, X[cur], do_final)
        for e_ in ENGS:
            P.op(e_, lambda e: e.nop())
        P.emit()
    return nc


import os
DBG = int(os.environ.get("NSA_DBG", "9"))


def run(inputs, n_cores=8, **bkw):
    shared, vidx = host_prepare(inputs)
    x = np.asarray(inputs["x"], np.float32)
    p = np.asarray(inputs["p"], np.float32)
    shapes = {k: v.shape for k, v in shared.items()}
    shapes["x"] = (S, D)
    shapes["p"] = (4, S, 256)
    nc = bass.Bass("TRN2", target_bir_lowering=False)
    build(nc, shapes, vidx, **bkw)
    in_maps = []
    for b in range(n_cores):
        m = dict(shared)
        m["x"] = np.ascontiguousarray(x[b])
        m["p"] = np.ascontiguousarray(p[:, b])
        in_maps.append(m)
    res = run_bass_kernel_spmd(nc, in_maps, core_ids=list(range(n_cores)))
    return np.stack([np.asarray(r["out"], np.float32) for r in res.results], axis=0)


def kernel(**inputs):
    return run(inputs, n_cores=8)
```

```python
import contextlib
import os
import math
import numpy as np
import concourse.bass as bass
import concourse.mybir as mybir
from concourse.bass_utils import run_bass_kernel_spmd

F32 = mybir.dt.float32
BF16 = mybir.dt.bfloat16
AF = mybir.ActivationFunctionType
ALU = mybir.AluOpType

S = 4096
D = 1024
DFF = 2816
NT = 8
TW_ = 512
ENGS = ["pe", "act", "dve", "pool", "sp"]
DMA_RING = {"sp": 8, "act": 2, "pool": 6, "pe": 2, "dve": 2}


class Tk:
    __slots__ = ("w", "r")

    def __init__(self):
        self.w = None
        self.r = []


def tks(n):
    return [Tk() for _ in range(n)]


class Op:
    __slots__ = ("eng", "fn", "deps", "is_dma", "need_sig", "sigval", "ev")

    def __init__(self, eng, fn, is_dma):
        self.eng = eng
        self.fn = fn
        self.deps = []
        self.is_dma = is_dma
        self.need_sig = False
        self.sigval = None
        self.ev = None


class Prog:
    def __init__(self, nc):
        self.nc = nc
        self.ops = {e: [] for e in ENGS}
        self.last = {e: None for e in ENGS}
        self.dmas_since_barrier = []
        self.pending = {e: [] for e in ENGS}

    def _rec(self, eng, fn, reads, writes, is_dma):
        op = Op(eng, fn, is_dma)
        deps = []
        for t in reads:
            if t.w is not None:
                deps.append((t.w, 0))
        for t in writes:
            if t.w is not None:
                deps.append((t.w, 1))
            for r in t.r:
                deps.append((r, 1))
        for d in self.pending[eng]:
            deps.append((d, 0))
        self.pending[eng] = []
        op.deps = deps
        for t in writes:
            t.w = op
            t.r = []
        for t in reads:
            if t.w is not op:
                t.r.append(op)
        self.ops[eng].append(op)
        self.last[eng] = op
        if is_dma:
            self.dmas_since_barrier.append(op)
        return op

    def op(self, eng, fn, reads=(), writes=()):
        return self._rec(eng, fn, list(reads), list(writes), False)

    def dma(self, eng, out, in_, reads=(), writes=()):
        def fn(e):
            return e.dma_start(out=out, in_=in_)
        return self._rec(eng, fn, list(reads), list(writes), True)

    def barrier(self):
        deps = [o for o in self.last.values() if o is not None] + self.dmas_since_barrier
        self.dmas_since_barrier = []
        for e in ENGS:
            self.pending[e] = list(deps)

    @staticmethod
    def _skip(d, ename, kind):
        if d.eng == ename:
            if ename in ("pe", "sp"):
                return True
            if kind == 1:
                return True
        return False

    def emit(self):
        nc = self.nc
        with contextlib.ExitStack() as st:
            esem = {e: st.enter_context(nc.semaphore("s_" + e)) for e in ENGS}
            rings = {e: [st.enter_context(nc.semaphore("d_%s%d" % (e, i))) for i in range(DMA_RING[e])]
                     for e in ENGS}
            for e in ENGS:
                for op in self.ops[e]:
                    for d, kind in op.deps:
                        if d.is_dma or self._skip(d, e, kind):
                            continue
                        d.need_sig = True
            for e in ENGS:
                c = 0
                ring_cnt = [0] * DMA_RING[e]
                nd = 0
                for op in self.ops[e]:
                    if op.is_dma:
                        slot = nd % DMA_RING[e]
                        prev = ring_cnt[slot] * 16
                        ring_cnt[slot] += 1
                        op.ev = (rings[e][slot], ring_cnt[slot] * 16, prev)
                        nd += 1
                    elif op.need_sig:
                        c += 1
                        op.sigval = c
            if os.environ.get("NSA_VERBOSE"):
                print("ops per engine", {e: len(self.ops[e]) for e in ENGS},
                      "sig counts", {e: max([o.sigval or 0 for o in self.ops[e]] + [0]) for e in ENGS},
                      "ring max", {e: max([o.ev[1] for o in self.ops[e] if o.is_dma] + [0]) for e in ENGS}, flush=True)
            blk = st.enter_context(nc.Block())

            def run(ename, eng):
                known = {}

                def wait(sem, val):
                    k = id(sem)
                    if known.get(k, 0) >= val:
                        return
                    known[k] = val
                    eng.wait_ge(sem, val)
                for op in self.ops[ename]:
                    for d, kind in op.deps:
                        if d.is_dma:
                            wait(d.ev[0], d.ev[1])
                        elif not self._skip(d, ename, kind):
                            wait(esem[d.eng], d.sigval)
                    if op.is_dma:
                        sem, tgt, prev = op.ev
                        if prev > 0:
                            wait(sem, prev)
                        op.fn(eng).then_inc(sem, 16)
                    else:
                        ins = op.fn(eng)
                        if op.need_sig:
                            ins.then_inc(esem[ename], 1)

            @blk.sync
            def _(sync):
                run("sp", sync)

            @blk.tensor
            def _(tensor):
                run("pe", tensor)

            @blk.scalar
            def _(scalar):
                run("act", scalar)

            @blk.vector
            def _(vector):
                run("dve", vector)

            @blk.gpsimd
            def _(gpsimd):
                run("pool", gpsimd)


def MM(out, lhsT, rhs, start, stop):
    return lambda e: e.matmul(out, lhsT=lhsT, rhs=rhs, start=start, stop=stop)


def TRN(out, in_, ident):
    return lambda e: e.transpose(out=out, in_=in_, identity=ident)


def ACTF(out, in_, func, bias=None, scale=None):
    kw = {}
    if bias is not None:
        kw["bias"] = bias
    if scale is not None:
        kw["scale"] = scale
    return lambda e: e.activation(out=out, in_=in_, func=func, **kw)


def TT(out, in0, in1, op):
    return lambda e: e.tensor_tensor(out=out, in0=in0, in1=in1, op=op)


def STT(out, in0, scalar, in1, op0, op1):
    return lambda e: e.scalar_tensor_tensor(out=out, in0=in0, scalar=scalar, in1=in1, op0=op0, op1=op1)


def TS(out, in0, s1, s2, op0, op1=None):
    if op1 is None:
        return lambda e: e.tensor_scalar(out=out, in0=in0, scalar1=s1, scalar2=None, op0=op0)
    return lambda e: e.tensor_scalar(out=out, in0=in0, scalar1=s1, scalar2=s2, op0=op0, op1=op1)


def CP(out, in_):
    return lambda e: e.tensor_copy(out=out, in_=in_)


def MSET(out, v):
    return lambda e: e.memset(out, v)


def RCP(out, in_):
    return lambda e: e.reciprocal(out=out, in_=in_)


NSA_OFF = 4080
NSA_M = 8192
NSAW_OFF = 512
NSAW_M = 2048


def _t5_bucket_np(n):
    n = np.maximum(n, 0)
    nf = np.maximum(n, 1).astype(np.float32)
    large = 16 + (np.log(nf / np.float32(16)) / np.float32(math.log(128.0)) * np.float32(16)).astype(np.int32)
    large = np.minimum(large, 31)
    return np.where(n < 16, n, large)


def lin_layout(w):
    K, M = w.shape
    kc, mc = K // 128, M // 128
    return np.ascontiguousarray(w.reshape(kc, 128, mc, 128).transpose(2, 1, 0, 3).reshape(mc, 128, kc * 128))


def fm(v):
    return np.ascontiguousarray(v.reshape(-1, 128).T)


class VecPack:
    def __init__(self):
        self.cols = []
        self.idx = {}
        self.n = 0

    def add(self, name, arr):
        arr = np.asarray(arr, np.float32)
        assert arr.shape[0] == 128
        self.idx[name] = (self.n, arr.shape[1])
        self.cols.append(arr)
        self.n += arr.shape[1]

    def build(self):
        return np.ascontiguousarray(np.concatenate(self.cols, axis=1))


def host_prepare(inp):
    f = lambda a: np.asarray(a, np.float32)
    shared = {}
    vp = VecPack()
    for l in range(4):
        for nm in ("ffn1_norm", "mix_norm", "ffn2_norm", "ple_norm"):
            vp.add("%s%d" % (nm, l), fm(f(inp[nm])[l]))
        for fi, pre in enumerate(("ffn1", "ffn2")):
            wg = f(inp[pre + "_w_gate"])[l]
            wu = f(inp[pre + "_w_up"])[l]
            wd = f(inp[pre + "_w_down"])[l]
            for hf in range(2):
                cs = slice(hf * 1408, (hf + 1) * 1408)
                shared["wg_%d_%d_%d" % (l, fi, hf)] = lin_layout(wg[:, cs])
                shared["wu_%d_%d_%d" % (l, fi, hf)] = lin_layout(wu[:, cs])
                shared["wd_%d_%d_%d" % (l, fi, hf)] = lin_layout(wd[cs, :])
        shared["pleg_%d" % l] = lin_layout(f(inp["ple_w_gate"])[l])
        shared["plei_%d" % l] = lin_layout(f(inp["ple_w_in"])[l])
    vp.add("final_norm", fm(f(inp["final_norm"])))
    for j in range(2):
        shared["pw1_%d" % j] = lin_layout(f(inp["conv_w_pw1"])[j])
        shared["pw2_%d" % j] = lin_layout(f(inp["conv_w_pw2"])[j])
        vp.add("b_pw1_%d" % j, fm(f(inp["conv_b_pw1"])[j]))
        vp.add("b_dw_%d" % j, fm(f(inp["conv_b_dw"])[j]))
        vp.add("ln_g_%d" % j, fm(f(inp["conv_ln_g"])[j]))
        vp.add("ln_b_%d" % j, fm(f(inp["conv_ln_b"])[j]))
        vp.add("b_pw2_%d" % j, fm(f(inp["conv_b_pw2"])[j]))
        wdw = f(inp["conv_w_dw"])[j]
        vp.add("w_dw_%d" % j, np.ascontiguousarray(wdw.reshape(31, 8, 128).transpose(2, 1, 0).reshape(128, 248)))
        w_in = f(inp["nsa_w_in"])[j]
        cols = [np.arange(1024)]
        for g in range(4):
            for kind in (0, 1, 2, 4):
                c = 1024 + kind * 256 + g * 64 + np.arange(64)
                cols.append(np.concatenate([c, c]))
        cols = np.concatenate(cols)
        wcat = np.concatenate([w_in[:, cols], w_in[:, 2560:2608], np.zeros((1024, 80), np.float32)], axis=1)
        shared["nsa_wf_%d" % j] = lin_layout(wcat)
        tc = np.concatenate([1024 + 3 * 256 + np.arange(256), 1024 + 5 * 256 + np.arange(256), 2560 + np.arange(48)])
        shared["nsa_wt_%d" % j] = np.ascontiguousarray(w_in[:, tc].reshape(8, 128, 560).transpose(1, 0, 2))
        shared["nsa_wo_%d" % j] = lin_layout(f(inp["nsa_w_out"])[j])
        for nm, src in (("wk1", "nsa_cmp_wk1"), ("wv1", "nsa_cmp_wv1")):
            w1 = f(inp[src])[j]
            shared["%s_%d" % (nm, j)] = np.ascontiguousarray(w1.reshape(32, 64, 256).transpose(1, 0, 2))
        w2k = f(inp["nsa_cmp_wk2"])[j]
        w2kd = np.concatenate([w2k, w2k], axis=1)
        shared["w2k_%d" % j] = np.ascontiguousarray(w2kd.reshape(2, 128, 128).transpose(1, 0, 2))
        w2v = f(inp["nsa_cmp_wv2"])[j]
        shared["w2v_%d" % j] = np.ascontiguousarray(w2v.reshape(2, 128, 64).transpose(1, 0, 2))
        shared["posk_%d" % j] = np.ascontiguousarray(f(inp["nsa_cmp_pos_k"])[j].T)
        shared["posv_%d" % j] = np.ascontiguousarray(f(inp["nsa_cmp_pos_v"])[j].T)
    rb = f(inp["rel_bias"])
    ext = np.concatenate([rb, np.full((1, 16), -30000.0, np.float32)], axis=0)
    dist = np.arange(NSA_M) - NSA_OFF
    idx = np.where(dist >= 0, _t5_bucket_np(dist), 32)
    shared["gvec"] = np.ascontiguousarray(ext[idx].T)
    distw = np.arange(NSAW_M) - NSAW_OFF
    idxw = np.where((distw >= 0) & (distw < 512), _t5_bucket_np(distw), 32)
    shared["gwvec"] = np.ascontiguousarray(ext[idxw].T)
    vp.add("eps_rms", np.full((128, 1), 1e-6, np.float32))
    vp.add("eps_ln", np.full((128, 1), 1e-5, np.float32))
    vp.add("eps_z", np.full((128, 1), 1e-30, np.float32))
    shared["vecs"] = vp.build()
    shared["ident"] = np.eye(128, dtype=np.float32)
    t = np.arange(S)
    j = np.arange(64)[None, :]
    cur = (t // 64)[:, None]
    valid = (j * 64 <= t[:, None])
    forced = (j == 0) | (j == cur) | (j == cur - 1)
    vm = (valid & ~forced).astype(np.float32)
    am = np.where(forced, 1e9, np.where(valid, 0.0, -1.0)).astype(np.float32)
    shared["vmask"] = np.ascontiguousarray(vm.reshape(32, 128, 64).transpose(1, 0, 2))
    shared["amask"] = np.ascontiguousarray(am.reshape(32, 128, 64).transpose(1, 0, 2))
    c = np.arange(256)[:, None]
    ov = ((c * 16 < j * 64 + 64) & (c * 16 + 32 > j * 64) & (c < 255)).astype(np.float32)
    shared["overlap"] = np.ascontiguousarray(ov.reshape(2, 128, 64).transpose(1, 0, 2))
    ex = np.zeros((64, 32, 128), np.float32)
    for kt in range(32):
        ex[2 * kt, kt, 0:64] = 1.0
        ex[2 * kt + 1, kt, 64:128] = 1.0
    shared["expand"] = ex
    return shared, vp.idx


ARENA_F32 = 47600


class Ctx:
    pass


def build(nc, shapes, vidx, layers=(0, 1, 2, 3), stages=("ffn1", "mix", "ffn2", "ple"), do_final=True):
    P = Prog(nc)
    C = Ctx()
    din = {}
    for name, shp in shapes.items():
        din[name] = nc.dram_tensor(name, list(shp), F32, kind="ExternalInput").ap()
    out_d = nc.dram_tensor("out", [S, D], F32, kind="ExternalOutput").ap()

    def scratch(name, shape, dt):
        return nc.dram_tensor(name, list(shape), dt, kind="Internal").ap()
    X = [scratch("xs%d" % i, [8, 128, S], F32) for i in range(3)]
    qkT_d = scratch("qkT", [24, 128, S], BF16)
    vtok_d = scratch("vtok", [8, 128, 32, 65], BF16)
    gT_d = scratch("gT", [48, S], F32)
    fr_d = scratch("frow", [3, 512], F32)
    frd_tk = tks(3)
    oT_d = scratch("oT", [8, 128, S], BF16)
    grep_d = scratch("grep", [16, 128 * NSA_M], F32)
    gwrep_d = scratch("gwrep", [16, 128 * NSAW_M], F32)

    with contextlib.ExitStack() as st:
        arena = st.enter_context(nc.sbuf_tensor("arena", [128, ARENA_F32], F32))
        psb = [st.enter_context(nc.psum_tensor("psb%d" % i, [128, 512], F32)) for i in range(8)]
        pst = tks(8)
        top = [0]

        def alloc(nfree, dt=F32):
            n32 = nfree if dt == F32 else (nfree + 1) // 2
            assert top[0] + n32 <= ARENA_F32, ("arena overflow", top[0], n32)
            a = arena[:, top[0]:top[0] + n32]
            top[0] += n32
            if dt != F32:
                a = a.bitcast(dt)
                a = a[:, 0:nfree]
            return a

        def a3(nfree, dt, **kw):
            pat = kw.pop("pat")
            return alloc(nfree, dt).rearrange(pat, **kw)

        nv = shapes["vecs"][1]
        vecs = alloc(nv)
        vecs_tk = Tk()
        P.dma("sp", vecs, din["vecs"], writes=[vecs_tk])
        ident = alloc(128)
        ident_tk = Tk()
        P.dma("sp", ident, din["ident"], writes=[ident_tk])
        ones_bf = alloc(128, BF16)
        ones_tk = Tk()
        P.op("dve", MSET(ones_bf, 1.0), writes=[ones_tk])
        ident_bf = alloc(128, BF16)
        identbf_tk = Tk()
        P.op("dve", CP(ident_bf, ident), reads=[ident_tk], writes=[identbf_tk])
        base_top = top[0]

        def V(name, c0=0, n=None):
            o, w = vidx[name]
            if n is None:
                n = w - c0
            return vecs[:, o + c0:o + c0 + n]

        def xtile_ap(Xd, i):
            return Xd[:, :, i * TW_:(i + 1) * TW_].rearrange("c p t -> p c t")

        def load_w(dst, src, mc, wt):
            for m in range(mc):
                P.dma("pool", dst[:, m, :], src[m], writes=[wt[m]])

        def rmsnorm(xt, xtk, gain, hT, htk, sq, sqtk, rstd, rstd_tk, out_f32=None):
            for c in range(8):
                sl = c % 2
                P.op("act", ACTF(sq[:, sl, :], xt[:, c, :], AF.Square), reads=[xtk[c]], writes=[sqtk[sl]])
                P.op("pe", MM(psb[6][:, :], ones_bf, sq[:, sl, :], c == 0, c == 7),
                     reads=[sqtk[sl], ones_tk], writes=[pst[6]])
            P.op("act", ACTF(rstd, psb[6][:, :], AF.Sqrt, bias=V("eps_rms"), scale=1.0 / D),
                 reads=[pst[6], vecs_tk], writes=[rstd_tk])
            P.op("dve", RCP(rstd, rstd), reads=[rstd_tk], writes=[rstd_tk])
            for c in range(8):
                P.op("dve", STT(hT[:, c, :], xt[:, c, :], gain[:, c:c + 1], rstd, ALU.mult, ALU.mult),
                     reads=[xtk[c], rstd_tk, vecs_tk], writes=[htk[c]])

        def pass_input():
            xin = [alloc(D) for _ in range(2)]
            xin_tk = tks(2)
            xo = [a3(8 * 128, F32, pat="p (c t) -> p c t", c=8) for _ in range(2)]
            xo_tk = tks(2)
            for tt in range(32):
                b = tt % 2
                P.dma("sp", xin[b], din["x"][tt * 128:(tt + 1) * 128, :], writes=[xin_tk[b]])
                for c in range(8):
                    bank = c // 4
                    P.op("pe", TRN(psb[bank][:, (c % 4) * 128:(c % 4 + 1) * 128], xin[b][:, c * 128:(c + 1) * 128], ident),
                         reads=[xin_tk[b], ident_tk], writes=[pst[bank]])
                P.op("act", ACTF(xo[b][:, 0:4, :], psb[0][:, :].rearrange("p (c t) -> p c t", c=4), AF.Copy),
                     reads=[pst[0]], writes=[xo_tk[b]])
                P.op("dve", CP(xo[b][:, 4:8, :], psb[1][:, :].rearrange("p (c t) -> p c t", c=4)),
                     reads=[pst[1], xo_tk[b]], writes=[xo_tk[b]])
                P.dma("sp", X[0][:, :, tt * 128:(tt + 1) * 128].rearrange("c p t -> p c t"), xo[b], reads=[xo_tk[b]])

        def pass_ffn_half(l, fi, hf, Xn, Xr, Xo, norm_name):
            wg = a3(11 * 1024, BF16, pat="p (m f) -> p m f", m=11)
            wu = a3(11 * 1024, BF16, pat="p (m f) -> p m f", m=11)
            wd = a3(8 * 1408, BF16, pat="p (m f) -> p m f", m=8)
            wg_tk, wu_tk, wd_tk = tks(11), tks(11), tks(8)
            key = "%d_%d_%d" % (l, fi, hf)
            for m in range(11):
                P.dma("pool", wg[:, m, :], din["wg_" + key][m], writes=[wg_tk[m]])
                P.dma("pool", wu[:, m, :], din["wu_" + key][m], writes=[wu_tk[m]])
            load_w(wd, din["wd_" + key], 8, wd_tk)
            same = Xn is Xr
            xn = [a3(8 * 512, F32, pat="p (c t) -> p c t", c=8) for _ in range(2)]
            xn_tk = [tks(8) for _ in range(2)]
            if same:
                xr, xr_tk = xn, xn_tk
            else:
                xr = [a3(8 * 512, F32, pat="p (c t) -> p c t", c=8) for _ in range(2)]
                xr_tk = [tks(8) for _ in range(2)]
            hT = a3(8 * 512, BF16, pat="p (c t) -> p c t", c=8)
            htk = tks(8)
            sq = a3(2 * 512, BF16, pat="p (c t) -> p c t", c=2)
            sqtk = tks(2)
            rstd = alloc(512)
            rstd_tk = Tk()
            act_ = a3(11 * 512, BF16, pat="p (c t) -> p c t", c=11)
            atk = tks(11)
            sg = [alloc(512) for _ in range(2)]
            sgtk = tks(2)
            gain = V(norm_name)

            def load(i):
                b = i % 2
                P.dma("sp", xn[b], xtile_ap(Xn, i), writes=xn_tk[b])
                if not same:
                    P.dma("sp", xr[b], xtile_ap(Xr, i), writes=xr_tk[b])
            load(0)
            for i in range(NT):
                b = i % 2
                if i + 1 < NT:
                    load(i + 1)
                rmsnorm(xn[b], xn_tk[b], gain, hT, htk, sq, sqtk, rstd, rstd_tk)
                for j in range(11):
                    pg, pu = j % 2, 2 + j % 2
                    for c in range(8):
                        P.op("pe", MM(psb[pg][:, :], wg[:, j, c * 128:(c + 1) * 128], hT[:, c, :], c == 0, c == 7),
                             reads=[wg_tk[j], htk[c]], writes=[pst[pg]])
                    for c in range(8):
                        P.op("pe", MM(psb[pu][:, :], wu[:, j, c * 128:(c + 1) * 128], hT[:, c, :], c == 0, c == 7),
                             reads=[wu_tk[j], htk[c]], writes=[pst[pu]])
                    P.op("act", ACTF(sg[j % 2], psb[pg][:, :], AF.Silu), reads=[pst[pg]], writes=[sgtk[j % 2]])
                    P.op("dve", TT(act_[:, j, :], sg[j % 2], psb[pu][:, :], ALU.mult),
                         reads=[sgtk[j % 2], pst[pu]], writes=[atk[j]])
                for m in range(8):
                    py = 4 + m % 2
                    for j in range(11):
                        P.op("pe", MM(psb[py][:, :], wd[:, m, j * 128:(j + 1) * 128], act_[:, j, :], j == 0, j == 10),
                             reads=[wd_tk[m], atk[j]], writes=[pst[py]])
                    P.op("dve", STT(xr[b][:, m, :], psb[py][:, :], 0.5, xr[b][:, m, :], ALU.mult, ALU.add),
                         reads=[pst[py], xr_tk[b][m]], writes=[xr_tk[b][m]])
                P.dma("sp", xtile_ap(Xo, i), xr[b], reads=xr_tk[b])

        def pass_ple(l, Xc):
            wgp = a3(8 * 1024, BF16, pat="p (m f) -> p m f", m=8)
            wip = a3(8 * 256, BF16, pat="p (m f) -> p m f", m=8)
            wgp_tk, wip_tk = tks(8), tks(8)
            load_w(wgp, din["pleg_%d" % l], 8, wgp_tk)
            load_w(wip, din["plei_%d" % l], 8, wip_tk)
            xn = [a3(8 * 512, F32, pat="p (c t) -> p c t", c=8) for _ in range(2)]
            xn_tk = [tks(8) for _ in range(2)]
            pin = [a3(4 * 256, F32, pat="p (s f) -> p s f", s=4) for _ in range(2)]
            pin_tk = tks(2)
            pT = a3(2 * 512, BF16, pat="p (c t) -> p c t", c=2)
            pT_tk = tks(2)
            hT = a3(8 * 512, BF16, pat="p (c t) -> p c t", c=8)
            htk = tks(8)
            sq = a3(2 * 512, BF16, pat="p (c t) -> p c t", c=2)
            sqtk = tks(2)
            rstd = alloc(512)
            rstd_tk = Tk()
            sg = [alloc(512) for _ in range(2)]
            sgtk = tks(2)
            gain = V("ple_norm%d" % l)
            pl = din["p"][l]

            def load(i):
                b = i % 2
                P.dma("sp", xn[b], xtile_ap(Xc, i), writes=xn_tk[b])
                P.dma("sp", pin[b], pl[i * 512:(i + 1) * 512, :].rearrange("(s p) f -> p s f", p=128), writes=[pin_tk[b]])
            load(0)
            for i in range(NT):
                b = i % 2
                if i + 1 < NT:
                    load(i + 1)
                rmsnorm(xn[b], xn_tk[b], gain, hT, htk, sq, sqtk, rstd, rstd_tk)
                for kc in range(2):
                    for s in range(4):
                        P.op("pe", TRN(psb[7][:, s * 128:(s + 1) * 128], pin[b][:, s, kc * 128:(kc + 1) * 128], ident),
                             reads=[pin_tk[b], ident_tk], writes=[pst[7]])
                    P.op("act", ACTF(pT[:, kc, :], psb[7][:, :], AF.Copy), reads=[pst[7]], writes=[pT_tk[kc]])
                for m in range(8):
                    pg, pi = m % 2, 2 + m % 2
                    for c in range(8):
                        P.op("pe", MM(psb[pg][:, :], wgp[:, m, c * 128:(c + 1) * 128], hT[:, c, :], c == 0, c == 7),
                             reads=[wgp_tk[m], htk[c]], writes=[pst[pg]])
                    for c in range(2):
                        P.op("pe", MM(psb[pi][:, :], wip[:, m, c * 128:(c + 1) * 128], pT[:, c, :], c == 0, c == 1),
                             reads=[wip_tk[m], pT_tk[c]], writes=[pst[pi]])
                    P.op("act", ACTF(sg[m % 2], psb[pg][:, :], AF.Sigmoid), reads=[pst[pg]], writes=[sgtk[m % 2]])
                    P.op("dve", TT(sg[m % 2], sg[m % 2], psb[pi][:, :], ALU.mult),
                         reads=[sgtk[m % 2], pst[pi]], writes=[sgtk[m % 2]])
                    P.op("dve", TT(xn[b][:, m, :], xn[b][:, m, :], sg[m % 2], ALU.add),
                         reads=[sgtk[m % 2], xn_tk[b][m]], writes=[xn_tk[b][m]])
                P.dma("sp", xtile_ap(Xc, i), xn[b], reads=xn_tk[b])

        def pass_conv(l, jx, Xc):
            w1 = a3(16 * 1024, BF16, pat="p (m f) -> p m f", m=16)
            w2 = a3(8 * 1024, BF16, pat="p (m f) -> p m f", m=8)
            w1_tk, w2_tk = tks(16), tks(8)
            load_w(w1, din["pw1_%d" % jx], 16, w1_tk)
            load_w(w2, din["pw2_%d" % jx], 8, w2_tk)
            diag = a3(31 * 8 * 128, BF16, pat="p (j c m) -> p j c m", j=31, c=8)
            diag_tk = tks(8)
            wdw = V("w_dw_%d" % jx)
            for c in range(8):
                for j in range(31):
                    P.op("dve", TS(diag[:, j, c, :], ident_bf, wdw[:, c * 31 + j:c * 31 + j + 1], None, ALU.mult),
                         reads=[identbf_tk, vecs_tk], writes=[diag_tk[c]])
            xn = [a3(8 * 512, F32, pat="p (c t) -> p c t", c=8)] * 2
            xn_tk = [tks(8)] * 2
            hT = a3(8 * 512, BF16, pat="p (c t) -> p c t", c=8)
            htk = tks(8)
            sq = a3(2 * 512, BF16, pat="p (c t) -> p c t", c=2)
            sqtk = tks(2)
            rstd = alloc(512)
            rstd_tk = Tk()
            ub = a3(8 * 542, BF16, pat="p (c t) -> p c t", c=8)
            ub_tk = tks(8)
            yb = a3(8 * 512, F32, pat="p (c t) -> p c t", c=8)
            yb_tk = tks(8)
            ybf = a3(2 * 512, BF16, pat="p (c t) -> p c t", c=2)
            ybf_tk = tks(2)
            ysq = a3(2 * 512, BF16, pat="p (c t) -> p c t", c=2)
            ysq_tk = tks(2)
            sg = [alloc(512) for _ in range(2)]
            sgtk = tks(2)
            mu = alloc(512)
            mu_tk = Tk()
            rs = alloc(512)
            rs_tk = Tk()
            tmp = alloc(512)
            tmp_tk = Tk()
            gain = V("mix_norm%d" % l)
            b1 = V("b_pw1_%d" % jx)
            bdw = V("b_dw_%d" % jx)
            lng = V("ln_g_%d" % jx)
            lnb = V("ln_b_%d" % jx)
            b2 = V("b_pw2_%d" % jx)
            for c in range(8):
                P.op("dve", MSET(ub[:, c, 0:30], 0.0), writes=[ub_tk[c]])

            def load(i):
                b = i % 2
                P.dma("sp", xn[b], xtile_ap(Xc, i), writes=xn_tk[b])
            for i in range(NT):
                b = i % 2
                load(i)
                rmsnorm(xn[b], xn_tk[b], gain, hT, htk, sq, sqtk, rstd, rstd_tk)
                for m in range(8):
                    pa, pg = m % 2, 2 + m % 2
                    for c in range(8):
                        P.op("pe", MM(psb[pa][:, :], w1[:, m, c * 128:(c + 1) * 128], hT[:, c, :], c == 0, c == 7),
                             reads=[w1_tk[m], htk[c]], writes=[pst[pa]])
                    for c in range(8):
                        P.op("pe", MM(psb[pg][:, :], w1[:, 8 + m, c * 128:(c + 1) * 128], hT[:, c, :], c == 0, c == 7),
                             reads=[w1_tk[8 + m], htk[c]], writes=[pst[pg]])
                    P.op("act", ACTF(sg[m % 2], psb[pg][:, :], AF.Sigmoid, bias=b1[:, 8 + m:9 + m]),
                         reads=[pst[pg], vecs_tk], writes=[sgtk[m % 2]])
                    P.op("dve", STT(ub[:, m, 30:542], psb[pa][:, :], b1[:, m:m + 1], sg[m % 2], ALU.add, ALU.mult),
                         reads=[pst[pa], sgtk[m % 2], vecs_tk], writes=[ub_tk[m]])
                for m in range(8):
                    py = 4 + m % 2
                    for j in range(31):
                        P.op("pe", MM(psb[py][:, :], diag[:, j, m, :], ub[:, m, j:j + 512], j == 0, j == 30),
                             reads=[diag_tk[m], ub_tk[m]], writes=[pst[py]])
                    P.op("act", ACTF(yb[:, m, :], psb[py][:, :], AF.Identity, bias=bdw[:, m:m + 1]),
                         reads=[pst[py], vecs_tk], writes=[yb_tk[m]])
                    P.op("dve", CP(ub[:, m, 0:30], ub[:, m, 512:542]), reads=[ub_tk[m]], writes=[ub_tk[m]])
                    sl = m % 2
                    P.op("dve", CP(ybf[:, sl, :], yb[:, m, :]), reads=[yb_tk[m]], writes=[ybf_tk[sl]])
                    P.op("act", ACTF(ysq[:, sl, :], yb[:, m, :], AF.Square), reads=[yb_tk[m]], writes=[ysq_tk[sl]])
                    P.op("pe", MM(psb[6][:, :], ones_bf, ybf[:, sl, :], m == 0, m == 7),
                         reads=[ybf_tk[sl], ones_tk], writes=[pst[6]])
                    P.op("pe", MM(psb[7][:, :], ones_bf, ysq[:, sl, :], m == 0, m == 7),
                         reads=[ysq_tk[sl], ones_tk], writes=[pst[7]])
                P.op("dve", TS(mu, psb[6][:, :], 1.0 / D, None, ALU.mult), reads=[pst[6]], writes=[mu_tk])
                P.op("dve", TT(tmp, mu, mu, ALU.mult), reads=[mu_tk], writes=[tmp_tk])
                P.op("dve", STT(tmp, psb[7][:, :], 1.0 / D, tmp, ALU.mult, ALU.subtract),
                     reads=[pst[7], tmp_tk], writes=[tmp_tk])
                P.op("dve", TS(tmp, tmp, 0.0, None, ALU.max), reads=[tmp_tk], writes=[tmp_tk])
                P.op("act", ACTF(rs, tmp, AF.Sqrt, bias=V("eps_ln"), scale=1.0), reads=[tmp_tk, vecs_tk], writes=[rs_tk])
                P.op("dve", RCP(rs, rs), reads=[rs_tk], writes=[rs_tk])
                P.op("dve", STT(mu, mu, -1.0, rs, ALU.mult, ALU.mult), reads=[mu_tk, rs_tk], writes=[mu_tk])
                for m in range(8):
                    P.op("dve", TT(yb[:, m, :], yb[:, m, :], rs, ALU.mult), reads=[yb_tk[m], rs_tk], writes=[yb_tk[m]])
                    P.op("dve", TT(yb[:, m, :], yb[:, m, :], mu, ALU.add), reads=[yb_tk[m], mu_tk], writes=[yb_tk[m]])
                    P.op("act", ACTF(hT[:, m, :], yb[:, m, :], AF.Silu, bias=lnb[:, m:m + 1], scale=lng[:, m:m + 1]),
                         reads=[yb_tk[m], vecs_tk], writes=[htk[m]])
                for m in range(8):
                    po = m % 2
                    for c in range(8):
                        P.op("pe", MM(psb[po][:, :], w2[:, m, c * 128:(c + 1) * 128], hT[:, c, :], c == 0, c == 7),
                             reads=[w2_tk[m], htk[c]], writes=[pst[po]])
                    P.op("dve", STT(xn[b][:, m, :], psb[po][:, :], b2[:, m:m + 1], xn[b][:, m, :], ALU.add, ALU.add),
                         reads=[pst[po], xn_tk[b][m], vecs_tk], writes=[xn_tk[b][m]])
                P.dma("sp", xtile_ap(Xc, i), xn[b], reads=xn_tk[b])

        def pass_nsa_tables():
            for h in range(16):
                src = bass.AP(tensor=din["gvec"].tensor, offset=h * NSA_M, ap=[[0, 128], [1, NSA_M]])
                dst = bass.AP(tensor=grep_d.tensor, offset=h * 128 * NSA_M, ap=[[NSA_M, 128], [1, NSA_M]])
                P.dma("sp", dst, src)
                src = bass.AP(tensor=din["gwvec"].tensor, offset=h * NSAW_M, ap=[[0, 128], [1, NSAW_M]])
                dst = bass.AP(tensor=gwrep_d.tensor, offset=h * 128 * NSAW_M, ap=[[NSAW_M, 128], [1, NSAW_M]])
                P.dma("sp", dst, src)

        def pass_nsa_proj(l, jx, Xc):
            wf = a3(25 * 1024, BF16, pat="p (m f) -> p m f", m=25)
            wf_tk = tks(25)
            load_w(wf, din["nsa_wf_%d" % jx], 25, wf_tk)
            wt = a3(8 * 560, BF16, pat="p (c f) -> p c f", c=8)
            wt_tk = Tk()
            P.dma("pool", wt, din["nsa_wt_%d" % jx], writes=[wt_tk])
            xn = [a3(8 * 512, F32, pat="p (c t) -> p c t", c=8) for _ in range(2)]
            xn_tk = [tks(8) for _ in range(2)]
            hT = a3(8 * 512, BF16, pat="p (c t) -> p c t", c=8)
            htk = tks(8)
            sq = a3(2 * 512, BF16, pat="p (c t) -> p c t", c=2)
            sqtk = tks(2)
            rstd = alloc(512)
            rstd_tk = Tk()
            stg = [a3(24 * 512, BF16, pat="p (m t) -> p m t", m=24) for _ in range(2)]
            stg_tk = tks(2)
            vst = [a3(4 * 8 * 65, BF16, pat="p (s k e) -> p s k e", s=4, k=8) for _ in range(2)]
            vst_tk = tks(2)
            gst = [alloc(512) for _ in range(2)]
            gst_tk = tks(2)
            gain = V("mix_norm%d" % l)
            for b in range(2):
                P.op("dve", MSET(vst[b], 1.0), writes=[vst_tk[b]])

            def load(i):
                b = i % 2
                P.dma("sp", xn[b], xtile_ap(Xc, i), writes=xn_tk[b])
            load(0)
            for i in range(NT):
                b = i % 2
                if i + 1 < NT:
                    load(i + 1)
                rmsnorm(xn[b], xn_tk[b], gain, hT, htk, sq, sqtk, rstd, rstd_tk)
                for m in range(25):
                    pb = m % 4
                    for c in range(8):
                        P.op("pe", MM(psb[pb][:, :], wf[:, m, c * 128:(c + 1) * 128], hT[:, c, :], c == 0, c == 7),
                             reads=[wf_tk[m], htk[c]], writes=[pst[pb]])
                    if m == 24:
                        P.op("act", ACTF(gst[b], psb[pb][:, :], AF.Sigmoid), reads=[pst[pb]], writes=[gst_tk[b]])
                        P.dma("sp", gT_d[:, i * 512:(i + 1) * 512], gst[b][0:48, :], reads=[gst_tk[b]])
                    elif m % 2 == 0:
                        P.op("act", ACTF(stg[b][:, m, :], psb[pb][:, :], AF.Copy), reads=[pst[pb]], writes=[stg_tk[b]])
                    else:
                        P.op("dve", CP(stg[b][:, m, :], psb[pb][:, :]), reads=[pst[pb]], writes=[stg_tk[b]])
                P.dma("sp", qkT_d[:, :, i * 512:(i + 1) * 512].rearrange("m p t -> p m t"), stg[b], reads=[stg_tk[b]])
                for s in range(4):
                    pv = 4 + s % 2
                    for c in range(8):
                        P.op("pe", MM(psb[pv][:, :], hT[:, c, s * 128:(s + 1) * 128], wt[:, c, 0:512], c == 0, c == 7),
                             reads=[wt_tk, htk[c]], writes=[pst[pv]])
                    P.op("dve", CP(vst[b][:, s, :, 0:64], psb[pv][:, :].rearrange("p (k e) -> p k e", k=8)),
                         reads=[pst[pv]], writes=[vst_tk[b]])
                for s in range(4):
                    P.dma("sp", vtok_d[:, :, i * 4 + s, :].rearrange("k p e -> p k e"), vst[b][:, s, :, :], reads=[vst_tk[b]])

        def pass_nsa_group(jx, g):
            ksx = alloc(S, BF16)
            ksx_tk, ex_tk = Tk(), Tk()
            P.dma("sp", ksx[0:64], qkT_d[8 + 4 * g + 2, 0:64, :], writes=[ksx_tk])
            P.dma("pool", ksx[64:128], din["expand"].rearrange("n k m -> n (k m)"), writes=[ex_tk])
            kw = alloc(S, BF16)
            kw_tk = Tk()
            P.dma("sp", kw[0:64], qkT_d[8 + 4 * g + 3, 0:64, :], writes=[kw_tk])
            qx = [alloc(S, BF16) for _ in range(2)]
            qx_tk = tks(2)
            nmq_tk = [tks(8) for _ in range(2)]
            vs = a3(32 * 65, BF16, pat="p (s e) -> p s e", s=32)
            vw = a3(32 * 65, BF16, pat="p (s e) -> p s e", s=32)
            vs_tk, vw_tk = Tk(), Tk()
            P.dma("sp", vs, vtok_d[g], writes=[vs_tk])
            P.dma("sp", vw, vtok_d[4 + g], writes=[vw_tk])
            kcT = alloc(256, BF16)
            kcT_tk = Tk()
            vca = a3(2 * 66, BF16, pat="p (c e) -> p c e", c=2)
            vca_tk = Tk()
            ovl1 = a3(2 * 66, BF16, pat="p (c e) -> p c e", c=2)
            ovl1_tk = Tk()
            ovl = a3(2 * 64, F32, pat="p (c e) -> p c e", c=2)
            ovl_tk = Tk()
            P.dma("sp", ovl, din["overlap"], writes=[ovl_tk])
            imp = a3(32 * 64, F32, pat="p (s f) -> p s f", s=32)
            imp_tk = tks(32)
            NSL = 4
            Ef = [alloc(512) for _ in range(NSL)]
            Ef_tk = tks(NSL)
            Eb = [alloc(512, BF16) for _ in range(NSL)]
            Eb_tk = tks(NSL)
            sm = alloc(8)
            sm_tk = tks(8)
            grp_top = top[0]

            kc = alloc(S, BF16)
            vc = alloc(S, BF16)
            kc_tk, vc_tk = Tk(), Tk()
            P.dma("sp", kc[0:64], qkT_d[8 + 4 * g + 0, 0:64, :], writes=[kc_tk])
            P.dma("sp", vc[0:64], qkT_d[8 + 4 * g + 1, 0:64, :], writes=[vc_tk])
            wk1 = a3(32 * 256, BF16, pat="p (l j) -> p l j", l=32)
            wv1 = a3(32 * 256, BF16, pat="p (l j) -> p l j", l=32)
            wk1_tk, wv1_tk = Tk(), Tk()
            P.dma("pool", wk1[0:64], din["wk1_%d" % jx], writes=[wk1_tk])
            P.dma("pool", wv1[0:64], din["wv1_%d" % jx], writes=[wv1_tk])
            w2k = a3(2 * 128, BF16, pat="p (c m) -> p c m", c=2)
            w2v = a3(2 * 64, BF16, pat="p (c m) -> p c m", c=2)
            w2k_tk, w2v_tk = Tk(), Tk()
            P.dma("pool", w2k, din["w2k_%d" % jx], writes=[w2k_tk])
            P.dma("pool", w2v, din["w2v_%d" % jx], writes=[w2v_tk])
            posk = alloc(32, BF16)
            posv = alloc(32, BF16)
            posk_tk, posv_tk = Tk(), Tk()
            P.dma("pool", posk[0:64], din["posk_%d" % jx], writes=[posk_tk])
            P.dma("pool", posv[0:64], din["posv_%d" % jx], writes=[posv_tk])
            cb = alloc(4)
            cb_tk = Tk()
            xg = a3(2 * 256, F32, pat="p (c t) -> p c t", c=2)
            xg_tk = tks(2)
            t1 = a3(2 * 256, F32, pat="p (c t) -> p c t", c=2)
            t1_tk = tks(2)
            gl = a3(2 * 256, BF16, pat="p (c t) -> p c t", c=2)
            gl_tk = tks(2)
            P.op("dve", MSET(kcT, 0.0), writes=[kcT_tk])
            P.op("dve", MSET(vca, 0.0), writes=[vca_tk])
            P.op("dve", MSET(vca[:, :, 64:65], 1.0), writes=[vca_tk])
            P.op("dve", MSET(ovl1[:, :, 64:65], 1.0), writes=[ovl1_tk])
            P.op("dve", CP(ovl1[:, :, 0:64], ovl), reads=[ovl_tk], writes=[ovl1_tk])
            for which, (src, src_tk, w1s, w1_tk, pos, pos_tk) in enumerate(
                    ((kc, kc_tk, wk1, wk1_tk, posk, posk_tk), (vc, vc_tk, wv1, wv1_tk, posv, posv_tk))):
                for jc in range(2):
                    for l_ in range(32):
                        P.op("pe", MM(psb[4][:, jc:jc + 1], w1s[0:64, l_, jc * 128:(jc + 1) * 128], pos[0:64, l_:l_ + 1],
                                      l_ == 0, l_ == 31), reads=[w1_tk, pos_tk], writes=[pst[4]])
                P.op("dve", CP(cb[:, 0:2], psb[4][:, 0:2]), reads=[pst[4]], writes=[cb_tk])
                for jc in range(2):
                    pb = 5 + jc
                    for l_ in range(32):
                        P.op("pe", MM(psb[pb][:, 0:255], w1s[0:64, l_, jc * 128:(jc + 1) * 128],
                                      src[0:64, l_:l_ + 16 * 254 + 1:16], l_ == 0, l_ == 31),
                             reads=[w1_tk, src_tk], writes=[pst[pb]])
                    P.op("act", ACTF(xg[:, jc, 0:255], psb[pb][:, 0:255], AF.Identity, bias=cb[:, jc:jc + 1]),
                         reads=[pst[pb], cb_tk], writes=[xg_tk[jc]])
                    P.op("dve", TT(t1[:, jc, 0:255], xg[:, jc, 0:255], xg[:, jc, 0:255], ALU.mult),
                         reads=[xg_tk[jc]], writes=[t1_tk[jc]])
                    P.op("dve", TS(t1[:, jc, 0:255], t1[:, jc, 0:255], 0.044715, 1.0, ALU.mult, ALU.add),
                         reads=[t1_tk[jc]], writes=[t1_tk[jc]])
                    P.op("dve", TT(t1[:, jc, 0:255], t1[:, jc, 0:255], xg[:, jc, 0:255], ALU.mult),
                         reads=[t1_tk[jc], xg_tk[jc]], writes=[t1_tk[jc]])
                    P.op("act", ACTF(t1[:, jc, 0:255], t1[:, jc, 0:255], AF.Sigmoid, scale=1.5957691),
                         reads=[t1_tk[jc]], writes=[t1_tk[jc]])
                    P.op("dve", TT(gl[:, jc, 0:255], t1[:, jc, 0:255], xg[:, jc, 0:255], ALU.mult),
                         reads=[t1_tk[jc], xg_tk[jc]], writes=[gl_tk[jc]])
                if which == 0:
                    for jc in range(2):
                        P.op("pe", MM(psb[7][:, 0:255], w2k[:, jc, :], gl[:, jc, 0:255], jc == 0, jc == 1),
                             reads=[w2k_tk, gl_tk[jc]], writes=[pst[7]])
                    P.op("act", ACTF(kcT[:, 0:255], psb[7][:, 0:255], AF.Copy), reads=[pst[7]], writes=[kcT_tk])
                else:
                    for ct in range(2):
                        ncr = 128 if ct == 0 else 127
                        for jc in range(2):
                            P.op("pe", MM(psb[7][0:ncr, 0:64], gl[:, jc, ct * 128:ct * 128 + ncr], w2v[:, jc, :], jc == 0, jc == 1),
                                 reads=[w2v_tk, gl_tk[jc]], writes=[pst[7]])
                        P.op("act", ACTF(vca[0:ncr, ct, 0:64], psb[7][0:ncr, 0:64], AF.Copy), reads=[pst[7]], writes=[vca_tk])
            P.barrier()
            top[0] = grp_top
            if DBG < 3:
                return

            tabc = alloc(6144)
            tabc_tk = tks(3)
            tabs = alloc(2560)
            tabs_tk = Tk()
            tabw = alloc(1408)
            tabw_tk = Tk()
            vmk = a3(32 * 64, F32, pat="p (s f) -> p s f", s=32)
            amk = a3(32 * 64, F32, pat="p (s f) -> p s f", s=32)
            vmk_tk, amk_tk = Tk(), Tk()
            P.dma("sp", vmk, din["vmask"], writes=[vmk_tk])
            P.dma("sp", amk, din["amask"], writes=[amk_tk])
            sc = alloc(64)
            sc2 = alloc(64)
            m8a = alloc(8)
            m8b = alloc(8)
            nm = alloc(128)
            sc_tk, sc2_tk, m8a_tk, m8b_tk, nm_tk = Tk(), Tk(), Tk(), Tk(), Tk()
            gq = [alloc(3 * 512) for _ in range(2)]
            gq_tk = tks(2)
            ostg = [alloc(512) for _ in range(3)]
            ostg_tk = tks(3)
            lrow = [alloc(512) for _ in range(3)]
            lrow_tk = tks(3)
            facb = [alloc(512) for _ in range(3)]
            facb_tk = tks(3)
            Oac = [alloc(512) for _ in range(2)]
            Oac_tk = tks(2)
            ost = [alloc(512, BF16) for _ in range(2)]
            ost_tk = tks(2)
            step = [0]
            zcnt = [0]
            b31c = alloc(1)
            b31_tk = Tk()

            def load_tabc(h):
                src = bass.AP(tensor=grep_d.tensor, offset=h * 128 * NSA_M + (NSA_OFF - 2048),
                              ap=[[NSA_M - 16, 128], [1, 6144]])
                P.dma("sp", tabc, src, writes=tabc_tk)
                for pz in range(3):
                    P.op("act", ACTF(tabc[:, pz * 2048:(pz + 1) * 2048], tabc[:, pz * 2048:(pz + 1) * 2048], AF.Exp),
                         reads=[tabc_tk[pz]], writes=[tabc_tk[pz]])

            for r in range(4):
                h = 4 * g + r
                pr, hf = r // 2, r % 2
                b = r % 2
                P.dma("sp", qx[b][0:64], qkT_d[2 * g + pr, hf * 64:(hf + 1) * 64, :], writes=[qx_tk[b]])
                load_tabc(h)
                for QB in range(8):
                    qs = slice(QB * 512, (QB + 1) * 512)
                    cts = [0] if QB < 4 else [0, 1]
                    slots = []
                    for ct in cts:
                        sl = step[0] % NSL
                        step[0] += 1
                        P.op("pe", MM(psb[sl][:, :], kcT[0:64, ct * 128:(ct + 1) * 128], qx[b][0:64, qs], True, True),
                             reads=[kcT_tk, qx_tk[b]], writes=[pst[sl]])
                        P.op("act", ACTF(Ef[sl], psb[sl][:, :], AF.Exp, scale=0.125), reads=[pst[sl]], writes=[Ef_tk[sl]])
                        sj = QB * 512 + 2017 - 2048 * ct
                        P.op("dve", TT(Eb[sl], Ef[sl], tabc[:, sj:sj + 512], ALU.mult),
                             reads=[Ef_tk[sl]] + tabc_tk, writes=[Eb_tk[sl]])
                        slots.append((ct, sl))
                    for s in range(4):
                        for (ct, sl) in slots:
                            P.op("pe", MM(psb[4][:, s * 65:(s + 1) * 65], Eb[sl][:, s * 128:(s + 1) * 128], ovl1[:, ct, 0:65],
                                          ct == cts[0], ct == cts[-1]), reads=[Eb_tk[sl], ovl1_tk], writes=[pst[4]])
                    for s in range(4):
                        qt = QB * 4 + s
                        k = qt % 8
                        z = sm[:, k:k + 1]
                        P.op("dve", TS(z, psb[4][:, s * 65 + 64:s * 65 + 65], 1e-30, None, ALU.max), reads=[pst[4]], writes=[sm_tk[k]])
                        P.op("dve", RCP(z, z), reads=[sm_tk[k]], writes=[sm_tk[k]])
                        if r == 0:
                            P.op("dve", TS(imp[:, qt, :], psb[4][:, s * 65:s * 65 + 64], z, None, ALU.mult),
                                 reads=[pst[4], sm_tk[k]], writes=[imp_tk[qt]])
                        else:
                            P.op("dve", STT(imp[:, qt, :], psb[4][:, s * 65:s * 65 + 64], z, imp[:, qt, :], ALU.mult, ALU.add),
                                 reads=[pst[4], sm_tk[k], imp_tk[qt]], writes=[imp_tk[qt]])
            if DBG < 4:
                return
            for qt in range(32):
                P.op("dve", TT(sc, imp[:, qt, :], vmk[:, qt, :], ALU.mult), reads=[imp_tk[qt], vmk_tk], writes=[sc_tk])
                P.op("dve", TT(sc, sc, amk[:, qt, :], ALU.add), reads=[sc_tk, amk_tk], writes=[sc_tk])
                P.op("dve", lambda e: e.max(out=m8a, in_=sc), reads=[sc_tk], writes=[m8a_tk])
                P.op("dve", lambda e: e.match_replace(out=sc2, in_to_replace=m8a, in_values=sc, imm_value=-1e30),
                     reads=[sc_tk, m8a_tk], writes=[sc2_tk])
                P.op("dve", lambda e: e.max(out=m8b, in_=sc2), reads=[sc2_tk], writes=[m8b_tk])
                P.op("dve", TS(nm[:, 0:64], sc, m8b[:, 7:8], -30000.0, ALU.is_lt, ALU.mult), reads=[sc_tk, m8b_tk], writes=[nm_tk])
                P.op("dve", TS(nm[:, 64:128], sc, m8b[:, 7:8], -30000.0, ALU.is_lt, ALU.mult), reads=[sc_tk, m8b_tk, nm_tk], writes=[nm_tk])
                P.op("pe", TRN(psb[7][:, 0:128], nm, ident), reads=[nm_tk, ident_tk], writes=[pst[7]])
                for b in range(2):
                    P.op("act", ACTF(qx[b][64:128, qt * 128:(qt + 1) * 128], psb[7][64:128, 0:128], AF.Copy),
                         reads=[pst[7]], writes=[nmq_tk[b][qt // 4]])
            if DBG < 5:
                return
            for r in range(4):
                h = 4 * g + r
                pr, hf = r // 2, r % 2
                b = r % 2
                P.dma("sp", qx[b][0:64], qkT_d[2 * g + pr, hf * 64:(hf + 1) * 64, :], writes=[qx_tk[b]])
                load_tabc(h)
                src = bass.AP(tensor=grep_d.tensor, offset=h * 128 * NSA_M + (NSA_OFF - 384),
                              ap=[[NSA_M - 1, 128], [1, 2560]])
                P.dma("sp", tabs, src, writes=[tabs_tk])
                P.op("act", ACTF(b31c, tabs[:, 2559:2560], AF.Copy), reads=[tabs_tk], writes=[b31_tk])
                P.op("act", ACTF(tabs, tabs, AF.Exp), reads=[tabs_tk, b31_tk], writes=[tabs_tk])
                src = bass.AP(tensor=gwrep_d.tensor, offset=h * 128 * NSAW_M + (NSAW_OFF - 384),
                              ap=[[NSAW_M - 1, 128], [1, 1408]])
                P.dma("sp", tabw, src, writes=[tabw_tk])
                P.op("act", ACTF(tabw, tabw, AF.Exp), reads=[tabw_tk], writes=[tabw_tk])
                tiles = []
                for QB in range(8):
                    qs = slice(QB * 512, (QB + 1) * 512)
                    cts = [0] if QB < 4 else [0, 1]
                    for ct in cts:
                        sj = QB * 512 + 2017 - 2048 * ct
                        tiles.append(dict(j=0, QB=QB, first=ct == cts[0], last=ct == cts[-1],
                                          lhsT=kcT[0:64, ct * 128:(ct + 1) * 128], lt=[kcT_tk], rhs=qx[b][0:64, qs], rt=[qx_tk[b]],
                                          tab=tabc[:, sj:sj + 512], tt=tabc_tk, v=vca[:, ct, 0:65], vt=[vca_tk], acc=4, far=False))
                    for kt in range(4 * QB + 4):
                        off = min(QB * 512 - kt * 128, 1664) + 384
                        tiles.append(dict(j=1, QB=QB, first=kt == 0, last=kt == 4 * QB + 3,
                                          lhsT=ksx[:, kt * 128:(kt + 1) * 128], lt=[ksx_tk, ex_tk], rhs=qx[b][:, qs],
                                          rt=[qx_tk[b], nmq_tk[b][QB]],
                                          tab=tabs[:, off:off + 512], tt=[tabs_tk], v=vs[:, kt, :], vt=[vs_tk], acc=5,
                                          far=(QB * 512 - kt * 128 >= 1664)))
                    k0 = max(0, 4 * QB - 4)
                    for kt in range(k0, 4 * QB + 4):
                        off = QB * 512 - kt * 128 + 384
                        tiles.append(dict(j=2, QB=QB, first=kt == k0, last=kt == 4 * QB + 3,
                                          lhsT=kw[0:64, kt * 128:(kt + 1) * 128], lt=[kw_tk], rhs=qx[b][0:64, qs], rt=[qx_tk[b]],
                                          tab=tabw[:, off:off + 512], tt=[tabw_tk], v=vw[:, kt, :], vt=[vw_tk], acc=6, far=False))

                def fin_a(t):
                    zb = zcnt[0] % 3
                    zcnt[0] += 1
                    t["zb"] = zb
                    P.op("act", ACTF(ostg[zb][0:65, :], psb[t["acc"]][0:65, :], AF.Copy), reads=[pst[t["acc"]]], writes=[ostg_tk[zb]])

                def fin_b(t):
                    j, QB, zb = t["j"], t["QB"], t["zb"]
                    gb_ = QB % 2
                    P.op("act", ACTF(lrow[zb][64:65, :], ostg[zb][64:65, :], AF.Ln, bias=V("eps_z")[64:65]),
                         reads=[ostg_tk[zb], vecs_tk], writes=[lrow_tk[zb]])
                    P.op("act", ACTF(lrow[zb][64:65, :], lrow[zb][64:65, :], AF.Exp, scale=-1.0), reads=[lrow_tk[zb]], writes=[lrow_tk[zb]])
                    P.op("dve", TT(lrow[zb][64:65, :], lrow[zb][64:65, :], gq[gb_][64:65, j * 512:(j + 1) * 512], ALU.mult),
                         reads=[lrow_tk[zb], gq_tk[gb_]], writes=[lrow_tk[zb]])
                    P.dma("sp", fr_d[zb:zb + 1, :], lrow[zb][64:65, :], reads=[lrow_tk[zb]], writes=[frd_tk[zb]])
                    P.dma("sp", facb[zb][0:64, :], bass.AP(tensor=fr_d.tensor, offset=zb * 512, ap=[[0, 64], [1, 512]]),
                          reads=[frd_tk[zb]], writes=[facb_tk[zb]])

                def fin_c(t):
                    j, QB, zb = t["j"], t["QB"], t["zb"]
                    qs = slice(QB * 512, (QB + 1) * 512)
                    ob = QB % 2
                    if j == 0:
                        P.op("dve", TT(Oac[ob][0:64], ostg[zb][0:64, :], facb[zb][0:64, :], ALU.mult),
                             reads=[ostg_tk[zb], facb_tk[zb]], writes=[Oac_tk[ob]])
                    else:
                        P.op("dve", TT(ostg[zb][0:64, :], ostg[zb][0:64, :], facb[zb][0:64, :], ALU.mult),
                             reads=[ostg_tk[zb], facb_tk[zb]], writes=[ostg_tk[zb]])
                        P.op("pool", TT(Oac[ob][0:64], Oac[ob][0:64], ostg[zb][0:64, :], ALU.add),
                             reads=[ostg_tk[zb], Oac_tk[ob]], writes=[Oac_tk[ob]])
                    if j == 2:
                        P.op("act", ACTF(ost[ob][0:64], Oac[ob][0:64], AF.Copy), reads=[Oac_tk[ob]], writes=[ost_tk[ob]])
                        P.dma("sp", oT_d[2 * g + pr, hf * 64:(hf + 1) * 64, qs], ost[ob][0:64], reads=[ost_tk[ob]])

                LOOK = 3
                pend = []
                n = len(tiles)
                lastQB = -1
                for i in range(n + LOOK):
                    if i < n:
                        t = tiles[i]
                        if t["QB"] != lastQB:
                            lastQB = t["QB"]
                            gb_ = lastQB % 2
                            P.dma("sp", gq[gb_][64:65, :].rearrange("p (j t) -> p j t", j=3),
                                  gT_d[h * 3:h * 3 + 3, lastQB * 512:(lastQB + 1) * 512].rearrange("(o j) t -> o j t", o=1),
                                  writes=[gq_tk[gb_]])
                        sl = step[0] % NSL
                        step[0] += 1
                        t["sl"] = sl
                        P.op("pe", MM(psb[sl][:, :], t["lhsT"], t["rhs"], True, True), reads=t["lt"] + t["rt"], writes=[pst[sl]])
                    jx_ = i - LOOK
                    if jx_ >= 0:
                        t = tiles[jx_]
                        sl = t["sl"]
                        if t["far"]:
                            P.op("act", ACTF(Eb[sl], psb[sl][:, :], AF.Exp, scale=0.125, bias=b31c), reads=[pst[sl], b31_tk], writes=[Eb_tk[sl]])
                        else:
                            P.op("act", ACTF(Ef[sl], psb[sl][:, :], AF.Exp, scale=0.125), reads=[pst[sl]], writes=[Ef_tk[sl]])
                            P.op("dve", TT(Eb[sl], Ef[sl], t["tab"], ALU.mult), reads=[Ef_tk[sl]] + list(t["tt"]), writes=[Eb_tk[sl]])
                        P.op("pe", MM(psb[t["acc"]][0:65, :], t["v"], Eb[sl], t["first"], t["last"]),
                             reads=[Eb_tk[sl]] + t["vt"], writes=[pst[t["acc"]]])
                        if t["last"]:
                            fin_a(t)
                            pend.append([i + 2, 0, t])
                    k = 0
                    while k < len(pend):
                        due, stage, t = pend[k]
                        if due <= i:
                            if stage == 0:
                                fin_b(t)
                                pend[k] = [i + 5, 1, t]
                                k += 1
                            else:
                                fin_c(t)
                                pend.pop(k)
                        else:
                            k += 1
                for due, stage, t in sorted(pend, key=lambda x: x[0]):
                    if stage == 0:
                        fin_b(t)
                    fin_c(t)

        def pass_nsa_out(jx, Xc):
            wo = a3(8 * 1024, BF16, pat="p (m f) -> p m f", m=8)
            wo_tk = tks(8)
            load_w(wo, din["nsa_wo_%d" % jx], 8, wo_tk)
            xn = [a3(8 * 512, F32, pat="p (c t) -> p c t", c=8) for _ in range(2)]
            xn_tk = [tks(8) for _ in range(2)]
            ot = [a3(8 * 512, BF16, pat="p (c t) -> p c t", c=8) for _ in range(2)]
            ot_tk = tks(2)

            def load(i):
                b = i % 2
                P.dma("sp", xn[b], xtile_ap(Xc, i), writes=xn_tk[b])
                P.dma("sp", ot[b], oT_d[:, :, i * 512:(i + 1) * 512].rearrange("m p t -> p m t"), writes=[ot_tk[b]])
            load(0)
            for i in range(NT):
                b = i % 2
                if i + 1 < NT:
                    load(i + 1)
                for m in range(8):
                    po = m % 2
                    for c in range(8):
                        P.op("pe", MM(psb[po][:, :], wo[:, m, c * 128:(c + 1) * 128], ot[b][:, c, :], c == 0, c == 7),
                             reads=[wo_tk[m], ot_tk[b]], writes=[pst[po]])
                    P.op("dve", TT(xn[b][:, m, :], xn[b][:, m, :], psb[po][:, :], ALU.add),
                         reads=[pst[po], xn_tk[b][m]], writes=[xn_tk[b][m]])
                P.dma("sp", xtile_ap(Xc, i), xn[b], reads=xn_tk[b])

        def pass_final(Xc, do_norm):
            xn = [a3(8 * 512, F32, pat="p (c t) -> p c t", c=8) for _ in range(2)]
            xn_tk = [tks(8) for _ in range(2)]
            sq = a3(2 * 512, BF16, pat="p (c t) -> p c t", c=2)
            sqtk = tks(2)
            rstd = alloc(512)
            rstd_tk = Tk()
            yo = [alloc(D) for _ in range(2)]
            yo_tk = tks(2)
            gain = V("final_norm")

            def load(i):
                b = i % 2
                P.dma("sp", xn[b], xtile_ap(Xc, i), writes=xn_tk[b])
            load(0)
            for i in range(NT):
                b = i % 2
                if i + 1 < NT:
                    load(i + 1)
                xt, xtk = xn[b], xn_tk[b]
                if do_norm:
                    for c in range(8):
                        sl = c % 2
                        P.op("act", ACTF(sq[:, sl, :], xt[:, c, :], AF.Square), reads=[xtk[c]], writes=[sqtk[sl]])
                        P.op("pe", MM(psb[6][:, :], ones_bf, sq[:, sl, :], c == 0, c == 7),
                             reads=[sqtk[sl], ones_tk], writes=[pst[6]])
                    P.op("act", ACTF(rstd, psb[6][:, :], AF.Sqrt, bias=V("eps_rms"), scale=1.0 / D),
                         reads=[pst[6], vecs_tk], writes=[rstd_tk])
                    P.op("dve", RCP(rstd, rstd), reads=[rstd_tk], writes=[rstd_tk])
                    for c in range(8):
                        P.op("dve", STT(xt[:, c, :], xt[:, c, :], gain[:, c:c + 1], rstd, ALU.mult, ALU.mult),
                             reads=[xtk[c], rstd_tk, vecs_tk], writes=[xtk[c]])
                for s in range(4):
                    yb_ = (i * 4 + s) % 2
                    for c in range(8):
                        bank = c // 4
                        P.op("pe", TRN(psb[bank][:, (c % 4) * 128:(c % 4 + 1) * 128], xt[:, c, s * 128:(s + 1) * 128], ident),
                             reads=[xtk[c], ident_tk], writes=[pst[bank]])
                    P.op("act", ACTF(yo[yb_][:, 0:512], psb[0][:, :], AF.Copy), reads=[pst[0]], writes=[yo_tk[yb_]])
                    P.op("dve", CP(yo[yb_][:, 512:1024], psb[1][:, :]), reads=[pst[1], yo_tk[yb_]], writes=[yo_tk[yb_]])
                    t0 = i * 512 + s * 128
                    P.dma("sp", out_d[t0:t0 + 128, :], yo[yb_], reads=[yo_tk[yb_]])

        def phase(fn, *a):
            top[0] = base_top
            fn(*a)
            P.barrier()

        phase(pass_input)
        has_nsa = any((l % 2 == 1) for l in layers) and "mix" in stages
        if has_nsa:
            phase(pass_nsa_tables)
        cur = 0
        for l in layers:
            jx = l // 2
            if "ffn1" in stages:
                a_, b_, c_ = cur, (cur + 1) % 3, (cur + 2) % 3
                phase(pass_ffn_half, l, 0, 0, X[a_], X[a_], X[b_], "ffn1_norm%d" % l)
                phase(pass_ffn_half, l, 0, 1, X[a_], X[b_], X[c_], "ffn1_norm%d" % l)
                cur = c_
            if "mix" in stages:
                if l % 2 == 0:
                    phase(pass_conv, l, jx, X[cur])
                else:
                    phase(pass_nsa_proj, l, jx, X[cur])
                    if DBG >= 2:
                        for g in range(4 if DBG >= 9 else 1):
                            phase(pass_nsa_group, jx, g)
                    if DBG >= 9:
                        phase(pass_nsa_out, jx, X[cur])
            if "ffn2" in stages:
                a_, b_, c_ = cur, (cur + 1) % 3, (cur + 2) % 3
                phase(pass_ffn_half, l, 1, 0, X[a_], X[a_], X[b_], "ffn2_norm%d" % l)
                phase(pass_ffn_half, l, 1, 1, X[a_], X[b_], X[c_], "ffn2_norm%d" % l)
                cur = c_
            if "ple" in stages:
                phase(pass_ple, l, X[cur])
        phase(pass_final, X[cur], do_final)
        for e_ in ENGS:
            P.op(e_, lambda e: e.nop())
        P.emit()
    return nc


import os
DBG = int(os.environ.get("NSA_DBG", "9"))


def run(inputs, n_cores=8, **bkw):
    shared, vidx = host_prepare(inputs)
    x = np.asarray(inputs["x"], np.float32)
    p = np.asarray(inputs["p"], np.float32)
    shapes = {k: v.shape for k, v in shared.items()}
    shapes["x"] = (S, D)
    shapes["p"] = (4, S, 256)
    nc = bass.Bass("TRN2", target_bir_lowering=False)
    build(nc, shapes, vidx, **bkw)
    in_maps = []
    for b in range(n_cores):
        m = dict(shared)
        m["x"] = np.ascontiguousarray(x[b])
        m["p"] = np.ascontiguousarray(p[:, b])
        in_maps.append(m)
    res = run_bass_kernel_spmd(nc, in_maps, core_ids=list(range(n_cores)))
    return np.stack([np.asarray(r["out"], np.float32) for r in res.results], axis=0)


def kernel(**inputs):
    return run(inputs, n_cores=8)
```

```python
import contextlib
import os
import math
import numpy as np
import concourse.bass as bass
import concourse.mybir as mybir
from concourse.bass_utils import run_bass_kernel_spmd

F32 = mybir.dt.float32
BF16 = mybir.dt.bfloat16
AF = mybir.ActivationFunctionType
ALU = mybir.AluOpType

S = 4096
D = 1024
DFF = 2816
NT = 8
TW_ = 512
ENGS = ["pe", "act", "dve", "pool", "sp"]
DMA_RING = {"sp": 8, "act": 2, "pool": 6, "pe": 2, "dve": 2}


class Tk:
    __slots__ = ("w", "r")

    def __init__(self):
        self.w = None
        self.r = []


def tks(n):
    return [Tk() for _ in range(n)]


class Op:
    __slots__ = ("eng", "fn", "deps", "is_dma", "need_sig", "sigval", "ev")

    def __init__(self, eng, fn, is_dma):
        self.eng = eng
        self.fn = fn
        self.deps = []
        self.is_dma = is_dma
        self.need_sig = False
        self.sigval = None
        self.ev = None


class Prog:
    def __init__(self, nc):
        self.nc = nc
        self.ops = {e: [] for e in ENGS}
        self.last = {e: None for e in ENGS}
        self.dmas_since_barrier = []
        self.pending = {e: [] for e in ENGS}

    def _rec(self, eng, fn, reads, writes, is_dma):
        op = Op(eng, fn, is_dma)
        deps = []
        for t in reads:
            if t.w is not None:
                deps.append((t.w, 0))
        for t in writes:
            if t.w is not None:
                deps.append((t.w, 1))
            for r in t.r:
                deps.append((r, 1))
        for d in self.pending[eng]:
            deps.append((d, 0))
        self.pending[eng] = []
        op.deps = deps
        for t in writes:
            t.w = op
            t.r = []
        for t in reads:
            if t.w is not op:
                t.r.append(op)
        self.ops[eng].append(op)
        self.last[eng] = op
        if is_dma:
            self.dmas_since_barrier.append(op)
        return op

    def op(self, eng, fn, reads=(), writes=()):
        return self._rec(eng, fn, list(reads), list(writes), False)

    def dma(self, eng, out, in_, reads=(), writes=()):
        def fn(e):
            return e.dma_start(out=out, in_=in_)
        return self._rec(eng, fn, list(reads), list(writes), True)

    def barrier(self):
        deps = [o for o in self.last.values() if o is not None] + self.dmas_since_barrier
        self.dmas_since_barrier = []
        for e in ENGS:
            self.pending[e] = list(deps)

    @staticmethod
    def _skip(d, ename, kind):
        if d.eng == ename:
            if ename in ("pe", "sp"):
                return True
            if kind == 1:
                return True
        return False

    def emit(self):
        nc = self.nc
        with contextlib.ExitStack() as st:
            esem = {e: st.enter_context(nc.semaphore("s_" + e)) for e in ENGS}
            rings = {e: [st.enter_context(nc.semaphore("d_%s%d" % (e, i))) for i in range(DMA_RING[e])]
                     for e in ENGS}
            for e in ENGS:
                for op in self.ops[e]:
                    for d, kind in op.deps:
                        if d.is_dma or self._skip(d, e, kind):
                            continue
                        d.need_sig = True
            for e in ENGS:
                c = 0
                ring_cnt = [0] * DMA_RING[e]
                nd = 0
                for op in self.ops[e]:
                    if op.is_dma:
                        slot = nd % DMA_RING[e]
                        prev = ring_cnt[slot] * 16
                        ring_cnt[slot] += 1
                        op.ev = (rings[e][slot], ring_cnt[slot] * 16, prev)
                        nd += 1
                    elif op.need_sig:
                        c += 1
                        op.sigval = c
            if os.environ.get("NSA_VERBOSE"):
                print("ops per engine", {e: len(self.ops[e]) for e in ENGS},
                      "sig counts", {e: max([o.sigval or 0 for o in self.ops[e]] + [0]) for e in ENGS},
                      "ring max", {e: max([o.ev[1] for o in self.ops[e] if o.is_dma] + [0]) for e in ENGS}, flush=True)
            blk = st.enter_context(nc.Block())

            def run(ename, eng):
                known = {}

                def wait(sem, val):
                    k = id(sem)
                    if known.get(k, 0) >= val:
                        return
                    known[k] = val
                    eng.wait_ge(sem, val)
                for op in self.ops[ename]:
                    for d, kind in op.deps:
                        if d.is_dma:
                            wait(d.ev[0], d.ev[1])
                        elif not self._skip(d, ename, kind):
                            wait(esem[d.eng], d.sigval)
                    if op.is_dma:
                        sem, tgt, prev = op.ev
                        if prev > 0:
                            wait(sem, prev)
                        op.fn(eng).then_inc(sem, 16)
                    else:
                        ins = op.fn(eng)
                        if op.need_sig:
                            ins.then_inc(esem[ename], 1)

            @blk.sync
            def _(sync):
                run("sp", sync)

            @blk.tensor
            def _(tensor):
                run("pe", tensor)

            @blk.scalar
            def _(scalar):
                run("act", scalar)

            @blk.vector
            def _(vector):
                run("dve", vector)

            @blk.gpsimd
            def _(gpsimd):
                run("pool", gpsimd)


def MM(out, lhsT, rhs, start, stop):
    return lambda e: e.matmul(out, lhsT=lhsT, rhs=rhs, start=start, stop=stop)


def TRN(out, in_, ident):
    return lambda e: e.transpose(out=out, in_=in_, identity=ident)


def ACTF(out, in_, func, bias=None, scale=None):
    kw = {}
    if bias is not None:
        kw["bias"] = bias
    if scale is not None:
        kw["scale"] = scale
    return lambda e: e.activation(out=out, in_=in_, func=func, **kw)


def TT(out, in0, in1, op):
    return lambda e: e.tensor_tensor(out=out, in0=in0, in1=in1, op=op)


def STT(out, in0, scalar, in1, op0, op1):
    return lambda e: e.scalar_tensor_tensor(out=out, in0=in0, scalar=scalar, in1=in1, op0=op0, op1=op1)


def TS(out, in0, s1, s2, op0, op1=None):
    if op1 is None:
        return lambda e: e.tensor_scalar(out=out, in0=in0, scalar1=s1, scalar2=None, op0=op0)
    return lambda e: e.tensor_scalar(out=out, in0=in0, scalar1=s1, scalar2=s2, op0=op0, op1=op1)


def CP(out, in_):
    return lambda e: e.tensor_copy(out=out, in_=in_)


def MSET(out, v):
    return lambda e: e.memset(out, v)


def RCP(out, in_):
    return lambda e: e.reciprocal(out=out, in_=in_)


NSA_OFF = 4080
NSA_M = 8192
NSAW_OFF = 512
NSAW_M = 2048


def _t5_bucket_np(n):
    n = np.maximum(n, 0)
    nf = np.maximum(n, 1).astype(np.float32)
    large = 16 + (np.log(nf / np.float32(16)) / np.float32(math.log(128.0)) * np.float32(16)).astype(np.int32)
    large = np.minimum(large, 31)
    return np.where(n < 16, n, large)


def lin_layout(w):
    K, M = w.shape
    kc, mc = K // 128, M // 128
    return np.ascontiguousarray(w.reshape(kc, 128, mc, 128).transpose(2, 1, 0, 3).reshape(mc, 128, kc * 128))


def fm(v):
    return np.ascontiguousarray(v.reshape(-1, 128).T)


class VecPack:
    def __init__(self):
        self.cols = []
        self.idx = {}
        self.n = 0

    def add(self, name, arr):
        arr = np.asarray(arr, np.float32)
        assert arr.shape[0] == 128
        self.idx[name] = (self.n, arr.shape[1])
        self.cols.append(arr)
        self.n += arr.shape[1]

    def build(self):
        return np.ascontiguousarray(np.concatenate(self.cols, axis=1))


def host_prepare(inp):
    f = lambda a: np.asarray(a, np.float32)
    shared = {}
    vp = VecPack()
    for l in range(4):
        for nm in ("ffn1_norm", "mix_norm", "ffn2_norm", "ple_norm"):
            vp.add("%s%d" % (nm, l), fm(f(inp[nm])[l]))
        for fi, pre in enumerate(("ffn1", "ffn2")):
            wg = f(inp[pre + "_w_gate"])[l]
            wu = f(inp[pre + "_w_up"])[l]
            wd = f(inp[pre + "_w_down"])[l]
            for hf in range(2):
                cs = slice(hf * 1408, (hf + 1) * 1408)
                shared["wg_%d_%d_%d" % (l, fi, hf)] = lin_layout(wg[:, cs])
                shared["wu_%d_%d_%d" % (l, fi, hf)] = lin_layout(wu[:, cs])
                shared["wd_%d_%d_%d" % (l, fi, hf)] = lin_layout(wd[cs, :])
        shared["pleg_%d" % l] = lin_layout(f(inp["ple_w_gate"])[l])
        shared["plei_%d" % l] = lin_layout(f(inp["ple_w_in"])[l])
    vp.add("final_norm", fm(f(inp["final_norm"])))
    for j in range(2):
        shared["pw1_%d" % j] = lin_layout(f(inp["conv_w_pw1"])[j])
        shared["pw2_%d" % j] = lin_layout(f(inp["conv_w_pw2"])[j])
        vp.add("b_pw1_%d" % j, fm(f(inp["conv_b_pw1"])[j]))
        vp.add("b_dw_%d" % j, fm(f(inp["conv_b_dw"])[j]))
        vp.add("ln_g_%d" % j, fm(f(inp["conv_ln_g"])[j]))
        vp.add("ln_b_%d" % j, fm(f(inp["conv_ln_b"])[j]))
        vp.add("b_pw2_%d" % j, fm(f(inp["conv_b_pw2"])[j]))
        wdw = f(inp["conv_w_dw"])[j]
        vp.add("w_dw_%d" % j, np.ascontiguousarray(wdw.reshape(31, 8, 128).transpose(2, 1, 0).reshape(128, 248)))
        w_in = f(inp["nsa_w_in"])[j]
        cols = [np.arange(1024)]
        for g in range(4):
            for kind in (0, 1, 2, 4):
                c = 1024 + kind * 256 + g * 64 + np.arange(64)
                cols.append(np.concatenate([c, c]))
        cols = np.concatenate(cols)
        wcat = np.concatenate([w_in[:, cols], w_in[:, 2560:2608], np.zeros((1024, 80), np.float32)], axis=1)
        shared["nsa_wf_%d" % j] = lin_layout(wcat)
        tc = np.concatenate([1024 + 3 * 256 + np.arange(256), 1024 + 5 * 256 + np.arange(256), 2560 + np.arange(48)])
        shared["nsa_wt_%d" % j] = np.ascontiguousarray(w_in[:, tc].reshape(8, 128, 560).transpose(1, 0, 2))
        shared["nsa_wo_%d" % j] = lin_layout(f(inp["nsa_w_out"])[j])
        for nm, src in (("wk1", "nsa_cmp_wk1"), ("wv1", "nsa_cmp_wv1")):
            w1 = f(inp[src])[j]
            shared["%s_%d" % (nm, j)] = np.ascontiguousarray(w1.reshape(32, 64, 256).transpose(1, 0, 2))
        w2k = f(inp["nsa_cmp_wk2"])[j]
        w2kd = np.concatenate([w2k, w2k], axis=1)
        shared["w2k_%d" % j] = np.ascontiguousarray(w2kd.reshape(2, 128, 128).transpose(1, 0, 2))
        w2v = f(inp["nsa_cmp_wv2"])[j]
        shared["w2v_%d" % j] = np.ascontiguousarray(w2v.reshape(2, 128, 64).transpose(1, 0, 2))
        shared["posk_%d" % j] = np.ascontiguousarray(f(inp["nsa_cmp_pos_k"])[j].T)
        shared["posv_%d" % j] = np.ascontiguousarray(f(inp["nsa_cmp_pos_v"])[j].T)
    rb = f(inp["rel_bias"])
    ext = np.concatenate([rb, np.full((1, 16), -30000.0, np.float32)], axis=0)
    dist = np.arange(NSA_M) - NSA_OFF
    idx = np.where(dist >= 0, _t5_bucket_np(dist), 32)
    shared["gvec"] = np.ascontiguousarray(ext[idx].T)
    distw = np.arange(NSAW_M) - NSAW_OFF
    idxw = np.where((distw >= 0) & (distw < 512), _t5_bucket_np(distw), 32)
    shared["gwvec"] = np.ascontiguousarray(ext[idxw].T)
    vp.add("eps_rms", np.full((128, 1), 1e-6, np.float32))
    vp.add("eps_ln", np.full((128, 1), 1e-5, np.float32))
    vp.add("eps_z", np.full((128, 1), 1e-30, np.float32))
    shared["vecs"] = vp.build()
    shared["ident"] = np.eye(128, dtype=np.float32)
    t = np.arange(S)
    j = np.arange(64)[None, :]
    cur = (t // 64)[:, None]
    valid = (j * 64 <= t[:, None])
    forced = (j == 0) | (j == cur) | (j == cur - 1)
    vm = (valid & ~forced).astype(np.float32)
    am = np.where(forced, 1e9, np.where(valid, 0.0, -1.0)).astype(np.float32)
    shared["vmask"] = np.ascontiguousarray(vm.reshape(32, 128, 64).transpose(1, 0, 2))
    shared["amask"] = np.ascontiguousarray(am.reshape(32, 128, 64).transpose(1, 0, 2))
    c = np.arange(256)[:, None]
    ov = ((c * 16 < j * 64 + 64) & (c * 16 + 32 > j * 64) & (c < 255)).astype(np.float32)
    shared["overlap"] = np.ascontiguousarray(ov.reshape(2, 128, 64).transpose(1, 0, 2))
    ex = np.zeros((64, 32, 128), np.float32)
    for kt in range(32):
        ex[2 * kt, kt, 0:64] = 1.0
        ex[2 * kt + 1, kt, 64:128] = 1.0
    shared["expand"] = ex
    return shared, vp.idx


ARENA_F32 = 47600


class Ctx:
    pass


def build(nc, shapes, vidx, layers=(0, 1, 2, 3), stages=("ffn1", "mix", "ffn2", "ple"), do_final=True):
    P = Prog(nc)
    C = Ctx()
    din = {}
    for name, shp in shapes.items():
        din[name] = nc.dram_tensor(name, list(shp), F32, kind="ExternalInput").ap()
    out_d = nc.dram_tensor("out", [S, D], F32, kind="ExternalOutput").ap()

    def scratch(name, shape, dt):
        return nc.dram_tensor(name, list(shape), dt, kind="Internal").ap()
    X = [scratch("xs%d" % i, [8, 128, S], F32) for i in range(3)]
    qkT_d = scratch("qkT", [24, 128, S], BF16)
    vtok_d = scratch("vtok", [8, 128, 32, 65], BF16)
    gT_d = scratch("gT", [48, S], F32)
    fr_d = scratch("frow", [3, 512], F32)
    oc_d = scratch("ocT", [4, 64, S], F32)
    frd_tk = tks(3)
    oT_d = scratch("oT", [8, 128, S], BF16)
    grep_d = scratch("grep", [16, 128 * NSA_M], F32)
    gwrep_d = scratch("gwrep", [16, 128 * NSAW_M], F32)

    with contextlib.ExitStack() as st:
        arena = st.enter_context(nc.sbuf_tensor("arena", [128, ARENA_F32], F32))
        psb = [st.enter_context(nc.psum_tensor("psb%d" % i, [128, 512], F32)) for i in range(8)]
        pst = tks(8)
        top = [0]

        def alloc(nfree, dt=F32):
            n32 = nfree if dt == F32 else (nfree + 1) // 2
            assert top[0] + n32 <= ARENA_F32, ("arena overflow", top[0], n32)
            a = arena[:, top[0]:top[0] + n32]
            top[0] += n32
            if dt != F32:
                a = a.bitcast(dt)
                a = a[:, 0:nfree]
            return a

        def a3(nfree, dt, **kw):
            pat = kw.pop("pat")
            return alloc(nfree, dt).rearrange(pat, **kw)

        nv = shapes["vecs"][1]
        vecs = alloc(nv)
        vecs_tk = Tk()
        P.dma("sp", vecs, din["vecs"], writes=[vecs_tk])
        ident = alloc(128)
        ident_tk = Tk()
        P.dma("sp", ident, din["ident"], writes=[ident_tk])
        ones_bf = alloc(128, BF16)
        ones_tk = Tk()
        P.op("dve", MSET(ones_bf, 1.0), writes=[ones_tk])
        ident_bf = alloc(128, BF16)
        identbf_tk = Tk()
        P.op("dve", CP(ident_bf, ident), reads=[ident_tk], writes=[identbf_tk])
        base_top = top[0]

        def V(name, c0=0, n=None):
            o, w = vidx[name]
            if n is None:
                n = w - c0
            return vecs[:, o + c0:o + c0 + n]

        def xtile_ap(Xd, i):
            return Xd[:, :, i * TW_:(i + 1) * TW_].rearrange("c p t -> p c t")

        def load_w(dst, src, mc, wt):
            for m in range(mc):
                P.dma("pool", dst[:, m, :], src[m], writes=[wt[m]])

        def rmsnorm(xt, xtk, gain, hT, htk, sq, sqtk, rstd, rstd_tk, out_f32=None):
            for c in range(8):
                sl = c % 2
                P.op("act", ACTF(sq[:, sl, :], xt[:, c, :], AF.Square), reads=[xtk[c]], writes=[sqtk[sl]])
                P.op("pe", MM(psb[6][:, :], ones_bf, sq[:, sl, :], c == 0, c == 7),
                     reads=[sqtk[sl], ones_tk], writes=[pst[6]])
            P.op("act", ACTF(rstd, psb[6][:, :], AF.Sqrt, bias=V("eps_rms"), scale=1.0 / D),
                 reads=[pst[6], vecs_tk], writes=[rstd_tk])
            P.op("dve", RCP(rstd, rstd), reads=[rstd_tk], writes=[rstd_tk])
            for c in range(8):
                P.op("dve", STT(hT[:, c, :], xt[:, c, :], gain[:, c:c + 1], rstd, ALU.mult, ALU.mult),
                     reads=[xtk[c], rstd_tk, vecs_tk], writes=[htk[c]])

        def pass_input():
            xin = [alloc(D) for _ in range(2)]
            xin_tk = tks(2)
            xo = [a3(8 * 128, F32, pat="p (c t) -> p c t", c=8) for _ in range(2)]
            xo_tk = tks(2)
            for tt in range(32):
                b = tt % 2
                P.dma("sp", xin[b], din["x"][tt * 128:(tt + 1) * 128, :], writes=[xin_tk[b]])
                for c in range(8):
                    bank = c // 4
                    P.op("pe", TRN(psb[bank][:, (c % 4) * 128:(c % 4 + 1) * 128], xin[b][:, c * 128:(c + 1) * 128], ident),
                         reads=[xin_tk[b], ident_tk], writes=[pst[bank]])
                P.op("act", ACTF(xo[b][:, 0:4, :], psb[0][:, :].rearrange("p (c t) -> p c t", c=4), AF.Copy),
                     reads=[pst[0]], writes=[xo_tk[b]])
                P.op("dve", CP(xo[b][:, 4:8, :], psb[1][:, :].rearrange("p (c t) -> p c t", c=4)),
                     reads=[pst[1], xo_tk[b]], writes=[xo_tk[b]])
                P.dma("sp", X[0][:, :, tt * 128:(tt + 1) * 128].rearrange("c p t -> p c t"), xo[b], reads=[xo_tk[b]])

        def pass_ffn_half(l, fi, hf, Xn, Xr, Xo, norm_name):
            wg = a3(11 * 1024, BF16, pat="p (m f) -> p m f", m=11)
            wu = a3(11 * 1024, BF16, pat="p (m f) -> p m f", m=11)
            wd = a3(8 * 1408, BF16, pat="p (m f) -> p m f", m=8)
            wg_tk, wu_tk, wd_tk = tks(11), tks(11), tks(8)
            key = "%d_%d_%d" % (l, fi, hf)
            for m in range(11):
                P.dma("pool", wg[:, m, :], din["wg_" + key][m], writes=[wg_tk[m]])
                P.dma("pool", wu[:, m, :], din["wu_" + key][m], writes=[wu_tk[m]])
            load_w(wd, din["wd_" + key], 8, wd_tk)
            same = Xn is Xr
            xn = [a3(8 * 512, F32, pat="p (c t) -> p c t", c=8) for _ in range(2)]
            xn_tk = [tks(8) for _ in range(2)]
            if same:
                xr, xr_tk = xn, xn_tk
            else:
                xr = [a3(8 * 512, F32, pat="p (c t) -> p c t", c=8) for _ in range(2)]
                xr_tk = [tks(8) for _ in range(2)]
            hT = a3(8 * 512, BF16, pat="p (c t) -> p c t", c=8)
            htk = tks(8)
            sq = a3(2 * 512, BF16, pat="p (c t) -> p c t", c=2)
            sqtk = tks(2)
            rstd = alloc(512)
            rstd_tk = Tk()
            act_ = a3(11 * 512, BF16, pat="p (c t) -> p c t", c=11)
            atk = tks(11)
            sg = [alloc(512) for _ in range(2)]
            sgtk = tks(2)
            gain = V(norm_name)

            def load(i):
                b = i % 2
                P.dma("sp", xn[b], xtile_ap(Xn, i), writes=xn_tk[b])
                if not same:
                    P.dma("sp", xr[b], xtile_ap(Xr, i), writes=xr_tk[b])
            load(0)
            for i in range(NT):
                b = i % 2
                if i + 1 < NT:
                    load(i + 1)
                rmsnorm(xn[b], xn_tk[b], gain, hT, htk, sq, sqtk, rstd, rstd_tk)
                for j in range(11):
                    pg, pu = j % 2, 2 + j % 2
                    for c in range(8):
                        P.op("pe", MM(psb[pg][:, :], wg[:, j, c * 128:(c + 1) * 128], hT[:, c, :], c == 0, c == 7),
                             reads=[wg_tk[j], htk[c]], writes=[pst[pg]])
                    for c in range(8):
                        P.op("pe", MM(psb[pu][:, :], wu[:, j, c * 128:(c + 1) * 128], hT[:, c, :], c == 0, c == 7),
                             reads=[wu_tk[j], htk[c]], writes=[pst[pu]])
                    P.op("act", ACTF(sg[j % 2], psb[pg][:, :], AF.Silu), reads=[pst[pg]], writes=[sgtk[j % 2]])
                    P.op("dve", TT(act_[:, j, :], sg[j % 2], psb[pu][:, :], ALU.mult),
                         reads=[sgtk[j % 2], pst[pu]], writes=[atk[j]])
                for m in range(8):
                    py = 4 + m % 2
                    for j in range(11):
                        P.op("pe", MM(psb[py][:, :], wd[:, m, j * 128:(j + 1) * 128], act_[:, j, :], j == 0, j == 10),
                             reads=[wd_tk[m], atk[j]], writes=[pst[py]])
                    P.op("dve", STT(xr[b][:, m, :], psb[py][:, :], 0.5, xr[b][:, m, :], ALU.mult, ALU.add),
                         reads=[pst[py], xr_tk[b][m]], writes=[xr_tk[b][m]])
                P.dma("sp", xtile_ap(Xo, i), xr[b], reads=xr_tk[b])

        def pass_ple(l, Xc):
            wgp = a3(8 * 1024, BF16, pat="p (m f) -> p m f", m=8)
            wip = a3(8 * 256, BF16, pat="p (m f) -> p m f", m=8)
            wgp_tk, wip_tk = tks(8), tks(8)
            load_w(wgp, din["pleg_%d" % l], 8, wgp_tk)
            load_w(wip, din["plei_%d" % l], 8, wip_tk)
            xn = [a3(8 * 512, F32, pat="p (c t) -> p c t", c=8) for _ in range(2)]
            xn_tk = [tks(8) for _ in range(2)]
            pin = [a3(4 * 256, F32, pat="p (s f) -> p s f", s=4) for _ in range(2)]
            pin_tk = tks(2)
            pT = a3(2 * 512, BF16, pat="p (c t) -> p c t", c=2)
            pT_tk = tks(2)
            hT = a3(8 * 512, BF16, pat="p (c t) -> p c t", c=8)
            htk = tks(8)
            sq = a3(2 * 512, BF16, pat="p (c t) -> p c t", c=2)
            sqtk = tks(2)
            rstd = alloc(512)
            rstd_tk = Tk()
            sg = [alloc(512) for _ in range(2)]
            sgtk = tks(2)
            gain = V("ple_norm%d" % l)
            pl = din["p"][l]

            def load(i):
                b = i % 2
                P.dma("sp", xn[b], xtile_ap(Xc, i), writes=xn_tk[b])
                P.dma("sp", pin[b], pl[i * 512:(i + 1) * 512, :].rearrange("(s p) f -> p s f", p=128), writes=[pin_tk[b]])
            load(0)
            for i in range(NT):
                b = i % 2
                if i + 1 < NT:
                    load(i + 1)
                rmsnorm(xn[b], xn_tk[b], gain, hT, htk, sq, sqtk, rstd, rstd_tk)
                for kc in range(2):
                    for s in range(4):
                        P.op("pe", TRN(psb[7][:, s * 128:(s + 1) * 128], pin[b][:, s, kc * 128:(kc + 1) * 128], ident),
                             reads=[pin_tk[b], ident_tk], writes=[pst[7]])
                    P.op("act", ACTF(pT[:, kc, :], psb[7][:, :], AF.Copy), reads=[pst[7]], writes=[pT_tk[kc]])
                for m in range(8):
                    pg, pi = m % 2, 2 + m % 2
                    for c in range(8):
                        P.op("pe", MM(psb[pg][:, :], wgp[:, m, c * 128:(c + 1) * 128], hT[:, c, :], c == 0, c == 7),
                             reads=[wgp_tk[m], htk[c]], writes=[pst[pg]])
                    for c in range(2):
                        P.op("pe", MM(psb[pi][:, :], wip[:, m, c * 128:(c + 1) * 128], pT[:, c, :], c == 0, c == 1),
                             reads=[wip_tk[m], pT_tk[c]], writes=[pst[pi]])
                    P.op("act", ACTF(sg[m % 2], psb[pg][:, :], AF.Sigmoid), reads=[pst[pg]], writes=[sgtk[m % 2]])
                    P.op("dve", TT(sg[m % 2], sg[m % 2], psb[pi][:, :], ALU.mult),
                         reads=[sgtk[m % 2], pst[pi]], writes=[sgtk[m % 2]])
                    P.op("dve", TT(xn[b][:, m, :], xn[b][:, m, :], sg[m % 2], ALU.add),
                         reads=[sgtk[m % 2], xn_tk[b][m]], writes=[xn_tk[b][m]])
                P.dma("sp", xtile_ap(Xc, i), xn[b], reads=xn_tk[b])

        def pass_conv(l, jx, Xc):
            w1 = a3(16 * 1024, BF16, pat="p (m f) -> p m f", m=16)
            w2 = a3(8 * 1024, BF16, pat="p (m f) -> p m f", m=8)
            w1_tk, w2_tk = tks(16), tks(8)
            load_w(w1, din["pw1_%d" % jx], 16, w1_tk)
            load_w(w2, din["pw2_%d" % jx], 8, w2_tk)
            diag = a3(31 * 8 * 128, BF16, pat="p (j c m) -> p j c m", j=31, c=8)
            diag_tk = tks(8)
            wdw = V("w_dw_%d" % jx)
            for c in range(8):
                for j in range(31):
                    P.op("dve", TS(diag[:, j, c, :], ident_bf, wdw[:, c * 31 + j:c * 31 + j + 1], None, ALU.mult),
                         reads=[identbf_tk, vecs_tk], writes=[diag_tk[c]])
            xn = [a3(8 * 512, F32, pat="p (c t) -> p c t", c=8)] * 2
            xn_tk = [tks(8)] * 2
            hT = a3(8 * 512, BF16, pat="p (c t) -> p c t", c=8)
            htk = tks(8)
            sq = a3(2 * 512, BF16, pat="p (c t) -> p c t", c=2)
            sqtk = tks(2)
            rstd = alloc(512)
            rstd_tk = Tk()
            ub = a3(8 * 542, BF16, pat="p (c t) -> p c t", c=8)
            ub_tk = tks(8)
            yb = a3(8 * 512, F32, pat="p (c t) -> p c t", c=8)
            yb_tk = tks(8)
            ybf = a3(2 * 512, BF16, pat="p (c t) -> p c t", c=2)
            ybf_tk = tks(2)
            ysq = a3(2 * 512, BF16, pat="p (c t) -> p c t", c=2)
            ysq_tk = tks(2)
            sg = [alloc(512) for _ in range(2)]
            sgtk = tks(2)
            mu = alloc(512)
            mu_tk = Tk()
            rs = alloc(512)
            rs_tk = Tk()
            tmp = alloc(512)
            tmp_tk = Tk()
            gain = V("mix_norm%d" % l)
            b1 = V("b_pw1_%d" % jx)
            bdw = V("b_dw_%d" % jx)
            lng = V("ln_g_%d" % jx)
            lnb = V("ln_b_%d" % jx)
            b2 = V("b_pw2_%d" % jx)
            for c in range(8):
                P.op("dve", MSET(ub[:, c, 0:30], 0.0), writes=[ub_tk[c]])

            def load(i):
                b = i % 2
                P.dma("sp", xn[b], xtile_ap(Xc, i), writes=xn_tk[b])
            for i in range(NT):
                b = i % 2
                load(i)
                rmsnorm(xn[b], xn_tk[b], gain, hT, htk, sq, sqtk, rstd, rstd_tk)
                for m in range(8):
                    pa, pg = m % 2, 2 + m % 2
                    for c in range(8):
                        P.op("pe", MM(psb[pa][:, :], w1[:, m, c * 128:(c + 1) * 128], hT[:, c, :], c == 0, c == 7),
                             reads=[w1_tk[m], htk[c]], writes=[pst[pa]])
                    for c in range(8):
                        P.op("pe", MM(psb[pg][:, :], w1[:, 8 + m, c * 128:(c + 1) * 128], hT[:, c, :], c == 0, c == 7),
                             reads=[w1_tk[8 + m], htk[c]], writes=[pst[pg]])
                    P.op("act", ACTF(sg[m % 2], psb[pg][:, :], AF.Sigmoid, bias=b1[:, 8 + m:9 + m]),
                         reads=[pst[pg], vecs_tk], writes=[sgtk[m % 2]])
                    P.op("dve", STT(ub[:, m, 30:542], psb[pa][:, :], b1[:, m:m + 1], sg[m % 2], ALU.add, ALU.mult),
                         reads=[pst[pa], sgtk[m % 2], vecs_tk], writes=[ub_tk[m]])
                for m in range(8):
                    py = 4 + m % 2
                    for j in range(31):
                        P.op("pe", MM(psb[py][:, :], diag[:, j, m, :], ub[:, m, j:j + 512], j == 0, j == 30),
                             reads=[diag_tk[m], ub_tk[m]], writes=[pst[py]])
                    P.op("act", ACTF(yb[:, m, :], psb[py][:, :], AF.Identity, bias=bdw[:, m:m + 1]),
                         reads=[pst[py], vecs_tk], writes=[yb_tk[m]])
                    P.op("dve", CP(ub[:, m, 0:30], ub[:, m, 512:542]), reads=[ub_tk[m]], writes=[ub_tk[m]])
                    sl = m % 2
                    P.op("dve", CP(ybf[:, sl, :], yb[:, m, :]), reads=[yb_tk[m]], writes=[ybf_tk[sl]])
                    P.op("act", ACTF(ysq[:, sl, :], yb[:, m, :], AF.Square), reads=[yb_tk[m]], writes=[ysq_tk[sl]])
                    P.op("pe", MM(psb[6][:, :], ones_bf, ybf[:, sl, :], m == 0, m == 7),
                         reads=[ybf_tk[sl], ones_tk], writes=[pst[6]])
                    P.op("pe", MM(psb[7][:, :], ones_bf, ysq[:, sl, :], m == 0, m == 7),
                         reads=[ysq_tk[sl], ones_tk], writes=[pst[7]])
                P.op("dve", TS(mu, psb[6][:, :], 1.0 / D, None, ALU.mult), reads=[pst[6]], writes=[mu_tk])
                P.op("dve", TT(tmp, mu, mu, ALU.mult), reads=[mu_tk], writes=[tmp_tk])
                P.op("dve", STT(tmp, psb[7][:, :], 1.0 / D, tmp, ALU.mult, ALU.subtract),
                     reads=[pst[7], tmp_tk], writes=[tmp_tk])
                P.op("dve", TS(tmp, tmp, 0.0, None, ALU.max), reads=[tmp_tk], writes=[tmp_tk])
                P.op("act", ACTF(rs, tmp, AF.Sqrt, bias=V("eps_ln"), scale=1.0), reads=[tmp_tk, vecs_tk], writes=[rs_tk])
                P.op("dve", RCP(rs, rs), reads=[rs_tk], writes=[rs_tk])
                P.op("dve", STT(mu, mu, -1.0, rs, ALU.mult, ALU.mult), reads=[mu_tk, rs_tk], writes=[mu_tk])
                for m in range(8):
                    P.op("dve", TT(yb[:, m, :], yb[:, m, :], rs, ALU.mult), reads=[yb_tk[m], rs_tk], writes=[yb_tk[m]])
                    P.op("dve", TT(yb[:, m, :], yb[:, m, :], mu, ALU.add), reads=[yb_tk[m], mu_tk], writes=[yb_tk[m]])
                    P.op("act", ACTF(hT[:, m, :], yb[:, m, :], AF.Silu, bias=lnb[:, m:m + 1], scale=lng[:, m:m + 1]),
                         reads=[yb_tk[m], vecs_tk], writes=[htk[m]])
                for m in range(8):
                    po = m % 2
                    for c in range(8):
                        P.op("pe", MM(psb[po][:, :], w2[:, m, c * 128:(c + 1) * 128], hT[:, c, :], c == 0, c == 7),
                             reads=[w2_tk[m], htk[c]], writes=[pst[po]])
                    P.op("dve", STT(xn[b][:, m, :], psb[po][:, :], b2[:, m:m + 1], xn[b][:, m, :], ALU.add, ALU.add),
                         reads=[pst[po], xn_tk[b][m], vecs_tk], writes=[xn_tk[b][m]])
                P.dma("sp", xtile_ap(Xc, i), xn[b], reads=xn_tk[b])

        def pass_nsa_tables():
            for h in range(16):
                src = bass.AP(tensor=din["gvec"].tensor, offset=h * NSA_M, ap=[[0, 128], [1, NSA_M]])
                dst = bass.AP(tensor=grep_d.tensor, offset=h * 128 * NSA_M, ap=[[NSA_M, 128], [1, NSA_M]])
                P.dma("sp", dst, src)
                src = bass.AP(tensor=din["gwvec"].tensor, offset=h * NSAW_M, ap=[[0, 128], [1, NSAW_M]])
                dst = bass.AP(tensor=gwrep_d.tensor, offset=h * 128 * NSAW_M, ap=[[NSAW_M, 128], [1, NSAW_M]])
                P.dma("sp", dst, src)

        def pass_nsa_proj(l, jx, Xc):
            wf = a3(25 * 1024, BF16, pat="p (m f) -> p m f", m=25)
            wf_tk = tks(25)
            load_w(wf, din["nsa_wf_%d" % jx], 25, wf_tk)
            wt = a3(8 * 560, BF16, pat="p (c f) -> p c f", c=8)
            wt_tk = Tk()
            P.dma("pool", wt, din["nsa_wt_%d" % jx], writes=[wt_tk])
            xn = [a3(8 * 512, F32, pat="p (c t) -> p c t", c=8) for _ in range(2)]
            xn_tk = [tks(8) for _ in range(2)]
            hT = a3(8 * 512, BF16, pat="p (c t) -> p c t", c=8)
            htk = tks(8)
            sq = a3(2 * 512, BF16, pat="p (c t) -> p c t", c=2)
            sqtk = tks(2)
            rstd = alloc(512)
            rstd_tk = Tk()
            stg = [a3(24 * 512, BF16, pat="p (m t) -> p m t", m=24) for _ in range(2)]
            stg_tk = tks(2)
            vst = [a3(4 * 8 * 65, BF16, pat="p (s k e) -> p s k e", s=4, k=8) for _ in range(2)]
            vst_tk = tks(2)
            gst = [alloc(512) for _ in range(2)]
            gst_tk = tks(2)
            gain = V("mix_norm%d" % l)
            for b in range(2):
                P.op("dve", MSET(vst[b], 1.0), writes=[vst_tk[b]])

            def load(i):
                b = i % 2
                P.dma("sp", xn[b], xtile_ap(Xc, i), writes=xn_tk[b])
            load(0)
            for i in range(NT):
                b = i % 2
                if i + 1 < NT:
                    load(i + 1)
                rmsnorm(xn[b], xn_tk[b], gain, hT, htk, sq, sqtk, rstd, rstd_tk)
                for m in range(25):
                    pb = m % 4
                    for c in range(8):
                        P.op("pe", MM(psb[pb][:, :], wf[:, m, c * 128:(c + 1) * 128], hT[:, c, :], c == 0, c == 7),
                             reads=[wf_tk[m], htk[c]], writes=[pst[pb]])
                    if m == 24:
                        P.op("act", ACTF(gst[b], psb[pb][:, :], AF.Sigmoid), reads=[pst[pb]], writes=[gst_tk[b]])
                        P.dma("sp", gT_d[:, i * 512:(i + 1) * 512], gst[b][0:48, :], reads=[gst_tk[b]])
                    elif m % 2 == 0:
                        P.op("act", ACTF(stg[b][:, m, :], psb[pb][:, :], AF.Copy), reads=[pst[pb]], writes=[stg_tk[b]])
                    else:
                        P.op("dve", CP(stg[b][:, m, :], psb[pb][:, :]), reads=[pst[pb]], writes=[stg_tk[b]])
                P.dma("sp", qkT_d[:, :, i * 512:(i + 1) * 512].rearrange("m p t -> p m t"), stg[b], reads=[stg_tk[b]])
                for s in range(4):
                    pv = 4 + s % 2
                    for c in range(8):
                        P.op("pe", MM(psb[pv][:, :], hT[:, c, s * 128:(s + 1) * 128], wt[:, c, 0:512], c == 0, c == 7),
                             reads=[wt_tk, htk[c]], writes=[pst[pv]])
                    P.op("dve", CP(vst[b][:, s, :, 0:64], psb[pv][:, :].rearrange("p (k e) -> p k e", k=8)),
                         reads=[pst[pv]], writes=[vst_tk[b]])
                for s in range(4):
                    P.dma("sp", vtok_d[:, :, i * 4 + s, :].rearrange("k p e -> p k e"), vst[b][:, s, :, :], reads=[vst_tk[b]])

        def pass_nsa_group(jx, g):
            ksx = alloc(S, BF16)
            ksx_tk, ex_tk = Tk(), Tk()
            P.dma("sp", ksx[0:64], qkT_d[8 + 4 * g + 2, 0:64, :], writes=[ksx_tk])
            P.dma("pool", ksx[64:128], din["expand"].rearrange("n k m -> n (k m)"), writes=[ex_tk])
            kw = alloc(S, BF16)
            kw_tk = Tk()
            P.dma("sp", kw[0:64], qkT_d[8 + 4 * g + 3, 0:64, :], writes=[kw_tk])
            qx = [alloc(S, BF16) for _ in range(2)]
            qx_tk = tks(2)
            nmq_tk = [tks(8) for _ in range(2)]
            vs = a3(32 * 65, BF16, pat="p (s e) -> p s e", s=32)
            vw = a3(32 * 65, BF16, pat="p (s e) -> p s e", s=32)
            vs_tk, vw_tk = Tk(), Tk()
            P.dma("sp", vs, vtok_d[g], writes=[vs_tk])
            P.dma("sp", vw, vtok_d[4 + g], writes=[vw_tk])
            kcT = alloc(256, BF16)
            kcT_tk = Tk()
            vca = a3(2 * 66, BF16, pat="p (c e) -> p c e", c=2)
            vca_tk = Tk()
            ovl1 = a3(2 * 66, BF16, pat="p (c e) -> p c e", c=2)
            ovl1_tk = Tk()
            ovl = a3(2 * 64, F32, pat="p (c e) -> p c e", c=2)
            ovl_tk = Tk()
            P.dma("sp", ovl, din["overlap"], writes=[ovl_tk])
            imp = a3(32 * 64, F32, pat="p (s f) -> p s f", s=32)
            imp_tk = tks(32)
            NSL = 6
            SB = [0, 1, 2, 3, 4, 7]
            Ef = [alloc(512) for _ in range(NSL)]
            Ef_tk = tks(NSL)
            Eb = [alloc(512, BF16) for _ in range(NSL)]
            Eb_tk = tks(NSL)
            sm = alloc(8)
            sm_tk = tks(8)
            grp_top = top[0]

            kc = alloc(S, BF16)
            vc = alloc(S, BF16)
            kc_tk, vc_tk = Tk(), Tk()
            P.dma("sp", kc[0:64], qkT_d[8 + 4 * g + 0, 0:64, :], writes=[kc_tk])
            P.dma("sp", vc[0:64], qkT_d[8 + 4 * g + 1, 0:64, :], writes=[vc_tk])
            wk1 = a3(32 * 256, BF16, pat="p (l j) -> p l j", l=32)
            wv1 = a3(32 * 256, BF16, pat="p (l j) -> p l j", l=32)
            wk1_tk, wv1_tk = Tk(), Tk()
            P.dma("pool", wk1[0:64], din["wk1_%d" % jx], writes=[wk1_tk])
            P.dma("pool", wv1[0:64], din["wv1_%d" % jx], writes=[wv1_tk])
            w2k = a3(2 * 128, BF16, pat="p (c m) -> p c m", c=2)
            w2v = a3(2 * 64, BF16, pat="p (c m) -> p c m", c=2)
            w2k_tk, w2v_tk = Tk(), Tk()
            P.dma("pool", w2k, din["w2k_%d" % jx], writes=[w2k_tk])
            P.dma("pool", w2v, din["w2v_%d" % jx], writes=[w2v_tk])
            posk = alloc(32, BF16)
            posv = alloc(32, BF16)
            posk_tk, posv_tk = Tk(), Tk()
            P.dma("pool", posk[0:64], din["posk_%d" % jx], writes=[posk_tk])
            P.dma("pool", posv[0:64], din["posv_%d" % jx], writes=[posv_tk])
            cb = alloc(4)
            cb_tk = Tk()
            xg = a3(2 * 256, F32, pat="p (c t) -> p c t", c=2)
            xg_tk = tks(2)
            t1 = a3(2 * 256, F32, pat="p (c t) -> p c t", c=2)
            t1_tk = tks(2)
            gl = a3(2 * 256, BF16, pat="p (c t) -> p c t", c=2)
            gl_tk = tks(2)
            P.op("dve", MSET(kcT, 0.0), writes=[kcT_tk])
            P.op("dve", MSET(vca, 0.0), writes=[vca_tk])
            P.op("dve", MSET(vca[:, :, 64:65], 1.0), writes=[vca_tk])
            P.op("dve", MSET(ovl1[:, :, 64:65], 1.0), writes=[ovl1_tk])
            P.op("dve", CP(ovl1[:, :, 0:64], ovl), reads=[ovl_tk], writes=[ovl1_tk])
            for which, (src, src_tk, w1s, w1_tk, pos, pos_tk) in enumerate(
                    ((kc, kc_tk, wk1, wk1_tk, posk, posk_tk), (vc, vc_tk, wv1, wv1_tk, posv, posv_tk))):
                for jc in range(2):
                    for l_ in range(32):
                        P.op("pe", MM(psb[4][:, jc:jc + 1], w1s[0:64, l_, jc * 128:(jc + 1) * 128], pos[0:64, l_:l_ + 1],
                                      l_ == 0, l_ == 31), reads=[w1_tk, pos_tk], writes=[pst[4]])
                P.op("dve", CP(cb[:, 0:2], psb[4][:, 0:2]), reads=[pst[4]], writes=[cb_tk])
                for jc in range(2):
                    pb = 5 + jc
                    for l_ in range(32):
                        P.op("pe", MM(psb[pb][:, 0:255], w1s[0:64, l_, jc * 128:(jc + 1) * 128],
                                      src[0:64, l_:l_ + 16 * 254 + 1:16], l_ == 0, l_ == 31),
                             reads=[w1_tk, src_tk], writes=[pst[pb]])
                    P.op("act", ACTF(xg[:, jc, 0:255], psb[pb][:, 0:255], AF.Identity, bias=cb[:, jc:jc + 1]),
                         reads=[pst[pb], cb_tk], writes=[xg_tk[jc]])
                    P.op("dve", TT(t1[:, jc, 0:255], xg[:, jc, 0:255], xg[:, jc, 0:255], ALU.mult),
                         reads=[xg_tk[jc]], writes=[t1_tk[jc]])
                    P.op("dve", TS(t1[:, jc, 0:255], t1[:, jc, 0:255], 0.044715, 1.0, ALU.mult, ALU.add),
                         reads=[t1_tk[jc]], writes=[t1_tk[jc]])
                    P.op("dve", TT(t1[:, jc, 0:255], t1[:, jc, 0:255], xg[:, jc, 0:255], ALU.mult),
                         reads=[t1_tk[jc], xg_tk[jc]], writes=[t1_tk[jc]])
                    P.op("act", ACTF(t1[:, jc, 0:255], t1[:, jc, 0:255], AF.Sigmoid, scale=1.5957691),
                         reads=[t1_tk[jc]], writes=[t1_tk[jc]])
                    P.op("dve", TT(gl[:, jc, 0:255], t1[:, jc, 0:255], xg[:, jc, 0:255], ALU.mult),
                         reads=[t1_tk[jc], xg_tk[jc]], writes=[gl_tk[jc]])
                if which == 0:
                    for jc in range(2):
                        P.op("pe", MM(psb[7][:, 0:255], w2k[:, jc, :], gl[:, jc, 0:255], jc == 0, jc == 1),
                             reads=[w2k_tk, gl_tk[jc]], writes=[pst[7]])
                    P.op("act", ACTF(kcT[:, 0:255], psb[7][:, 0:255], AF.Copy), reads=[pst[7]], writes=[kcT_tk])
                else:
                    for ct in range(2):
                        ncr = 128 if ct == 0 else 127
                        for jc in range(2):
                            P.op("pe", MM(psb[7][0:ncr, 0:64], gl[:, jc, ct * 128:ct * 128 + ncr], w2v[:, jc, :], jc == 0, jc == 1),
                                 reads=[w2v_tk, gl_tk[jc]], writes=[pst[7]])
                        P.op("act", ACTF(vca[0:ncr, ct, 0:64], psb[7][0:ncr, 0:64], AF.Copy), reads=[pst[7]], writes=[vca_tk])
            P.barrier()
            top[0] = grp_top
            if DBG < 3:
                return

            tabc = alloc(6144)
            tabc_tk = tks(3)
            tabs2 = [alloc(2560) for _ in range(2)]
            tabs_tk2 = tks(2)
            tabw2 = [alloc(1408) for _ in range(2)]
            tabw_tk2 = tks(2)
            b31c2 = [alloc(1) for _ in range(2)]
            b31_tk2 = tks(2)
            vmk = a3(32 * 64, F32, pat="p (s f) -> p s f", s=32)
            amk = a3(32 * 64, F32, pat="p (s f) -> p s f", s=32)
            vmk_tk, amk_tk = Tk(), Tk()
            P.dma("sp", vmk, din["vmask"], writes=[vmk_tk])
            P.dma("sp", amk, din["amask"], writes=[amk_tk])
            sc = alloc(64)
            sc2 = alloc(64)
            m8a = alloc(8)
            m8b = alloc(8)
            nm = alloc(128)
            sc_tk, sc2_tk, m8a_tk, m8b_tk, nm_tk = Tk(), Tk(), Tk(), Tk(), Tk()
            gq = [alloc(3 * 512) for _ in range(2)]
            gq_tk = tks(2)
            ostg = [alloc(512) for _ in range(3)]
            ostg_tk = tks(3)
            lrow = [alloc(512) for _ in range(3)]
            lrow_tk = tks(3)
            facb = [alloc(512) for _ in range(3)]
            facb_tk = tks(3)
            Oac = [alloc(512) for _ in range(2)]
            Oac_tk = tks(2)
            ost = [alloc(512, BF16) for _ in range(2)]
            ost_tk = tks(2)
            step = [0]
            zcnt = [0]

            def load_tabc(h):
                src = bass.AP(tensor=grep_d.tensor, offset=h * 128 * NSA_M + (NSA_OFF - 2048),
                              ap=[[NSA_M - 16, 128], [1, 6144]])
                P.dma("sp", tabc, src, writes=tabc_tk)
                for pz in range(3):
                    P.op("act", ACTF(tabc[:, pz * 2048:(pz + 1) * 2048], tabc[:, pz * 2048:(pz + 1) * 2048], AF.Exp),
                         reads=[tabc_tk[pz]], writes=[tabc_tk[pz]])

            def fin_a(t):
                zb = zcnt[0] % 3
                zcnt[0] += 1
                t["zb"] = zb
                P.op("dve", CP(ostg[zb][0:65, :], psb[t["acc"]][0:65, :]), reads=[pst[t["acc"]]], writes=[ostg_tk[zb]])

            def fin_b(t):
                j, QB, zb = t["j"], t["QB"], t["zb"]
                gb_ = t["gb"]
                P.op("act", ACTF(lrow[zb][64:65, :], ostg[zb][64:65, :], AF.Ln, bias=V("eps_z")[64:65]),
                     reads=[ostg_tk[zb], vecs_tk], writes=[lrow_tk[zb]])
                P.op("act", ACTF(lrow[zb][64:65, :], lrow[zb][64:65, :], AF.Exp, scale=-1.0), reads=[lrow_tk[zb]], writes=[lrow_tk[zb]])
                P.op("dve", TT(lrow[zb][64:65, :], lrow[zb][64:65, :], gq[gb_][64:65, j * 512:(j + 1) * 512], ALU.mult),
                     reads=[lrow_tk[zb], gq_tk[gb_]], writes=[lrow_tk[zb]])
                P.dma("sp", fr_d[zb:zb + 1, :], lrow[zb][64:65, :], reads=[lrow_tk[zb]], writes=[frd_tk[zb]])
                P.dma("sp", facb[zb][0:64, :], bass.AP(tensor=fr_d.tensor, offset=zb * 512, ap=[[0, 64], [1, 512]]),
                      reads=[frd_tk[zb]], writes=[facb_tk[zb]])

            def fin_c(t):
                j, QB, zb = t["j"], t["QB"], t["zb"]
                qs = slice(QB * 512, (QB + 1) * 512)
                ob = t["ob"]
                r_, pr_, hf_ = t["r"], t["r"] // 2, t["r"] % 2
                if j == 0:
                    P.op("dve", TT(Oac[ob][0:64], ostg[zb][0:64, :], facb[zb][0:64, :], ALU.mult),
                         reads=[ostg_tk[zb], facb_tk[zb]], writes=[Oac_tk[ob]])
                    P.dma("sp", oc_d[r_, :, qs], Oac[ob][0:64], reads=[Oac_tk[ob]])
                else:
                    P.op("dve", TT(ostg[zb][0:64, :], ostg[zb][0:64, :], facb[zb][0:64, :], ALU.mult),
                         reads=[ostg_tk[zb], facb_tk[zb]], writes=[ostg_tk[zb]])
                    P.op("pool", TT(Oac[ob][0:64], Oac[ob][0:64], ostg[zb][0:64, :], ALU.add),
                         reads=[ostg_tk[zb], Oac_tk[ob]], writes=[Oac_tk[ob]])
                if j == 2:
                    P.op("dve", CP(ost[ob][0:64], Oac[ob][0:64]), reads=[Oac_tk[ob]], writes=[ost_tk[ob]])
                    P.dma("sp", oT_d[2 * g + pr_, hf_ * 64:(hf_ + 1) * 64, qs], ost[ob][0:64], reads=[ost_tk[ob]])

            def load_gq(h, QB, gb_):
                P.dma("sp", gq[gb_][64:65, :].rearrange("p (j t) -> p j t", j=3),
                      gT_d[h * 3:h * 3 + 3, QB * 512:(QB + 1) * 512].rearrange("(o j) t -> o j t", o=1),
                      writes=[gq_tk[gb_]])

            gcnt = [0]
            for r in range(4):
                h = 4 * g + r
                pr, hf = r // 2, r % 2
                b = r % 2
                P.dma("sp", qx[b][0:64], qkT_d[2 * g + pr, hf * 64:(hf + 1) * 64, :], writes=[qx_tk[b]])
                load_tabc(h)
                prev_t = None
                its = []
                for QB in range(8):
                    its.append(dict(QB=QB))

                def front(it):
                    QB = it["QB"]
                    qs = slice(QB * 512, (QB + 1) * 512)
                    cts = [0] if QB < 4 else [0, 1]
                    gb_ = gcnt[0] % 2
                    gcnt[0] += 1
                    load_gq(h, QB, gb_)
                    slots = []
                    for ct in cts:
                        sl = step[0] % 4
                        step[0] += 1
                        P.op("pe", MM(psb[sl][:, :], kcT[0:64, ct * 128:(ct + 1) * 128], qx[b][0:64, qs], True, True),
                             reads=[kcT_tk, qx_tk[b]], writes=[pst[sl]])
                        P.op("act", ACTF(Ef[sl], psb[sl][:, :], AF.Exp, scale=0.125), reads=[pst[sl]], writes=[Ef_tk[sl]])
                        sj = QB * 512 + 2017 - 2048 * ct
                        P.op("dve", TT(Eb[sl], Ef[sl], tabc[:, sj:sj + 512], ALU.mult),
                             reads=[Ef_tk[sl]] + tabc_tk, writes=[Eb_tk[sl]])
                        slots.append((ct, sl))
                    it["slots"] = slots
                    it["cts"] = cts
                    it["t"] = dict(j=0, QB=QB, acc=5, gb=gb_, ob=QB % 2, r=r)

                def back(it):
                    QB, slots, cts, t = it["QB"], it["slots"], it["cts"], it["t"]
                    for (ct, sl) in slots:
                        P.op("pe", MM(psb[5][0:65, :], vca[:, ct, 0:65], Eb[sl], ct == cts[0], ct == cts[-1]),
                             reads=[Eb_tk[sl], vca_tk], writes=[pst[5]])
                    fin_a(t)
                    for s in range(4):
                        for (ct, sl) in slots:
                            P.op("pe", MM(psb[4][:, s * 65:(s + 1) * 65], Eb[sl][:, s * 128:(s + 1) * 128], ovl1[:, ct, 0:65],
                                          ct == cts[0], ct == cts[-1]), reads=[Eb_tk[sl], ovl1_tk], writes=[pst[4]])
                    fin_b(t)
                    for s in range(4):
                        qt = QB * 4 + s
                        k = qt % 8
                        z = sm[:, k:k + 1]
                        P.op("dve", TS(z, psb[4][:, s * 65 + 64:s * 65 + 65], 1e-30, None, ALU.max), reads=[pst[4]], writes=[sm_tk[k]])
                        P.op("dve", RCP(z, z), reads=[sm_tk[k]], writes=[sm_tk[k]])
                        if r == 0:
                            P.op("dve", TS(imp[:, qt, :], psb[4][:, s * 65:s * 65 + 64], z, None, ALU.mult),
                                 reads=[pst[4], sm_tk[k]], writes=[imp_tk[qt]])
                        else:
                            P.op("dve", STT(imp[:, qt, :], psb[4][:, s * 65:s * 65 + 64], z, imp[:, qt, :], ALU.mult, ALU.add),
                                 reads=[pst[4], sm_tk[k], imp_tk[qt]], writes=[imp_tk[qt]])

                for ii in range(len(its) + 1):
                    if ii < len(its):
                        front(its[ii])
                    if ii >= 1:
                        back(its[ii - 1])
                        if prev_t is not None:
                            fin_c(prev_t)
                        prev_t = its[ii - 1]["t"]
                fin_c(prev_t)
            if DBG < 4:
                return
            for qt in range(32):
                P.op("dve", TT(sc, imp[:, qt, :], vmk[:, qt, :], ALU.mult), reads=[imp_tk[qt], vmk_tk], writes=[sc_tk])
                P.op("dve", TT(sc, sc, amk[:, qt, :], ALU.add), reads=[sc_tk, amk_tk], writes=[sc_tk])
                P.op("dve", lambda e: e.max(out=m8a, in_=sc), reads=[sc_tk], writes=[m8a_tk])
                P.op("dve", lambda e: e.match_replace(out=sc2, in_to_replace=m8a, in_values=sc, imm_value=-1e30),
                     reads=[sc_tk, m8a_tk], writes=[sc2_tk])
                P.op("dve", lambda e: e.max(out=m8b, in_=sc2), reads=[sc2_tk], writes=[m8b_tk])
                P.op("dve", TS(nm[:, 0:64], sc, m8b[:, 7:8], -30000.0, ALU.is_lt, ALU.mult), reads=[sc_tk, m8b_tk], writes=[nm_tk])
                P.op("dve", TS(nm[:, 64:128], sc, m8b[:, 7:8], -30000.0, ALU.is_lt, ALU.mult), reads=[sc_tk, m8b_tk, nm_tk], writes=[nm_tk])
                P.op("pe", TRN(psb[7][:, 0:128], nm, ident), reads=[nm_tk, ident_tk], writes=[pst[7]])
                for b in range(2):
                    P.op("act", ACTF(qx[b][64:128, qt * 128:(qt + 1) * 128], psb[7][64:128, 0:128], AF.Copy),
                         reads=[pst[7]], writes=[nmq_tk[b][qt // 4]])
            if DBG < 5:
                return

            def load_tabs(r):
                h = 4 * g + r
                tb = r % 2
                pr, hf = r // 2, r % 2
                P.dma("sp", qx[tb][0:64], qkT_d[2 * g + pr, hf * 64:(hf + 1) * 64, :], writes=[qx_tk[tb]])
                src = bass.AP(tensor=grep_d.tensor, offset=h * 128 * NSA_M + (NSA_OFF - 384),
                              ap=[[NSA_M - 1, 128], [1, 2560]])
                P.dma("sp", tabs2[tb], src, writes=[tabs_tk2[tb]])
                src = bass.AP(tensor=gwrep_d.tensor, offset=h * 128 * NSAW_M + (NSAW_OFF - 384),
                              ap=[[NSAW_M - 1, 128], [1, 1408]])
                P.dma("sp", tabw2[tb], src, writes=[tabw_tk2[tb]])

            def exp_tabs(r):
                tb = r % 2
                P.op("act", ACTF(b31c2[tb], tabs2[tb][:, 2559:2560], AF.Copy), reads=[tabs_tk2[tb]], writes=[b31_tk2[tb]])
                P.op("act", ACTF(tabs2[tb], tabs2[tb], AF.Exp), reads=[tabs_tk2[tb], b31_tk2[tb]], writes=[tabs_tk2[tb]])
                P.op("act", ACTF(tabw2[tb], tabw2[tb], AF.Exp), reads=[tabw_tk2[tb]], writes=[tabw_tk2[tb]])

            load_tabs(0)
            exp_tabs(0)
            for r in range(4):
                h = 4 * g + r
                pr, hf = r // 2, r % 2
                b = r % 2
                tabs, tabs_tk, tabw, tabw_tk, b31c, b31_tk = tabs2[b], tabs_tk2[b], tabw2[b], tabw_tk2[b], b31c2[b], b31_tk2[b]
                if r + 1 < 4:
                    load_tabs(r + 1)
                tiles = []
                for QB in range(8):
                    qs = slice(QB * 512, (QB + 1) * 512)
                    for kt in range(4 * QB + 4):
                        off = min(QB * 512 - kt * 128, 1664) + 384
                        tiles.append(dict(j=1, QB=QB, first=kt == 0, last=kt == 4 * QB + 3, r=r, ob=QB % 2,
                                          lhsT=ksx[:, kt * 128:(kt + 1) * 128], lt=[ksx_tk, ex_tk], rhs=qx[b][:, qs],
                                          rt=[qx_tk[b], nmq_tk[b][QB]],
                                          tab=tabs[:, off:off + 512], tt=[tabs_tk], v=vs[:, kt, :], vt=[vs_tk], acc=5,
                                          far=(QB * 512 - kt * 128 >= 1664)))
                    k0 = max(0, 4 * QB - 4)
                    for kt in range(k0, 4 * QB + 4):
                        off = QB * 512 - kt * 128 + 384
                        tiles.append(dict(j=2, QB=QB, first=kt == k0, last=kt == 4 * QB + 3, r=r, ob=QB % 2,
                                          lhsT=kw[0:64, kt * 128:(kt + 1) * 128], lt=[kw_tk], rhs=qx[b][0:64, qs], rt=[qx_tk[b]],
                                          tab=tabw[:, off:off + 512], tt=[tabw_tk], v=vw[:, kt, :], vt=[vw_tk], acc=6, far=False))
                LOOK = 5
                pend = []
                n = len(tiles)
                lastQB = -1
                cur_gb = 0
                for i in range(n + LOOK):
                    if i == (n * 3) // 5 and r + 1 < 4:
                        exp_tabs(r + 1)
                    if i < n:
                        t = tiles[i]
                        if t["QB"] != lastQB:
                            lastQB = t["QB"]
                            cur_gb = gcnt[0] % 2
                            gcnt[0] += 1
                            load_gq(h, lastQB, cur_gb)
                            ob = lastQB % 2
                            P.dma("sp", Oac[ob][0:64], oc_d[r, :, lastQB * 512:(lastQB + 1) * 512], writes=[Oac_tk[ob]])
                        t["gb"] = cur_gb
                        sl = step[0] % NSL
                        step[0] += 1
                        t["sl"] = sl
                        P.op("pe", MM(psb[SB[sl]][:, :], t["lhsT"], t["rhs"], True, True), reads=t["lt"] + t["rt"], writes=[pst[SB[sl]]])
                    jx_ = i - LOOK
                    if jx_ >= 0:
                        t = tiles[jx_]
                        sl = t["sl"]
                        if t["far"]:
                            P.op("act", ACTF(Eb[sl], psb[SB[sl]][:, :], AF.Exp, scale=0.125, bias=b31c), reads=[pst[SB[sl]], b31_tk], writes=[Eb_tk[sl]])
                        else:
                            P.op("act", ACTF(Ef[sl], psb[SB[sl]][:, :], AF.Exp, scale=0.125), reads=[pst[SB[sl]]], writes=[Ef_tk[sl]])
                            P.op("dve", TT(Eb[sl], Ef[sl], t["tab"], ALU.mult), reads=[Ef_tk[sl]] + list(t["tt"]), writes=[Eb_tk[sl]])
                        P.op("pe", MM(psb[t["acc"]][0:65, :], t["v"], Eb[sl], t["first"], t["last"]),
                             reads=[Eb_tk[sl]] + t["vt"], writes=[pst[t["acc"]]])
                        if t["last"]:
                            fin_a(t)
                            pend.append([i + 2, 0, t])
                    k = 0
                    while k < len(pend):
                        due, stage, t = pend[k]
                        if due <= i:
                            if stage == 0:
                                fin_b(t)
                                pend[k] = [i + 5, 1, t]
                                k += 1
                            else:
                                fin_c(t)
                                pend.pop(k)
                        else:
                            k += 1
                for due, stage, t in sorted(pend, key=lambda x: x[0]):
                    if stage == 0:
                        fin_b(t)
                    fin_c(t)

        def pass_nsa_out(jx, Xc):
            wo = a3(8 * 1024, BF16, pat="p (m f) -> p m f", m=8)
            wo_tk = tks(8)
            load_w(wo, din["nsa_wo_%d" % jx], 8, wo_tk)
            xn = [a3(8 * 512, F32, pat="p (c t) -> p c t", c=8) for _ in range(2)]
            xn_tk = [tks(8) for _ in range(2)]
            ot = [a3(8 * 512, BF16, pat="p (c t) -> p c t", c=8) for _ in range(2)]
            ot_tk = tks(2)

            def load(i):
                b = i % 2
                P.dma("sp", xn[b], xtile_ap(Xc, i), writes=xn_tk[b])
                P.dma("sp", ot[b], oT_d[:, :, i * 512:(i + 1) * 512].rearrange("m p t -> p m t"), writes=[ot_tk[b]])
            load(0)
            for i in range(NT):
                b = i % 2
                if i + 1 < NT:
                    load(i + 1)
                for m in range(8):
                    po = m % 2
                    for c in range(8):
                        P.op("pe", MM(psb[po][:, :], wo[:, m, c * 128:(c + 1) * 128], ot[b][:, c, :], c == 0, c == 7),
                             reads=[wo_tk[m], ot_tk[b]], writes=[pst[po]])
                    P.op("dve", TT(xn[b][:, m, :], xn[b][:, m, :], psb[po][:, :], ALU.add),
                         reads=[pst[po], xn_tk[b][m]], writes=[xn_tk[b][m]])
                P.dma("sp", xtile_ap(Xc, i), xn[b], reads=xn_tk[b])

        def pass_final(Xc, do_norm):
            xn = [a3(8 * 512, F32, pat="p (c t) -> p c t", c=8) for _ in range(2)]
            xn_tk = [tks(8) for _ in range(2)]
            sq = a3(2 * 512, BF16, pat="p (c t) -> p c t", c=2)
            sqtk = tks(2)
            rstd = alloc(512)
            rstd_tk = Tk()
            yo = [alloc(D) for _ in range(2)]
            yo_tk = tks(2)
            gain = V("final_norm")

            def load(i):
                b = i % 2
                P.dma("sp", xn[b], xtile_ap(Xc, i), writes=xn_tk[b])
            load(0)
            for i in range(NT):
                b = i % 2
                if i + 1 < NT:
                    load(i + 1)
                xt, xtk = xn[b], xn_tk[b]
                if do_norm:
                    for c in range(8):
                        sl = c % 2
                        P.op("act", ACTF(sq[:, sl, :], xt[:, c, :], AF.Square), reads=[xtk[c]], writes=[sqtk[sl]])
                        P.op("pe", MM(psb[6][:, :], ones_bf, sq[:, sl, :], c == 0, c == 7),
                             reads=[sqtk[sl], ones_tk], writes=[pst[6]])
                    P.op("act", ACTF(rstd, psb[6][:, :], AF.Sqrt, bias=V("eps_rms"), scale=1.0 / D),
                         reads=[pst[6], vecs_tk], writes=[rstd_tk])
                    P.op("dve", RCP(rstd, rstd), reads=[rstd_tk], writes=[rstd_tk])
                    for c in range(8):
                        P.op("dve", STT(xt[:, c, :], xt[:, c, :], gain[:, c:c + 1], rstd, ALU.mult, ALU.mult),
                             reads=[xtk[c], rstd_tk, vecs_tk], writes=[xtk[c]])
                for s in range(4):
                    yb_ = (i * 4 + s) % 2
                    for c in range(8):
                        bank = c // 4
                        P.op("pe", TRN(psb[bank][:, (c % 4) * 128:(c % 4 + 1) * 128], xt[:, c, s * 128:(s + 1) * 128], ident),
                             reads=[xtk[c], ident_tk], writes=[pst[bank]])
                    P.op("act", ACTF(yo[yb_][:, 0:512], psb[0][:, :], AF.Copy), reads=[pst[0]], writes=[yo_tk[yb_]])
                    P.op("dve", CP(yo[yb_][:, 512:1024], psb[1][:, :]), reads=[pst[1], yo_tk[yb_]], writes=[yo_tk[yb_]])
                    t0 = i * 512 + s * 128
                    P.dma("sp", out_d[t0:t0 + 128, :], yo[yb_], reads=[yo_tk[yb_]])

        def phase(fn, *a):
            top[0] = base_top
            fn(*a)
            P.barrier()

        phase(pass_input)
        has_nsa = any((l % 2 == 1) for l in layers) and "mix" in stages
        if has_nsa:
            phase(pass_nsa_tables)
        cur = 0
        for l in layers:
            jx = l // 2
            if "ffn1" in stages:
                a_, b_, c_ = cur, (cur + 1) % 3, (cur + 2) % 3
                phase(pass_ffn_half, l, 0, 0, X[a_], X[a_], X[b_], "ffn1_norm%d" % l)
                phase(pass_ffn_half, l, 0, 1, X[a_], X[b_], X[c_], "ffn1_norm%d" % l)
                cur = c_
            if "mix" in stages:
                if l % 2 == 0:
                    phase(pass_conv, l, jx, X[cur])
                else:
                    phase(pass_nsa_proj, l, jx, X[cur])
                    if DBG >= 2:
                        for g in range(4 if DBG >= 9 else 1):
                            phase(pass_nsa_group, jx, g)
                    if DBG >= 9:
                        phase(pass_nsa_out, jx, X[cur])
            if "ffn2" in stages:
                a_, b_, c_ = cur, (cur + 1) % 3, (cur + 2) % 3
                phase(pass_ffn_half, l, 1, 0, X[a_], X[a_], X[b_], "ffn2_norm%d" % l)
                phase(pass_ffn_half, l, 1, 1, X[a_], X[b_], X[c_], "ffn2_norm%d" % l)
                cur = c_
            if "ple" in stages:
                phase(pass_ple, l, X[cur])
        phase(pass_final, X[cur], do_final)
        for e_ in ENGS:
            P.op(e_, lambda e: e.nop())
        P.emit()
    return nc


import os
DBG = int(os.environ.get("NSA_DBG", "9"))


def run(inputs, n_cores=8, **bkw):
    shared, vidx = host_prepare(inputs)
    x = np.asarray(inputs["x"], np.float32)
    p = np.asarray(inputs["p"], np.float32)
    shapes = {k: v.shape for k, v in shared.items()}
    shapes["x"] = (S, D)
    shapes["p"] = (4, S, 256)
    nc = bass.Bass("TRN2", target_bir_lowering=False)
    build(nc, shapes, vidx, **bkw)
    in_maps = []
    for b in range(n_cores):
        m = dict(shared)
        m["x"] = np.ascontiguousarray(x[b])
        m["p"] = np.ascontiguousarray(p[:, b])
        in_maps.append(m)
    res = run_bass_kernel_spmd(nc, in_maps, core_ids=list(range(n_cores)))
    return np.stack([np.asarray(r["out"], np.float32) for r in res.results], axis=0)


def kernel(**inputs):
    return run(inputs, n_cores=8)
```

```python
import contextlib
import os
import math
import numpy as np
import concourse.bass as bass
import concourse.mybir as mybir
from concourse.bass_utils import run_bass_kernel_spmd

F32 = mybir.dt.float32
BF16 = mybir.dt.bfloat16
AF = mybir.ActivationFunctionType
ALU = mybir.AluOpType

S = 4096
D = 1024
DFF = 2816
NT = 8
TW_ = 512
ENGS = ["pe", "act", "dve", "pool", "sp"]
DMA_RING = {"sp": 8, "act": 2, "pool": 6, "pe": 2, "dve": 2}


class Tk:
    __slots__ = ("w", "r")

    def __init__(self):
        self.w = None
        self.r = []


def tks(n):
    return [Tk() for _ in range(n)]


class Op:
    __slots__ = ("eng", "fn", "deps", "is_dma", "need_sig", "sigval", "ev")

    def __init__(self, eng, fn, is_dma):
        self.eng = eng
        self.fn = fn
        self.deps = []
        self.is_dma = is_dma
        self.need_sig = False
        self.sigval = None
        self.ev = None


class Prog:
    def __init__(self, nc):
        self.nc = nc
        self.ops = {e: [] for e in ENGS}
        self.last = {e: None for e in ENGS}
        self.dmas_since_barrier = []
        self.pending = {e: [] for e in ENGS}

    def _rec(self, eng, fn, reads, writes, is_dma):
        op = Op(eng, fn, is_dma)
        deps = []
        for t in reads:
            if t.w is not None:
                deps.append((t.w, 0))
        for t in writes:
            if t.w is not None:
                deps.append((t.w, 1))
            for r in t.r:
                deps.append((r, 1))
        for d in self.pending[eng]:
            deps.append((d, 0))
        self.pending[eng] = []
        op.deps = deps
        for t in writes:
            t.w = op
            t.r = []
        for t in reads:
            if t.w is not op:
                t.r.append(op)
        self.ops[eng].append(op)
        self.last[eng] = op
        if is_dma:
            self.dmas_since_barrier.append(op)
        return op

    def op(self, eng, fn, reads=(), writes=()):
        return self._rec(eng, fn, list(reads), list(writes), False)

    def dma(self, eng, out, in_, reads=(), writes=()):
        def fn(e):
            return e.dma_start(out=out, in_=in_)
        return self._rec(eng, fn, list(reads), list(writes), True)

    def barrier(self):
        deps = [o for o in self.last.values() if o is not None] + self.dmas_since_barrier
        self.dmas_since_barrier = []
        for e in ENGS:
            self.pending[e] = list(deps)

    @staticmethod
    def _skip(d, ename, kind):
        if d.eng == ename:
            if ename in ("pe", "sp"):
                return True
            if kind == 1:
                return True
        return False

    def emit(self):
        nc = self.nc
        with contextlib.ExitStack() as st:
            esem = {e: st.enter_context(nc.semaphore("s_" + e)) for e in ENGS}
            rings = {e: [st.enter_context(nc.semaphore("d_%s%d" % (e, i))) for i in range(DMA_RING[e])]
                     for e in ENGS}
            for e in ENGS:
                for op in self.ops[e]:
                    for d, kind in op.deps:
                        if d.is_dma or self._skip(d, e, kind):
                            continue
                        d.need_sig = True
            for e in ENGS:
                c = 0
                ring_cnt = [0] * DMA_RING[e]
                nd = 0
                for op in self.ops[e]:
                    if op.is_dma:
                        slot = nd % DMA_RING[e]
                        prev = ring_cnt[slot] * 16
                        ring_cnt[slot] += 1
                        op.ev = (rings[e][slot], ring_cnt[slot] * 16, prev)
                        nd += 1
                    elif op.need_sig:
                        c += 1
                        op.sigval = c
            if os.environ.get("NSA_VERBOSE"):
                print("ops per engine", {e: len(self.ops[e]) for e in ENGS},
                      "sig counts", {e: max([o.sigval or 0 for o in self.ops[e]] + [0]) for e in ENGS},
                      "ring max", {e: max([o.ev[1] for o in self.ops[e] if o.is_dma] + [0]) for e in ENGS}, flush=True)
            blk = st.enter_context(nc.Block())

            def run(ename, eng):
                known = {}

                def wait(sem, val):
                    k = id(sem)
                    if known.get(k, 0) >= val:
                        return
                    known[k] = val
                    eng.wait_ge(sem, val)
                for op in self.ops[ename]:
                    for d, kind in op.deps:
                        if d.is_dma:
                            wait(d.ev[0], d.ev[1])
                        elif not self._skip(d, ename, kind):
                            wait(esem[d.eng], d.sigval)
                    if op.is_dma:
                        sem, tgt, prev = op.ev
                        if prev > 0:
                            wait(sem, prev)
                        op.fn(eng).then_inc(sem, 16)
                    else:
                        ins = op.fn(eng)
                        if op.need_sig:
                            ins.then_inc(esem[ename], 1)

            @blk.sync
            def _(sync):
                run("sp", sync)

            @blk.tensor
            def _(tensor):
                run("pe", tensor)

            @blk.scalar
            def _(scalar):
                run("act", scalar)

            @blk.vector
            def _(vector):
                run("dve", vector)

            @blk.gpsimd
            def _(gpsimd):
                run("pool", gpsimd)


def MM(out, lhsT, rhs, start, stop):
    return lambda e: e.matmul(out, lhsT=lhsT, rhs=rhs, start=start, stop=stop)


def TRN(out, in_, ident):
    return lambda e: e.transpose(out=out, in_=in_, identity=ident)


def ACTF(out, in_, func, bias=None, scale=None):
    kw = {}
    if bias is not None:
        kw["bias"] = bias
    if scale is not None:
        kw["scale"] = scale
    return lambda e: e.activation(out=out, in_=in_, func=func, **kw)


def TT(out, in0, in1, op):
    return lambda e: e.tensor_tensor(out=out, in0=in0, in1=in1, op=op)


def STT(out, in0, scalar, in1, op0, op1):
    return lambda e: e.scalar_tensor_tensor(out=out, in0=in0, scalar=scalar, in1=in1, op0=op0, op1=op1)


def TS(out, in0, s1, s2, op0, op1=None):
    if op1 is None:
        return lambda e: e.tensor_scalar(out=out, in0=in0, scalar1=s1, scalar2=None, op0=op0)
    return lambda e: e.tensor_scalar(out=out, in0=in0, scalar1=s1, scalar2=s2, op0=op0, op1=op1)


def CP(out, in_):
    return lambda e: e.tensor_copy(out=out, in_=in_)


def MSET(out, v):
    return lambda e: e.memset(out, v)


def RCP(out, in_):
    return lambda e: e.reciprocal(out=out, in_=in_)


NSA_OFF = 4080
NSA_M = 8192
NSAW_OFF = 512
NSAW_M = 2048


def _t5_bucket_np(n):
    n = np.maximum(n, 0)
    nf = np.maximum(n, 1).astype(np.float32)
    large = 16 + (np.log(nf / np.float32(16)) / np.float32(math.log(128.0)) * np.float32(16)).astype(np.int32)
    large = np.minimum(large, 31)
    return np.where(n < 16, n, large)


def lin_layout(w):
    K, M = w.shape
    kc, mc = K // 128, M // 128
    return np.ascontiguousarray(w.reshape(kc, 128, mc, 128).transpose(2, 1, 0, 3).reshape(mc, 128, kc * 128))


def fm(v):
    return np.ascontiguousarray(v.reshape(-1, 128).T)


class VecPack:
    def __init__(self):
        self.cols = []
        self.idx = {}
        self.n = 0

    def add(self, name, arr):
        arr = np.asarray(arr, np.float32)
        assert arr.shape[0] == 128
        self.idx[name] = (self.n, arr.shape[1])
        self.cols.append(arr)
        self.n += arr.shape[1]

    def build(self):
        return np.ascontiguousarray(np.concatenate(self.cols, axis=1))


def host_prepare(inp):
    f = lambda a: np.asarray(a, np.float32)
    shared = {}
    vp = VecPack()
    for l in range(4):
        for nm in ("ffn1_norm", "mix_norm", "ffn2_norm", "ple_norm"):
            vp.add("%s%d" % (nm, l), fm(f(inp[nm])[l]))
        for fi, pre in enumerate(("ffn1", "ffn2")):
            wg = f(inp[pre + "_w_gate"])[l]
            wu = f(inp[pre + "_w_up"])[l]
            wd = f(inp[pre + "_w_down"])[l]
            for hf in range(2):
                cs = slice(hf * 1408, (hf + 1) * 1408)
                shared["wg_%d_%d_%d" % (l, fi, hf)] = lin_layout(wg[:, cs])
                shared["wu_%d_%d_%d" % (l, fi, hf)] = lin_layout(wu[:, cs])
                shared["wd_%d_%d_%d" % (l, fi, hf)] = lin_layout(wd[cs, :])
        shared["pleg_%d" % l] = lin_layout(f(inp["ple_w_gate"])[l])
        shared["plei_%d" % l] = lin_layout(f(inp["ple_w_in"])[l])
    vp.add("final_norm", fm(f(inp["final_norm"])))
    for j in range(2):
        shared["pw1_%d" % j] = lin_layout(f(inp["conv_w_pw1"])[j])
        shared["pw2_%d" % j] = lin_layout(f(inp["conv_w_pw2"])[j])
        vp.add("b_pw1_%d" % j, fm(f(inp["conv_b_pw1"])[j]))
        vp.add("b_dw_%d" % j, fm(f(inp["conv_b_dw"])[j]))
        vp.add("ln_g_%d" % j, fm(f(inp["conv_ln_g"])[j]))
        vp.add("ln_b_%d" % j, fm(f(inp["conv_ln_b"])[j]))
        vp.add("b_pw2_%d" % j, fm(f(inp["conv_b_pw2"])[j]))
        wdw = f(inp["conv_w_dw"])[j]
        vp.add("w_dw_%d" % j, np.ascontiguousarray(wdw.reshape(31, 8, 128).transpose(2, 1, 0).reshape(128, 248)))
        w_in = f(inp["nsa_w_in"])[j]
        cols = [np.arange(1024)]
        for g in range(4):
            for kind in (0, 1, 2, 4):
                c = 1024 + kind * 256 + g * 64 + np.arange(64)
                cols.append(np.concatenate([c, c]))
        cols = np.concatenate(cols)
        wcat = np.concatenate([w_in[:, cols], w_in[:, 2560:2608], np.zeros((1024, 80), np.float32)], axis=1)
        shared["nsa_wf_%d" % j] = lin_layout(wcat)
        tc = np.concatenate([1024 + 3 * 256 + np.arange(256), 1024 + 5 * 256 + np.arange(256), 2560 + np.arange(48)])
        shared["nsa_wt_%d" % j] = np.ascontiguousarray(w_in[:, tc].reshape(8, 128, 560).transpose(1, 0, 2))
        shared["nsa_wo_%d" % j] = lin_layout(f(inp["nsa_w_out"])[j])
        for nm, src in (("wk1", "nsa_cmp_wk1"), ("wv1", "nsa_cmp_wv1")):
            w1 = f(inp[src])[j]
            shared["%s_%d" % (nm, j)] = np.ascontiguousarray(w1.reshape(32, 64, 256).transpose(1, 0, 2))
        w2k = f(inp["nsa_cmp_wk2"])[j]
        w2kd = np.concatenate([w2k, w2k], axis=1)
        shared["w2k_%d" % j] = np.ascontiguousarray(w2kd.reshape(2, 128, 128).transpose(1, 0, 2))
        w2v = f(inp["nsa_cmp_wv2"])[j]
        shared["w2v_%d" % j] = np.ascontiguousarray(w2v.reshape(2, 128, 64).transpose(1, 0, 2))
        shared["posk_%d" % j] = np.ascontiguousarray(f(inp["nsa_cmp_pos_k"])[j].T)
        shared["posv_%d" % j] = np.ascontiguousarray(f(inp["nsa_cmp_pos_v"])[j].T)
    rb = f(inp["rel_bias"])
    ext = np.concatenate([rb, np.full((1, 16), -30000.0, np.float32)], axis=0)
    dist = np.arange(NSA_M) - NSA_OFF
    idx = np.where(dist >= 0, _t5_bucket_np(dist), 32)
    shared["gvec"] = np.ascontiguousarray(ext[idx].T)
    distw = np.arange(NSAW_M) - NSAW_OFF
    idxw = np.where((distw >= 0) & (distw < 512), _t5_bucket_np(distw), 32)
    shared["gwvec"] = np.ascontiguousarray(ext[idxw].T)
    vp.add("eps_rms", np.full((128, 1), 1e-6, np.float32))
    vp.add("eps_ln", np.full((128, 1), 1e-5, np.float32))
    vp.add("eps_z", np.full((128, 1), 1e-30, np.float32))
    shared["vecs"] = vp.build()
    shared["ident"] = np.eye(128, dtype=np.float32)
    t = np.arange(S)
    j = np.arange(64)[None, :]
    cur = (t // 64)[:, None]
    valid = (j * 64 <= t[:, None])
    forced = (j == 0) | (j == cur) | (j == cur - 1)
    vm = (valid & ~forced).astype(np.float32)
    am = np.where(forced, 1e9, np.where(valid, 0.0, -1.0)).astype(np.float32)
    shared["vmask"] = np.ascontiguousarray(vm.reshape(32, 128, 64).transpose(1, 0, 2))
    shared["amask"] = np.ascontiguousarray(am.reshape(32, 128, 64).transpose(1, 0, 2))
    c = np.arange(256)[:, None]
    ov = ((c * 16 < j * 64 + 64) & (c * 16 + 32 > j * 64) & (c < 255)).astype(np.float32)
    shared["overlap"] = np.ascontiguousarray(ov.reshape(2, 128, 64).transpose(1, 0, 2))
    ex = np.zeros((64, 32, 128), np.float32)
    for kt in range(32):
        ex[2 * kt, kt, 0:64] = 1.0
        ex[2 * kt + 1, kt, 64:128] = 1.0
    shared["expand"] = ex
    return shared, vp.idx


ARENA_F32 = 47600


class Ctx:
    pass


def build(nc, shapes, vidx, layers=(0, 1, 2, 3), stages=("ffn1", "mix", "ffn2", "ple"), do_final=True):
    P = Prog(nc)
    C = Ctx()
    din = {}
    for name, shp in shapes.items():
        din[name] = nc.dram_tensor(name, list(shp), F32, kind="ExternalInput").ap()
    out_d = nc.dram_tensor("out", [S, D], F32, kind="ExternalOutput").ap()

    def scratch(name, shape, dt):
        return nc.dram_tensor(name, list(shape), dt, kind="Internal").ap()
    X = [scratch("xs%d" % i, [8, 128, S], F32) for i in range(3)]
    qkT_d = scratch("qkT", [24, 128, S], BF16)
    vtok_d = scratch("vtok", [8, 128, 32, 65], BF16)
    gT_d = scratch("gT", [48, S], F32)
    fr_d = scratch("frow", [3, 512], F32)
    oc_d = scratch("ocT", [4, 64, S], F32)
    frd_tk = tks(3)
    oT_d = scratch("oT", [8, 128, S], BF16)
    grep_d = scratch("grep", [16, 128 * NSA_M], F32)
    gwrep_d = scratch("gwrep", [16, 128 * NSAW_M], F32)

    with contextlib.ExitStack() as st:
        arena = st.enter_context(nc.sbuf_tensor("arena", [128, ARENA_F32], F32))
        psb = [st.enter_context(nc.psum_tensor("psb%d" % i, [128, 512], F32)) for i in range(8)]
        pst = tks(8)
        top = [0]

        def alloc(nfree, dt=F32):
            n32 = nfree if dt == F32 else (nfree + 1) // 2
            assert top[0] + n32 <= ARENA_F32, ("arena overflow", top[0], n32)
            a = arena[:, top[0]:top[0] + n32]
            top[0] += n32
            if dt != F32:
                a = a.bitcast(dt)
                a = a[:, 0:nfree]
            return a

        def a3(nfree, dt, **kw):
            pat = kw.pop("pat")
            return alloc(nfree, dt).rearrange(pat, **kw)

        nv = shapes["vecs"][1]
        vecs = alloc(nv)
        vecs_tk = Tk()
        P.dma("sp", vecs, din["vecs"], writes=[vecs_tk])
        ident = alloc(128)
        ident_tk = Tk()
        P.dma("sp", ident, din["ident"], writes=[ident_tk])
        ones_bf = alloc(128, BF16)
        ones_tk = Tk()
        P.op("dve", MSET(ones_bf, 1.0), writes=[ones_tk])
        ident_bf = alloc(128, BF16)
        identbf_tk = Tk()
        P.op("dve", CP(ident_bf, ident), reads=[ident_tk], writes=[identbf_tk])
        base_top = top[0]

        def V(name, c0=0, n=None):
            o, w = vidx[name]
            if n is None:
                n = w - c0
            return vecs[:, o + c0:o + c0 + n]

        def xtile_ap(Xd, i):
            return Xd[:, :, i * TW_:(i + 1) * TW_].rearrange("c p t -> p c t")

        def load_w(dst, src, mc, wt):
            for m in range(mc):
                P.dma("pool", dst[:, m, :], src[m], writes=[wt[m]])

        def rmsnorm(xt, xtk, gain, hT, htk, sq, sqtk, rstd, rstd_tk, out_f32=None):
            for c in range(8):
                sl = c % 2
                P.op("act", ACTF(sq[:, sl, :], xt[:, c, :], AF.Square), reads=[xtk[c]], writes=[sqtk[sl]])
                P.op("pe", MM(psb[6][:, :], ones_bf, sq[:, sl, :], c == 0, c == 7),
                     reads=[sqtk[sl], ones_tk], writes=[pst[6]])
            P.op("act", ACTF(rstd, psb[6][:, :], AF.Sqrt, bias=V("eps_rms"), scale=1.0 / D),
                 reads=[pst[6], vecs_tk], writes=[rstd_tk])
            P.op("dve", RCP(rstd, rstd), reads=[rstd_tk], writes=[rstd_tk])
            for c in range(8):
                P.op("dve", STT(hT[:, c, :], xt[:, c, :], gain[:, c:c + 1], rstd, ALU.mult, ALU.mult),
                     reads=[xtk[c], rstd_tk, vecs_tk], writes=[htk[c]])

        def pass_input():
            xin = [alloc(D) for _ in range(2)]
            xin_tk = tks(2)
            xo = [a3(8 * 128, F32, pat="p (c t) -> p c t", c=8) for _ in range(2)]
            xo_tk = tks(2)
            for tt in range(32):
                b = tt % 2
                P.dma("sp", xin[b], din["x"][tt * 128:(tt + 1) * 128, :], writes=[xin_tk[b]])
                for c in range(8):
                    bank = c // 4
                    P.op("pe", TRN(psb[bank][:, (c % 4) * 128:(c % 4 + 1) * 128], xin[b][:, c * 128:(c + 1) * 128], ident),
                         reads=[xin_tk[b], ident_tk], writes=[pst[bank]])
                P.op("act", ACTF(xo[b][:, 0:4, :], psb[0][:, :].rearrange("p (c t) -> p c t", c=4), AF.Copy),
                     reads=[pst[0]], writes=[xo_tk[b]])
                P.op("dve", CP(xo[b][:, 4:8, :], psb[1][:, :].rearrange("p (c t) -> p c t", c=4)),
                     reads=[pst[1], xo_tk[b]], writes=[xo_tk[b]])
                P.dma("sp", X[0][:, :, tt * 128:(tt + 1) * 128].rearrange("c p t -> p c t"), xo[b], reads=[xo_tk[b]])

        def pass_ffn_half(l, fi, hf, Xn, Xr, Xo, norm_name):
            wg = a3(11 * 1024, BF16, pat="p (m f) -> p m f", m=11)
            wu = a3(11 * 1024, BF16, pat="p (m f) -> p m f", m=11)
            wd = a3(8 * 1408, BF16, pat="p (m f) -> p m f", m=8)
            wg_tk, wu_tk, wd_tk = tks(11), tks(11), tks(8)
            key = "%d_%d_%d" % (l, fi, hf)
            for m in range(11):
                P.dma("pool", wg[:, m, :], din["wg_" + key][m], writes=[wg_tk[m]])
                P.dma("pool", wu[:, m, :], din["wu_" + key][m], writes=[wu_tk[m]])
            load_w(wd, din["wd_" + key], 8, wd_tk)
            same = Xn is Xr
            xn = [a3(8 * 512, F32, pat="p (c t) -> p c t", c=8) for _ in range(2)]
            xn_tk = [tks(8) for _ in range(2)]
            if same:
                xr, xr_tk = xn, xn_tk
            else:
                xr = [a3(8 * 512, F32, pat="p (c t) -> p c t", c=8) for _ in range(2)]
                xr_tk = [tks(8) for _ in range(2)]
            hT = a3(8 * 512, BF16, pat="p (c t) -> p c t", c=8)
            htk = tks(8)
            sq = a3(2 * 512, BF16, pat="p (c t) -> p c t", c=2)
            sqtk = tks(2)
            rstd = alloc(512)
            rstd_tk = Tk()
            act_ = a3(11 * 512, BF16, pat="p (c t) -> p c t", c=11)
            atk = tks(11)
            sg = [alloc(512) for _ in range(2)]
            sgtk = tks(2)
            gain = V(norm_name)

            def load(i):
                b = i % 2
                P.dma("sp", xn[b], xtile_ap(Xn, i), writes=xn_tk[b])
                if not same:
                    P.dma("sp", xr[b], xtile_ap(Xr, i), writes=xr_tk[b])
            load(0)
            for i in range(NT):
                b = i % 2
                if i + 1 < NT:
                    load(i + 1)
                rmsnorm(xn[b], xn_tk[b], gain, hT, htk, sq, sqtk, rstd, rstd_tk)
                for j in range(11):
                    pg, pu = j % 2, 2 + j % 2
                    for c in range(8):
                        P.op("pe", MM(psb[pg][:, :], wg[:, j, c * 128:(c + 1) * 128], hT[:, c, :], c == 0, c == 7),
                             reads=[wg_tk[j], htk[c]], writes=[pst[pg]])
                    for c in range(8):
                        P.op("pe", MM(psb[pu][:, :], wu[:, j, c * 128:(c + 1) * 128], hT[:, c, :], c == 0, c == 7),
                             reads=[wu_tk[j], htk[c]], writes=[pst[pu]])
                    P.op("act", ACTF(sg[j % 2], psb[pg][:, :], AF.Silu), reads=[pst[pg]], writes=[sgtk[j % 2]])
                    P.op("dve", TT(act_[:, j, :], sg[j % 2], psb[pu][:, :], ALU.mult),
                         reads=[sgtk[j % 2], pst[pu]], writes=[atk[j]])
                for m in range(8):
                    py = 4 + m % 2
                    for j in range(11):
                        P.op("pe", MM(psb[py][:, :], wd[:, m, j * 128:(j + 1) * 128], act_[:, j, :], j == 0, j == 10),
                             reads=[wd_tk[m], atk[j]], writes=[pst[py]])
                    P.op("dve", STT(xr[b][:, m, :], psb[py][:, :], 0.5, xr[b][:, m, :], ALU.mult, ALU.add),
                         reads=[pst[py], xr_tk[b][m]], writes=[xr_tk[b][m]])
                P.dma("sp", xtile_ap(Xo, i), xr[b], reads=xr_tk[b])

        def pass_ple(l, Xc):
            wgp = a3(8 * 1024, BF16, pat="p (m f) -> p m f", m=8)
            wip = a3(8 * 256, BF16, pat="p (m f) -> p m f", m=8)
            wgp_tk, wip_tk = tks(8), tks(8)
            load_w(wgp, din["pleg_%d" % l], 8, wgp_tk)
            load_w(wip, din["plei_%d" % l], 8, wip_tk)
            xn = [a3(8 * 512, F32, pat="p (c t) -> p c t", c=8) for _ in range(2)]
            xn_tk = [tks(8) for _ in range(2)]
            pin = [a3(4 * 256, F32, pat="p (s f) -> p s f", s=4) for _ in range(2)]
            pin_tk = tks(2)
            pT = a3(2 * 512, BF16, pat="p (c t) -> p c t", c=2)
            pT_tk = tks(2)
            hT = a3(8 * 512, BF16, pat="p (c t) -> p c t", c=8)
            htk = tks(8)
            sq = a3(2 * 512, BF16, pat="p (c t) -> p c t", c=2)
            sqtk = tks(2)
            rstd = alloc(512)
            rstd_tk = Tk()
            sg = [alloc(512) for _ in range(2)]
            sgtk = tks(2)
            gain = V("ple_norm%d" % l)
            pl = din["p"][l]

            def load(i):
                b = i % 2
                P.dma("sp", xn[b], xtile_ap(Xc, i), writes=xn_tk[b])
                P.dma("sp", pin[b], pl[i * 512:(i + 1) * 512, :].rearrange("(s p) f -> p s f", p=128), writes=[pin_tk[b]])
            load(0)
            for i in range(NT):
                b = i % 2
                if i + 1 < NT:
                    load(i + 1)
                rmsnorm(xn[b], xn_tk[b], gain, hT, htk, sq, sqtk, rstd, rstd_tk)
                for kc in range(2):
                    for s in range(4):
                        P.op("pe", TRN(psb[7][:, s * 128:(s + 1) * 128], pin[b][:, s, kc * 128:(kc + 1) * 128], ident),
                             reads=[pin_tk[b], ident_tk], writes=[pst[7]])
                    P.op("act", ACTF(pT[:, kc, :], psb[7][:, :], AF.Copy), reads=[pst[7]], writes=[pT_tk[kc]])
                for m in range(8):
                    pg, pi = m % 2, 2 + m % 2
                    for c in range(8):
                        P.op("pe", MM(psb[pg][:, :], wgp[:, m, c * 128:(c + 1) * 128], hT[:, c, :], c == 0, c == 7),
                             reads=[wgp_tk[m], htk[c]], writes=[pst[pg]])
                    for c in range(2):
                        P.op("pe", MM(psb[pi][:, :], wip[:, m, c * 128:(c + 1) * 128], pT[:, c, :], c == 0, c == 1),
                             reads=[wip_tk[m], pT_tk[c]], writes=[pst[pi]])
                    P.op("act", ACTF(sg[m % 2], psb[pg][:, :], AF.Sigmoid), reads=[pst[pg]], writes=[sgtk[m % 2]])
                    P.op("dve", TT(sg[m % 2], sg[m % 2], psb[pi][:, :], ALU.mult),
                         reads=[sgtk[m % 2], pst[pi]], writes=[sgtk[m % 2]])
                    P.op("dve", TT(xn[b][:, m, :], xn[b][:, m, :], sg[m % 2], ALU.add),
                         reads=[sgtk[m % 2], xn_tk[b][m]], writes=[xn_tk[b][m]])
                P.dma("sp", xtile_ap(Xc, i), xn[b], reads=xn_tk[b])

        def pass_conv(l, jx, Xc):
            w1 = a3(16 * 1024, BF16, pat="p (m f) -> p m f", m=16)
            w2 = a3(8 * 1024, BF16, pat="p (m f) -> p m f", m=8)
            w1_tk, w2_tk = tks(16), tks(8)
            load_w(w1, din["pw1_%d" % jx], 16, w1_tk)
            load_w(w2, din["pw2_%d" % jx], 8, w2_tk)
            diag = a3(31 * 8 * 128, BF16, pat="p (j c m) -> p j c m", j=31, c=8)
            diag_tk = tks(8)
            wdw = V("w_dw_%d" % jx)
            for c in range(8):
                for j in range(31):
                    P.op("dve", TS(diag[:, j, c, :], ident_bf, wdw[:, c * 31 + j:c * 31 + j + 1], None, ALU.mult),
                         reads=[identbf_tk, vecs_tk], writes=[diag_tk[c]])
            xn = [a3(8 * 512, F32, pat="p (c t) -> p c t", c=8)] * 2
            xn_tk = [tks(8)] * 2
            hT = a3(8 * 512, BF16, pat="p (c t) -> p c t", c=8)
            htk = tks(8)
            sq = a3(2 * 512, BF16, pat="p (c t) -> p c t", c=2)
            sqtk = tks(2)
            rstd = alloc(512)
            rstd_tk = Tk()
            ub = a3(8 * 542, BF16, pat="p (c t) -> p c t", c=8)
            ub_tk = tks(8)
            yb = a3(8 * 512, F32, pat="p (c t) -> p c t", c=8)
            yb_tk = tks(8)
            ybf = a3(2 * 512, BF16, pat="p (c t) -> p c t", c=2)
            ybf_tk = tks(2)
            ysq = a3(2 * 512, BF16, pat="p (c t) -> p c t", c=2)
            ysq_tk = tks(2)
            sg = [alloc(512) for _ in range(2)]
            sgtk = tks(2)
            mu = alloc(512)
            mu_tk = Tk()
            rs = alloc(512)
            rs_tk = Tk()
            tmp = alloc(512)
            tmp_tk = Tk()
            gain = V("mix_norm%d" % l)
            b1 = V("b_pw1_%d" % jx)
            bdw = V("b_dw_%d" % jx)
            lng = V("ln_g_%d" % jx)
            lnb = V("ln_b_%d" % jx)
            b2 = V("b_pw2_%d" % jx)
            for c in range(8):
                P.op("dve", MSET(ub[:, c, 0:30], 0.0), writes=[ub_tk[c]])

            def load(i):
                b = i % 2
                P.dma("sp", xn[b], xtile_ap(Xc, i), writes=xn_tk[b])
            for i in range(NT):
                b = i % 2
                load(i)
                rmsnorm(xn[b], xn_tk[b], gain, hT, htk, sq, sqtk, rstd, rstd_tk)
                for m in range(8):
                    pa, pg = m % 2, 2 + m % 2
                    for c in range(8):
                        P.op("pe", MM(psb[pa][:, :], w1[:, m, c * 128:(c + 1) * 128], hT[:, c, :], c == 0, c == 7),
                             reads=[w1_tk[m], htk[c]], writes=[pst[pa]])
                    for c in range(8):
                        P.op("pe", MM(psb[pg][:, :], w1[:, 8 + m, c * 128:(c + 1) * 128], hT[:, c, :], c == 0, c == 7),
                             reads=[w1_tk[8 + m], htk[c]], writes=[pst[pg]])
                    P.op("act", ACTF(sg[m % 2], psb[pg][:, :], AF.Sigmoid, bias=b1[:, 8 + m:9 + m]),
                         reads=[pst[pg], vecs_tk], writes=[sgtk[m % 2]])
                    P.op("dve", STT(ub[:, m, 30:542], psb[pa][:, :], b1[:, m:m + 1], sg[m % 2], ALU.add, ALU.mult),
                         reads=[pst[pa], sgtk[m % 2], vecs_tk], writes=[ub_tk[m]])
                for m in range(8):
                    py = 4 + m % 2
                    for j in range(31):
                        P.op("pe", MM(psb[py][:, :], diag[:, j, m, :], ub[:, m, j:j + 512], j == 0, j == 30),
                             reads=[diag_tk[m], ub_tk[m]], writes=[pst[py]])
                    P.op("act", ACTF(yb[:, m, :], psb[py][:, :], AF.Identity, bias=bdw[:, m:m + 1]),
                         reads=[pst[py], vecs_tk], writes=[yb_tk[m]])
                    P.op("dve", CP(ub[:, m, 0:30], ub[:, m, 512:542]), reads=[ub_tk[m]], writes=[ub_tk[m]])
                    sl = m % 2
                    P.op("dve", CP(ybf[:, sl, :], yb[:, m, :]), reads=[yb_tk[m]], writes=[ybf_tk[sl]])
                    P.op("act", ACTF(ysq[:, sl, :], yb[:, m, :], AF.Square), reads=[yb_tk[m]], writes=[ysq_tk[sl]])
                    P.op("pe", MM(psb[6][:, :], ones_bf, ybf[:, sl, :], m == 0, m == 7),
                         reads=[ybf_tk[sl], ones_tk], writes=[pst[6]])
                    P.op("pe", MM(psb[7][:, :], ones_bf, ysq[:, sl, :], m == 0, m == 7),
                         reads=[ysq_tk[sl], ones_tk], writes=[pst[7]])
                P.op("dve", TS(mu, psb[6][:, :], 1.0 / D, None, ALU.mult), reads=[pst[6]], writes=[mu_tk])
                P.op("dve", TT(tmp, mu, mu, ALU.mult), reads=[mu_tk], writes=[tmp_tk])
                P.op("dve", STT(tmp, psb[7][:, :], 1.0 / D, tmp, ALU.mult, ALU.subtract),
                     reads=[pst[7], tmp_tk], writes=[tmp_tk])
                P.op("dve", TS(tmp, tmp, 0.0, None, ALU.max), reads=[tmp_tk], writes=[tmp_tk])
                P.op("act", ACTF(rs, tmp, AF.Sqrt, bias=V("eps_ln"), scale=1.0), reads=[tmp_tk, vecs_tk], writes=[rs_tk])
                P.op("dve", RCP(rs, rs), reads=[rs_tk], writes=[rs_tk])
                P.op("dve", STT(mu, mu, -1.0, rs, ALU.mult, ALU.mult), reads=[mu_tk, rs_tk], writes=[mu_tk])
                for m in range(8):
                    P.op("dve", TT(yb[:, m, :], yb[:, m, :], rs, ALU.mult), reads=[yb_tk[m], rs_tk], writes=[yb_tk[m]])
                    P.op("dve", TT(yb[:, m, :], yb[:, m, :], mu, ALU.add), reads=[yb_tk[m], mu_tk], writes=[yb_tk[m]])
                    P.op("act", ACTF(hT[:, m, :], yb[:, m, :], AF.Silu, bias=lnb[:, m:m + 1], scale=lng[:, m:m + 1]),
                         reads=[yb_tk[m], vecs_tk], writes=[htk[m]])
                for m in range(8):
                    po = m % 2
                    for c in range(8):
                        P.op("pe", MM(psb[po][:, :], w2[:, m, c * 128:(c + 1) * 128], hT[:, c, :], c == 0, c == 7),
                             reads=[w2_tk[m], htk[c]], writes=[pst[po]])
                    P.op("dve", STT(xn[b][:, m, :], psb[po][:, :], b2[:, m:m + 1], xn[b][:, m, :], ALU.add, ALU.add),
                         reads=[pst[po], xn_tk[b][m], vecs_tk], writes=[xn_tk[b][m]])
                P.dma("sp", xtile_ap(Xc, i), xn[b], reads=xn_tk[b])

        def pass_nsa_tables():
            for h in range(16):
                src = bass.AP(tensor=din["gvec"].tensor, offset=h * NSA_M, ap=[[0, 128], [1, NSA_M]])
                dst = bass.AP(tensor=grep_d.tensor, offset=h * 128 * NSA_M, ap=[[NSA_M, 128], [1, NSA_M]])
                P.dma("sp", dst, src)
                src = bass.AP(tensor=din["gwvec"].tensor, offset=h * NSAW_M, ap=[[0, 128], [1, NSAW_M]])
                dst = bass.AP(tensor=gwrep_d.tensor, offset=h * 128 * NSAW_M, ap=[[NSAW_M, 128], [1, NSAW_M]])
                P.dma("sp", dst, src)

        def pass_nsa_proj(l, jx, Xc):
            wf = a3(25 * 1024, BF16, pat="p (m f) -> p m f", m=25)
            wf_tk = tks(25)
            load_w(wf, din["nsa_wf_%d" % jx], 25, wf_tk)
            wt = a3(8 * 560, BF16, pat="p (c f) -> p c f", c=8)
            wt_tk = Tk()
            P.dma("pool", wt, din["nsa_wt_%d" % jx], writes=[wt_tk])
            xn = [a3(8 * 512, F32, pat="p (c t) -> p c t", c=8) for _ in range(2)]
            xn_tk = [tks(8) for _ in range(2)]
            hT = a3(8 * 512, BF16, pat="p (c t) -> p c t", c=8)
            htk = tks(8)
            sq = a3(2 * 512, BF16, pat="p (c t) -> p c t", c=2)
            sqtk = tks(2)
            rstd = alloc(512)
            rstd_tk = Tk()
            stg = [a3(24 * 512, BF16, pat="p (m t) -> p m t", m=24) for _ in range(2)]
            stg_tk = tks(2)
            vst = [a3(4 * 8 * 65, BF16, pat="p (s k e) -> p s k e", s=4, k=8) for _ in range(2)]
            vst_tk = tks(2)
            gst = [alloc(512) for _ in range(2)]
            gst_tk = tks(2)
            gain = V("mix_norm%d" % l)
            for b in range(2):
                P.op("dve", MSET(vst[b], 1.0), writes=[vst_tk[b]])

            def load(i):
                b = i % 2
                P.dma("sp", xn[b], xtile_ap(Xc, i), writes=xn_tk[b])
            load(0)
            for i in range(NT):
                b = i % 2
                if i + 1 < NT:
                    load(i + 1)
                rmsnorm(xn[b], xn_tk[b], gain, hT, htk, sq, sqtk, rstd, rstd_tk)
                for m in range(25):
                    pb = m % 4
                    for c in range(8):
                        P.op("pe", MM(psb[pb][:, :], wf[:, m, c * 128:(c + 1) * 128], hT[:, c, :], c == 0, c == 7),
                             reads=[wf_tk[m], htk[c]], writes=[pst[pb]])
                    if m == 24:
                        P.op("act", ACTF(gst[b], psb[pb][:, :], AF.Sigmoid), reads=[pst[pb]], writes=[gst_tk[b]])
                        P.dma("sp", gT_d[:, i * 512:(i + 1) * 512], gst[b][0:48, :], reads=[gst_tk[b]])
                    elif m % 2 == 0:
                        P.op("act", ACTF(stg[b][:, m, :], psb[pb][:, :], AF.Copy), reads=[pst[pb]], writes=[stg_tk[b]])
                    else:
                        P.op("dve", CP(stg[b][:, m, :], psb[pb][:, :]), reads=[pst[pb]], writes=[stg_tk[b]])
                P.dma("sp", qkT_d[:, :, i * 512:(i + 1) * 512].rearrange("m p t -> p m t"), stg[b], reads=[stg_tk[b]])
                for s in range(4):
                    pv = 4 + s % 2
                    for c in range(8):
                        P.op("pe", MM(psb[pv][:, :], hT[:, c, s * 128:(s + 1) * 128], wt[:, c, 0:512], c == 0, c == 7),
                             reads=[wt_tk, htk[c]], writes=[pst[pv]])
                    P.op("dve", CP(vst[b][:, s, :, 0:64], psb[pv][:, :].rearrange("p (k e) -> p k e", k=8)),
                         reads=[pst[pv]], writes=[vst_tk[b]])
                for s in range(4):
                    P.dma("sp", vtok_d[:, :, i * 4 + s, :].rearrange("k p e -> p k e"), vst[b][:, s, :, :], reads=[vst_tk[b]])

        def pass_nsa_group(jx, g):
            ksx = alloc(S, BF16)
            ksx_tk, ex_tk = Tk(), Tk()
            P.dma("sp", ksx[0:64], qkT_d[8 + 4 * g + 2, 0:64, :], writes=[ksx_tk])
            P.dma("pool", ksx[64:128], din["expand"].rearrange("n k m -> n (k m)"), writes=[ex_tk])
            kw = alloc(S, BF16)
            kw_tk = Tk()
            P.dma("sp", kw[0:64], qkT_d[8 + 4 * g + 3, 0:64, :], writes=[kw_tk])
            qx = [alloc(S, BF16) for _ in range(2)]
            qx_tk = tks(2)
            nmq_tk = [tks(8) for _ in range(2)]
            vs = a3(32 * 65, BF16, pat="p (s e) -> p s e", s=32)
            vw = a3(32 * 65, BF16, pat="p (s e) -> p s e", s=32)
            vs_tk, vw_tk = Tk(), Tk()
            P.dma("sp", vs, vtok_d[g], writes=[vs_tk])
            P.dma("sp", vw, vtok_d[4 + g], writes=[vw_tk])
            kcT = alloc(256, BF16)
            kcT_tk = Tk()
            vca = a3(2 * 66, BF16, pat="p (c e) -> p c e", c=2)
            vca_tk = Tk()
            ovl1 = a3(2 * 66, BF16, pat="p (c e) -> p c e", c=2)
            ovl1_tk = Tk()
            ovl = a3(2 * 64, F32, pat="p (c e) -> p c e", c=2)
            ovl_tk = Tk()
            P.dma("sp", ovl, din["overlap"], writes=[ovl_tk])
            imp = a3(32 * 64, F32, pat="p (s f) -> p s f", s=32)
            imp_tk = tks(32)
            NSL = 6
            SB = [0, 1, 2, 3, 4, 7]
            Ef = [alloc(512) for _ in range(NSL)]
            Ef_tk = tks(NSL)
            Eb = [alloc(512, BF16) for _ in range(NSL)]
            Eb_tk = tks(NSL)
            sm = alloc(8)
            sm_tk = tks(8)
            grp_top = top[0]

            kc = alloc(S, BF16)
            vc = alloc(S, BF16)
            kc_tk, vc_tk = Tk(), Tk()
            P.dma("sp", kc[0:64], qkT_d[8 + 4 * g + 0, 0:64, :], writes=[kc_tk])
            P.dma("sp", vc[0:64], qkT_d[8 + 4 * g + 1, 0:64, :], writes=[vc_tk])
            wk1 = a3(32 * 256, BF16, pat="p (l j) -> p l j", l=32)
            wv1 = a3(32 * 256, BF16, pat="p (l j) -> p l j", l=32)
            wk1_tk, wv1_tk = Tk(), Tk()
            P.dma("pool", wk1[0:64], din["wk1_%d" % jx], writes=[wk1_tk])
            P.dma("pool", wv1[0:64], din["wv1_%d" % jx], writes=[wv1_tk])
            w2k = a3(2 * 128, BF16, pat="p (c m) -> p c m", c=2)
            w2v = a3(2 * 64, BF16, pat="p (c m) -> p c m", c=2)
            w2k_tk, w2v_tk = Tk(), Tk()
            P.dma("pool", w2k, din["w2k_%d" % jx], writes=[w2k_tk])
            P.dma("pool", w2v, din["w2v_%d" % jx], writes=[w2v_tk])
            posk = alloc(32, BF16)
            posv = alloc(32, BF16)
            posk_tk, posv_tk = Tk(), Tk()
            P.dma("pool", posk[0:64], din["posk_%d" % jx], writes=[posk_tk])
            P.dma("pool", posv[0:64], din["posv_%d" % jx], writes=[posv_tk])
            cb = alloc(4)
            cb_tk = Tk()
            xg = a3(2 * 256, F32, pat="p (c t) -> p c t", c=2)
            xg_tk = tks(2)
            t1 = a3(2 * 256, F32, pat="p (c t) -> p c t", c=2)
            t1_tk = tks(2)
            gl = a3(2 * 256, BF16, pat="p (c t) -> p c t", c=2)
            gl_tk = tks(2)
            P.op("dve", MSET(kcT, 0.0), writes=[kcT_tk])
            P.op("dve", MSET(vca, 0.0), writes=[vca_tk])
            P.op("dve", MSET(vca[:, :, 64:65], 1.0), writes=[vca_tk])
            P.op("dve", MSET(ovl1[:, :, 64:65], 1.0), writes=[ovl1_tk])
            P.op("dve", CP(ovl1[:, :, 0:64], ovl), reads=[ovl_tk], writes=[ovl1_tk])
            for which, (src, src_tk, w1s, w1_tk, pos, pos_tk) in enumerate(
                    ((kc, kc_tk, wk1, wk1_tk, posk, posk_tk), (vc, vc_tk, wv1, wv1_tk, posv, posv_tk))):
                for jc in range(2):
                    for l_ in range(32):
                        P.op("pe", MM(psb[4][:, jc:jc + 1], w1s[0:64, l_, jc * 128:(jc + 1) * 128], pos[0:64, l_:l_ + 1],
                                      l_ == 0, l_ == 31), reads=[w1_tk, pos_tk], writes=[pst[4]])
                P.op("dve", CP(cb[:, 0:2], psb[4][:, 0:2]), reads=[pst[4]], writes=[cb_tk])
                for jc in range(2):
                    pb = 5 + jc
                    for l_ in range(32):
                        P.op("pe", MM(psb[pb][:, 0:255], w1s[0:64, l_, jc * 128:(jc + 1) * 128],
                                      src[0:64, l_:l_ + 16 * 254 + 1:16], l_ == 0, l_ == 31),
                             reads=[w1_tk, src_tk], writes=[pst[pb]])
                    P.op("act", ACTF(xg[:, jc, 0:255], psb[pb][:, 0:255], AF.Identity, bias=cb[:, jc:jc + 1]),
                         reads=[pst[pb], cb_tk], writes=[xg_tk[jc]])
                    P.op("dve", TT(t1[:, jc, 0:255], xg[:, jc, 0:255], xg[:, jc, 0:255], ALU.mult),
                         reads=[xg_tk[jc]], writes=[t1_tk[jc]])
                    P.op("dve", TS(t1[:, jc, 0:255], t1[:, jc, 0:255], 0.044715, 1.0, ALU.mult, ALU.add),
                         reads=[t1_tk[jc]], writes=[t1_tk[jc]])
                    P.op("dve", TT(t1[:, jc, 0:255], t1[:, jc, 0:255], xg[:, jc, 0:255], ALU.mult),
                         reads=[t1_tk[jc], xg_tk[jc]], writes=[t1_tk[jc]])
                    P.op("act", ACTF(t1[:, jc, 0:255], t1[:, jc, 0:255], AF.Sigmoid, scale=1.5957691),
                         reads=[t1_tk[jc]], writes=[t1_tk[jc]])
                    P.op("dve", TT(gl[:, jc, 0:255], t1[:, jc, 0:255], xg[:, jc, 0:255], ALU.mult),
                         reads=[t1_tk[jc], xg_tk[jc]], writes=[gl_tk[jc]])
                if which == 0:
                    for jc in range(2):
                        P.op("pe", MM(psb[7][:, 0:255], w2k[:, jc, :], gl[:, jc, 0:255], jc == 0, jc == 1),
                             reads=[w2k_tk, gl_tk[jc]], writes=[pst[7]])
                    P.op("act", ACTF(kcT[:, 0:255], psb[7][:, 0:255], AF.Copy), reads=[pst[7]], writes=[kcT_tk])
                else:
                    for ct in range(2):
                        ncr = 128 if ct == 0 else 127
                        for jc in range(2):
                            P.op("pe", MM(psb[7][0:ncr, 0:64], gl[:, jc, ct * 128:ct * 128 + ncr], w2v[:, jc, :], jc == 0, jc == 1),
                                 reads=[w2v_tk, gl_tk[jc]], writes=[pst[7]])
                        P.op("act", ACTF(vca[0:ncr, ct, 0:64], psb[7][0:ncr, 0:64], AF.Copy), reads=[pst[7]], writes=[vca_tk])
            P.barrier()
            top[0] = grp_top
            if DBG < 3:
                return

            tabc = alloc(6144)
            tabc_tk = tks(3)
            tabs2 = [alloc(2560) for _ in range(2)]
            tabs_tk2 = tks(2)
            tabw2 = [alloc(1408) for _ in range(2)]
            tabw_tk2 = tks(2)
            b31c2 = [alloc(1) for _ in range(2)]
            b31_tk2 = tks(2)
            vmk = a3(32 * 64, F32, pat="p (s f) -> p s f", s=32)
            amk = a3(32 * 64, F32, pat="p (s f) -> p s f", s=32)
            vmk_tk, amk_tk = Tk(), Tk()
            P.dma("sp", vmk, din["vmask"], writes=[vmk_tk])
            P.dma("sp", amk, din["amask"], writes=[amk_tk])
            sc = alloc(64)
            sc2 = alloc(64)
            m8a = alloc(8)
            m8b = alloc(8)
            nm = alloc(128)
            sc_tk, sc2_tk, m8a_tk, m8b_tk, nm_tk = Tk(), Tk(), Tk(), Tk(), Tk()
            gq = [alloc(3 * 512) for _ in range(2)]
            gq_tk = tks(2)
            ostg = [alloc(512) for _ in range(3)]
            ostg_tk = tks(3)
            lrow = [alloc(512) for _ in range(3)]
            lrow_tk = tks(3)
            facb = [alloc(512) for _ in range(3)]
            facb_tk = tks(3)
            Oac = [alloc(512) for _ in range(2)]
            Oac_tk = tks(2)
            ost = [alloc(512, BF16) for _ in range(2)]
            ost_tk = tks(2)
            step = [0]
            zcnt = [0]

            def load_tabc(h):
                src = bass.AP(tensor=grep_d.tensor, offset=h * 128 * NSA_M + (NSA_OFF - 2048),
                              ap=[[NSA_M - 16, 128], [1, 6144]])
                P.dma("sp", tabc, src, writes=tabc_tk)
                for pz in range(3):
                    P.op("act", ACTF(tabc[:, pz * 2048:(pz + 1) * 2048], tabc[:, pz * 2048:(pz + 1) * 2048], AF.Exp),
                         reads=[tabc_tk[pz]], writes=[tabc_tk[pz]])

            def fin_a(t):
                zb = zcnt[0] % 3
                zcnt[0] += 1
                t["zb"] = zb
                P.op("dve", CP(ostg[zb][0:65, :], psb[t["acc"]][0:65, :]), reads=[pst[t["acc"]]], writes=[ostg_tk[zb]])

            def fin_b(t):
                j, QB, zb = t["j"], t["QB"], t["zb"]
                gb_ = t["gb"]
                P.op("act", ACTF(lrow[zb][64:65, :], ostg[zb][64:65, :], AF.Ln, bias=V("eps_z")[64:65]),
                     reads=[ostg_tk[zb], vecs_tk], writes=[lrow_tk[zb]])
                P.op("act", ACTF(lrow[zb][64:65, :], lrow[zb][64:65, :], AF.Exp, scale=-1.0), reads=[lrow_tk[zb]], writes=[lrow_tk[zb]])
                P.op("dve", TT(lrow[zb][64:65, :], lrow[zb][64:65, :], gq[gb_][64:65, j * 512:(j + 1) * 512], ALU.mult),
                     reads=[lrow_tk[zb], gq_tk[gb_]], writes=[lrow_tk[zb]])
                P.dma("sp", fr_d[zb:zb + 1, :], lrow[zb][64:65, :], reads=[lrow_tk[zb]], writes=[frd_tk[zb]])
                P.dma("sp", facb[zb][0:64, :], bass.AP(tensor=fr_d.tensor, offset=zb * 512, ap=[[0, 64], [1, 512]]),
                      reads=[frd_tk[zb]], writes=[facb_tk[zb]])

            def fin_c(t):
                j, QB, zb = t["j"], t["QB"], t["zb"]
                qs = slice(QB * 512, (QB + 1) * 512)
                ob = t["ob"]
                r_, pr_, hf_ = t["r"], t["r"] // 2, t["r"] % 2
                if j == 0:
                    P.op("dve", TT(Oac[ob][0:64], ostg[zb][0:64, :], facb[zb][0:64, :], ALU.mult),
                         reads=[ostg_tk[zb], facb_tk[zb]], writes=[Oac_tk[ob]])
                    P.dma("sp", oc_d[r_, :, qs], Oac[ob][0:64], reads=[Oac_tk[ob]])
                else:
                    P.op("dve", TT(ostg[zb][0:64, :], ostg[zb][0:64, :], facb[zb][0:64, :], ALU.mult),
                         reads=[ostg_tk[zb], facb_tk[zb]], writes=[ostg_tk[zb]])
                    P.op("pool", TT(Oac[ob][0:64], Oac[ob][0:64], ostg[zb][0:64, :], ALU.add),
                         reads=[ostg_tk[zb], Oac_tk[ob]], writes=[Oac_tk[ob]])
                if j == 2:
                    P.op("dve", CP(ost[ob][0:64], Oac[ob][0:64]), reads=[Oac_tk[ob]], writes=[ost_tk[ob]])
                    P.dma("sp", oT_d[2 * g + pr_, hf_ * 64:(hf_ + 1) * 64, qs], ost[ob][0:64], reads=[ost_tk[ob]])

            def load_gq(h, QB, gb_):
                P.dma("sp", gq[gb_][64:65, :].rearrange("p (j t) -> p j t", j=3),
                      gT_d[h * 3:h * 3 + 3, QB * 512:(QB + 1) * 512].rearrange("(o j) t -> o j t", o=1),
                      writes=[gq_tk[gb_]])

            gcnt = [0]
            for r in range(4):
                h = 4 * g + r
                pr, hf = r // 2, r % 2
                b = r % 2
                P.dma("sp", qx[b][0:64], qkT_d[2 * g + pr, hf * 64:(hf + 1) * 64, :], writes=[qx_tk[b]])
                load_tabc(h)
                prev_t = None
                its = []
                for QB in range(8):
                    its.append(dict(QB=QB))

                def front(it):
                    QB = it["QB"]
                    qs = slice(QB * 512, (QB + 1) * 512)
                    cts = [0] if QB < 4 else [0, 1]
                    gb_ = gcnt[0] % 2
                    gcnt[0] += 1
                    load_gq(h, QB, gb_)
                    slots = []
                    for ct in cts:
                        sl = step[0] % 4
                        step[0] += 1
                        P.op("pe", MM(psb[sl][:, :], kcT[0:64, ct * 128:(ct + 1) * 128], qx[b][0:64, qs], True, True),
                             reads=[kcT_tk, qx_tk[b]], writes=[pst[sl]])
                        P.op("act", ACTF(Ef[sl], psb[sl][:, :], AF.Exp, scale=0.125), reads=[pst[sl]], writes=[Ef_tk[sl]])
                        sj = QB * 512 + 2017 - 2048 * ct
                        P.op("dve", TT(Eb[sl], Ef[sl], tabc[:, sj:sj + 512], ALU.mult),
                             reads=[Ef_tk[sl]] + tabc_tk, writes=[Eb_tk[sl]])
                        slots.append((ct, sl))
                    it["slots"] = slots
                    it["cts"] = cts
                    it["t"] = dict(j=0, QB=QB, acc=5, gb=gb_, ob=QB % 2, r=r)

                def back(it):
                    QB, slots, cts, t = it["QB"], it["slots"], it["cts"], it["t"]
                    for (ct, sl) in slots:
                        P.op("pe", MM(psb[5][0:65, :], vca[:, ct, 0:65], Eb[sl], ct == cts[0], ct == cts[-1]),
                             reads=[Eb_tk[sl], vca_tk], writes=[pst[5]])
                    fin_a(t)
                    for s in range(4):
                        for (ct, sl) in slots:
                            P.op("pe", MM(psb[4][:, s * 65:(s + 1) * 65], Eb[sl][:, s * 128:(s + 1) * 128], ovl1[:, ct, 0:65],
                                          ct == cts[0], ct == cts[-1]), reads=[Eb_tk[sl], ovl1_tk], writes=[pst[4]])
                    fin_b(t)
                    for s in range(4):
                        qt = QB * 4 + s
                        k = qt % 8
                        z = sm[:, k:k + 1]
                        P.op("dve", TS(z, psb[4][:, s * 65 + 64:s * 65 + 65], 1e-30, None, ALU.max), reads=[pst[4]], writes=[sm_tk[k]])
                        P.op("dve", RCP(z, z), reads=[sm_tk[k]], writes=[sm_tk[k]])
                        if r == 0:
                            P.op("dve", TS(imp[:, qt, :], psb[4][:, s * 65:s * 65 + 64], z, None, ALU.mult),
                                 reads=[pst[4], sm_tk[k]], writes=[imp_tk[qt]])
                        else:
                            P.op("dve", STT(imp[:, qt, :], psb[4][:, s * 65:s * 65 + 64], z, imp[:, qt, :], ALU.mult, ALU.add),
                                 reads=[pst[4], sm_tk[k], imp_tk[qt]], writes=[imp_tk[qt]])

                for ii in range(len(its) + 1):
                    if ii < len(its):
                        front(its[ii])
                    if ii >= 1:
                        back(its[ii - 1])
                        if prev_t is not None:
                            fin_c(prev_t)
                        prev_t = its[ii - 1]["t"]
                fin_c(prev_t)
            if DBG < 4:
                return
            for qt in range(32):
                P.op("dve", TT(sc, imp[:, qt, :], vmk[:, qt, :], ALU.mult), reads=[imp_tk[qt], vmk_tk], writes=[sc_tk])
                P.op("dve", TT(sc, sc, amk[:, qt, :], ALU.add), reads=[sc_tk, amk_tk], writes=[sc_tk])
                P.op("dve", lambda e: e.max(out=m8a, in_=sc), reads=[sc_tk], writes=[m8a_tk])
                P.op("dve", lambda e: e.match_replace(out=sc2, in_to_replace=m8a, in_values=sc, imm_value=-1e30),
                     reads=[sc_tk, m8a_tk], writes=[sc2_tk])
                P.op("dve", lambda e: e.max(out=m8b, in_=sc2), reads=[sc2_tk], writes=[m8b_tk])
                P.op("dve", TS(nm[:, 0:64], sc, m8b[:, 7:8], -30000.0, ALU.is_lt, ALU.mult), reads=[sc_tk, m8b_tk], writes=[nm_tk])
                P.op("dve", TS(nm[:, 64:128], sc, m8b[:, 7:8], -30000.0, ALU.is_lt, ALU.mult), reads=[sc_tk, m8b_tk, nm_tk], writes=[nm_tk])
                P.op("pe", TRN(psb[7][:, 0:128], nm, ident), reads=[nm_tk, ident_tk], writes=[pst[7]])
                for b in range(2):
                    P.op("act", ACTF(qx[b][64:128, qt * 128:(qt + 1) * 128], psb[7][64:128, 0:128], AF.Copy),
                         reads=[pst[7]], writes=[nmq_tk[b][qt // 4]])
            if DBG < 5:
                return

            def load_tabs(r):
                h = 4 * g + r
                tb = r % 2
                pr, hf = r // 2, r % 2
                P.dma("sp", qx[tb][0:64], qkT_d[2 * g + pr, hf * 64:(hf + 1) * 64, :], writes=[qx_tk[tb]])
                src = bass.AP(tensor=grep_d.tensor, offset=h * 128 * NSA_M + (NSA_OFF - 384),
                              ap=[[NSA_M - 1, 128], [1, 2560]])
                P.dma("sp", tabs2[tb], src, writes=[tabs_tk2[tb]])
                src = bass.AP(tensor=gwrep_d.tensor, offset=h * 128 * NSAW_M + (NSAW_OFF - 384),
                              ap=[[NSAW_M - 1, 128], [1, 1408]])
                P.dma("sp", tabw2[tb], src, writes=[tabw_tk2[tb]])

            def exp_tabs(r):
                tb = r % 2
                P.op("act", ACTF(b31c2[tb], tabs2[tb][:, 2559:2560], AF.Copy), reads=[tabs_tk2[tb]], writes=[b31_tk2[tb]])
                P.op("act", ACTF(tabs2[tb], tabs2[tb], AF.Exp), reads=[tabs_tk2[tb], b31_tk2[tb]], writes=[tabs_tk2[tb]])
                P.op("act", ACTF(tabw2[tb], tabw2[tb], AF.Exp), reads=[tabw_tk2[tb]], writes=[tabw_tk2[tb]])

            load_tabs(0)
            exp_tabs(0)
            for r in range(4):
                h = 4 * g + r
                pr, hf = r // 2, r % 2
                b = r % 2
                tabs, tabs_tk, tabw, tabw_tk, b31c, b31_tk = tabs2[b], tabs_tk2[b], tabw2[b], tabw_tk2[b], b31c2[b], b31_tk2[b]
                if r + 1 < 4:
                    load_tabs(r + 1)
                tiles = []
                for QB in range(8):
                    qs = slice(QB * 512, (QB + 1) * 512)
                    q0 = QB * 512
                    for kt in range(4 * QB + 4):
                        off = min(QB * 512 - kt * 128, 1664) + 384
                        c0, c1 = 128 * max(0, kt - 4 * QB), 512
                        tiles.append(dict(j=1, QB=QB, first=kt == 0, last=kt == 4 * QB + 3, r=r, ob=QB % 2, c0=c0, c1=c1,
                                          lhsT=ksx[:, kt * 128:(kt + 1) * 128], lt=[ksx_tk, ex_tk], rhs=qx[b][:, q0 + c0:q0 + c1],
                                          rt=[qx_tk[b], nmq_tk[b][QB]],
                                          tab=tabs[:, off + c0:off + c1], tt=[tabs_tk], v=vs[:, kt, :], vt=[vs_tk], acc=5,
                                          far=(QB * 512 - kt * 128 >= 1664)))
                    k0 = max(0, 4 * QB - 4)
                    for kt in range(k0, 4 * QB + 4):
                        off = QB * 512 - kt * 128 + 384
                        if kt == k0:
                            c0, c1 = 0, 512
                        else:
                            c0 = 128 * max(0, kt - 4 * QB)
                            c1 = 128 * (min(3, kt - 4 * QB + 4) + 1)
                        tiles.append(dict(j=2, QB=QB, first=kt == k0, last=kt == 4 * QB + 3, r=r, ob=QB % 2, c0=c0, c1=c1,
                                          lhsT=kw[0:64, kt * 128:(kt + 1) * 128], lt=[kw_tk], rhs=qx[b][0:64, q0 + c0:q0 + c1], rt=[qx_tk[b]],
                                          tab=tabw[:, off + c0:off + c1], tt=[tabw_tk], v=vw[:, kt, :], vt=[vw_tk], acc=6, far=False))
                LOOK = 5
                pend = []
                n = len(tiles)
                lastQB = -1
                cur_gb = 0
                for i in range(n + LOOK):
                    if i == (n * 3) // 5 and r + 1 < 4:
                        exp_tabs(r + 1)
                    if i < n:
                        t = tiles[i]
                        if t["QB"] != lastQB:
                            lastQB = t["QB"]
                            cur_gb = gcnt[0] % 2
                            gcnt[0] += 1
                            load_gq(h, lastQB, cur_gb)
                            ob = lastQB % 2
                            P.dma("sp", Oac[ob][0:64], oc_d[r, :, lastQB * 512:(lastQB + 1) * 512], writes=[Oac_tk[ob]])
                        t["gb"] = cur_gb
                        sl = step[0] % NSL
                        step[0] += 1
                        t["sl"] = sl
                        P.op("pe", MM(psb[SB[sl]][:, t["c0"]:t["c1"]], t["lhsT"], t["rhs"], True, True), reads=t["lt"] + t["rt"], writes=[pst[SB[sl]]])
                    jx_ = i - LOOK
                    if jx_ >= 0:
                        t = tiles[jx_]
                        sl = t["sl"]
                        c0, c1 = t["c0"], t["c1"]
                        if t["far"]:
                            P.op("act", ACTF(Eb[sl][:, c0:c1], psb[SB[sl]][:, c0:c1], AF.Exp, scale=0.125, bias=b31c), reads=[pst[SB[sl]], b31_tk], writes=[Eb_tk[sl]])
                        else:
                            P.op("act", ACTF(Ef[sl][:, c0:c1], psb[SB[sl]][:, c0:c1], AF.Exp, scale=0.125), reads=[pst[SB[sl]]], writes=[Ef_tk[sl]])
                            P.op("dve", TT(Eb[sl][:, c0:c1], Ef[sl][:, c0:c1], t["tab"], ALU.mult), reads=[Ef_tk[sl]] + list(t["tt"]), writes=[Eb_tk[sl]])
                        P.op("pe", MM(psb[t["acc"]][0:65, c0:c1], t["v"], Eb[sl][:, c0:c1], t["first"], t["last"]),
                             reads=[Eb_tk[sl]] + t["vt"], writes=[pst[t["acc"]]])
                        if t["last"]:
                            fin_a(t)
                            pend.append([i + 2, 0, t])
                    k = 0
                    while k < len(pend):
                        due, stage, t = pend[k]
                        if due <= i:
                            if stage == 0:
                                fin_b(t)
                                pend[k] = [i + 5, 1, t]
                                k += 1
                            else:
                                fin_c(t)
                                pend.pop(k)
                        else:
                            k += 1
                for due, stage, t in sorted(pend, key=lambda x: x[0]):
                    if stage == 0:
                        fin_b(t)
                    fin_c(t)

        def pass_nsa_out(jx, Xc):
            wo = a3(8 * 1024, BF16, pat="p (m f) -> p m f", m=8)
            wo_tk = tks(8)
            load_w(wo, din["nsa_wo_%d" % jx], 8, wo_tk)
            xn = [a3(8 * 512, F32, pat="p (c t) -> p c t", c=8) for _ in range(2)]
            xn_tk = [tks(8) for _ in range(2)]
            ot = [a3(8 * 512, BF16, pat="p (c t) -> p c t", c=8) for _ in range(2)]
            ot_tk = tks(2)

            def load(i):
                b = i % 2
                P.dma("sp", xn[b], xtile_ap(Xc, i), writes=xn_tk[b])
                P.dma("sp", ot[b], oT_d[:, :, i * 512:(i + 1) * 512].rearrange("m p t -> p m t"), writes=[ot_tk[b]])
            load(0)
            for i in range(NT):
                b = i % 2
                if i + 1 < NT:
                    load(i + 1)
                for m in range(8):
                    po = m % 2
                    for c in range(8):
                        P.op("pe", MM(psb[po][:, :], wo[:, m, c * 128:(c + 1) * 128], ot[b][:, c, :], c == 0, c == 7),
                             reads=[wo_tk[m], ot_tk[b]], writes=[pst[po]])
                    P.op("dve", TT(xn[b][:, m, :], xn[b][:, m, :], psb[po][:, :], ALU.add),
                         reads=[pst[po], xn_tk[b][m]], writes=[xn_tk[b][m]])
                P.dma("sp", xtile_ap(Xc, i), xn[b], reads=xn_tk[b])

        def pass_final(Xc, do_norm):
            xn = [a3(8 * 512, F32, pat="p (c t) -> p c t", c=8) for _ in range(2)]
            xn_tk = [tks(8) for _ in range(2)]
            sq = a3(2 * 512, BF16, pat="p (c t) -> p c t", c=2)
            sqtk = tks(2)
            rstd = alloc(512)
            rstd_tk = Tk()
            yo = [alloc(D) for _ in range(2)]
            yo_tk = tks(2)
            gain = V("final_norm")

            def load(i):
                b = i % 2
                P.dma("sp", xn[b], xtile_ap(Xc, i), writes=xn_tk[b])
            load(0)
            for i in range(NT):
                b = i % 2
                if i + 1 < NT:
                    load(i + 1)
                xt, xtk = xn[b], xn_tk[b]
                if do_norm:
                    for c in range(8):
                        sl = c % 2
                        P.op("act", ACTF(sq[:, sl, :], xt[:, c, :], AF.Square), reads=[xtk[c]], writes=[sqtk[sl]])
                        P.op("pe", MM(psb[6][:, :], ones_bf, sq[:, sl, :], c == 0, c == 7),
                             reads=[sqtk[sl], ones_tk], writes=[pst[6]])
                    P.op("act", ACTF(rstd, psb[6][:, :], AF.Sqrt, bias=V("eps_rms"), scale=1.0 / D),
                         reads=[pst[6], vecs_tk], writes=[rstd_tk])
                    P.op("dve", RCP(rstd, rstd), reads=[rstd_tk], writes=[rstd_tk])
                    for c in range(8):
                        P.op("dve", STT(xt[:, c, :], xt[:, c, :], gain[:, c:c + 1], rstd, ALU.mult, ALU.mult),
                             reads=[xtk[c], rstd_tk, vecs_tk], writes=[xtk[c]])
                for s in range(4):
                    yb_ = (i * 4 + s) % 2
                    for c in range(8):
                        bank = c // 4
                        P.op("pe", TRN(psb[bank][:, (c % 4) * 128:(c % 4 + 1) * 128], xt[:, c, s * 128:(s + 1) * 128], ident),
                             reads=[xtk[c], ident_tk], writes=[pst[bank]])
                    P.op("act", ACTF(yo[yb_][:, 0:512], psb[0][:, :], AF.Copy), reads=[pst[0]], writes=[yo_tk[yb_]])
                    P.op("dve", CP(yo[yb_][:, 512:1024], psb[1][:, :]), reads=[pst[1], yo_tk[yb_]], writes=[yo_tk[yb_]])
                    t0 = i * 512 + s * 128
                    P.dma("sp", out_d[t0:t0 + 128, :], yo[yb_], reads=[yo_tk[yb_]])

        def phase(fn, *a):
            top[0] = base_top
            fn(*a)
            P.barrier()

        phase(pass_input)
        has_nsa = any((l % 2 == 1) for l in layers) and "mix" in stages
        if has_nsa:
            phase(pass_nsa_tables)
        cur = 0
        for l in layers:
            jx = l // 2
            if "ffn1" in stages:
                a_, b_, c_ = cur, (cur + 1) % 3, (cur + 2) % 3
                phase(pass_ffn_half, l, 0, 0, X[a_], X[a_], X[b_], "ffn1_norm%d" % l)
                phase(pass_ffn_half, l, 0, 1, X[a_], X[b_], X[c_], "ffn1_norm%d" % l)
                cur = c_
            if "mix" in stages:
                if l % 2 == 0:
                    phase(pass_conv, l, jx, X[cur])
                else:
                    phase(pass_nsa_proj, l, jx, X[cur])
                    if DBG >= 2:
                        for g in range(4 if DBG >= 9 else 1):
                            phase(pass_nsa_group, jx, g)
                    if DBG >= 9:
                        phase(pass_nsa_out, jx, X[cur])
            if "ffn2" in stages:
                a_, b_, c_ = cur, (cur + 1) % 3, (cur + 2) % 3
                phase(pass_ffn_half, l, 1, 0, X[a_], X[a_], X[b_], "ffn2_norm%d" % l)
                phase(pass_ffn_half, l, 1, 1, X[a_], X[b_], X[c_], "ffn2_norm%d" % l)
                cur = c_
            if "ple" in stages:
                phase(pass_ple, l, X[cur])
        phase(pass_final, X[cur], do_final)
        for e_ in ENGS:
            P.op(e_, lambda e: e.nop())
        P.emit()
    return nc


import os
DBG = int(os.environ.get("NSA_DBG", "9"))


def run(inputs, n_cores=8, **bkw):
    shared, vidx = host_prepare(inputs)
    x = np.asarray(inputs["x"], np.float32)
    p = np.asarray(inputs["p"], np.float32)
    shapes = {k: v.shape for k, v in shared.items()}
    shapes["x"] = (S, D)
    shapes["p"] = (4, S, 256)
    nc = bass.Bass("TRN2", target_bir_lowering=False)
    build(nc, shapes, vidx, **bkw)
    in_maps = []
    for b in range(n_cores):
        m = dict(shared)
        m["x"] = np.ascontiguousarray(x[b])
        m["p"] = np.ascontiguousarray(p[:, b])
        in_maps.append(m)
    res = run_bass_kernel_spmd(nc, in_maps, core_ids=list(range(n_cores)))
    return np.stack([np.asarray(r["out"], np.float32) for r in res.results], axis=0)


def kernel(**inputs):
    return run(inputs, n_cores=8)
```

```python
import contextlib
import os
import math
import numpy as np
import concourse.bass as bass
import concourse.mybir as mybir
from concourse.bass_utils import run_bass_kernel_spmd

F32 = mybir.dt.float32
BF16 = mybir.dt.bfloat16
AF = mybir.ActivationFunctionType
ALU = mybir.AluOpType

S = 4096
D = 1024
DFF = 2816
NT = 8
TW_ = 512
ENGS = ["pe", "act", "dve", "pool", "sp"]
DMA_RING = {"sp": 8, "act": 2, "pool": 6, "pe": 2, "dve": 2}


class Tk:
    __slots__ = ("w", "r")

    def __init__(self):
        self.w = None
        self.r = []


def tks(n):
    return [Tk() for _ in range(n)]


class Op:
    __slots__ = ("eng", "fn", "deps", "is_dma", "need_sig", "sigval", "ev")

    def __init__(self, eng, fn, is_dma):
        self.eng = eng
        self.fn = fn
        self.deps = []
        self.is_dma = is_dma
        self.need_sig = False
        self.sigval = None
        self.ev = None


class Prog:
    def __init__(self, nc):
        self.nc = nc
        self.ops = {e: [] for e in ENGS}
        self.last = {e: None for e in ENGS}
        self.dmas_since_barrier = []
        self.pending = {e: [] for e in ENGS}

    def _rec(self, eng, fn, reads, writes, is_dma):
        op = Op(eng, fn, is_dma)
        deps = []
        for t in reads:
            if t.w is not None:
                deps.append((t.w, 0))
        for t in writes:
            if t.w is not None:
                deps.append((t.w, 1))
            for r in t.r:
                deps.append((r, 1))
        for d in self.pending[eng]:
            deps.append((d, 0))
        self.pending[eng] = []
        op.deps = deps
        for t in writes:
            t.w = op
            t.r = []
        for t in reads:
            if t.w is not op:
                t.r.append(op)
        self.ops[eng].append(op)
        self.last[eng] = op
        if is_dma:
            self.dmas_since_barrier.append(op)
        return op

    def op(self, eng, fn, reads=(), writes=()):
        return self._rec(eng, fn, list(reads), list(writes), False)

    def dma(self, eng, out, in_, reads=(), writes=()):
        def fn(e):
            return e.dma_start(out=out, in_=in_)
        return self._rec(eng, fn, list(reads), list(writes), True)

    def barrier(self):
        deps = [o for o in self.last.values() if o is not None] + self.dmas_since_barrier
        self.dmas_since_barrier = []
        for e in ENGS:
            self.pending[e] = list(deps)

    @staticmethod
    def _skip(d, ename, kind):
        if d.eng == ename:
            if ename in ("pe", "sp"):
                return True
            if kind == 1:
                return True
        return False

    def emit(self):
        nc = self.nc
        with contextlib.ExitStack() as st:
            esem = {e: st.enter_context(nc.semaphore("s_" + e)) for e in ENGS}
            rings = {e: [st.enter_context(nc.semaphore("d_%s%d" % (e, i))) for i in range(DMA_RING[e])]
                     for e in ENGS}
            for e in ENGS:
                for op in self.ops[e]:
                    for d, kind in op.deps:
                        if d.is_dma or self._skip(d, e, kind):
                            continue
                        d.need_sig = True
            for e in ENGS:
                c = 0
                ring_cnt = [0] * DMA_RING[e]
                nd = 0
                for op in self.ops[e]:
                    if op.is_dma:
                        slot = nd % DMA_RING[e]
                        prev = ring_cnt[slot] * 16
                        ring_cnt[slot] += 1
                        op.ev = (rings[e][slot], ring_cnt[slot] * 16, prev)
                        nd += 1
                    elif op.need_sig:
                        c += 1
                        op.sigval = c
            if os.environ.get("NSA_VERBOSE"):
                print("ops per engine", {e: len(self.ops[e]) for e in ENGS},
                      "sig counts", {e: max([o.sigval or 0 for o in self.ops[e]] + [0]) for e in ENGS},
                      "ring max", {e: max([o.ev[1] for o in self.ops[e] if o.is_dma] + [0]) for e in ENGS}, flush=True)
            blk = st.enter_context(nc.Block())

            def run(ename, eng):
                known = {}

                def wait(sem, val):
                    k = id(sem)
                    if known.get(k, 0) >= val:
                        return
                    known[k] = val
                    eng.wait_ge(sem, val)
                for op in self.ops[ename]:
                    for d, kind in op.deps:
                        if d.is_dma:
                            wait(d.ev[0], d.ev[1])
                        elif not self._skip(d, ename, kind):
                            wait(esem[d.eng], d.sigval)
                    if op.is_dma:
                        sem, tgt, prev = op.ev
                        if prev > 0:
                            wait(sem, prev)
                        op.fn(eng).then_inc(sem, 16)
                    else:
                        ins = op.fn(eng)
                        if op.need_sig:
                            ins.then_inc(esem[ename], 1)

            @blk.sync
            def _(sync):
                run("sp", sync)

            @blk.tensor
            def _(tensor):
                run("pe", tensor)

            @blk.scalar
            def _(scalar):
                run("act", scalar)

            @blk.vector
            def _(vector):
                run("dve", vector)

            @blk.gpsimd
            def _(gpsimd):
                run("pool", gpsimd)


def MM(out, lhsT, rhs, start, stop):
    return lambda e: e.matmul(out, lhsT=lhsT, rhs=rhs, start=start, stop=stop)


def TRN(out, in_, ident):
    return lambda e: e.transpose(out=out, in_=in_, identity=ident)


def ACTF(out, in_, func, bias=None, scale=None):
    kw = {}
    if bias is not None:
        kw["bias"] = bias
    if scale is not None:
        kw["scale"] = scale
    return lambda e: e.activation(out=out, in_=in_, func=func, **kw)


def TT(out, in0, in1, op):
    return lambda e: e.tensor_tensor(out=out, in0=in0, in1=in1, op=op)


def STT(out, in0, scalar, in1, op0, op1):
    return lambda e: e.scalar_tensor_tensor(out=out, in0=in0, scalar=scalar, in1=in1, op0=op0, op1=op1)


def TS(out, in0, s1, s2, op0, op1=None):
    if op1 is None:
        return lambda e: e.tensor_scalar(out=out, in0=in0, scalar1=s1, scalar2=None, op0=op0)
    return lambda e: e.tensor_scalar(out=out, in0=in0, scalar1=s1, scalar2=s2, op0=op0, op1=op1)


def CP(out, in_):
    return lambda e: e.tensor_copy(out=out, in_=in_)


def MSET(out, v):
    return lambda e: e.memset(out, v)


def RCP(out, in_):
    return lambda e: e.reciprocal(out=out, in_=in_)


NSA_OFF = 4080
NSA_M = 8192
NSAW_OFF = 512
NSAW_M = 2048


def _t5_bucket_np(n):
    n = np.maximum(n, 0)
    nf = np.maximum(n, 1).astype(np.float32)
    large = 16 + (np.log(nf / np.float32(16)) / np.float32(math.log(128.0)) * np.float32(16)).astype(np.int32)
    large = np.minimum(large, 31)
    return np.where(n < 16, n, large)


def lin_layout(w):
    K, M = w.shape
    kc, mc = K // 128, M // 128
    return np.ascontiguousarray(w.reshape(kc, 128, mc, 128).transpose(2, 1, 0, 3).reshape(mc, 128, kc * 128))


def fm(v):
    return np.ascontiguousarray(v.reshape(-1, 128).T)


class VecPack:
    def __init__(self):
        self.cols = []
        self.idx = {}
        self.n = 0

    def add(self, name, arr):
        arr = np.asarray(arr, np.float32)
        assert arr.shape[0] == 128
        self.idx[name] = (self.n, arr.shape[1])
        self.cols.append(arr)
        self.n += arr.shape[1]

    def build(self):
        return np.ascontiguousarray(np.concatenate(self.cols, axis=1))


def host_prepare(inp):
    f = lambda a: np.asarray(a, np.float32)
    shared = {}
    vp = VecPack()
    for l in range(4):
        for nm in ("ffn1_norm", "mix_norm", "ffn2_norm", "ple_norm"):
            vp.add("%s%d" % (nm, l), fm(f(inp[nm])[l]))
        for fi, pre in enumerate(("ffn1", "ffn2")):
            wg = f(inp[pre + "_w_gate"])[l]
            wu = f(inp[pre + "_w_up"])[l]
            wd = f(inp[pre + "_w_down"])[l]
            for hf in range(2):
                cs = slice(hf * 1408, (hf + 1) * 1408)
                shared["wg_%d_%d_%d" % (l, fi, hf)] = lin_layout(wg[:, cs])
                shared["wu_%d_%d_%d" % (l, fi, hf)] = lin_layout(wu[:, cs])
                shared["wd_%d_%d_%d" % (l, fi, hf)] = lin_layout(wd[cs, :])
        shared["pleg_%d" % l] = lin_layout(f(inp["ple_w_gate"])[l])
        shared["plei_%d" % l] = lin_layout(f(inp["ple_w_in"])[l])
    vp.add("final_norm", fm(f(inp["final_norm"])))
    for j in range(2):
        shared["pw1_%d" % j] = lin_layout(f(inp["conv_w_pw1"])[j])
        shared["pw2_%d" % j] = lin_layout(f(inp["conv_w_pw2"])[j])
        vp.add("b_pw1_%d" % j, fm(f(inp["conv_b_pw1"])[j]))
        vp.add("b_dw_%d" % j, fm(f(inp["conv_b_dw"])[j]))
        vp.add("ln_g_%d" % j, fm(f(inp["conv_ln_g"])[j]))
        vp.add("ln_b_%d" % j, fm(f(inp["conv_ln_b"])[j]))
        vp.add("b_pw2_%d" % j, fm(f(inp["conv_b_pw2"])[j]))
        wdw = f(inp["conv_w_dw"])[j]
        vp.add("w_dw_%d" % j, np.ascontiguousarray(wdw.reshape(31, 8, 128).transpose(2, 1, 0).reshape(128, 248)))
        w_in = f(inp["nsa_w_in"])[j]
        cols = [np.arange(1024)]
        for g in range(4):
            for kind in (0, 1, 2, 4):
                c = 1024 + kind * 256 + g * 64 + np.arange(64)
                cols.append(np.concatenate([c, c]))
        cols = np.concatenate(cols)
        wcat = np.concatenate([w_in[:, cols], w_in[:, 2560:2608], np.zeros((1024, 80), np.float32)], axis=1)
        shared["nsa_wf_%d" % j] = lin_layout(wcat)
        tc = np.concatenate([1024 + 3 * 256 + np.arange(256), 1024 + 5 * 256 + np.arange(256), 2560 + np.arange(48)])
        shared["nsa_wt_%d" % j] = np.ascontiguousarray(w_in[:, tc].reshape(8, 128, 560).transpose(1, 0, 2))
        shared["nsa_wo_%d" % j] = lin_layout(f(inp["nsa_w_out"])[j])
        for nm, src in (("wk1", "nsa_cmp_wk1"), ("wv1", "nsa_cmp_wv1")):
            w1 = f(inp[src])[j]
            shared["%s_%d" % (nm, j)] = np.ascontiguousarray(w1.reshape(32, 64, 256).transpose(1, 0, 2))
        w2k = f(inp["nsa_cmp_wk2"])[j]
        w2kd = np.concatenate([w2k, w2k], axis=1)
        shared["w2k_%d" % j] = np.ascontiguousarray(w2kd.reshape(2, 128, 128).transpose(1, 0, 2))
        w2v = f(inp["nsa_cmp_wv2"])[j]
        shared["w2v_%d" % j] = np.ascontiguousarray(w2v.reshape(2, 128, 64).transpose(1, 0, 2))
        shared["posk_%d" % j] = np.ascontiguousarray(f(inp["nsa_cmp_pos_k"])[j].T)
        shared["posv_%d" % j] = np.ascontiguousarray(f(inp["nsa_cmp_pos_v"])[j].T)
    rb = f(inp["rel_bias"])
    ext = np.concatenate([rb, np.full((1, 16), -30000.0, np.float32)], axis=0)
    dist = np.arange(NSA_M) - NSA_OFF
    idx = np.where(dist >= 0, _t5_bucket_np(dist), 32)
    shared["gvec"] = np.ascontiguousarray(ext[idx].T)
    distw = np.arange(NSAW_M) - NSAW_OFF
    idxw = np.where((distw >= 0) & (distw < 512), _t5_bucket_np(distw), 32)
    shared["gwvec"] = np.ascontiguousarray(ext[idxw].T)
    vp.add("eps_rms", np.full((128, 1), 1e-6, np.float32))
    vp.add("eps_ln", np.full((128, 1), 1e-5, np.float32))
    vp.add("eps_z", np.full((128, 1), 1e-30, np.float32))
    shared["vecs"] = vp.build()
    shared["ident"] = np.eye(128, dtype=np.float32)
    t = np.arange(S)
    j = np.arange(64)[None, :]
    cur = (t // 64)[:, None]
    valid = (j * 64 <= t[:, None])
    forced = (j == 0) | (j == cur) | (j == cur - 1)
    vm = (valid & ~forced).astype(np.float32)
    am = np.where(forced, 1e9, np.where(valid, 0.0, -1.0)).astype(np.float32)
    shared["vmask"] = np.ascontiguousarray(vm.reshape(32, 128, 64).transpose(1, 0, 2))
    shared["amask"] = np.ascontiguousarray(am.reshape(32, 128, 64).transpose(1, 0, 2))
    c = np.arange(256)[:, None]
    ov = ((c * 16 < j * 64 + 64) & (c * 16 + 32 > j * 64) & (c < 255)).astype(np.float32)
    shared["overlap"] = np.ascontiguousarray(ov.reshape(2, 128, 64).transpose(1, 0, 2))
    ex = np.zeros((64, 32, 128), np.float32)
    for kt in range(32):
        ex[2 * kt, kt, 0:64] = 1.0
        ex[2 * kt + 1, kt, 64:128] = 1.0
    shared["expand"] = ex
    return shared, vp.idx


ARENA_F32 = 47600


class Ctx:
    pass


def build(nc, shapes, vidx, layers=(0, 1, 2, 3), stages=("ffn1", "mix", "ffn2", "ple"), do_final=True):
    P = Prog(nc)
    C = Ctx()
    din = {}
    for name, shp in shapes.items():
        din[name] = nc.dram_tensor(name, list(shp), F32, kind="ExternalInput").ap()
    out_d = nc.dram_tensor("out", [S, D], F32, kind="ExternalOutput").ap()

    def scratch(name, shape, dt):
        return nc.dram_tensor(name, list(shape), dt, kind="Internal").ap()
    X = [scratch("xs%d" % i, [8, 128, S], F32) for i in range(3)]
    qkT_d = scratch("qkT", [24, 128, S], BF16)
    vtok_d = scratch("vtok", [8, 128, 32, 65], BF16)
    gT_d = scratch("gT", [48, S], F32)
    fr_d = scratch("frow", [3, 512], F32)
    oc_d = scratch("ocT", [4, 64, S], F32)
    frd_tk = tks(3)
    oT_d = scratch("oT", [8, 128, S], BF16)
    grep_d = scratch("grep", [16, 128 * NSA_M], F32)
    gwrep_d = scratch("gwrep", [16, 128 * NSAW_M], F32)

    with contextlib.ExitStack() as st:
        arena = st.enter_context(nc.sbuf_tensor("arena", [128, ARENA_F32], F32))
        psb = [st.enter_context(nc.psum_tensor("psb%d" % i, [128, 512], F32)) for i in range(8)]
        pst = tks(8)
        top = [0]

        def alloc(nfree, dt=F32):
            n32 = nfree if dt == F32 else (nfree + 1) // 2
            assert top[0] + n32 <= ARENA_F32, ("arena overflow", top[0], n32)
            a = arena[:, top[0]:top[0] + n32]
            top[0] += n32
            if dt != F32:
                a = a.bitcast(dt)
                a = a[:, 0:nfree]
            return a

        def a3(nfree, dt, **kw):
            pat = kw.pop("pat")
            return alloc(nfree, dt).rearrange(pat, **kw)

        nv = shapes["vecs"][1]
        vecs = alloc(nv)
        vecs_tk = Tk()
        P.dma("sp", vecs, din["vecs"], writes=[vecs_tk])
        ident = alloc(128)
        ident_tk = Tk()
        P.dma("sp", ident, din["ident"], writes=[ident_tk])
        ones_bf = alloc(128, BF16)
        ones_tk = Tk()
        P.op("dve", MSET(ones_bf, 1.0), writes=[ones_tk])
        ident_bf = alloc(128, BF16)
        identbf_tk = Tk()
        P.op("dve", CP(ident_bf, ident), reads=[ident_tk], writes=[identbf_tk])
        base_top = top[0]

        def V(name, c0=0, n=None):
            o, w = vidx[name]
            if n is None:
                n = w - c0
            return vecs[:, o + c0:o + c0 + n]

        def xtile_ap(Xd, i):
            return Xd[:, :, i * TW_:(i + 1) * TW_].rearrange("c p t -> p c t")

        def load_w(dst, src, mc, wt):
            for m in range(mc):
                P.dma("pool", dst[:, m, :], src[m], writes=[wt[m]])

        def rmsnorm(xt, xtk, gain, hT, htk, sq, sqtk, rstd, rstd_tk, out_f32=None):
            for c in range(8):
                sl = c % 2
                P.op("act", ACTF(sq[:, sl, :], xt[:, c, :], AF.Square), reads=[xtk[c]], writes=[sqtk[sl]])
                P.op("pe", MM(psb[6][:, :], ones_bf, sq[:, sl, :], c == 0, c == 7),
                     reads=[sqtk[sl], ones_tk], writes=[pst[6]])
            P.op("act", ACTF(rstd, psb[6][:, :], AF.Sqrt, bias=V("eps_rms"), scale=1.0 / D),
                 reads=[pst[6], vecs_tk], writes=[rstd_tk])
            P.op("dve", RCP(rstd, rstd), reads=[rstd_tk], writes=[rstd_tk])
            for c in range(8):
                P.op("dve", STT(hT[:, c, :], xt[:, c, :], gain[:, c:c + 1], rstd, ALU.mult, ALU.mult),
                     reads=[xtk[c], rstd_tk, vecs_tk], writes=[htk[c]])

        def pass_input():
            xin = [alloc(D) for _ in range(2)]
            xin_tk = tks(2)
            xo = [a3(8 * 128, F32, pat="p (c t) -> p c t", c=8) for _ in range(2)]
            xo_tk = tks(2)
            for tt in range(32):
                b = tt % 2
                P.dma("sp", xin[b], din["x"][tt * 128:(tt + 1) * 128, :], writes=[xin_tk[b]])
                for c in range(8):
                    bank = c // 4
                    P.op("pe", TRN(psb[bank][:, (c % 4) * 128:(c % 4 + 1) * 128], xin[b][:, c * 128:(c + 1) * 128], ident),
                         reads=[xin_tk[b], ident_tk], writes=[pst[bank]])
                P.op("act", ACTF(xo[b][:, 0:4, :], psb[0][:, :].rearrange("p (c t) -> p c t", c=4), AF.Copy),
                     reads=[pst[0]], writes=[xo_tk[b]])
                P.op("dve", CP(xo[b][:, 4:8, :], psb[1][:, :].rearrange("p (c t) -> p c t", c=4)),
                     reads=[pst[1], xo_tk[b]], writes=[xo_tk[b]])
                P.dma("sp", X[0][:, :, tt * 128:(tt + 1) * 128].rearrange("c p t -> p c t"), xo[b], reads=[xo_tk[b]])

        def pass_ffn_half(l, fi, hf, Xn, Xr, Xo, norm_name):
            wg = a3(11 * 1024, BF16, pat="p (m f) -> p m f", m=11)
            wu = a3(11 * 1024, BF16, pat="p (m f) -> p m f", m=11)
            wd = a3(8 * 1408, BF16, pat="p (m f) -> p m f", m=8)
            wg_tk, wu_tk, wd_tk = tks(11), tks(11), tks(8)
            key = "%d_%d_%d" % (l, fi, hf)
            for m in range(11):
                P.dma("pool", wg[:, m, :], din["wg_" + key][m], writes=[wg_tk[m]])
                P.dma("pool", wu[:, m, :], din["wu_" + key][m], writes=[wu_tk[m]])
            load_w(wd, din["wd_" + key], 8, wd_tk)
            same = Xn is Xr
            xn = [a3(8 * 512, F32, pat="p (c t) -> p c t", c=8) for _ in range(2)]
            xn_tk = [tks(8) for _ in range(2)]
            if same:
                xr, xr_tk = xn, xn_tk
            else:
                xr = [a3(8 * 512, F32, pat="p (c t) -> p c t", c=8) for _ in range(2)]
                xr_tk = [tks(8) for _ in range(2)]
            hT = a3(8 * 512, BF16, pat="p (c t) -> p c t", c=8)
            htk = tks(8)
            sq = a3(2 * 512, BF16, pat="p (c t) -> p c t", c=2)
            sqtk = tks(2)
            rstd = alloc(512)
            rstd_tk = Tk()
            act_ = a3(11 * 512, BF16, pat="p (c t) -> p c t", c=11)
            atk = tks(11)
            sg = [alloc(512) for _ in range(2)]
            sgtk = tks(2)
            gain = V(norm_name)

            def load(i):
                b = i % 2
                P.dma("sp", xn[b], xtile_ap(Xn, i), writes=xn_tk[b])
                if not same:
                    P.dma("sp", xr[b], xtile_ap(Xr, i), writes=xr_tk[b])
            load(0)
            for i in range(NT):
                b = i % 2
                if i + 1 < NT:
                    load(i + 1)
                rmsnorm(xn[b], xn_tk[b], gain, hT, htk, sq, sqtk, rstd, rstd_tk)
                for j in range(11):
                    pg, pu = j % 2, 2 + j % 2
                    for c in range(8):
                        P.op("pe", MM(psb[pg][:, :], wg[:, j, c * 128:(c + 1) * 128], hT[:, c, :], c == 0, c == 7),
                             reads=[wg_tk[j], htk[c]], writes=[pst[pg]])
                    for c in range(8):
                        P.op("pe", MM(psb[pu][:, :], wu[:, j, c * 128:(c + 1) * 128], hT[:, c, :], c == 0, c == 7),
                             reads=[wu_tk[j], htk[c]], writes=[pst[pu]])
                    P.op("act", ACTF(sg[j % 2], psb[pg][:, :], AF.Silu), reads=[pst[pg]], writes=[sgtk[j % 2]])
                    P.op("dve", TT(act_[:, j, :], sg[j % 2], psb[pu][:, :], ALU.mult),
                         reads=[sgtk[j % 2], pst[pu]], writes=[atk[j]])
                for m in range(8):
                    py = 4 + m % 2
                    for j in range(11):
                        P.op("pe", MM(psb[py][:, :], wd[:, m, j * 128:(j + 1) * 128], act_[:, j, :], j == 0, j == 10),
                             reads=[wd_tk[m], atk[j]], writes=[pst[py]])
                    P.op("dve", STT(xr[b][:, m, :], psb[py][:, :], 0.5, xr[b][:, m, :], ALU.mult, ALU.add),
                         reads=[pst[py], xr_tk[b][m]], writes=[xr_tk[b][m]])
                P.dma("sp", xtile_ap(Xo, i), xr[b], reads=xr_tk[b])

        def pass_ple(l, Xc):
            wgp = a3(8 * 1024, BF16, pat="p (m f) -> p m f", m=8)
            wip = a3(8 * 256, BF16, pat="p (m f) -> p m f", m=8)
            wgp_tk, wip_tk = tks(8), tks(8)
            load_w(wgp, din["pleg_%d" % l], 8, wgp_tk)
            load_w(wip, din["plei_%d" % l], 8, wip_tk)
            xn = [a3(8 * 512, F32, pat="p (c t) -> p c t", c=8) for _ in range(2)]
            xn_tk = [tks(8) for _ in range(2)]
            pin = [a3(4 * 256, F32, pat="p (s f) -> p s f", s=4) for _ in range(2)]
            pin_tk = tks(2)
            pT = a3(2 * 512, BF16, pat="p (c t) -> p c t", c=2)
            pT_tk = tks(2)
            hT = a3(8 * 512, BF16, pat="p (c t) -> p c t", c=8)
            htk = tks(8)
            sq = a3(2 * 512, BF16, pat="p (c t) -> p c t", c=2)
            sqtk = tks(2)
            rstd = alloc(512)
            rstd_tk = Tk()
            sg = [alloc(512) for _ in range(2)]
            sgtk = tks(2)
            gain = V("ple_norm%d" % l)
            pl = din["p"][l]

            def load(i):
                b = i % 2
                P.dma("sp", xn[b], xtile_ap(Xc, i), writes=xn_tk[b])
                P.dma("sp", pin[b], pl[i * 512:(i + 1) * 512, :].rearrange("(s p) f -> p s f", p=128), writes=[pin_tk[b]])
            load(0)
            for i in range(NT):
                b = i % 2
                if i + 1 < NT:
                    load(i + 1)
                rmsnorm(xn[b], xn_tk[b], gain, hT, htk, sq, sqtk, rstd, rstd_tk)
                for kc in range(2):
                    for s in range(4):
                        P.op("pe", TRN(psb[7][:, s * 128:(s + 1) * 128], pin[b][:, s, kc * 128:(kc + 1) * 128], ident),
                             reads=[pin_tk[b], ident_tk], writes=[pst[7]])
                    P.op("act", ACTF(pT[:, kc, :], psb[7][:, :], AF.Copy), reads=[pst[7]], writes=[pT_tk[kc]])
                for m in range(8):
                    pg, pi = m % 2, 2 + m % 2
                    for c in range(8):
                        P.op("pe", MM(psb[pg][:, :], wgp[:, m, c * 128:(c + 1) * 128], hT[:, c, :], c == 0, c == 7),
                             reads=[wgp_tk[m], htk[c]], writes=[pst[pg]])
                    for c in range(2):
                        P.op("pe", MM(psb[pi][:, :], wip[:, m, c * 128:(c + 1) * 128], pT[:, c, :], c == 0, c == 1),
                             reads=[wip_tk[m], pT_tk[c]], writes=[pst[pi]])
                    P.op("act", ACTF(sg[m % 2], psb[pg][:, :], AF.Sigmoid), reads=[pst[pg]], writes=[sgtk[m % 2]])
                    P.op("dve", TT(sg[m % 2], sg[m % 2], psb[pi][:, :], ALU.mult),
                         reads=[sgtk[m % 2], pst[pi]], writes=[sgtk[m % 2]])
                    P.op("dve", TT(xn[b][:, m, :], xn[b][:, m, :], sg[m % 2], ALU.add),
                         reads=[sgtk[m % 2], xn_tk[b][m]], writes=[xn_tk[b][m]])
                P.dma("sp", xtile_ap(Xc, i), xn[b], reads=xn_tk[b])

        def pass_conv(l, jx, Xc):
            w1 = a3(16 * 1024, BF16, pat="p (m f) -> p m f", m=16)
            w2 = a3(8 * 1024, BF16, pat="p (m f) -> p m f", m=8)
            w1_tk, w2_tk = tks(16), tks(8)
            load_w(w1, din["pw1_%d" % jx], 16, w1_tk)
            load_w(w2, din["pw2_%d" % jx], 8, w2_tk)
            diag = a3(31 * 8 * 128, BF16, pat="p (j c m) -> p j c m", j=31, c=8)
            diag_tk = tks(8)
            wdw = V("w_dw_%d" % jx)
            for c in range(8):
                for j in range(31):
                    P.op("dve", TS(diag[:, j, c, :], ident_bf, wdw[:, c * 31 + j:c * 31 + j + 1], None, ALU.mult),
                         reads=[identbf_tk, vecs_tk], writes=[diag_tk[c]])
            xn = [a3(8 * 512, F32, pat="p (c t) -> p c t", c=8)] * 2
            xn_tk = [tks(8)] * 2
            hT = a3(8 * 512, BF16, pat="p (c t) -> p c t", c=8)
            htk = tks(8)
            sq = a3(2 * 512, BF16, pat="p (c t) -> p c t", c=2)
            sqtk = tks(2)
            rstd = alloc(512)
            rstd_tk = Tk()
            ub = a3(8 * 542, BF16, pat="p (c t) -> p c t", c=8)
            ub_tk = tks(8)
            yb = a3(8 * 512, F32, pat="p (c t) -> p c t", c=8)
            yb_tk = tks(8)
            ybf = a3(2 * 512, BF16, pat="p (c t) -> p c t", c=2)
            ybf_tk = tks(2)
            ysq = a3(2 * 512, BF16, pat="p (c t) -> p c t", c=2)
            ysq_tk = tks(2)
            sg = [alloc(512) for _ in range(2)]
            sgtk = tks(2)
            mu = alloc(512)
            mu_tk = Tk()
            rs = alloc(512)
            rs_tk = Tk()
            tmp = alloc(512)
            tmp_tk = Tk()
            gain = V("mix_norm%d" % l)
            b1 = V("b_pw1_%d" % jx)
            bdw = V("b_dw_%d" % jx)
            lng = V("ln_g_%d" % jx)
            lnb = V("ln_b_%d" % jx)
            b2 = V("b_pw2_%d" % jx)
            for c in range(8):
                P.op("dve", MSET(ub[:, c, 0:30], 0.0), writes=[ub_tk[c]])

            def load(i):
                b = i % 2
                P.dma("sp", xn[b], xtile_ap(Xc, i), writes=xn_tk[b])
            for i in range(NT):
                b = i % 2
                load(i)
                rmsnorm(xn[b], xn_tk[b], gain, hT, htk, sq, sqtk, rstd, rstd_tk)
                for m in range(8):
                    pa, pg = m % 2, 2 + m % 2
                    for c in range(8):
                        P.op("pe", MM(psb[pa][:, :], w1[:, m, c * 128:(c + 1) * 128], hT[:, c, :], c == 0, c == 7),
                             reads=[w1_tk[m], htk[c]], writes=[pst[pa]])
                    for c in range(8):
                        P.op("pe", MM(psb[pg][:, :], w1[:, 8 + m, c * 128:(c + 1) * 128], hT[:, c, :], c == 0, c == 7),
                             reads=[w1_tk[8 + m], htk[c]], writes=[pst[pg]])
                    P.op("act", ACTF(sg[m % 2], psb[pg][:, :], AF.Sigmoid, bias=b1[:, 8 + m:9 + m]),
                         reads=[pst[pg], vecs_tk], writes=[sgtk[m % 2]])
                    P.op("dve", STT(ub[:, m, 30:542], psb[pa][:, :], b1[:, m:m + 1], sg[m % 2], ALU.add, ALU.mult),
                         reads=[pst[pa], sgtk[m % 2], vecs_tk], writes=[ub_tk[m]])
                for m in range(8):
                    py = 4 + m % 2
                    for j in range(31):
                        P.op("pe", MM(psb[py][:, :], diag[:, j, m, :], ub[:, m, j:j + 512], j == 0, j == 30),
                             reads=[diag_tk[m], ub_tk[m]], writes=[pst[py]])
                    P.op("act", ACTF(yb[:, m, :], psb[py][:, :], AF.Identity, bias=bdw[:, m:m + 1]),
                         reads=[pst[py], vecs_tk], writes=[yb_tk[m]])
                    P.op("dve", CP(ub[:, m, 0:30], ub[:, m, 512:542]), reads=[ub_tk[m]], writes=[ub_tk[m]])
                    sl = m % 2
                    P.op("dve", CP(ybf[:, sl, :], yb[:, m, :]), reads=[yb_tk[m]], writes=[ybf_tk[sl]])
                    P.op("act", ACTF(ysq[:, sl, :], yb[:, m, :], AF.Square), reads=[yb_tk[m]], writes=[ysq_tk[sl]])
                    P.op("pe", MM(psb[6][:, :], ones_bf, ybf[:, sl, :], m == 0, m == 7),
                         reads=[ybf_tk[sl], ones_tk], writes=[pst[6]])
                    P.op("pe", MM(psb[7][:, :], ones_bf, ysq[:, sl, :], m == 0, m == 7),
                         reads=[ysq_tk[sl], ones_tk], writes=[pst[7]])
                P.op("dve", TS(mu, psb[6][:, :], 1.0 / D, None, ALU.mult), reads=[pst[6]], writes=[mu_tk])
                P.op("dve", TT(tmp, mu, mu, ALU.mult), reads=[mu_tk], writes=[tmp_tk])
                P.op("dve", STT(tmp, psb[7][:, :], 1.0 / D, tmp, ALU.mult, ALU.subtract),
                     reads=[pst[7], tmp_tk], writes=[tmp_tk])
                P.op("dve", TS(tmp, tmp, 0.0, None, ALU.max), reads=[tmp_tk], writes=[tmp_tk])
                P.op("act", ACTF(rs, tmp, AF.Sqrt, bias=V("eps_ln"), scale=1.0), reads=[tmp_tk, vecs_tk], writes=[rs_tk])
                P.op("dve", RCP(rs, rs), reads=[rs_tk], writes=[rs_tk])
                P.op("dve", STT(mu, mu, -1.0, rs, ALU.mult, ALU.mult), reads=[mu_tk, rs_tk], writes=[mu_tk])
                for m in range(8):
                    P.op("dve", TT(yb[:, m, :], yb[:, m, :], rs, ALU.mult), reads=[yb_tk[m], rs_tk], writes=[yb_tk[m]])
                    P.op("dve", TT(yb[:, m, :], yb[:, m, :], mu, ALU.add), reads=[yb_tk[m], mu_tk], writes=[yb_tk[m]])
                    P.op("act", ACTF(hT[:, m, :], yb[:, m, :], AF.Silu, bias=lnb[:, m:m + 1], scale=lng[:, m:m + 1]),
                         reads=[yb_tk[m], vecs_tk], writes=[htk[m]])
                for m in range(8):
                    po = m % 2
                    for c in range(8):
                        P.op("pe", MM(psb[po][:, :], w2[:, m, c * 128:(c + 1) * 128], hT[:, c, :], c == 0, c == 7),
                             reads=[w2_tk[m], htk[c]], writes=[pst[po]])
                    P.op("dve", STT(xn[b][:, m, :], psb[po][:, :], b2[:, m:m + 1], xn[b][:, m, :], ALU.add, ALU.add),
                         reads=[pst[po], xn_tk[b][m], vecs_tk], writes=[xn_tk[b][m]])
                P.dma("sp", xtile_ap(Xc, i), xn[b], reads=xn_tk[b])

        def pass_nsa_tables():
            for h in range(16):
                src = bass.AP(tensor=din["gvec"].tensor, offset=h * NSA_M, ap=[[0, 128], [1, NSA_M]])
                dst = bass.AP(tensor=grep_d.tensor, offset=h * 128 * NSA_M, ap=[[NSA_M, 128], [1, NSA_M]])
                P.dma("sp", dst, src)
                src = bass.AP(tensor=din["gwvec"].tensor, offset=h * NSAW_M, ap=[[0, 128], [1, NSAW_M]])
                dst = bass.AP(tensor=gwrep_d.tensor, offset=h * 128 * NSAW_M, ap=[[NSAW_M, 128], [1, NSAW_M]])
                P.dma("sp", dst, src)

        def pass_nsa_proj(l, jx, Xc):
            wf = a3(25 * 1024, BF16, pat="p (m f) -> p m f", m=25)
            wf_tk = tks(25)
            load_w(wf, din["nsa_wf_%d" % jx], 25, wf_tk)
            wt = a3(8 * 560, BF16, pat="p (c f) -> p c f", c=8)
            wt_tk = Tk()
            P.dma("pool", wt, din["nsa_wt_%d" % jx], writes=[wt_tk])
            xn = [a3(8 * 512, F32, pat="p (c t) -> p c t", c=8) for _ in range(2)]
            xn_tk = [tks(8) for _ in range(2)]
            hT = a3(8 * 512, BF16, pat="p (c t) -> p c t", c=8)
            htk = tks(8)
            sq = a3(2 * 512, BF16, pat="p (c t) -> p c t", c=2)
            sqtk = tks(2)
            rstd = alloc(512)
            rstd_tk = Tk()
            stg = [a3(24 * 512, BF16, pat="p (m t) -> p m t", m=24) for _ in range(2)]
            stg_tk = tks(2)
            vst = [a3(4 * 8 * 65, BF16, pat="p (s k e) -> p s k e", s=4, k=8) for _ in range(2)]
            vst_tk = tks(2)
            gst = [alloc(512) for _ in range(2)]
            gst_tk = tks(2)
            gain = V("mix_norm%d" % l)
            for b in range(2):
                P.op("dve", MSET(vst[b], 1.0), writes=[vst_tk[b]])

            def load(i):
                b = i % 2
                P.dma("sp", xn[b], xtile_ap(Xc, i), writes=xn_tk[b])
            load(0)
            for i in range(NT):
                b = i % 2
                if i + 1 < NT:
                    load(i + 1)
                rmsnorm(xn[b], xn_tk[b], gain, hT, htk, sq, sqtk, rstd, rstd_tk)
                for m in range(25):
                    pb = m % 4
                    for c in range(8):
                        P.op("pe", MM(psb[pb][:, :], wf[:, m, c * 128:(c + 1) * 128], hT[:, c, :], c == 0, c == 7),
                             reads=[wf_tk[m], htk[c]], writes=[pst[pb]])
                    if m == 24:
                        P.op("act", ACTF(gst[b], psb[pb][:, :], AF.Sigmoid), reads=[pst[pb]], writes=[gst_tk[b]])
                        P.dma("sp", gT_d[:, i * 512:(i + 1) * 512], gst[b][0:48, :], reads=[gst_tk[b]])
                    elif m % 2 == 0:
                        P.op("act", ACTF(stg[b][:, m, :], psb[pb][:, :], AF.Copy), reads=[pst[pb]], writes=[stg_tk[b]])
                    else:
                        P.op("dve", CP(stg[b][:, m, :], psb[pb][:, :]), reads=[pst[pb]], writes=[stg_tk[b]])
                P.dma("sp", qkT_d[:, :, i * 512:(i + 1) * 512].rearrange("m p t -> p m t"), stg[b], reads=[stg_tk[b]])
                for s in range(4):
                    pv = 4 + s % 2
                    for c in range(8):
                        P.op("pe", MM(psb[pv][:, :], hT[:, c, s * 128:(s + 1) * 128], wt[:, c, 0:512], c == 0, c == 7),
                             reads=[wt_tk, htk[c]], writes=[pst[pv]])
                    P.op("dve", CP(vst[b][:, s, :, 0:64], psb[pv][:, :].rearrange("p (k e) -> p k e", k=8)),
                         reads=[pst[pv]], writes=[vst_tk[b]])
                for s in range(4):
                    P.dma("sp", vtok_d[:, :, i * 4 + s, :].rearrange("k p e -> p k e"), vst[b][:, s, :, :], reads=[vst_tk[b]])

        def pass_nsa_group(jx, g):
            ksx = alloc(S, BF16)
            ksx_tk, ex_tk = Tk(), Tk()
            P.dma("sp", ksx[0:64], qkT_d[8 + 4 * g + 2, 0:64, :], writes=[ksx_tk])
            P.dma("pool", ksx[64:128], din["expand"].rearrange("n k m -> n (k m)"), writes=[ex_tk])
            kw = alloc(S, BF16)
            kw_tk = Tk()
            P.dma("sp", kw[0:64], qkT_d[8 + 4 * g + 3, 0:64, :], writes=[kw_tk])
            qx = [alloc(S, BF16) for _ in range(2)]
            qx_tk = tks(2)
            nmq_tk = [tks(8) for _ in range(2)]
            vs = a3(32 * 65, BF16, pat="p (s e) -> p s e", s=32)
            vw = a3(32 * 65, BF16, pat="p (s e) -> p s e", s=32)
            vs_tk, vw_tk = Tk(), Tk()
            P.dma("sp", vs, vtok_d[g], writes=[vs_tk])
            P.dma("sp", vw, vtok_d[4 + g], writes=[vw_tk])
            kcT = alloc(256, BF16)
            kcT_tk = Tk()
            vca = a3(2 * 66, BF16, pat="p (c e) -> p c e", c=2)
            vca_tk = Tk()
            ovl1 = a3(2 * 66, BF16, pat="p (c e) -> p c e", c=2)
            ovl1_tk = Tk()
            ovl = a3(2 * 64, F32, pat="p (c e) -> p c e", c=2)
            ovl_tk = Tk()
            P.dma("sp", ovl, din["overlap"], writes=[ovl_tk])
            imp = a3(32 * 64, F32, pat="p (s f) -> p s f", s=32)
            imp_tk = tks(32)
            NSL = 6
            SB = [0, 1, 2, 3, 4, 7]
            Ef = [alloc(512) for _ in range(NSL)]
            Ef_tk = tks(NSL)
            Eb = [alloc(512, BF16) for _ in range(NSL)]
            Eb_tk = tks(NSL)
            sm = alloc(8)
            sm_tk = tks(8)
            imt = [a3(4 * 64, F32, pat="p (s f) -> p s f", s=4) for _ in range(2)]
            imt_tk = tks(2)
            grp_top = top[0]

            kc = alloc(S, BF16)
            vc = alloc(S, BF16)
            kc_tk, vc_tk = Tk(), Tk()
            P.dma("sp", kc[0:64], qkT_d[8 + 4 * g + 0, 0:64, :], writes=[kc_tk])
            P.dma("sp", vc[0:64], qkT_d[8 + 4 * g + 1, 0:64, :], writes=[vc_tk])
            wk1 = a3(32 * 256, BF16, pat="p (l j) -> p l j", l=32)
            wv1 = a3(32 * 256, BF16, pat="p (l j) -> p l j", l=32)
            wk1_tk, wv1_tk = Tk(), Tk()
            P.dma("pool", wk1[0:64], din["wk1_%d" % jx], writes=[wk1_tk])
            P.dma("pool", wv1[0:64], din["wv1_%d" % jx], writes=[wv1_tk])
            w2k = a3(2 * 128, BF16, pat="p (c m) -> p c m", c=2)
            w2v = a3(2 * 64, BF16, pat="p (c m) -> p c m", c=2)
            w2k_tk, w2v_tk = Tk(), Tk()
            P.dma("pool", w2k, din["w2k_%d" % jx], writes=[w2k_tk])
            P.dma("pool", w2v, din["w2v_%d" % jx], writes=[w2v_tk])
            posk = alloc(32, BF16)
            posv = alloc(32, BF16)
            posk_tk, posv_tk = Tk(), Tk()
            P.dma("pool", posk[0:64], din["posk_%d" % jx], writes=[posk_tk])
            P.dma("pool", posv[0:64], din["posv_%d" % jx], writes=[posv_tk])
            cb = alloc(4)
            cb_tk = Tk()
            xg = a3(2 * 256, F32, pat="p (c t) -> p c t", c=2)
            xg_tk = tks(2)
            t1 = a3(2 * 256, F32, pat="p (c t) -> p c t", c=2)
            t1_tk = tks(2)
            gl = a3(2 * 256, BF16, pat="p (c t) -> p c t", c=2)
            gl_tk = tks(2)
            P.op("dve", MSET(kcT, 0.0), writes=[kcT_tk])
            P.op("dve", MSET(vca, 0.0), writes=[vca_tk])
            P.op("dve", MSET(vca[:, :, 64:65], 1.0), writes=[vca_tk])
            P.op("dve", MSET(ovl1[:, :, 64:65], 1.0), writes=[ovl1_tk])
            P.op("dve", CP(ovl1[:, :, 0:64], ovl), reads=[ovl_tk], writes=[ovl1_tk])
            for which, (src, src_tk, w1s, w1_tk, pos, pos_tk) in enumerate(
                    ((kc, kc_tk, wk1, wk1_tk, posk, posk_tk), (vc, vc_tk, wv1, wv1_tk, posv, posv_tk))):
                for jc in range(2):
                    for l_ in range(32):
                        P.op("pe", MM(psb[4][:, jc:jc + 1], w1s[0:64, l_, jc * 128:(jc + 1) * 128], pos[0:64, l_:l_ + 1],
                                      l_ == 0, l_ == 31), reads=[w1_tk, pos_tk], writes=[pst[4]])
                P.op("dve", CP(cb[:, 0:2], psb[4][:, 0:2]), reads=[pst[4]], writes=[cb_tk])
                for jc in range(2):
                    pb = 5 + jc
                    for l_ in range(32):
                        P.op("pe", MM(psb[pb][:, 0:255], w1s[0:64, l_, jc * 128:(jc + 1) * 128],
                                      src[0:64, l_:l_ + 16 * 254 + 1:16], l_ == 0, l_ == 31),
                             reads=[w1_tk, src_tk], writes=[pst[pb]])
                    P.op("act", ACTF(xg[:, jc, 0:255], psb[pb][:, 0:255], AF.Identity, bias=cb[:, jc:jc + 1]),
                         reads=[pst[pb], cb_tk], writes=[xg_tk[jc]])
                    P.op("dve", TT(t1[:, jc, 0:255], xg[:, jc, 0:255], xg[:, jc, 0:255], ALU.mult),
                         reads=[xg_tk[jc]], writes=[t1_tk[jc]])
                    P.op("dve", TS(t1[:, jc, 0:255], t1[:, jc, 0:255], 0.044715, 1.0, ALU.mult, ALU.add),
                         reads=[t1_tk[jc]], writes=[t1_tk[jc]])
                    P.op("dve", TT(t1[:, jc, 0:255], t1[:, jc, 0:255], xg[:, jc, 0:255], ALU.mult),
                         reads=[t1_tk[jc], xg_tk[jc]], writes=[t1_tk[jc]])
                    P.op("act", ACTF(t1[:, jc, 0:255], t1[:, jc, 0:255], AF.Sigmoid, scale=1.5957691),
                         reads=[t1_tk[jc]], writes=[t1_tk[jc]])
                    P.op("dve", TT(gl[:, jc, 0:255], t1[:, jc, 0:255], xg[:, jc, 0:255], ALU.mult),
                         reads=[t1_tk[jc], xg_tk[jc]], writes=[gl_tk[jc]])
                if which == 0:
                    for jc in range(2):
                        P.op("pe", MM(psb[7][:, 0:255], w2k[:, jc, :], gl[:, jc, 0:255], jc == 0, jc == 1),
                             reads=[w2k_tk, gl_tk[jc]], writes=[pst[7]])
                    P.op("act", ACTF(kcT[:, 0:255], psb[7][:, 0:255], AF.Copy), reads=[pst[7]], writes=[kcT_tk])
                else:
                    for ct in range(2):
                        ncr = 128 if ct == 0 else 127
                        for jc in range(2):
                            P.op("pe", MM(psb[7][0:ncr, 0:64], gl[:, jc, ct * 128:ct * 128 + ncr], w2v[:, jc, :], jc == 0, jc == 1),
                                 reads=[w2v_tk, gl_tk[jc]], writes=[pst[7]])
                        P.op("act", ACTF(vca[0:ncr, ct, 0:64], psb[7][0:ncr, 0:64], AF.Copy), reads=[pst[7]], writes=[vca_tk])
            P.barrier()
            top[0] = grp_top
            if DBG < 3:
                return

            tabc = alloc(6144)
            tabc_tk = tks(3)
            tabs2 = [alloc(2560) for _ in range(2)]
            tabs_tk2 = tks(2)
            tabw2 = [alloc(1408) for _ in range(2)]
            tabw_tk2 = tks(2)
            b31c2 = [alloc(1) for _ in range(2)]
            b31_tk2 = tks(2)
            vmk = a3(32 * 64, F32, pat="p (s f) -> p s f", s=32)
            amk = a3(32 * 64, F32, pat="p (s f) -> p s f", s=32)
            vmk_tk, amk_tk = Tk(), Tk()
            P.dma("sp", vmk, din["vmask"], writes=[vmk_tk])
            P.dma("sp", amk, din["amask"], writes=[amk_tk])
            sc = alloc(64)
            sc2 = alloc(64)
            m8a = alloc(8)
            m8b = alloc(8)
            nm = alloc(128)
            sc_tk, sc2_tk, m8a_tk, m8b_tk, nm_tk = Tk(), Tk(), Tk(), Tk(), Tk()
            gq = [alloc(3 * 512) for _ in range(2)]
            gq_tk = tks(2)
            ostg = [alloc(512) for _ in range(3)]
            ostg_tk = tks(3)
            lrow = [alloc(512) for _ in range(3)]
            lrow_tk = tks(3)
            facb = [alloc(512) for _ in range(3)]
            facb_tk = tks(3)
            Oac = [alloc(512) for _ in range(2)]
            Oac_tk = tks(2)
            ost = [alloc(512, BF16) for _ in range(2)]
            ost_tk = tks(2)
            step = [0]
            zcnt = [0]
            ocd_tk = [tks(8) for _ in range(4)]

            def load_tabc(h):
                src = bass.AP(tensor=grep_d.tensor, offset=h * 128 * NSA_M + (NSA_OFF - 2048),
                              ap=[[NSA_M - 16, 128], [1, 6144]])
                P.dma("sp", tabc, src, writes=tabc_tk)
                for pz in range(3):
                    P.op("act", ACTF(tabc[:, pz * 2048:(pz + 1) * 2048], tabc[:, pz * 2048:(pz + 1) * 2048], AF.Exp),
                         reads=[tabc_tk[pz]], writes=[tabc_tk[pz]])

            def fin_a(t):
                zb = zcnt[0] % 3
                zcnt[0] += 1
                t["zb"] = zb
                P.op("dve", CP(ostg[zb][0:65, :], psb[t["acc"]][0:65, :]), reads=[pst[t["acc"]]], writes=[ostg_tk[zb]])

            def fin_b(t):
                j, QB, zb = t["j"], t["QB"], t["zb"]
                gb_ = t["gb"]
                P.op("act", ACTF(lrow[zb][64:65, :], ostg[zb][64:65, :], AF.Ln, bias=V("eps_z")[64:65]),
                     reads=[ostg_tk[zb], vecs_tk], writes=[lrow_tk[zb]])
                P.op("act", ACTF(lrow[zb][64:65, :], lrow[zb][64:65, :], AF.Exp, scale=-1.0), reads=[lrow_tk[zb]], writes=[lrow_tk[zb]])
                P.op("dve", TT(lrow[zb][64:65, :], lrow[zb][64:65, :], gq[gb_][64:65, j * 512:(j + 1) * 512], ALU.mult),
                     reads=[lrow_tk[zb], gq_tk[gb_]], writes=[lrow_tk[zb]])
                P.dma("sp", fr_d[zb:zb + 1, :], lrow[zb][64:65, :], reads=[lrow_tk[zb]], writes=[frd_tk[zb]])
                P.dma("sp", facb[zb][0:64, :], bass.AP(tensor=fr_d.tensor, offset=zb * 512, ap=[[0, 64], [1, 512]]),
                      reads=[frd_tk[zb]], writes=[facb_tk[zb]])

            def fin_c(t):
                j, QB, zb = t["j"], t["QB"], t["zb"]
                qs = slice(QB * 512, (QB + 1) * 512)
                ob = t["ob"]
                r_, pr_, hf_ = t["r"], t["r"] // 2, t["r"] % 2
                if j == 0:
                    P.op("dve", TT(Oac[ob][0:64], ostg[zb][0:64, :], facb[zb][0:64, :], ALU.mult),
                         reads=[ostg_tk[zb], facb_tk[zb]], writes=[Oac_tk[ob]])
                    P.dma("sp", oc_d[r_, :, qs], Oac[ob][0:64], reads=[Oac_tk[ob]], writes=[ocd_tk[r_][QB]])
                else:
                    P.op("dve", TT(ostg[zb][0:64, :], ostg[zb][0:64, :], facb[zb][0:64, :], ALU.mult),
                         reads=[ostg_tk[zb], facb_tk[zb]], writes=[ostg_tk[zb]])
                    P.op("pool", TT(Oac[ob][0:64], Oac[ob][0:64], ostg[zb][0:64, :], ALU.add),
                         reads=[ostg_tk[zb], Oac_tk[ob]], writes=[Oac_tk[ob]])
                if j == 2:
                    P.op("dve", CP(ost[ob][0:64], Oac[ob][0:64]), reads=[Oac_tk[ob]], writes=[ost_tk[ob]])
                    P.dma("sp", oT_d[2 * g + pr_, hf_ * 64:(hf_ + 1) * 64, qs], ost[ob][0:64], reads=[ost_tk[ob]])

            def load_gq(h, QB, gb_):
                P.dma("sp", gq[gb_][64:65, :].rearrange("p (j t) -> p j t", j=3),
                      gT_d[h * 3:h * 3 + 3, QB * 512:(QB + 1) * 512].rearrange("(o j) t -> o j t", o=1),
                      writes=[gq_tk[gb_]])

            gcnt = [0]
            for r in range(4):
                h = 4 * g + r
                pr, hf = r // 2, r % 2
                b = r % 2
                P.dma("sp", qx[b][0:64], qkT_d[2 * g + pr, hf * 64:(hf + 1) * 64, :], writes=[qx_tk[b]])
                load_tabc(h)
                prev_t = None
                its = []
                for QB in range(8):
                    its.append(dict(QB=QB))

                def front(it):
                    QB = it["QB"]
                    qs = slice(QB * 512, (QB + 1) * 512)
                    cts = [0] if QB < 4 else [0, 1]
                    gb_ = gcnt[0] % 2
                    gcnt[0] += 1
                    load_gq(h, QB, gb_)
                    slots = []
                    for ct in cts:
                        sl = step[0] % 4
                        step[0] += 1
                        P.op("pe", MM(psb[sl][:, :], kcT[0:64, ct * 128:(ct + 1) * 128], qx[b][0:64, qs], True, True),
                             reads=[kcT_tk, qx_tk[b]], writes=[pst[sl]])
                        P.op("act", ACTF(Ef[sl], psb[sl][:, :], AF.Exp, scale=0.125), reads=[pst[sl]], writes=[Ef_tk[sl]])
                        sj = QB * 512 + 2017 - 2048 * ct
                        P.op("dve", TT(Eb[sl], Ef[sl], tabc[:, sj:sj + 512], ALU.mult),
                             reads=[Ef_tk[sl]] + tabc_tk, writes=[Eb_tk[sl]])
                        slots.append((ct, sl))
                    it["slots"] = slots
                    it["cts"] = cts
                    it["t"] = dict(j=0, QB=QB, acc=5, gb=gb_, ob=QB % 2, r=r)

                def back(it):
                    QB, slots, cts, t = it["QB"], it["slots"], it["cts"], it["t"]
                    for (ct, sl) in slots:
                        P.op("pe", MM(psb[5][0:65, :], vca[:, ct, 0:65], Eb[sl], ct == cts[0], ct == cts[-1]),
                             reads=[Eb_tk[sl], vca_tk], writes=[pst[5]])
                    fin_a(t)
                    for s in range(4):
                        for (ct, sl) in slots:
                            P.op("pe", MM(psb[4][:, s * 65:(s + 1) * 65], Eb[sl][:, s * 128:(s + 1) * 128], ovl1[:, ct, 0:65],
                                          ct == cts[0], ct == cts[-1]), reads=[Eb_tk[sl], ovl1_tk], writes=[pst[4]])
                    fin_b(t)
                    zs = QB % 2
                    z4 = sm[:, zs * 4:zs * 4 + 4].rearrange("p (s o) -> p s o", o=1)
                    pv4 = psb[4][:, 0:260].rearrange("p (s e) -> p s e", e=65)
                    blk = imp[:, QB * 4:QB * 4 + 4, :]
                    btk = [imp_tk[QB * 4 + s] for s in range(4)]
                    P.op("dve", TS(z4, pv4[:, :, 64:65], 1e-30, None, ALU.max), reads=[pst[4]], writes=[sm_tk[zs]])
                    P.op("dve", RCP(z4, z4), reads=[sm_tk[zs]], writes=[sm_tk[zs]])
                    zb4 = z4.broadcast_to([128, 4, 64])
                    if r == 0:
                        P.op("dve", TT(blk, pv4[:, :, 0:64], zb4, ALU.mult), reads=[pst[4], sm_tk[zs]], writes=btk)
                    else:
                        P.op("dve", TT(imt[zs], pv4[:, :, 0:64], zb4, ALU.mult), reads=[pst[4], sm_tk[zs]], writes=[imt_tk[zs]])
                        P.op("dve", TT(blk, blk, imt[zs], ALU.add), reads=[imt_tk[zs]] + btk, writes=btk)

                for ii in range(len(its) + 1):
                    if ii < len(its):
                        front(its[ii])
                    if ii >= 1:
                        back(its[ii - 1])
                        if prev_t is not None:
                            fin_c(prev_t)
                        prev_t = its[ii - 1]["t"]
                fin_c(prev_t)
            if DBG < 4:
                return
            W4 = 4
            scs = [alloc(64) for _ in range(W4)]
            sc2s = [alloc(64) for _ in range(W4)]
            m8as = [alloc(8) for _ in range(W4)]
            m8bs = [alloc(8) for _ in range(W4)]
            nms = [alloc(128) for _ in range(W4)]
            scs_tk, sc2s_tk, m8as_tk, m8bs_tk, nms_tk = tks(W4), tks(W4), tks(W4), tks(W4), tks(W4)

            def mk_max(o, i):
                return lambda e: e.max(out=o, in_=i)

            def mk_mr(o, a_, v):
                return lambda e: e.match_replace(out=o, in_to_replace=a_, in_values=v, imm_value=-1e30)
            for q0_ in range(0, 32, W4):
                qts = list(range(q0_, q0_ + W4))
                for w, qt in enumerate(qts):
                    P.op("dve", TT(scs[w], imp[:, qt, :], vmk[:, qt, :], ALU.mult), reads=[imp_tk[qt], vmk_tk], writes=[scs_tk[w]])
                for w, qt in enumerate(qts):
                    P.op("dve", TT(scs[w], scs[w], amk[:, qt, :], ALU.add), reads=[scs_tk[w], amk_tk], writes=[scs_tk[w]])
                for w, qt in enumerate(qts):
                    P.op("dve", mk_max(m8as[w], scs[w]), reads=[scs_tk[w]], writes=[m8as_tk[w]])
                for w, qt in enumerate(qts):
                    P.op("dve", mk_mr(sc2s[w], m8as[w], scs[w]), reads=[scs_tk[w], m8as_tk[w]], writes=[sc2s_tk[w]])
                for w, qt in enumerate(qts):
                    P.op("dve", mk_max(m8bs[w], sc2s[w]), reads=[sc2s_tk[w]], writes=[m8bs_tk[w]])
                for w, qt in enumerate(qts):
                    P.op("dve", TS(nms[w][:, 0:64], scs[w], m8bs[w][:, 7:8], -30000.0, ALU.is_lt, ALU.mult),
                         reads=[scs_tk[w], m8bs_tk[w]], writes=[nms_tk[w]])
                for w, qt in enumerate(qts):
                    P.op("dve", TS(nms[w][:, 64:128], scs[w], m8bs[w][:, 7:8], -30000.0, ALU.is_lt, ALU.mult),
                         reads=[scs_tk[w], m8bs_tk[w], nms_tk[w]], writes=[nms_tk[w]])
                for w, qt in enumerate(qts):
                    P.op("pe", TRN(psb[7][:, w * 128:(w + 1) * 128], nms[w], ident), reads=[nms_tk[w], ident_tk], writes=[pst[7]])
                for b in range(2):
                    P.op("act", ACTF(qx[b][64:128, q0_ * 128:(q0_ + W4) * 128], psb[7][64:128, 0:W4 * 128], AF.Copy),
                         reads=[pst[7]], writes=[nmq_tk[b][q0_ // 4]])
            if DBG < 5:
                return

            def load_tabs(r):
                h = 4 * g + r
                tb = r % 2
                pr, hf = r // 2, r % 2
                P.dma("sp", qx[tb][0:64], qkT_d[2 * g + pr, hf * 64:(hf + 1) * 64, :], writes=[qx_tk[tb]])
                src = bass.AP(tensor=grep_d.tensor, offset=h * 128 * NSA_M + (NSA_OFF - 384),
                              ap=[[NSA_M - 1, 128], [1, 2560]])
                P.dma("sp", tabs2[tb], src, writes=[tabs_tk2[tb]])
                src = bass.AP(tensor=gwrep_d.tensor, offset=h * 128 * NSAW_M + (NSAW_OFF - 384),
                              ap=[[NSAW_M - 1, 128], [1, 1408]])
                P.dma("sp", tabw2[tb], src, writes=[tabw_tk2[tb]])

            def exp_tabs(r):
                tb = r % 2
                P.op("act", ACTF(b31c2[tb], tabs2[tb][:, 2559:2560], AF.Copy), reads=[tabs_tk2[tb]], writes=[b31_tk2[tb]])
                P.op("act", ACTF(tabs2[tb], tabs2[tb], AF.Exp), reads=[tabs_tk2[tb], b31_tk2[tb]], writes=[tabs_tk2[tb]])
                P.op("act", ACTF(tabw2[tb], tabw2[tb], AF.Exp), reads=[tabw_tk2[tb]], writes=[tabw_tk2[tb]])

            load_tabs(0)
            exp_tabs(0)
            for r in range(4):
                h = 4 * g + r
                pr, hf = r // 2, r % 2
                b = r % 2
                tabs, tabs_tk, tabw, tabw_tk, b31c, b31_tk = tabs2[b], tabs_tk2[b], tabw2[b], tabw_tk2[b], b31c2[b], b31_tk2[b]
                if r + 1 < 4:
                    load_tabs(r + 1)
                tiles = []
                for QB in range(8):
                    qs = slice(QB * 512, (QB + 1) * 512)
                    q0 = QB * 512
                    for kt in range(4 * QB + 4):
                        off = min(QB * 512 - kt * 128, 1664) + 384
                        c0, c1 = 128 * max(0, kt - 4 * QB), 512
                        tiles.append(dict(j=1, QB=QB, first=kt == 0, last=kt == 4 * QB + 3, r=r, ob=QB % 2, c0=c0, c1=c1,
                                          lhsT=ksx[:, kt * 128:(kt + 1) * 128], lt=[ksx_tk, ex_tk], rhs=qx[b][:, q0 + c0:q0 + c1],
                                          rt=[qx_tk[b], nmq_tk[b][QB]],
                                          tab=tabs[:, off + c0:off + c1], tt=[tabs_tk], v=vs[:, kt, :], vt=[vs_tk], acc=5,
                                          far=(QB * 512 - kt * 128 >= 1664)))
                    k0 = max(0, 4 * QB - 4)
                    for kt in range(k0, 4 * QB + 4):
                        off = QB * 512 - kt * 128 + 384
                        if kt == k0:
                            c0, c1 = 0, 512
                        else:
                            c0 = 128 * max(0, kt - 4 * QB)
                            c1 = 128 * (min(3, kt - 4 * QB + 4) + 1)
                        tiles.append(dict(j=2, QB=QB, first=kt == k0, last=kt == 4 * QB + 3, r=r, ob=QB % 2, c0=c0, c1=c1,
                                          lhsT=kw[0:64, kt * 128:(kt + 1) * 128], lt=[kw_tk], rhs=qx[b][0:64, q0 + c0:q0 + c1], rt=[qx_tk[b]],
                                          tab=tabw[:, off + c0:off + c1], tt=[tabw_tk], v=vw[:, kt, :], vt=[vw_tk], acc=6, far=False))
                LOOK = 5
                pend = []
                n = len(tiles)
                lastQB = -1
                cur_gb = 0
                for i in range(n + LOOK):
                    if i == (n * 3) // 5 and r + 1 < 4:
                        exp_tabs(r + 1)
                    if i < n:
                        t = tiles[i]
                        if t["QB"] != lastQB:
                            lastQB = t["QB"]
                            cur_gb = gcnt[0] % 2
                            gcnt[0] += 1
                            load_gq(h, lastQB, cur_gb)
                            ob = lastQB % 2
                            P.dma("sp", Oac[ob][0:64], oc_d[r, :, lastQB * 512:(lastQB + 1) * 512], reads=[ocd_tk[r][lastQB]], writes=[Oac_tk[ob]])
                        t["gb"] = cur_gb
                        sl = step[0] % NSL
                        step[0] += 1
                        t["sl"] = sl
                        P.op("pe", MM(psb[SB[sl]][:, t["c0"]:t["c1"]], t["lhsT"], t["rhs"], True, True), reads=t["lt"] + t["rt"], writes=[pst[SB[sl]]])
                    jx_ = i - LOOK
                    if jx_ >= 0:
                        t = tiles[jx_]
                        sl = t["sl"]
                        c0, c1 = t["c0"], t["c1"]
                        if t["far"]:
                            P.op("act", ACTF(Eb[sl][:, c0:c1], psb[SB[sl]][:, c0:c1], AF.Exp, scale=0.125, bias=b31c), reads=[pst[SB[sl]], b31_tk], writes=[Eb_tk[sl]])
                        else:
                            P.op("act", ACTF(Ef[sl][:, c0:c1], psb[SB[sl]][:, c0:c1], AF.Exp, scale=0.125), reads=[pst[SB[sl]]], writes=[Ef_tk[sl]])
                            P.op("dve", TT(Eb[sl][:, c0:c1], Ef[sl][:, c0:c1], t["tab"], ALU.mult), reads=[Ef_tk[sl]] + list(t["tt"]), writes=[Eb_tk[sl]])
                        P.op("pe", MM(psb[t["acc"]][0:65, c0:c1], t["v"], Eb[sl][:, c0:c1], t["first"], t["last"]),
                             reads=[Eb_tk[sl]] + t["vt"], writes=[pst[t["acc"]]])
                        if t["last"]:
                            fin_a(t)
                            pend.append([i + 2, 0, t])
                    k = 0
                    while k < len(pend):
                        due, stage, t = pend[k]
                        if due <= i:
                            if stage == 0:
                                fin_b(t)
                                pend[k] = [i + 5, 1, t]
                                k += 1
                            else:
                                fin_c(t)
                                pend.pop(k)
                        else:
                            k += 1
                for due, stage, t in sorted(pend, key=lambda x: x[0]):
                    if stage == 0:
                        fin_b(t)
                    fin_c(t)

        def pass_nsa_out(jx, Xc):
            wo = a3(8 * 1024, BF16, pat="p (m f) -> p m f", m=8)
            wo_tk = tks(8)
            load_w(wo, din["nsa_wo_%d" % jx], 8, wo_tk)
            xn = [a3(8 * 512, F32, pat="p (c t) -> p c t", c=8) for _ in range(2)]
            xn_tk = [tks(8) for _ in range(2)]
            ot = [a3(8 * 512, BF16, pat="p (c t) -> p c t", c=8) for _ in range(2)]
            ot_tk = tks(2)

            def load(i):
                b = i % 2
                P.dma("sp", xn[b], xtile_ap(Xc, i), writes=xn_tk[b])
                P.dma("sp", ot[b], oT_d[:, :, i * 512:(i + 1) * 512].rearrange("m p t -> p m t"), writes=[ot_tk[b]])
            load(0)
            for i in range(NT):
                b = i % 2
                if i + 1 < NT:
                    load(i + 1)
                for m in range(8):
                    po = m % 2
                    for c in range(8):
                        P.op("pe", MM(psb[po][:, :], wo[:, m, c * 128:(c + 1) * 128], ot[b][:, c, :], c == 0, c == 7),
                             reads=[wo_tk[m], ot_tk[b]], writes=[pst[po]])
                    P.op("dve", TT(xn[b][:, m, :], xn[b][:, m, :], psb[po][:, :], ALU.add),
                         reads=[pst[po], xn_tk[b][m]], writes=[xn_tk[b][m]])
                P.dma("sp", xtile_ap(Xc, i), xn[b], reads=xn_tk[b])

        def pass_final(Xc, do_norm):
            xn = [a3(8 * 512, F32, pat="p (c t) -> p c t", c=8) for _ in range(2)]
            xn_tk = [tks(8) for _ in range(2)]
            sq = a3(2 * 512, BF16, pat="p (c t) -> p c t", c=2)
            sqtk = tks(2)
            rstd = alloc(512)
            rstd_tk = Tk()
            yo = [alloc(D) for _ in range(2)]
            yo_tk = tks(2)
            gain = V("final_norm")

            def load(i):
                b = i % 2
                P.dma("sp", xn[b], xtile_ap(Xc, i), writes=xn_tk[b])
            load(0)
            for i in range(NT):
                b = i % 2
                if i + 1 < NT:
                    load(i + 1)
                xt, xtk = xn[b], xn_tk[b]
                if do_norm:
                    for c in range(8):
                        sl = c % 2
                        P.op("act", ACTF(sq[:, sl, :], xt[:, c, :], AF.Square), reads=[xtk[c]], writes=[sqtk[sl]])
                        P.op("pe", MM(psb[6][:, :], ones_bf, sq[:, sl, :], c == 0, c == 7),
                             reads=[sqtk[sl], ones_tk], writes=[pst[6]])
                    P.op("act", ACTF(rstd, psb[6][:, :], AF.Sqrt, bias=V("eps_rms"), scale=1.0 / D),
                         reads=[pst[6], vecs_tk], writes=[rstd_tk])
                    P.op("dve", RCP(rstd, rstd), reads=[rstd_tk], writes=[rstd_tk])
                    for c in range(8):
                        P.op("dve", STT(xt[:, c, :], xt[:, c, :], gain[:, c:c + 1], rstd, ALU.mult, ALU.mult),
                             reads=[xtk[c], rstd_tk, vecs_tk], writes=[xtk[c]])
                for s in range(4):
                    yb_ = (i * 4 + s) % 2
                    for c in range(8):
                        bank = c // 4
                        P.op("pe", TRN(psb[bank][:, (c % 4) * 128:(c % 4 + 1) * 128], xt[:, c, s * 128:(s + 1) * 128], ident),
                             reads=[xtk[c], ident_tk], writes=[pst[bank]])
                    P.op("act", ACTF(yo[yb_][:, 0:512], psb[0][:, :], AF.Copy), reads=[pst[0]], writes=[yo_tk[yb_]])
                    P.op("dve", CP(yo[yb_][:, 512:1024], psb[1][:, :]), reads=[pst[1], yo_tk[yb_]], writes=[yo_tk[yb_]])
                    t0 = i * 512 + s * 128
                    P.dma("sp", out_d[t0:t0 + 128, :], yo[yb_], reads=[yo_tk[yb_]])

        def phase(fn, *a):
            top[0] = base_top
            fn(*a)
            P.barrier()

        phase(pass_input)
        has_nsa = any((l % 2 == 1) for l in layers) and "mix" in stages
        if has_nsa:
            phase(pass_nsa_tables)
        cur = 0
        for l in layers:
            jx = l // 2
            if "ffn1" in stages:
                a_, b_, c_ = cur, (cur + 1) % 3, (cur + 2) % 3
                phase(pass_ffn_half, l, 0, 0, X[a_], X[a_], X[b_], "ffn1_norm%d" % l)
                phase(pass_ffn_half, l, 0, 1, X[a_], X[b_], X[c_], "ffn1_norm%d" % l)
                cur = c_
            if "mix" in stages:
                if l % 2 == 0:
                    phase(pass_conv, l, jx, X[cur])
                else:
                    phase(pass_nsa_proj, l, jx, X[cur])
                    if DBG >= 2:
                        for g in range(4 if DBG >= 9 else 1):
                            phase(pass_nsa_group, jx, g)
                    if DBG >= 9:
                        phase(pass_nsa_out, jx, X[cur])
            if "ffn2" in stages:
                a_, b_, c_ = cur, (cur + 1) % 3, (cur + 2) % 3
                phase(pass_ffn_half, l, 1, 0, X[a_], X[a_], X[b_], "ffn2_norm%d" % l)
                phase(pass_ffn_half, l, 1, 1, X[a_], X[b_], X[c_], "ffn2_norm%d" % l)
                cur = c_
            if "ple" in stages:
                phase(pass_ple, l, X[cur])
        phase(pass_final, X[cur], do_final)
        for e_ in ENGS:
            P.op(e_, lambda e: e.nop())
        P.emit()
    return nc


import os
DBG = int(os.environ.get("NSA_DBG", "9"))


def run(inputs, n_cores=8, **bkw):
    shared, vidx = host_prepare(inputs)
    x = np.asarray(inputs["x"], np.float32)
    p = np.asarray(inputs["p"], np.float32)
    shapes = {k: v.shape for k, v in shared.items()}
    shapes["x"] = (S, D)
    shapes["p"] = (4, S, 256)
    nc = bass.Bass("TRN2", target_bir_lowering=False)
    build(nc, shapes, vidx, **bkw)
    in_maps = []
    for b in range(n_cores):
        m = dict(shared)
        m["x"] = np.ascontiguousarray(x[b])
        m["p"] = np.ascontiguousarray(p[:, b])
        in_maps.append(m)
    res = run_bass_kernel_spmd(nc, in_maps, core_ids=list(range(n_cores)))
    return np.stack([np.asarray(r["out"], np.float32) for r in res.results], axis=0)


def kernel(**inputs):
    return run(inputs, n_cores=8)
```

```python
import contextlib
import os
import math
import numpy as np
import concourse.bass as bass
import concourse.mybir as mybir
from concourse.bass_utils import run_bass_kernel_spmd

F32 = mybir.dt.float32
BF16 = mybir.dt.bfloat16
AF = mybir.ActivationFunctionType
ALU = mybir.AluOpType

S = 4096
D = 1024
DFF = 2816
NT = 8
TW_ = 512
ENGS = ["pe", "act", "dve", "pool", "sp"]
DMA_RING = {"sp": 8, "act": 2, "pool": 6, "pe": 2, "dve": 2}


class Tk:
    __slots__ = ("w", "r")

    def __init__(self):
        self.w = None
        self.r = []


def tks(n):
    return [Tk() for _ in range(n)]


class Op:
    __slots__ = ("eng", "fn", "deps", "is_dma", "need_sig", "sigval", "ev")

    def __init__(self, eng, fn, is_dma):
        self.eng = eng
        self.fn = fn
        self.deps = []
        self.is_dma = is_dma
        self.need_sig = False
        self.sigval = None
        self.ev = None


class Prog:
    def __init__(self, nc):
        self.nc = nc
        self.ops = {e: [] for e in ENGS}
        self.last = {e: None for e in ENGS}
        self.dmas_since_barrier = []
        self.pending = {e: [] for e in ENGS}

    def _rec(self, eng, fn, reads, writes, is_dma):
        op = Op(eng, fn, is_dma)
        deps = []
        for t in reads:
            if t.w is not None:
                deps.append((t.w, 0))
        for t in writes:
            if t.w is not None:
                deps.append((t.w, 1))
            for r in t.r:
                deps.append((r, 1))
        for d in self.pending[eng]:
            deps.append((d, 0))
        self.pending[eng] = []
        op.deps = deps
        for t in writes:
            t.w = op
            t.r = []
        for t in reads:
            if t.w is not op:
                t.r.append(op)
        self.ops[eng].append(op)
        self.last[eng] = op
        if is_dma:
            self.dmas_since_barrier.append(op)
        return op

    def op(self, eng, fn, reads=(), writes=()):
        return self._rec(eng, fn, list(reads), list(writes), False)

    def dma(self, eng, out, in_, reads=(), writes=()):
        def fn(e):
            return e.dma_start(out=out, in_=in_)
        return self._rec(eng, fn, list(reads), list(writes), True)

    def barrier(self):
        deps = [o for o in self.last.values() if o is not None] + self.dmas_since_barrier
        self.dmas_since_barrier = []
        for e in ENGS:
            self.pending[e] = list(deps)

    @staticmethod
    def _skip(d, ename, kind):
        if d.eng == ename:
            if ename in ("pe", "sp"):
                return True
            if kind == 1:
                return True
        return False

    def emit(self):
        nc = self.nc
        with contextlib.ExitStack() as st:
            esem = {e: st.enter_context(nc.semaphore("s_" + e)) for e in ENGS}
            rings = {e: [st.enter_context(nc.semaphore("d_%s%d" % (e, i))) for i in range(DMA_RING[e])]
                     for e in ENGS}
            for e in ENGS:
                for op in self.ops[e]:
                    for d, kind in op.deps:
                        if d.is_dma or self._skip(d, e, kind):
                            continue
                        d.need_sig = True
            for e in ENGS:
                c = 0
                ring_cnt = [0] * DMA_RING[e]
                nd = 0
                for op in self.ops[e]:
                    if op.is_dma:
                        slot = nd % DMA_RING[e]
                        prev = ring_cnt[slot] * 16
                        ring_cnt[slot] += 1
                        op.ev = (rings[e][slot], ring_cnt[slot] * 16, prev)
                        nd += 1
                    elif op.need_sig:
                        c += 1
                        op.sigval = c
            if os.environ.get("NSA_VERBOSE"):
                print("ops per engine", {e: len(self.ops[e]) for e in ENGS},
                      "sig counts", {e: max([o.sigval or 0 for o in self.ops[e]] + [0]) for e in ENGS},
                      "ring max", {e: max([o.ev[1] for o in self.ops[e] if o.is_dma] + [0]) for e in ENGS}, flush=True)
            blk = st.enter_context(nc.Block())

            def run(ename, eng):
                known = {}

                def wait(sem, val):
                    k = id(sem)
                    if known.get(k, 0) >= val:
                        return
                    known[k] = val
                    eng.wait_ge(sem, val)
                for op in self.ops[ename]:
                    for d, kind in op.deps:
                        if d.is_dma:
                            wait(d.ev[0], d.ev[1])
                        elif not self._skip(d, ename, kind):
                            wait(esem[d.eng], d.sigval)
                    if op.is_dma:
                        sem, tgt, prev = op.ev
                        if prev > 0:
                            wait(sem, prev)
                        op.fn(eng).then_inc(sem, 16)
                    else:
                        ins = op.fn(eng)
                        if op.need_sig:
                            ins.then_inc(esem[ename], 1)

            @blk.sync
            def _(sync):
                run("sp", sync)

            @blk.tensor
            def _(tensor):
                run("pe", tensor)

            @blk.scalar
            def _(scalar):
                run("act", scalar)

            @blk.vector
            def _(vector):
                run("dve", vector)

            @blk.gpsimd
            def _(gpsimd):
                run("pool", gpsimd)


def MM(out, lhsT, rhs, start, stop):
    return lambda e: e.matmul(out, lhsT=lhsT, rhs=rhs, start=start, stop=stop)


def TRN(out, in_, ident):
    return lambda e: e.transpose(out=out, in_=in_, identity=ident)


def ACTF(out, in_, func, bias=None, scale=None):
    kw = {}
    if bias is not None:
        kw["bias"] = bias
    if scale is not None:
        kw["scale"] = scale
    return lambda e: e.activation(out=out, in_=in_, func=func, **kw)


def TT(out, in0, in1, op):
    return lambda e: e.tensor_tensor(out=out, in0=in0, in1=in1, op=op)


def STT(out, in0, scalar, in1, op0, op1):
    return lambda e: e.scalar_tensor_tensor(out=out, in0=in0, scalar=scalar, in1=in1, op0=op0, op1=op1)


def TS(out, in0, s1, s2, op0, op1=None):
    if op1 is None:
        return lambda e: e.tensor_scalar(out=out, in0=in0, scalar1=s1, scalar2=None, op0=op0)
    return lambda e: e.tensor_scalar(out=out, in0=in0, scalar1=s1, scalar2=s2, op0=op0, op1=op1)


def CP(out, in_):
    return lambda e: e.tensor_copy(out=out, in_=in_)


def MSET(out, v):
    return lambda e: e.memset(out, v)


def RCP(out, in_):
    return lambda e: e.reciprocal(out=out, in_=in_)


NSA_OFF = 4080
NSA_M = 8192
NSAW_OFF = 512
NSAW_M = 2048


def _t5_bucket_np(n):
    n = np.maximum(n, 0)
    nf = np.maximum(n, 1).astype(np.float32)
    large = 16 + (np.log(nf / np.float32(16)) / np.float32(math.log(128.0)) * np.float32(16)).astype(np.int32)
    large = np.minimum(large, 31)
    return np.where(n < 16, n, large)


def lin_layout(w):
    K, M = w.shape
    kc, mc = K // 128, M // 128
    return np.ascontiguousarray(w.reshape(kc, 128, mc, 128).transpose(2, 1, 0, 3).reshape(mc, 128, kc * 128))


def fm(v):
    return np.ascontiguousarray(v.reshape(-1, 128).T)


class VecPack:
    def __init__(self):
        self.cols = []
        self.idx = {}
        self.n = 0

    def add(self, name, arr):
        arr = np.asarray(arr, np.float32)
        assert arr.shape[0] == 128
        self.idx[name] = (self.n, arr.shape[1])
        self.cols.append(arr)
        self.n += arr.shape[1]

    def build(self):
        return np.ascontiguousarray(np.concatenate(self.cols, axis=1))


def host_prepare(inp):
    f = lambda a: np.asarray(a, np.float32)
    shared = {}
    vp = VecPack()
    for l in range(4):
        for nm in ("ffn1_norm", "mix_norm", "ffn2_norm", "ple_norm"):
            vp.add("%s%d" % (nm, l), fm(f(inp[nm])[l]))
        for fi, pre in enumerate(("ffn1", "ffn2")):
            wg = f(inp[pre + "_w_gate"])[l]
            wu = f(inp[pre + "_w_up"])[l]
            wd = f(inp[pre + "_w_down"])[l]
            for hf in range(2):
                cs = slice(hf * 1408, (hf + 1) * 1408)
                shared["wg_%d_%d_%d" % (l, fi, hf)] = lin_layout(wg[:, cs])
                shared["wu_%d_%d_%d" % (l, fi, hf)] = lin_layout(wu[:, cs])
                shared["wd_%d_%d_%d" % (l, fi, hf)] = lin_layout(wd[cs, :])
        shared["pleg_%d" % l] = lin_layout(f(inp["ple_w_gate"])[l])
        shared["plei_%d" % l] = lin_layout(f(inp["ple_w_in"])[l])
    vp.add("final_norm", fm(f(inp["final_norm"])))
    for j in range(2):
        shared["pw1_%d" % j] = lin_layout(f(inp["conv_w_pw1"])[j])
        shared["pw2_%d" % j] = lin_layout(f(inp["conv_w_pw2"])[j])
        vp.add("b_pw1_%d" % j, fm(f(inp["conv_b_pw1"])[j]))
        vp.add("b_dw_%d" % j, fm(f(inp["conv_b_dw"])[j]))
        vp.add("ln_g_%d" % j, fm(f(inp["conv_ln_g"])[j]))
        vp.add("ln_b_%d" % j, fm(f(inp["conv_ln_b"])[j]))
        vp.add("b_pw2_%d" % j, fm(f(inp["conv_b_pw2"])[j]))
        wdw = f(inp["conv_w_dw"])[j]
        vp.add("w_dw_%d" % j, np.ascontiguousarray(wdw.reshape(31, 8, 128).transpose(2, 1, 0).reshape(128, 248)))
        w_in = f(inp["nsa_w_in"])[j]
        cols = [np.arange(1024)]
        for g in range(4):
            for kind in (0, 1, 2, 4):
                c = 1024 + kind * 256 + g * 64 + np.arange(64)
                cols.append(np.concatenate([c, c]))
        cols = np.concatenate(cols)
        wcat = np.concatenate([w_in[:, cols], w_in[:, 2560:2608], np.zeros((1024, 80), np.float32)], axis=1)
        shared["nsa_wf_%d" % j] = lin_layout(wcat)
        tc = np.concatenate([1024 + 3 * 256 + np.arange(256), 1024 + 5 * 256 + np.arange(256), 2560 + np.arange(48)])
        shared["nsa_wt_%d" % j] = np.ascontiguousarray(w_in[:, tc].reshape(8, 128, 560).transpose(1, 0, 2))
        shared["nsa_wo_%d" % j] = lin_layout(f(inp["nsa_w_out"])[j])
        for nm, src in (("wk1", "nsa_cmp_wk1"), ("wv1", "nsa_cmp_wv1")):
            w1 = f(inp[src])[j]
            shared["%s_%d" % (nm, j)] = np.ascontiguousarray(w1.reshape(32, 64, 256).transpose(1, 0, 2))
        w2k = f(inp["nsa_cmp_wk2"])[j]
        w2kd = np.concatenate([w2k, w2k], axis=1)
        shared["w2k_%d" % j] = np.ascontiguousarray(w2kd.reshape(2, 128, 128).transpose(1, 0, 2))
        w2v = f(inp["nsa_cmp_wv2"])[j]
        shared["w2v_%d" % j] = np.ascontiguousarray(w2v.reshape(2, 128, 64).transpose(1, 0, 2))
        shared["posk_%d" % j] = np.ascontiguousarray(f(inp["nsa_cmp_pos_k"])[j].T)
        shared["posv_%d" % j] = np.ascontiguousarray(f(inp["nsa_cmp_pos_v"])[j].T)
    rb = f(inp["rel_bias"])
    ext = np.concatenate([rb, np.full((1, 16), -30000.0, np.float32)], axis=0)
    dist = np.arange(NSA_M) - NSA_OFF
    idx = np.where(dist >= 0, _t5_bucket_np(dist), 32)
    shared["gvec"] = np.ascontiguousarray(ext[idx].T)
    distw = np.arange(NSAW_M) - NSAW_OFF
    idxw = np.where((distw >= 0) & (distw < 512), _t5_bucket_np(distw), 32)
    shared["gwvec"] = np.ascontiguousarray(ext[idxw].T)
    vp.add("eps_rms", np.full((128, 1), 1e-6, np.float32))
    vp.add("eps_ln", np.full((128, 1), 1e-5, np.float32))
    vp.add("eps_z", np.full((128, 1), 1e-30, np.float32))
    shared["vecs"] = vp.build()
    shared["ident"] = np.eye(128, dtype=np.float32)
    t = np.arange(S)
    j = np.arange(64)[None, :]
    cur = (t // 64)[:, None]
    valid = (j * 64 <= t[:, None])
    forced = (j == 0) | (j == cur) | (j == cur - 1)
    vm = (valid & ~forced).astype(np.float32)
    am = np.where(forced, 1e9, np.where(valid, 0.0, -1.0)).astype(np.float32)
    shared["vmask"] = np.ascontiguousarray(vm.reshape(32, 128, 64).transpose(1, 0, 2))
    shared["amask"] = np.ascontiguousarray(am.reshape(32, 128, 64).transpose(1, 0, 2))
    c = np.arange(256)[:, None]
    ov = ((c * 16 < j * 64 + 64) & (c * 16 + 32 > j * 64) & (c < 255)).astype(np.float32)
    shared["overlap"] = np.ascontiguousarray(ov.reshape(2, 128, 64).transpose(1, 0, 2))
    ex = np.zeros((64, 32, 128), np.float32)
    for kt in range(32):
        ex[2 * kt, kt, 0:64] = 1.0
        ex[2 * kt + 1, kt, 64:128] = 1.0
    shared["expand"] = ex
    return shared, vp.idx


ARENA_F32 = 47600


class Ctx:
    pass


def build(nc, shapes, vidx, layers=(0, 1, 2, 3), stages=("ffn1", "mix", "ffn2", "ple"), do_final=True):
    P = Prog(nc)
    C = Ctx()
    din = {}
    for name, shp in shapes.items():
        din[name] = nc.dram_tensor(name, list(shp), F32, kind="ExternalInput").ap()
    out_d = nc.dram_tensor("out", [S, D], F32, kind="ExternalOutput").ap()

    def scratch(name, shape, dt):
        return nc.dram_tensor(name, list(shape), dt, kind="Internal").ap()
    X = [scratch("xs%d" % i, [8, 128, S], F32) for i in range(3)]
    qkT_d = scratch("qkT", [24, 128, S], BF16)
    vtok_d = scratch("vtok", [8, 128, 32, 65], BF16)
    gT_d = scratch("gT", [48, S], F32)
    fr_d = scratch("frow", [3, 512], F32)
    oc_d = scratch("ocT", [4, 64, S], F32)
    frd_tk = tks(3)
    oT_d = scratch("oT", [8, 128, S], BF16)
    grep_d = scratch("grep", [16, 128 * NSA_M], F32)
    gwrep_d = scratch("gwrep", [16, 128 * NSAW_M], F32)

    with contextlib.ExitStack() as st:
        arena = st.enter_context(nc.sbuf_tensor("arena", [128, ARENA_F32], F32))
        psb = [st.enter_context(nc.psum_tensor("psb%d" % i, [128, 512], F32)) for i in range(8)]
        pst = tks(8)
        top = [0]

        def alloc(nfree, dt=F32):
            n32 = nfree if dt == F32 else (nfree + 1) // 2
            assert top[0] + n32 <= ARENA_F32, ("arena overflow", top[0], n32)
            a = arena[:, top[0]:top[0] + n32]
            top[0] += n32
            if dt != F32:
                a = a.bitcast(dt)
                a = a[:, 0:nfree]
            return a

        def a3(nfree, dt, **kw):
            pat = kw.pop("pat")
            return alloc(nfree, dt).rearrange(pat, **kw)

        nv = shapes["vecs"][1]
        vecs = alloc(nv)
        vecs_tk = Tk()
        P.dma("sp", vecs, din["vecs"], writes=[vecs_tk])
        ident = alloc(128)
        ident_tk = Tk()
        P.dma("sp", ident, din["ident"], writes=[ident_tk])
        ones_bf = alloc(128, BF16)
        ones_tk = Tk()
        P.op("dve", MSET(ones_bf, 1.0), writes=[ones_tk])
        ident_bf = alloc(128, BF16)
        identbf_tk = Tk()
        P.op("dve", CP(ident_bf, ident), reads=[ident_tk], writes=[identbf_tk])
        base_top = top[0]

        def V(name, c0=0, n=None):
            o, w = vidx[name]
            if n is None:
                n = w - c0
            return vecs[:, o + c0:o + c0 + n]

        def xtile_ap(Xd, i):
            return Xd[:, :, i * TW_:(i + 1) * TW_].rearrange("c p t -> p c t")

        def load_w(dst, src, mc, wt):
            for m in range(mc):
                P.dma("pool", dst[:, m, :], src[m], writes=[wt[m]])

        def rmsnorm(xt, xtk, gain, hT, htk, sq, sqtk, rstd, rstd_tk, out_f32=None):
            for c in range(8):
                sl = c % 2
                P.op("act", ACTF(sq[:, sl, :], xt[:, c, :], AF.Square), reads=[xtk[c]], writes=[sqtk[sl]])
                P.op("pe", MM(psb[6][:, :], ones_bf, sq[:, sl, :], c == 0, c == 7),
                     reads=[sqtk[sl], ones_tk], writes=[pst[6]])
            P.op("act", ACTF(rstd, psb[6][:, :], AF.Sqrt, bias=V("eps_rms"), scale=1.0 / D),
                 reads=[pst[6], vecs_tk], writes=[rstd_tk])
            P.op("dve", RCP(rstd, rstd), reads=[rstd_tk], writes=[rstd_tk])
            for c in range(8):
                P.op("dve", STT(hT[:, c, :], xt[:, c, :], gain[:, c:c + 1], rstd, ALU.mult, ALU.mult),
                     reads=[xtk[c], rstd_tk, vecs_tk], writes=[htk[c]])

        def pass_input():
            NB = 4
            xin = [alloc(D) for _ in range(NB)]
            xin_tk = tks(NB)
            xo = [a3(8 * 128, F32, pat="p (c t) -> p c t", c=8) for _ in range(NB)]
            xo_tk = tks(NB)
            for tt in range(32):
                b = tt % NB
                P.dma("sp", xin[b], din["x"][tt * 128:(tt + 1) * 128, :], writes=[xin_tk[b]])
                for c in range(8):
                    bank = 2 * b + c // 4
                    P.op("pe", TRN(psb[bank][:, (c % 4) * 128:(c % 4 + 1) * 128], xin[b][:, c * 128:(c + 1) * 128], ident),
                         reads=[xin_tk[b], ident_tk], writes=[pst[bank]])
                P.op("act", ACTF(xo[b][:, 0:4, :], psb[2 * b][:, :].rearrange("p (c t) -> p c t", c=4), AF.Copy),
                     reads=[pst[2 * b]], writes=[xo_tk[b]])
                P.op("dve", CP(xo[b][:, 4:8, :], psb[2 * b + 1][:, :].rearrange("p (c t) -> p c t", c=4)),
                     reads=[pst[2 * b + 1], xo_tk[b]], writes=[xo_tk[b]])
                P.dma("sp", X[0][:, :, tt * 128:(tt + 1) * 128].rearrange("c p t -> p c t"), xo[b], reads=[xo_tk[b]])

        def pass_ffn_half(l, fi, hf, Xn, Xr, Xo, norm_name):
            wg = a3(11 * 1024, BF16, pat="p (m f) -> p m f", m=11)
            wu = a3(11 * 1024, BF16, pat="p (m f) -> p m f", m=11)
            wd = a3(8 * 1408, BF16, pat="p (m f) -> p m f", m=8)
            wg_tk, wu_tk, wd_tk = tks(11), tks(11), tks(8)
            key = "%d_%d_%d" % (l, fi, hf)
            for m in range(11):
                P.dma("pool", wg[:, m, :], din["wg_" + key][m], writes=[wg_tk[m]])
                P.dma("pool", wu[:, m, :], din["wu_" + key][m], writes=[wu_tk[m]])
            load_w(wd, din["wd_" + key], 8, wd_tk)
            same = Xn is Xr
            xn = [a3(8 * 512, F32, pat="p (c t) -> p c t", c=8) for _ in range(2)]
            xn_tk = [tks(8) for _ in range(2)]
            if same:
                xr, xr_tk = xn, xn_tk
            else:
                xr = [a3(8 * 512, F32, pat="p (c t) -> p c t", c=8) for _ in range(2)]
                xr_tk = [tks(8) for _ in range(2)]
            hTs = [a3(8 * 512, BF16, pat="p (c t) -> p c t", c=8) for _ in range(2)]
            htks = [tks(8) for _ in range(2)]
            sq = a3(2 * 512, BF16, pat="p (c t) -> p c t", c=2)
            sqtk = tks(2)
            rstd = alloc(512)
            rstd_tk = Tk()
            act_ = a3(11 * 512, BF16, pat="p (c t) -> p c t", c=11)
            atk = tks(11)
            sg = [alloc(512) for _ in range(2)]
            sgtk = tks(2)
            gain = V(norm_name)

            def load(i):
                b = i % 2
                P.dma("sp", xn[b], xtile_ap(Xn, i), writes=xn_tk[b])
                if not same:
                    P.dma("sp", xr[b], xtile_ap(Xr, i), writes=xr_tk[b])
            load(0)
            rmsnorm(xn[0], xn_tk[0], gain, hTs[0], htks[0], sq, sqtk, rstd, rstd_tk)
            for i in range(NT):
                b = i % 2
                hT, htk = hTs[b], htks[b]
                if i + 1 < NT:
                    load(i + 1)
                for j in range(11):
                    pg, pu = j % 2, 2 + j % 2
                    for c in range(8):
                        P.op("pe", MM(psb[pg][:, :], wg[:, j, c * 128:(c + 1) * 128], hT[:, c, :], c == 0, c == 7),
                             reads=[wg_tk[j], htk[c]], writes=[pst[pg]])
                    for c in range(8):
                        P.op("pe", MM(psb[pu][:, :], wu[:, j, c * 128:(c + 1) * 128], hT[:, c, :], c == 0, c == 7),
                             reads=[wu_tk[j], htk[c]], writes=[pst[pu]])
                    P.op("act", ACTF(sg[j % 2], psb[pg][:, :], AF.Silu), reads=[pst[pg]], writes=[sgtk[j % 2]])
                    P.op("dve", TT(act_[:, j, :], sg[j % 2], psb[pu][:, :], ALU.mult),
                         reads=[sgtk[j % 2], pst[pu]], writes=[atk[j]])
                if i + 1 < NT:
                    rmsnorm(xn[1 - b], xn_tk[1 - b], gain, hTs[1 - b], htks[1 - b], sq, sqtk, rstd, rstd_tk)
                for m in range(8):
                    py = 4 + m % 2
                    for j in range(11):
                        P.op("pe", MM(psb[py][:, :], wd[:, m, j * 128:(j + 1) * 128], act_[:, j, :], j == 0, j == 10),
                             reads=[wd_tk[m], atk[j]], writes=[pst[py]])
                    P.op("dve", STT(xr[b][:, m, :], psb[py][:, :], 0.5, xr[b][:, m, :], ALU.mult, ALU.add),
                         reads=[pst[py], xr_tk[b][m]], writes=[xr_tk[b][m]])
                P.dma("sp", xtile_ap(Xo, i), xr[b], reads=xr_tk[b])

        def pass_ple(l, Xc):
            wgp = a3(8 * 1024, BF16, pat="p (m f) -> p m f", m=8)
            wip = a3(8 * 256, BF16, pat="p (m f) -> p m f", m=8)
            wgp_tk, wip_tk = tks(8), tks(8)
            load_w(wgp, din["pleg_%d" % l], 8, wgp_tk)
            load_w(wip, din["plei_%d" % l], 8, wip_tk)
            xn = [a3(8 * 512, F32, pat="p (c t) -> p c t", c=8) for _ in range(2)]
            xn_tk = [tks(8) for _ in range(2)]
            pin = [a3(4 * 256, F32, pat="p (s f) -> p s f", s=4) for _ in range(2)]
            pin_tk = tks(2)
            pT = a3(2 * 512, BF16, pat="p (c t) -> p c t", c=2)
            pT_tk = tks(2)
            hT = a3(8 * 512, BF16, pat="p (c t) -> p c t", c=8)
            htk = tks(8)
            sq = a3(2 * 512, BF16, pat="p (c t) -> p c t", c=2)
            sqtk = tks(2)
            rstd = alloc(512)
            rstd_tk = Tk()
            sg = [alloc(512) for _ in range(2)]
            sgtk = tks(2)
            gain = V("ple_norm%d" % l)
            pl = din["p"][l]

            def load(i):
                b = i % 2
                P.dma("sp", xn[b], xtile_ap(Xc, i), writes=xn_tk[b])
                P.dma("sp", pin[b], pl[i * 512:(i + 1) * 512, :].rearrange("(s p) f -> p s f", p=128), writes=[pin_tk[b]])
            load(0)
            for i in range(NT):
                b = i % 2
                if i + 1 < NT:
                    load(i + 1)
                rmsnorm(xn[b], xn_tk[b], gain, hT, htk, sq, sqtk, rstd, rstd_tk)
                for kc in range(2):
                    for s in range(4):
                        P.op("pe", TRN(psb[7][:, s * 128:(s + 1) * 128], pin[b][:, s, kc * 128:(kc + 1) * 128], ident),
                             reads=[pin_tk[b], ident_tk], writes=[pst[7]])
                    P.op("act", ACTF(pT[:, kc, :], psb[7][:, :], AF.Copy), reads=[pst[7]], writes=[pT_tk[kc]])
                for m in range(8):
                    pg, pi = m % 2, 2 + m % 2
                    for c in range(8):
                        P.op("pe", MM(psb[pg][:, :], wgp[:, m, c * 128:(c + 1) * 128], hT[:, c, :], c == 0, c == 7),
                             reads=[wgp_tk[m], htk[c]], writes=[pst[pg]])
                    for c in range(2):
                        P.op("pe", MM(psb[pi][:, :], wip[:, m, c * 128:(c + 1) * 128], pT[:, c, :], c == 0, c == 1),
                             reads=[wip_tk[m], pT_tk[c]], writes=[pst[pi]])
                    P.op("act", ACTF(sg[m % 2], psb[pg][:, :], AF.Sigmoid), reads=[pst[pg]], writes=[sgtk[m % 2]])
                    P.op("dve", TT(sg[m % 2], sg[m % 2], psb[pi][:, :], ALU.mult),
                         reads=[sgtk[m % 2], pst[pi]], writes=[sgtk[m % 2]])
                    P.op("dve", TT(xn[b][:, m, :], xn[b][:, m, :], sg[m % 2], ALU.add),
                         reads=[sgtk[m % 2], xn_tk[b][m]], writes=[xn_tk[b][m]])
                P.dma("sp", xtile_ap(Xc, i), xn[b], reads=xn_tk[b])

        def pass_conv(l, jx, Xc):
            w1 = a3(16 * 1024, BF16, pat="p (m f) -> p m f", m=16)
            w2 = a3(8 * 1024, BF16, pat="p (m f) -> p m f", m=8)
            w1_tk, w2_tk = tks(16), tks(8)
            load_w(w1, din["pw1_%d" % jx], 16, w1_tk)
            load_w(w2, din["pw2_%d" % jx], 8, w2_tk)
            diag = a3(31 * 8 * 128, BF16, pat="p (j c m) -> p j c m", j=31, c=8)
            diag_tk = tks(8)
            wdw = V("w_dw_%d" % jx)
            for c in range(8):
                for j in range(31):
                    P.op("dve", TS(diag[:, j, c, :], ident_bf, wdw[:, c * 31 + j:c * 31 + j + 1], None, ALU.mult),
                         reads=[identbf_tk, vecs_tk], writes=[diag_tk[c]])
            xn = [a3(8 * 512, F32, pat="p (c t) -> p c t", c=8)] * 2
            xn_tk = [tks(8)] * 2
            hT = a3(8 * 512, BF16, pat="p (c t) -> p c t", c=8)
            htk = tks(8)
            sq = a3(2 * 512, BF16, pat="p (c t) -> p c t", c=2)
            sqtk = tks(2)
            rstd = alloc(512)
            rstd_tk = Tk()
            ub = a3(8 * 542, BF16, pat="p (c t) -> p c t", c=8)
            ub_tk = tks(8)
            yb = a3(8 * 512, F32, pat="p (c t) -> p c t", c=8)
            yb_tk = tks(8)
            ybf = a3(2 * 512, BF16, pat="p (c t) -> p c t", c=2)
            ybf_tk = tks(2)
            ysq = a3(2 * 512, BF16, pat="p (c t) -> p c t", c=2)
            ysq_tk = tks(2)
            sg = [alloc(512) for _ in range(2)]
            sgtk = tks(2)
            mu = alloc(512)
            mu_tk = Tk()
            rs = alloc(512)
            rs_tk = Tk()
            tmp = alloc(512)
            tmp_tk = Tk()
            gain = V("mix_norm%d" % l)
            b1 = V("b_pw1_%d" % jx)
            bdw = V("b_dw_%d" % jx)
            lng = V("ln_g_%d" % jx)
            lnb = V("ln_b_%d" % jx)
            b2 = V("b_pw2_%d" % jx)
            for c in range(8):
                P.op("dve", MSET(ub[:, c, 0:30], 0.0), writes=[ub_tk[c]])

            def load(i):
                b = i % 2
                P.dma("sp", xn[b], xtile_ap(Xc, i), writes=xn_tk[b])
            for i in range(NT):
                b = i % 2
                load(i)
                rmsnorm(xn[b], xn_tk[b], gain, hT, htk, sq, sqtk, rstd, rstd_tk)
                for m in range(8):
                    pa, pg = m % 2, 2 + m % 2
                    for c in range(8):
                        P.op("pe", MM(psb[pa][:, :], w1[:, m, c * 128:(c + 1) * 128], hT[:, c, :], c == 0, c == 7),
                             reads=[w1_tk[m], htk[c]], writes=[pst[pa]])
                    for c in range(8):
                        P.op("pe", MM(psb[pg][:, :], w1[:, 8 + m, c * 128:(c + 1) * 128], hT[:, c, :], c == 0, c == 7),
                             reads=[w1_tk[8 + m], htk[c]], writes=[pst[pg]])
                    P.op("act", ACTF(sg[m % 2], psb[pg][:, :], AF.Sigmoid, bias=b1[:, 8 + m:9 + m]),
                         reads=[pst[pg], vecs_tk], writes=[sgtk[m % 2]])
                    P.op("dve", STT(ub[:, m, 30:542], psb[pa][:, :], b1[:, m:m + 1], sg[m % 2], ALU.add, ALU.mult),
                         reads=[pst[pa], sgtk[m % 2], vecs_tk], writes=[ub_tk[m]])
                for m in range(8):
                    py = 4 + m % 2
                    for j in range(31):
                        P.op("pe", MM(psb[py][:, :], diag[:, j, m, :], ub[:, m, j:j + 512], j == 0, j == 30),
                             reads=[diag_tk[m], ub_tk[m]], writes=[pst[py]])
                    P.op("act", ACTF(yb[:, m, :], psb[py][:, :], AF.Identity, bias=bdw[:, m:m + 1]),
                         reads=[pst[py], vecs_tk], writes=[yb_tk[m]])
                    P.op("dve", CP(ub[:, m, 0:30], ub[:, m, 512:542]), reads=[ub_tk[m]], writes=[ub_tk[m]])
                    sl = m % 2
                    P.op("dve", CP(ybf[:, sl, :], yb[:, m, :]), reads=[yb_tk[m]], writes=[ybf_tk[sl]])
                    P.op("act", ACTF(ysq[:, sl, :], yb[:, m, :], AF.Square), reads=[yb_tk[m]], writes=[ysq_tk[sl]])
                    P.op("pe", MM(psb[6][:, :], ones_bf, ybf[:, sl, :], m == 0, m == 7),
                         reads=[ybf_tk[sl], ones_tk], writes=[pst[6]])
                    P.op("pe", MM(psb[7][:, :], ones_bf, ysq[:, sl, :], m == 0, m == 7),
                         reads=[ysq_tk[sl], ones_tk], writes=[pst[7]])
                P.op("dve", TS(mu, psb[6][:, :], 1.0 / D, None, ALU.mult), reads=[pst[6]], writes=[mu_tk])
                P.op("dve", TT(tmp, mu, mu, ALU.mult), reads=[mu_tk], writes=[tmp_tk])
                P.op("dve", STT(tmp, psb[7][:, :], 1.0 / D, tmp, ALU.mult, ALU.subtract),
                     reads=[pst[7], tmp_tk], writes=[tmp_tk])
                P.op("dve", TS(tmp, tmp, 0.0, None, ALU.max), reads=[tmp_tk], writes=[tmp_tk])
                P.op("act", ACTF(rs, tmp, AF.Sqrt, bias=V("eps_ln"), scale=1.0), reads=[tmp_tk, vecs_tk], writes=[rs_tk])
                P.op("dve", RCP(rs, rs), reads=[rs_tk], writes=[rs_tk])
                P.op("dve", STT(mu, mu, -1.0, rs, ALU.mult, ALU.mult), reads=[mu_tk, rs_tk], writes=[mu_tk])
                for m in range(8):
                    P.op("dve", TT(yb[:, m, :], yb[:, m, :], rs, ALU.mult), reads=[yb_tk[m], rs_tk], writes=[yb_tk[m]])
                    P.op("dve", TT(yb[:, m, :], yb[:, m, :], mu, ALU.add), reads=[yb_tk[m], mu_tk], writes=[yb_tk[m]])
                    P.op("act", ACTF(hT[:, m, :], yb[:, m, :], AF.Silu, bias=lnb[:, m:m + 1], scale=lng[:, m:m + 1]),
                         reads=[yb_tk[m], vecs_tk], writes=[htk[m]])
                for m in range(8):
                    po = m % 2
                    for c in range(8):
                        P.op("pe", MM(psb[po][:, :], w2[:, m, c * 128:(c + 1) * 128], hT[:, c, :], c == 0, c == 7),
                             reads=[w2_tk[m], htk[c]], writes=[pst[po]])
                    P.op("dve", STT(xn[b][:, m, :], psb[po][:, :], b2[:, m:m + 1], xn[b][:, m, :], ALU.add, ALU.add),
                         reads=[pst[po], xn_tk[b][m], vecs_tk], writes=[xn_tk[b][m]])
                P.dma("sp", xtile_ap(Xc, i), xn[b], reads=xn_tk[b])

        def pass_nsa_tables():
            for h in range(16):
                src = bass.AP(tensor=din["gvec"].tensor, offset=h * NSA_M, ap=[[0, 128], [1, NSA_M]])
                dst = bass.AP(tensor=grep_d.tensor, offset=h * 128 * NSA_M, ap=[[NSA_M, 128], [1, NSA_M]])
                P.dma("sp", dst, src)
                src = bass.AP(tensor=din["gwvec"].tensor, offset=h * NSAW_M, ap=[[0, 128], [1, NSAW_M]])
                dst = bass.AP(tensor=gwrep_d.tensor, offset=h * 128 * NSAW_M, ap=[[NSAW_M, 128], [1, NSAW_M]])
                P.dma("sp", dst, src)

        def pass_nsa_proj(l, jx, Xc):
            wf = a3(25 * 1024, BF16, pat="p (m f) -> p m f", m=25)
            wf_tk = tks(25)
            load_w(wf, din["nsa_wf_%d" % jx], 25, wf_tk)
            wt = a3(8 * 560, BF16, pat="p (c f) -> p c f", c=8)
            wt_tk = Tk()
            P.dma("pool", wt, din["nsa_wt_%d" % jx], writes=[wt_tk])
            xn = [a3(8 * 512, F32, pat="p (c t) -> p c t", c=8) for _ in range(2)]
            xn_tk = [tks(8) for _ in range(2)]
            hT = a3(8 * 512, BF16, pat="p (c t) -> p c t", c=8)
            htk = tks(8)
            sq = a3(2 * 512, BF16, pat="p (c t) -> p c t", c=2)
            sqtk = tks(2)
            rstd = alloc(512)
            rstd_tk = Tk()
            stg = [a3(24 * 512, BF16, pat="p (m t) -> p m t", m=24) for _ in range(2)]
            stg_tk = tks(2)
            vst = [a3(4 * 8 * 65, BF16, pat="p (s k e) -> p s k e", s=4, k=8) for _ in range(2)]
            vst_tk = tks(2)
            gst = [alloc(512) for _ in range(2)]
            gst_tk = tks(2)
            gain = V("mix_norm%d" % l)
            for b in range(2):
                P.op("dve", MSET(vst[b], 1.0), writes=[vst_tk[b]])

            def load(i):
                b = i % 2
                P.dma("sp", xn[b], xtile_ap(Xc, i), writes=xn_tk[b])
            load(0)
            for i in range(NT):
                b = i % 2
                if i + 1 < NT:
                    load(i + 1)
                rmsnorm(xn[b], xn_tk[b], gain, hT, htk, sq, sqtk, rstd, rstd_tk)
                for m in range(25):
                    pb = m % 4
                    for c in range(8):
                        P.op("pe", MM(psb[pb][:, :], wf[:, m, c * 128:(c + 1) * 128], hT[:, c, :], c == 0, c == 7),
                             reads=[wf_tk[m], htk[c]], writes=[pst[pb]])
                    if m == 24:
                        P.op("act", ACTF(gst[b], psb[pb][:, :], AF.Sigmoid), reads=[pst[pb]], writes=[gst_tk[b]])
                        P.dma("sp", gT_d[:, i * 512:(i + 1) * 512], gst[b][0:48, :], reads=[gst_tk[b]])
                    elif m % 2 == 0:
                        P.op("act", ACTF(stg[b][:, m, :], psb[pb][:, :], AF.Copy), reads=[pst[pb]], writes=[stg_tk[b]])
                    else:
                        P.op("dve", CP(stg[b][:, m, :], psb[pb][:, :]), reads=[pst[pb]], writes=[stg_tk[b]])
                P.dma("sp", qkT_d[:, :, i * 512:(i + 1) * 512].rearrange("m p t -> p m t"), stg[b], reads=[stg_tk[b]])
                for s in range(4):
                    pv = 4 + s % 2
                    for c in range(8):
                        P.op("pe", MM(psb[pv][:, :], hT[:, c, s * 128:(s + 1) * 128], wt[:, c, 0:512], c == 0, c == 7),
                             reads=[wt_tk, htk[c]], writes=[pst[pv]])
                    P.op("dve", CP(vst[b][:, s, :, 0:64], psb[pv][:, :].rearrange("p (k e) -> p k e", k=8)),
                         reads=[pst[pv]], writes=[vst_tk[b]])
                for s in range(4):
                    P.dma("sp", vtok_d[:, :, i * 4 + s, :].rearrange("k p e -> p k e"), vst[b][:, s, :, :], reads=[vst_tk[b]])

        def pass_nsa_group(jx, g):
            ksx = alloc(S, BF16)
            ksx_tk, ex_tk = Tk(), Tk()
            P.dma("sp", ksx[0:64], qkT_d[8 + 4 * g + 2, 0:64, :], writes=[ksx_tk])
            P.dma("pool", ksx[64:128], din["expand"].rearrange("n k m -> n (k m)"), writes=[ex_tk])
            kw = alloc(S, BF16)
            kw_tk = Tk()
            P.dma("sp", kw[0:64], qkT_d[8 + 4 * g + 3, 0:64, :], writes=[kw_tk])
            qx = [alloc(S, BF16) for _ in range(2)]
            qx_tk = tks(2)
            nmq_tk = [tks(8) for _ in range(2)]
            vs = a3(32 * 65, BF16, pat="p (s e) -> p s e", s=32)
            vw = a3(32 * 65, BF16, pat="p (s e) -> p s e", s=32)
            vs_tk, vw_tk = Tk(), Tk()
            P.dma("sp", vs, vtok_d[g], writes=[vs_tk])
            P.dma("sp", vw, vtok_d[4 + g], writes=[vw_tk])
            kcT = alloc(256, BF16)
            kcT_tk = Tk()
            vca = a3(2 * 66, BF16, pat="p (c e) -> p c e", c=2)
            vca_tk = Tk()
            ovl1 = a3(2 * 66, BF16, pat="p (c e) -> p c e", c=2)
            ovl1_tk = Tk()
            ovl = a3(2 * 64, F32, pat="p (c e) -> p c e", c=2)
            ovl_tk = Tk()
            P.dma("sp", ovl, din["overlap"], writes=[ovl_tk])
            imp = a3(32 * 64, F32, pat="p (s f) -> p s f", s=32)
            imp_tk = tks(32)
            NSL = 6
            SB = [0, 1, 2, 3, 4, 7]
            Ef = [alloc(512) for _ in range(NSL)]
            Ef_tk = tks(NSL)
            Eb = [alloc(512, BF16) for _ in range(NSL)]
            Eb_tk = tks(NSL)
            sm = alloc(8)
            sm_tk = tks(8)
            imt = [a3(4 * 64, F32, pat="p (s f) -> p s f", s=4) for _ in range(2)]
            imt_tk = tks(2)
            grp_top = top[0]

            kc = alloc(S, BF16)
            vc = alloc(S, BF16)
            kc_tk, vc_tk = Tk(), Tk()
            P.dma("sp", kc[0:64], qkT_d[8 + 4 * g + 0, 0:64, :], writes=[kc_tk])
            P.dma("sp", vc[0:64], qkT_d[8 + 4 * g + 1, 0:64, :], writes=[vc_tk])
            wk1 = a3(32 * 256, BF16, pat="p (l j) -> p l j", l=32)
            wv1 = a3(32 * 256, BF16, pat="p (l j) -> p l j", l=32)
            wk1_tk, wv1_tk = Tk(), Tk()
            P.dma("pool", wk1[0:64], din["wk1_%d" % jx], writes=[wk1_tk])
            P.dma("pool", wv1[0:64], din["wv1_%d" % jx], writes=[wv1_tk])
            w2k = a3(2 * 128, BF16, pat="p (c m) -> p c m", c=2)
            w2v = a3(2 * 64, BF16, pat="p (c m) -> p c m", c=2)
            w2k_tk, w2v_tk = Tk(), Tk()
            P.dma("pool", w2k, din["w2k_%d" % jx], writes=[w2k_tk])
            P.dma("pool", w2v, din["w2v_%d" % jx], writes=[w2v_tk])
            posk = alloc(32, BF16)
            posv = alloc(32, BF16)
            posk_tk, posv_tk = Tk(), Tk()
            P.dma("pool", posk[0:64], din["posk_%d" % jx], writes=[posk_tk])
            P.dma("pool", posv[0:64], din["posv_%d" % jx], writes=[posv_tk])
            cb = alloc(4)
            cb_tk = Tk()
            xg = a3(2 * 256, F32, pat="p (c t) -> p c t", c=2)
            xg_tk = tks(2)
            t1 = a3(2 * 256, F32, pat="p (c t) -> p c t", c=2)
            t1_tk = tks(2)
            gl = a3(2 * 256, BF16, pat="p (c t) -> p c t", c=2)
            gl_tk = tks(2)
            P.op("dve", MSET(kcT, 0.0), writes=[kcT_tk])
            P.op("dve", MSET(vca, 0.0), writes=[vca_tk])
            P.op("dve", MSET(vca[:, :, 64:65], 1.0), writes=[vca_tk])
            P.op("dve", MSET(ovl1[:, :, 64:65], 1.0), writes=[ovl1_tk])
            P.op("dve", CP(ovl1[:, :, 0:64], ovl), reads=[ovl_tk], writes=[ovl1_tk])
            for which, (src, src_tk, w1s, w1_tk, pos, pos_tk) in enumerate(
                    ((kc, kc_tk, wk1, wk1_tk, posk, posk_tk), (vc, vc_tk, wv1, wv1_tk, posv, posv_tk))):
                for jc in range(2):
                    for l_ in range(32):
                        P.op("pe", MM(psb[4][:, jc:jc + 1], w1s[0:64, l_, jc * 128:(jc + 1) * 128], pos[0:64, l_:l_ + 1],
                                      l_ == 0, l_ == 31), reads=[w1_tk, pos_tk], writes=[pst[4]])
                P.op("dve", CP(cb[:, 0:2], psb[4][:, 0:2]), reads=[pst[4]], writes=[cb_tk])
                for jc in range(2):
                    pb = 5 + jc
                    for l_ in range(32):
                        P.op("pe", MM(psb[pb][:, 0:255], w1s[0:64, l_, jc * 128:(jc + 1) * 128],
                                      src[0:64, l_:l_ + 16 * 254 + 1:16], l_ == 0, l_ == 31),
                             reads=[w1_tk, src_tk], writes=[pst[pb]])
                    P.op("act", ACTF(xg[:, jc, 0:255], psb[pb][:, 0:255], AF.Identity, bias=cb[:, jc:jc + 1]),
                         reads=[pst[pb], cb_tk], writes=[xg_tk[jc]])
                    P.op("dve", TT(t1[:, jc, 0:255], xg[:, jc, 0:255], xg[:, jc, 0:255], ALU.mult),
                         reads=[xg_tk[jc]], writes=[t1_tk[jc]])
                    P.op("dve", TS(t1[:, jc, 0:255], t1[:, jc, 0:255], 0.044715, 1.0, ALU.mult, ALU.add),
                         reads=[t1_tk[jc]], writes=[t1_tk[jc]])
                    P.op("dve", TT(t1[:, jc, 0:255], t1[:, jc, 0:255], xg[:, jc, 0:255], ALU.mult),
                         reads=[t1_tk[jc], xg_tk[jc]], writes=[t1_tk[jc]])
                    P.op("act", ACTF(t1[:, jc, 0:255], t1[:, jc, 0:255], AF.Sigmoid, scale=1.5957691),
                         reads=[t1_tk[jc]], writes=[t1_tk[jc]])
                    P.op("dve", TT(gl[:, jc, 0:255], t1[:, jc, 0:255], xg[:, jc, 0:255], ALU.mult),
                         reads=[t1_tk[jc], xg_tk[jc]], writes=[gl_tk[jc]])
                if which == 0:
                    for jc in range(2):
                        P.op("pe", MM(psb[7][:, 0:255], w2k[:, jc, :], gl[:, jc, 0:255], jc == 0, jc == 1),
                             reads=[w2k_tk, gl_tk[jc]], writes=[pst[7]])
                    P.op("act", ACTF(kcT[:, 0:255], psb[7][:, 0:255], AF.Copy), reads=[pst[7]], writes=[kcT_tk])
                else:
                    for ct in range(2):
                        ncr = 128 if ct == 0 else 127
                        for jc in range(2):
                            P.op("pe", MM(psb[7][0:ncr, 0:64], gl[:, jc, ct * 128:ct * 128 + ncr], w2v[:, jc, :], jc == 0, jc == 1),
                                 reads=[w2v_tk, gl_tk[jc]], writes=[pst[7]])
                        P.op("act", ACTF(vca[0:ncr, ct, 0:64], psb[7][0:ncr, 0:64], AF.Copy), reads=[pst[7]], writes=[vca_tk])
            P.barrier()
            top[0] = grp_top
            if DBG < 3:
                return

            tabc = alloc(6144)
            tabc_tk = tks(3)
            tabs2 = [alloc(2560) for _ in range(2)]
            tabs_tk2 = tks(2)
            tabw2 = [alloc(1408) for _ in range(2)]
            tabw_tk2 = tks(2)
            b31c2 = [alloc(1) for _ in range(2)]
            b31_tk2 = tks(2)
            vmk = a3(32 * 64, F32, pat="p (s f) -> p s f", s=32)
            amk = a3(32 * 64, F32, pat="p (s f) -> p s f", s=32)
            vmk_tk, amk_tk = Tk(), Tk()
            P.dma("sp", vmk, din["vmask"], writes=[vmk_tk])
            P.dma("sp", amk, din["amask"], writes=[amk_tk])
            sc = alloc(64)
            sc2 = alloc(64)
            m8a = alloc(8)
            m8b = alloc(8)
            nm = alloc(128)
            sc_tk, sc2_tk, m8a_tk, m8b_tk, nm_tk = Tk(), Tk(), Tk(), Tk(), Tk()
            gq = [alloc(3 * 512) for _ in range(2)]
            gq_tk = tks(2)
            ostg = [alloc(512) for _ in range(3)]
            ostg_tk = tks(3)
            lrow = [alloc(512) for _ in range(3)]
            lrow_tk = tks(3)
            facb = [alloc(512) for _ in range(3)]
            facb_tk = tks(3)
            Oac = [alloc(512) for _ in range(2)]
            Oac_tk = tks(2)
            ost = [alloc(512, BF16) for _ in range(2)]
            ost_tk = tks(2)
            step = [0]
            zcnt = [0]
            ocd_tk = [tks(8) for _ in range(4)]

            def load_tabc(h):
                src = bass.AP(tensor=grep_d.tensor, offset=h * 128 * NSA_M + (NSA_OFF - 2048),
                              ap=[[NSA_M - 16, 128], [1, 6144]])
                P.dma("sp", tabc, src, writes=tabc_tk)
                for pz in range(3):
                    P.op("act", ACTF(tabc[:, pz * 2048:(pz + 1) * 2048], tabc[:, pz * 2048:(pz + 1) * 2048], AF.Exp),
                         reads=[tabc_tk[pz]], writes=[tabc_tk[pz]])

            def fin_a(t):
                zb = zcnt[0] % 3
                zcnt[0] += 1
                t["zb"] = zb
                P.op("dve", CP(ostg[zb][0:65, :], psb[t["acc"]][0:65, :]), reads=[pst[t["acc"]]], writes=[ostg_tk[zb]])

            def fin_b(t):
                j, QB, zb = t["j"], t["QB"], t["zb"]
                gb_ = t["gb"]
                P.op("act", ACTF(lrow[zb][64:65, :], ostg[zb][64:65, :], AF.Ln, bias=V("eps_z")[64:65]),
                     reads=[ostg_tk[zb], vecs_tk], writes=[lrow_tk[zb]])
                P.op("act", ACTF(lrow[zb][64:65, :], lrow[zb][64:65, :], AF.Exp, scale=-1.0), reads=[lrow_tk[zb]], writes=[lrow_tk[zb]])
                P.op("dve", TT(lrow[zb][64:65, :], lrow[zb][64:65, :], gq[gb_][64:65, j * 512:(j + 1) * 512], ALU.mult),
                     reads=[lrow_tk[zb], gq_tk[gb_]], writes=[lrow_tk[zb]])
                P.dma("sp", fr_d[zb:zb + 1, :], lrow[zb][64:65, :], reads=[lrow_tk[zb]], writes=[frd_tk[zb]])
                P.dma("sp", facb[zb][0:64, :], bass.AP(tensor=fr_d.tensor, offset=zb * 512, ap=[[0, 64], [1, 512]]),
                      reads=[frd_tk[zb]], writes=[facb_tk[zb]])

            def fin_c(t):
                j, QB, zb = t["j"], t["QB"], t["zb"]
                qs = slice(QB * 512, (QB + 1) * 512)
                ob = t["ob"]
                r_, pr_, hf_ = t["r"], t["r"] // 2, t["r"] % 2
                if j == 0:
                    P.op("dve", TT(Oac[ob][0:64], ostg[zb][0:64, :], facb[zb][0:64, :], ALU.mult),
                         reads=[ostg_tk[zb], facb_tk[zb]], writes=[Oac_tk[ob]])
                    P.dma("sp", oc_d[r_, :, qs], Oac[ob][0:64], reads=[Oac_tk[ob]], writes=[ocd_tk[r_][QB]])
                else:
                    P.op("dve", TT(ostg[zb][0:64, :], ostg[zb][0:64, :], facb[zb][0:64, :], ALU.mult),
                         reads=[ostg_tk[zb], facb_tk[zb]], writes=[ostg_tk[zb]])
                    P.op("pool", TT(Oac[ob][0:64], Oac[ob][0:64], ostg[zb][0:64, :], ALU.add),
                         reads=[ostg_tk[zb], Oac_tk[ob]], writes=[Oac_tk[ob]])
                if j == 2:
                    P.op("dve", CP(ost[ob][0:64], Oac[ob][0:64]), reads=[Oac_tk[ob]], writes=[ost_tk[ob]])
                    P.dma("sp", oT_d[2 * g + pr_, hf_ * 64:(hf_ + 1) * 64, qs], ost[ob][0:64], reads=[ost_tk[ob]])

            def load_gq(h, QB, gb_):
                P.dma("sp", gq[gb_][64:65, :].rearrange("p (j t) -> p j t", j=3),
                      gT_d[h * 3:h * 3 + 3, QB * 512:(QB + 1) * 512].rearrange("(o j) t -> o j t", o=1),
                      writes=[gq_tk[gb_]])

            gcnt = [0]
            for r in range(4):
                h = 4 * g + r
                pr, hf = r // 2, r % 2
                b = r % 2
                P.dma("sp", qx[b][0:64], qkT_d[2 * g + pr, hf * 64:(hf + 1) * 64, :], writes=[qx_tk[b]])
                load_tabc(h)
                prev_t = None
                its = []
                for QB in range(8):
                    its.append(dict(QB=QB))

                def front(it):
                    QB = it["QB"]
                    qs = slice(QB * 512, (QB + 1) * 512)
                    cts = [0] if QB < 4 else [0, 1]
                    gb_ = gcnt[0] % 2
                    gcnt[0] += 1
                    load_gq(h, QB, gb_)
                    slots = []
                    for ct in cts:
                        sl = step[0] % 4
                        step[0] += 1
                        P.op("pe", MM(psb[sl][:, :], kcT[0:64, ct * 128:(ct + 1) * 128], qx[b][0:64, qs], True, True),
                             reads=[kcT_tk, qx_tk[b]], writes=[pst[sl]])
                        P.op("act", ACTF(Ef[sl], psb[sl][:, :], AF.Exp, scale=0.125), reads=[pst[sl]], writes=[Ef_tk[sl]])
                        sj = QB * 512 + 2017 - 2048 * ct
                        P.op("dve", TT(Eb[sl], Ef[sl], tabc[:, sj:sj + 512], ALU.mult),
                             reads=[Ef_tk[sl]] + tabc_tk, writes=[Eb_tk[sl]])
                        slots.append((ct, sl))
                    it["slots"] = slots
                    it["cts"] = cts
                    it["t"] = dict(j=0, QB=QB, acc=5, gb=gb_, ob=QB % 2, r=r)

                def back(it):
                    QB, slots, cts, t = it["QB"], it["slots"], it["cts"], it["t"]
                    for (ct, sl) in slots:
                        P.op("pe", MM(psb[5][0:65, :], vca[:, ct, 0:65], Eb[sl], ct == cts[0], ct == cts[-1]),
                             reads=[Eb_tk[sl], vca_tk], writes=[pst[5]])
                    fin_a(t)
                    for s in range(4):
                        for (ct, sl) in slots:
                            P.op("pe", MM(psb[4][:, s * 65:(s + 1) * 65], Eb[sl][:, s * 128:(s + 1) * 128], ovl1[:, ct, 0:65],
                                          ct == cts[0], ct == cts[-1]), reads=[Eb_tk[sl], ovl1_tk], writes=[pst[4]])
                    fin_b(t)
                    zs = QB % 2
                    z4 = sm[:, zs * 4:zs * 4 + 4].rearrange("p (s o) -> p s o", o=1)
                    pv4 = psb[4][:, 0:260].rearrange("p (s e) -> p s e", e=65)
                    blk = imp[:, QB * 4:QB * 4 + 4, :]
                    btk = [imp_tk[QB * 4 + s] for s in range(4)]
                    P.op("dve", TS(z4, pv4[:, :, 64:65], 1e-30, None, ALU.max), reads=[pst[4]], writes=[sm_tk[zs]])
                    P.op("dve", RCP(z4, z4), reads=[sm_tk[zs]], writes=[sm_tk[zs]])
                    zb4 = z4.broadcast_to([128, 4, 64])
                    if r == 0:
                        P.op("dve", TT(blk, pv4[:, :, 0:64], zb4, ALU.mult), reads=[pst[4], sm_tk[zs]], writes=btk)
                    else:
                        P.op("dve", TT(imt[zs], pv4[:, :, 0:64], zb4, ALU.mult), reads=[pst[4], sm_tk[zs]], writes=[imt_tk[zs]])
                        P.op("dve", TT(blk, blk, imt[zs], ALU.add), reads=[imt_tk[zs]] + btk, writes=btk)

                for ii in range(len(its) + 1):
                    if ii < len(its):
                        front(its[ii])
                    if ii >= 1:
                        back(its[ii - 1])
                        if prev_t is not None:
                            fin_c(prev_t)
                        prev_t = its[ii - 1]["t"]
                fin_c(prev_t)
            if DBG < 4:
                return
            W4 = 4
            scs = [alloc(64) for _ in range(W4)]
            sc2s = [alloc(64) for _ in range(W4)]
            m8as = [alloc(8) for _ in range(W4)]
            m8bs = [alloc(8) for _ in range(W4)]
            nms = [alloc(128) for _ in range(W4)]
            scs_tk, sc2s_tk, m8as_tk, m8bs_tk, nms_tk = tks(W4), tks(W4), tks(W4), tks(W4), tks(W4)

            def mk_max(o, i):
                return lambda e: e.max(out=o, in_=i)

            def mk_mr(o, a_, v):
                return lambda e: e.match_replace(out=o, in_to_replace=a_, in_values=v, imm_value=-1e30)
            for q0_ in range(0, 32, W4):
                qts = list(range(q0_, q0_ + W4))
                for w, qt in enumerate(qts):
                    P.op("dve", TT(scs[w], imp[:, qt, :], vmk[:, qt, :], ALU.mult), reads=[imp_tk[qt], vmk_tk], writes=[scs_tk[w]])
                for w, qt in enumerate(qts):
                    P.op("dve", TT(scs[w], scs[w], amk[:, qt, :], ALU.add), reads=[scs_tk[w], amk_tk], writes=[scs_tk[w]])
                for w, qt in enumerate(qts):
                    P.op("dve", mk_max(m8as[w], scs[w]), reads=[scs_tk[w]], writes=[m8as_tk[w]])
                for w, qt in enumerate(qts):
                    P.op("dve", mk_mr(sc2s[w], m8as[w], scs[w]), reads=[scs_tk[w], m8as_tk[w]], writes=[sc2s_tk[w]])
                for w, qt in enumerate(qts):
                    P.op("dve", mk_max(m8bs[w], sc2s[w]), reads=[sc2s_tk[w]], writes=[m8bs_tk[w]])
                for w, qt in enumerate(qts):
                    P.op("dve", TS(nms[w][:, 0:64], scs[w], m8bs[w][:, 7:8], -30000.0, ALU.is_lt, ALU.mult),
                         reads=[scs_tk[w], m8bs_tk[w]], writes=[nms_tk[w]])
                for w, qt in enumerate(qts):
                    P.op("dve", TS(nms[w][:, 64:128], scs[w], m8bs[w][:, 7:8], -30000.0, ALU.is_lt, ALU.mult),
                         reads=[scs_tk[w], m8bs_tk[w], nms_tk[w]], writes=[nms_tk[w]])
                for w, qt in enumerate(qts):
                    P.op("pe", TRN(psb[7][:, w * 128:(w + 1) * 128], nms[w], ident), reads=[nms_tk[w], ident_tk], writes=[pst[7]])
                for b in range(2):
                    P.op("act", ACTF(qx[b][64:128, q0_ * 128:(q0_ + W4) * 128], psb[7][64:128, 0:W4 * 128], AF.Copy),
                         reads=[pst[7]], writes=[nmq_tk[b][q0_ // 4]])
            if DBG < 5:
                return

            def load_tabs(r):
                h = 4 * g + r
                tb = r % 2
                pr, hf = r // 2, r % 2
                P.dma("sp", qx[tb][0:64], qkT_d[2 * g + pr, hf * 64:(hf + 1) * 64, :], writes=[qx_tk[tb]])
                src = bass.AP(tensor=grep_d.tensor, offset=h * 128 * NSA_M + (NSA_OFF - 384),
                              ap=[[NSA_M - 1, 128], [1, 2560]])
                P.dma("sp", tabs2[tb], src, writes=[tabs_tk2[tb]])
                src = bass.AP(tensor=gwrep_d.tensor, offset=h * 128 * NSAW_M + (NSAW_OFF - 384),
                              ap=[[NSAW_M - 1, 128], [1, 1408]])
                P.dma("sp", tabw2[tb], src, writes=[tabw_tk2[tb]])

            def exp_tabs(r):
                tb = r % 2
                P.op("act", ACTF(b31c2[tb], tabs2[tb][:, 2559:2560], AF.Copy), reads=[tabs_tk2[tb]], writes=[b31_tk2[tb]])
                P.op("act", ACTF(tabs2[tb], tabs2[tb], AF.Exp), reads=[tabs_tk2[tb], b31_tk2[tb]], writes=[tabs_tk2[tb]])
                P.op("act", ACTF(tabw2[tb], tabw2[tb], AF.Exp), reads=[tabw_tk2[tb]], writes=[tabw_tk2[tb]])

            load_tabs(0)
            exp_tabs(0)
            for r in range(4):
                h = 4 * g + r
                pr, hf = r // 2, r % 2
                b = r % 2
                tabs, tabs_tk, tabw, tabw_tk, b31c, b31_tk = tabs2[b], tabs_tk2[b], tabw2[b], tabw_tk2[b], b31c2[b], b31_tk2[b]
                if r + 1 < 4:
                    load_tabs(r + 1)
                tiles = []
                for QB in range(8):
                    qs = slice(QB * 512, (QB + 1) * 512)
                    q0 = QB * 512
                    for kt in range(4 * QB + 4):
                        off = min(QB * 512 - kt * 128, 1664) + 384
                        c0, c1 = 128 * max(0, kt - 4 * QB), 512
                        tiles.append(dict(j=1, QB=QB, first=kt == 0, last=kt == 4 * QB + 3, r=r, ob=QB % 2, c0=c0, c1=c1,
                                          lhsT=ksx[:, kt * 128:(kt + 1) * 128], lt=[ksx_tk, ex_tk], rhs=qx[b][:, q0 + c0:q0 + c1],
                                          rt=[qx_tk[b], nmq_tk[b][QB]],
                                          tab=tabs[:, off + c0:off + c1], tt=[tabs_tk], v=vs[:, kt, :], vt=[vs_tk], acc=5,
                                          far=(QB * 512 - kt * 128 >= 1664)))
                    k0 = max(0, 4 * QB - 4)
                    for kt in range(k0, 4 * QB + 4):
                        off = QB * 512 - kt * 128 + 384
                        if kt == k0:
                            c0, c1 = 0, 512
                        else:
                            c0 = 128 * max(0, kt - 4 * QB)
                            c1 = 128 * (min(3, kt - 4 * QB + 4) + 1)
                        tiles.append(dict(j=2, QB=QB, first=kt == k0, last=kt == 4 * QB + 3, r=r, ob=QB % 2, c0=c0, c1=c1,
                                          lhsT=kw[0:64, kt * 128:(kt + 1) * 128], lt=[kw_tk], rhs=qx[b][0:64, q0 + c0:q0 + c1], rt=[qx_tk[b]],
                                          tab=tabw[:, off + c0:off + c1], tt=[tabw_tk], v=vw[:, kt, :], vt=[vw_tk], acc=6, far=False))
                LOOK = 5
                pend = []
                n = len(tiles)
                lastQB = -1
                cur_gb = 0
                for i in range(n + LOOK):
                    if i == (n * 3) // 5 and r + 1 < 4:
                        exp_tabs(r + 1)
                    if i < n:
                        t = tiles[i]
                        if t["QB"] != lastQB:
                            lastQB = t["QB"]
                            cur_gb = gcnt[0] % 2
                            gcnt[0] += 1
                            load_gq(h, lastQB, cur_gb)
                            ob = lastQB % 2
                            P.dma("sp", Oac[ob][0:64], oc_d[r, :, lastQB * 512:(lastQB + 1) * 512], reads=[ocd_tk[r][lastQB]], writes=[Oac_tk[ob]])
                        t["gb"] = cur_gb
                        sl = step[0] % NSL
                        step[0] += 1
                        t["sl"] = sl
                        P.op("pe", MM(psb[SB[sl]][:, t["c0"]:t["c1"]], t["lhsT"], t["rhs"], True, True), reads=t["lt"] + t["rt"], writes=[pst[SB[sl]]])
                    jx_ = i - LOOK
                    if jx_ >= 0:
                        t = tiles[jx_]
                        sl = t["sl"]
                        c0, c1 = t["c0"], t["c1"]
                        if t["far"]:
                            P.op("act", ACTF(Eb[sl][:, c0:c1], psb[SB[sl]][:, c0:c1], AF.Exp, scale=0.125, bias=b31c), reads=[pst[SB[sl]], b31_tk], writes=[Eb_tk[sl]])
                        else:
                            P.op("act", ACTF(Ef[sl][:, c0:c1], psb[SB[sl]][:, c0:c1], AF.Exp, scale=0.125), reads=[pst[SB[sl]]], writes=[Ef_tk[sl]])
                            P.op("dve", TT(Eb[sl][:, c0:c1], Ef[sl][:, c0:c1], t["tab"], ALU.mult), reads=[Ef_tk[sl]] + list(t["tt"]), writes=[Eb_tk[sl]])
                        P.op("pe", MM(psb[t["acc"]][0:65, c0:c1], t["v"], Eb[sl][:, c0:c1], t["first"], t["last"]),
                             reads=[Eb_tk[sl]] + t["vt"], writes=[pst[t["acc"]]])
                        if t["last"]:
                            fin_a(t)
                            pend.append([i + 2, 0, t])
                    k = 0
                    while k < len(pend):
                        due, stage, t = pend[k]
                        if due <= i:
                            if stage == 0:
                                fin_b(t)
                                pend[k] = [i + 5, 1, t]
                                k += 1
                            else:
                                fin_c(t)
                                pend.pop(k)
                        else:
                            k += 1
                for due, stage, t in sorted(pend, key=lambda x: x[0]):
                    if stage == 0:
                        fin_b(t)
                    fin_c(t)

        def pass_nsa_out(jx, Xc):
            wo = a3(8 * 1024, BF16, pat="p (m f) -> p m f", m=8)
            wo_tk = tks(8)
            load_w(wo, din["nsa_wo_%d" % jx], 8, wo_tk)
            xn = [a3(8 * 512, F32, pat="p (c t) -> p c t", c=8) for _ in range(2)]
            xn_tk = [tks(8) for _ in range(2)]
            ot = [a3(8 * 512, BF16, pat="p (c t) -> p c t", c=8) for _ in range(2)]
            ot_tk = tks(2)

            def load(i):
                b = i % 2
                P.dma("sp", xn[b], xtile_ap(Xc, i), writes=xn_tk[b])
                P.dma("sp", ot[b], oT_d[:, :, i * 512:(i + 1) * 512].rearrange("m p t -> p m t"), writes=[ot_tk[b]])
            load(0)
            for i in range(NT):
                b = i % 2
                if i + 1 < NT:
                    load(i + 1)
                for m in range(8):
                    po = m % 2
                    for c in range(8):
                        P.op("pe", MM(psb[po][:, :], wo[:, m, c * 128:(c + 1) * 128], ot[b][:, c, :], c == 0, c == 7),
                             reads=[wo_tk[m], ot_tk[b]], writes=[pst[po]])
                    P.op("dve", TT(xn[b][:, m, :], xn[b][:, m, :], psb[po][:, :], ALU.add),
                         reads=[pst[po], xn_tk[b][m]], writes=[xn_tk[b][m]])
                P.dma("sp", xtile_ap(Xc, i), xn[b], reads=xn_tk[b])

        def pass_final(Xc, do_norm):
            xn = [a3(8 * 512, F32, pat="p (c t) -> p c t", c=8) for _ in range(2)]
            xn_tk = [tks(8) for _ in range(2)]
            sq = a3(2 * 512, BF16, pat="p (c t) -> p c t", c=2)
            sqtk = tks(2)
            rstd = alloc(512)
            rstd_tk = Tk()
            yo = [alloc(D) for _ in range(2)]
            yo_tk = tks(2)
            gain = V("final_norm")

            def load(i):
                b = i % 2
                P.dma("sp", xn[b], xtile_ap(Xc, i), writes=xn_tk[b])
            load(0)
            for i in range(NT):
                b = i % 2
                if i + 1 < NT:
                    load(i + 1)
                xt, xtk = xn[b], xn_tk[b]
                if do_norm:
                    for c in range(8):
                        sl = c % 2
                        P.op("act", ACTF(sq[:, sl, :], xt[:, c, :], AF.Square), reads=[xtk[c]], writes=[sqtk[sl]])
                        P.op("pe", MM(psb[6][:, :], ones_bf, sq[:, sl, :], c == 0, c == 7),
                             reads=[sqtk[sl], ones_tk], writes=[pst[6]])
                    P.op("act", ACTF(rstd, psb[6][:, :], AF.Sqrt, bias=V("eps_rms"), scale=1.0 / D),
                         reads=[pst[6], vecs_tk], writes=[rstd_tk])
                    P.op("dve", RCP(rstd, rstd), reads=[rstd_tk], writes=[rstd_tk])
                    for c in range(8):
                        P.op("dve", STT(xt[:, c, :], xt[:, c, :], gain[:, c:c + 1], rstd, ALU.mult, ALU.mult),
                             reads=[xtk[c], rstd_tk, vecs_tk], writes=[xtk[c]])
                for s in range(4):
                    yb_ = (i * 4 + s) % 2
                    for c in range(8):
                        bank = c // 4
                        P.op("pe", TRN(psb[bank][:, (c % 4) * 128:(c % 4 + 1) * 128], xt[:, c, s * 128:(s + 1) * 128], ident),
                             reads=[xtk[c], ident_tk], writes=[pst[bank]])
                    P.op("act", ACTF(yo[yb_][:, 0:512], psb[0][:, :], AF.Copy), reads=[pst[0]], writes=[yo_tk[yb_]])
                    P.op("dve", CP(yo[yb_][:, 512:1024], psb[1][:, :]), reads=[pst[1], yo_tk[yb_]], writes=[yo_tk[yb_]])
                    t0 = i * 512 + s * 128
                    P.dma("sp", out_d[t0:t0 + 128, :], yo[yb_], reads=[yo_tk[yb_]])

        def phase(fn, *a):
            top[0] = base_top
            fn(*a)
            P.barrier()

        phase(pass_input)
        has_nsa = any((l % 2 == 1) for l in layers) and "mix" in stages
        if has_nsa:
            phase(pass_nsa_tables)
        cur = 0
        for l in layers:
            jx = l // 2
            if "ffn1" in stages:
                a_, b_, c_ = cur, (cur + 1) % 3, (cur + 2) % 3
                phase(pass_ffn_half, l, 0, 0, X[a_], X[a_], X[b_], "ffn1_norm%d" % l)
                phase(pass_ffn_half, l, 0, 1, X[a_], X[b_], X[c_], "ffn1_norm%d" % l)
                cur = c_
            if "mix" in stages:
                if l % 2 == 0:
                    phase(pass_conv, l, jx, X[cur])
                else:
                    phase(pass_nsa_proj, l, jx, X[cur])
                    if DBG >= 2:
                        for g in range(4 if DBG >= 9 else 1):
                            phase(pass_nsa_group, jx, g)
                    if DBG >= 9:
                        phase(pass_nsa_out, jx, X[cur])
            if "ffn2" in stages:
                a_, b_, c_ = cur, (cur + 1) % 3, (cur + 2) % 3
                phase(pass_ffn_half, l, 1, 0, X[a_], X[a_], X[b_], "ffn2_norm%d" % l)
                phase(pass_ffn_half, l, 1, 1, X[a_], X[b_], X[c_], "ffn2_norm%d" % l)
                cur = c_
            if "ple" in stages:
                phase(pass_ple, l, X[cur])
        phase(pass_final, X[cur], do_final)
        for e_ in ENGS:
            P.op(e_, lambda e: e.nop())
        P.emit()
    return nc


import os
DBG = int(os.environ.get("NSA_DBG", "9"))


def run(inputs, n_cores=8, **bkw):
    shared, vidx = host_prepare(inputs)
    x = np.asarray(inputs["x"], np.float32)
    p = np.asarray(inputs["p"], np.float32)
    shapes = {k: v.shape for k, v in shared.items()}
    shapes["x"] = (S, D)
    shapes["p"] = (4, S, 256)
    nc = bass.Bass("TRN2", target_bir_lowering=False)
    build(nc, shapes, vidx, **bkw)
    in_maps = []
    for b in range(n_cores):
        m = dict(shared)
        m["x"] = np.ascontiguousarray(x[b])
        m["p"] = np.ascontiguousarray(p[:, b])
        in_maps.append(m)
    res = run_bass_kernel_spmd(nc, in_maps, core_ids=list(range(n_cores)))
    return np.stack([np.asarray(r["out"], np.float32) for r in res.results], axis=0)


def kernel(**inputs):
    return run(inputs, n_cores=8)
```
